# Optimizing a Trainium2 kernel written in Bass

```python
import jax, jax.numpy as jnp
from jax import lax
import numpy as np

D_MODEL = 2048
BATCH = 4
SEQ = 4096
DEPTH = 2

HEAD_DIM = 128
EPS = 1e-6
A_HEADS = 8
A_KV_HEADS = 2
A_GROUP = A_HEADS // A_KV_HEADS
A_WINDOW = 128
A_BLOCK = 128
B_HEADS = 4
GRID_W = 64
NB_ROWS_MAX = 8
NB_COLS = 16
C_HEADS = 4
C_Q_RANK = 512
C_KV_RANK = 256
C_NOPE = 128
C_ROPE = 64
C_V = 128
C_BLOCK = 128
ROPE_THETA = 10000.0
A_Q = A_HEADS * HEAD_DIM
A_KV = A_KV_HEADS * HEAD_DIM
B_W = B_HEADS * HEAD_DIM
C_OUT = C_HEADS * C_V
D_MIX = A_Q + B_W + C_OUT
IN_SPLITS = (A_Q, A_KV, A_KV, B_W, B_W, B_W, C_Q_RANK, C_KV_RANK, C_ROPE)
D_IN = A_Q + 2 * A_KV + 3 * B_W + C_Q_RANK + C_KV_RANK + C_ROPE
PEER_HEADS = 8
PEER_KEYS = 128
PEER_EXPERTS = PEER_KEYS * PEER_KEYS
PEER_KEY_DIM = 128
PEER_TOPK = 16
PEER_CHUNK = 128

kernel_name = 'hybrid_parallel_heads_peer_encoder'


def rms_norm(x, g):
    xf = x.astype(jnp.float32)
    y = xf * lax.rsqrt(jnp.mean(xf * xf, axis=-1, keepdims=True) + EPS)
    return (y * g.astype(jnp.float32)).astype(x.dtype)


def alibi_slopes(n):
    return jnp.asarray([2.0 ** (-8.0 * (h + 1) / n) for h in range(n)], dtype=jnp.float32)


def apply_rope(x):
    s_len, d = x.shape[1], x.shape[-1]
    half = d // 2
    inv = ROPE_THETA ** (-jnp.arange(half, dtype=jnp.float32) / half)
    ang = jnp.arange(s_len, dtype=jnp.float32)[:, None] * inv[None, :]
    cos = jnp.cos(ang)[None, :, None, :]
    sin = jnp.sin(ang)[None, :, None, :]
    x1 = x[..., :half].astype(jnp.float32)
    x2 = x[..., half:].astype(jnp.float32)
    return jnp.concatenate([x1 * cos - x2 * sin, x1 * sin + x2 * cos], axis=-1).astype(x.dtype)


def window_gqa_attention(q, k, v, sink):
    bsz, s_len = q.shape[0], q.shape[1]
    nb = s_len // A_BLOCK
    qb = q.reshape(bsz, nb, A_BLOCK, A_KV_HEADS, A_GROUP, HEAD_DIM)

    def band(t):
        tb = t.reshape(bsz, nb, A_BLOCK, A_KV_HEADS, HEAD_DIM)
        tp = jnp.pad(tb, ((0, 0), (1, 1), (0, 0), (0, 0), (0, 0)))
        return jnp.concatenate([tp[:, :-2], tp[:, 1:-1], tp[:, 2:]], axis=2)

    kb, vb = band(k), band(v)
    s = jnp.einsum('bnqkgd,bnskd->bnkgqs', qb, kb).astype(jnp.float32) * (HEAD_DIM ** -0.5)
    blk = jnp.arange(nb)[:, None, None] * A_BLOCK
    q_pos = blk + jnp.arange(A_BLOCK)[None, :, None]
    k_pos = blk + jnp.arange(3 * A_BLOCK)[None, None, :] - A_BLOCK
    dist = jnp.abs(q_pos - k_pos)
    valid = (dist <= A_WINDOW) & (k_pos >= 0) & (k_pos < s_len)
    slopes = alibi_slopes(A_HEADS).reshape(A_KV_HEADS, A_GROUP)
    s = s - slopes[None, None, :, :, None, None] * dist[None, :, None, None].astype(jnp.float32)
    s = jnp.where(valid[None, :, None, None], s, -jnp.inf)
    sk = sink.astype(jnp.float32).reshape(A_KV_HEADS, A_GROUP)[None, None, :, :, None, None]
    m = jnp.maximum(jnp.max(s, axis=-1, keepdims=True), sk)
    p = jnp.exp(s - m)
    p = p / (jnp.sum(p, axis=-1, keepdims=True) + jnp.exp(sk - m))
    o = jnp.einsum('bnkgqs,bnskd->bnqkgd', p.astype(v.dtype), vb)
    return o.reshape(bsz, s_len, A_Q)


def neighbourhood_attention(q, k, v, rel_bias):
    bsz, s_len = q.shape[0], q.shape[1]
    rows = s_len // GRID_W
    kr = min(NB_ROWS_MAX, rows)
    qg = jnp.moveaxis(q.reshape(bsz, rows, GRID_W, B_HEADS, HEAD_DIM), 1, 0)
    kg = k.reshape(bsz, rows, GRID_W, B_HEADS, HEAD_DIM)
    vg = v.reshape(bsz, rows, GRID_W, B_HEADS, HEAD_DIM)
    col = jnp.arange(GRID_W)
    c_start = jnp.clip(col - NB_COLS // 2, 0, GRID_W - NB_COLS)
    col_ok = (col[None, :] >= c_start[:, None]) & (col[None, :] < c_start[:, None] + NB_COLS)
    dc_idx = jnp.clip(col[None, :] - col[:, None] + NB_COLS - 1, 0, 2 * NB_COLS - 2)
    scale = HEAD_DIM ** -0.5

    def one_row(args):
        r, q_r = args
        r_start = jnp.clip(r - kr // 2, 0, rows - kr)
        k_r = lax.dynamic_slice_in_dim(kg, r_start, kr, axis=1)
        v_r = lax.dynamic_slice_in_dim(vg, r_start, kr, axis=1)
        dr_idx = r_start + jnp.arange(kr) - r + NB_ROWS_MAX - 1
        bias = rel_bias[:, dr_idx[None, :, None], dc_idx[:, None, :]].astype(jnp.float32)
        bias = jnp.where(col_ok[None, :, None, :], bias, -jnp.inf)
        s = jnp.einsum('bqhd,brkhd->bhqrk', q_r, k_r).astype(jnp.float32) * scale + bias[None]
        p = jax.nn.softmax(s.reshape(bsz, B_HEADS, GRID_W, kr * GRID_W), axis=-1)
        p = p.reshape(bsz, B_HEADS, GRID_W, kr, GRID_W).astype(v_r.dtype)
        return jnp.einsum('bhqrk,brkhd->bqhd', p, v_r)

    o = lax.map(one_row, (jnp.arange(rows), qg))
    return jnp.moveaxis(o, 0, 1).reshape(bsz, s_len, B_W)


def mla_attention(c_q, c_kv, k_rope, g_cq, g_ckv, w_uq, w_ukv):
    bsz, s_len = c_q.shape[0], c_q.shape[1]
    q = (rms_norm(c_q, g_cq) @ w_uq).reshape(bsz, s_len, C_HEADS, C_NOPE + C_ROPE)
    q_nope = q[..., :C_NOPE]
    q_pe = apply_rope(q[..., C_NOPE:])
    kv = (rms_norm(c_kv, g_ckv) @ w_ukv).reshape(bsz, s_len, C_HEADS, C_NOPE + C_V)
    k_nope = kv[..., :C_NOPE]
    v = kv[..., C_NOPE:]
    k_pe = apply_rope(k_rope[:, :, None, :])[:, :, 0]
    nb = s_len // C_BLOCK
    scale = (C_NOPE + C_ROPE) ** -0.5

    def to_blocks(t):
        return jnp.moveaxis(t.reshape(bsz, nb, C_BLOCK, *t.shape[2:]), 1, 0)

    def one_block(args):
        qn, qp = args
        s = (jnp.einsum('bqhd,bkhd->bhqk', qn, k_nope)
             + jnp.einsum('bqhd,bkd->bhqk', qp, k_pe)).astype(jnp.float32) * scale
        p = jax.nn.softmax(s, axis=-1).astype(v.dtype)
        return jnp.einsum('bhqk,bkhd->bqhd', p, v)

    o = lax.map(one_block, (to_blocks(q_nope), to_blocks(q_pe)))
    return jnp.moveaxis(o, 0, 1).reshape(bsz, s_len, C_OUT)


def hybrid_mixer(xn, w_in, a_sink, b_rel_bias, c_q_norm, c_kv_norm, c_w_uq, c_w_ukv, out_norm, w_o):
    bsz, s_len, _ = xn.shape
    h = xn @ w_in
    cuts = [int(c) for c in np.cumsum(IN_SPLITS)[:-1]]
    qa, ka, va, qb, kb, vb, cq, ckv, kr = jnp.split(h, cuts, axis=-1)
    o_a = window_gqa_attention(qa.reshape(bsz, s_len, A_HEADS, HEAD_DIM),
                               ka.reshape(bsz, s_len, A_KV_HEADS, HEAD_DIM),
                               va.reshape(bsz, s_len, A_KV_HEADS, HEAD_DIM), a_sink)
    o_b = neighbourhood_attention(qb.reshape(bsz, s_len, B_HEADS, HEAD_DIM),
                                  kb.reshape(bsz, s_len, B_HEADS, HEAD_DIM),
                                  vb.reshape(bsz, s_len, B_HEADS, HEAD_DIM), b_rel_bias)
    o_c = mla_attention(cq, ckv, kr, c_q_norm, c_kv_norm, c_w_uq, c_w_ukv)
    o = jnp.concatenate([rms_norm(o_a, out_norm[:A_Q]),
                         rms_norm(o_b, out_norm[A_Q:A_Q + B_W]),
                         rms_norm(o_c, out_norm[A_Q + B_W:])], axis=-1)
    return o @ w_o


def peer_ffn(x, w_q, sub_keys, u, vv):
    bsz, s_len, d = x.shape
    q = (x @ w_q).reshape(bsz, s_len, PEER_HEADS, 2, PEER_KEY_DIM // 2).astype(jnp.float32)
    s = jnp.einsum('bshcd,cnd->bshcn', q, sub_keys.astype(jnp.float32))
    top_s, top_i = lax.top_k(s, PEER_TOPK)
    cand_s = (top_s[..., 0, :, None] + top_s[..., 1, None, :]).reshape(bsz, s_len, PEER_HEADS, PEER_TOPK * PEER_TOPK)
    cand_i = (top_i[..., 0, :, None] * PEER_KEYS + top_i[..., 1, None, :]).reshape(bsz, s_len, PEER_HEADS, PEER_TOPK * PEER_TOPK)
    best_s, best_j = lax.top_k(cand_s, PEER_TOPK)
    idx = jnp.take_along_axis(cand_i, best_j, axis=-1)
    g = jax.nn.softmax(best_s, axis=-1).astype(x.dtype)
    n_tok = bsz * s_len
    n_sel = PEER_HEADS * PEER_TOPK
    xc = x.reshape(n_tok // PEER_CHUNK, PEER_CHUNK, d)
    ic = idx.reshape(n_tok // PEER_CHUNK, PEER_CHUNK, n_sel)
    gc = g.reshape(n_tok // PEER_CHUNK, PEER_CHUNK, n_sel)

    def chunk(args):
        xt, it, gt = args
        a = jnp.einsum('td,tkd->tk', xt, u[it])
        hsel = jax.nn.gelu(a) * gt
        return jnp.einsum('tk,tkd->td', hsel, vv[it])

    y = lax.map(chunk, (xc, ic, gc))
    return y.reshape(bsz, s_len, d)


def setup_inputs(seed: int = 0) -> dict:
    key = jax.random.key(seed)
    ks = jax.random.split(key, 17)
    f32 = jnp.float32

    def nrm(k, shape, scale):
        return jax.random.normal(k, shape, f32) * scale

    def gain(k, shape):
        return 1.0 + 0.05 * jax.random.normal(k, shape, f32)

    return {
        'x': nrm(ks[0], (BATCH, SEQ, D_MODEL), 1.0),
        'ln1': gain(ks[1], (DEPTH, D_MODEL)),
        'w_in': nrm(ks[2], (DEPTH, D_MODEL, D_IN), D_MODEL ** -0.5),
        'a_sink': nrm(ks[3], (DEPTH, A_HEADS), 1.0),
        'b_rel_bias': nrm(ks[4], (DEPTH, B_HEADS, 2 * NB_ROWS_MAX - 1, 2 * NB_COLS - 1), 0.5),
        'c_q_norm': gain(ks[5], (DEPTH, C_Q_RANK)),
        'c_kv_norm': gain(ks[6], (DEPTH, C_KV_RANK)),
        'c_w_uq': nrm(ks[7], (DEPTH, C_Q_RANK, C_HEADS * (C_NOPE + C_ROPE)), C_Q_RANK ** -0.5),
        'c_w_ukv': nrm(ks[8], (DEPTH, C_KV_RANK, C_HEADS * (C_NOPE + C_V)), C_KV_RANK ** -0.5),
        'out_norm': gain(ks[9], (DEPTH, D_MIX)),
        'w_o': nrm(ks[10], (DEPTH, D_MIX, D_MODEL), D_MIX ** -0.5),
        'ln2': gain(ks[11], (DEPTH, D_MODEL)),
        'peer_w_q': nrm(ks[12], (DEPTH, D_MODEL, PEER_HEADS * PEER_KEY_DIM), D_MODEL ** -0.5),
        'peer_sub_keys': nrm(ks[13], (DEPTH, 2, PEER_KEYS, PEER_KEY_DIM // 2), (PEER_KEY_DIM // 2) ** -0.5),
        'peer_u': nrm(ks[14], (DEPTH, PEER_EXPERTS, D_MODEL), D_MODEL ** -0.5),
        'peer_v': nrm(ks[15], (DEPTH, PEER_EXPERTS, D_MODEL), PEER_TOPK ** -0.5),
        'final_norm': gain(ks[16], (D_MODEL,)),
    }


def reference(x, ln1, w_in, a_sink, b_rel_bias, c_q_norm, c_kv_norm, c_w_uq, c_w_ukv, out_norm, w_o,
              ln2, peer_w_q, peer_sub_keys, peer_u, peer_v, final_norm):
    for l in range(DEPTH):
        x = x + hybrid_mixer(rms_norm(x, ln1[l]), w_in[l], a_sink[l], b_rel_bias[l], c_q_norm[l],
                             c_kv_norm[l], c_w_uq[l], c_w_ukv[l], out_norm[l], w_o[l])
        x = x + peer_ffn(rms_norm(x, ln2[l]), peer_w_q[l], peer_sub_keys[l], peer_u[l], peer_v[l])
    return rms_norm(x, final_norm)
```

```python
import numpy as np
from concourse.bass_utils import run_bass_kernel_spmd
from contextlib import ExitStack
import concourse.bass as bass
import concourse.mybir as mybir

F32 = mybir.dt.float32
BF16 = mybir.dt.bfloat16
AF = mybir.ActivationFunctionType
ALU = mybir.AluOpType
AX = mybir.AxisListType


class Res:
    __slots__ = ("name", "w", "r", "cin", "cout")

    def __init__(self, name):
        self.name = name
        self.w = {}
        self.r = {}
        self.cin = None
        self.cout = None


class Chan:
    __slots__ = ("sid", "cnt")

    def __init__(self, sid):
        self.sid = sid
        self.cnt = 0


class Eng:
    def __init__(self, name, sid):
        self.name = name
        self.sid = sid
        self.cnt = 0
        self.waited = {}
        self.prog = []


class KB:
    def __init__(self, nc):
        self.nc = nc
        self.es = ExitStack()
        self.sems = []
        self.engs = {}
        self.chans = []
        for k in ("pe", "act", "dve", "pool", "sp"):
            self.engs[k] = Eng(k, self.newsem("e_" + k))
        self.nres = 0
        self.pes = None
        self.ntens = 0
        self.free_ch = []
        self.phase_ch = []

    def newsem(self, name):
        s = self.es.enter_context(self.nc.semaphore("%s_%d" % (name, len(self.sems))))
        self.sems.append(s)
        return len(self.sems) - 1

    def res(self, name=None):
        self.nres += 1
        return Res(name or ("r%d" % self.nres))

    def sbuf(self, name, shape, dt):
        st = self.pes if self.pes is not None else self.es
        self.ntens += 1
        return st.enter_context(self.nc.sbuf_tensor("sb%d_%s" % (self.ntens, name), list(shape), dt))

    def psum(self, name, shape, dt):
        st = self.pes if self.pes is not None else self.es
        self.ntens += 1
        return st.enter_context(self.nc.psum_tensor("ps%d_%s" % (self.ntens, name), list(shape), dt))

    def begin_phase(self):
        self.pes = ExitStack()

    def end_phase(self):
        self.barrier()
        self.emit_block()
        self.pes.close()
        self.pes = None
        self.free_ch.extend(self.phase_ch)
        self.phase_ch = []

    def totals(self):
        t = {}
        for E in self.engs.values():
            t[E.sid] = E.cnt
        for ch in self.chans:
            t[ch.sid] = ch.cnt
        return t

    def barrier(self):
        tot = self.totals()
        for E in self.engs.values():
            waits = []
            for s, v in tot.items():
                if v > 0 and E.waited.get(s, 0) < v:
                    E.waited[s] = v
                    waits.append((s, v))
            E.prog.append((waits, None, None, 0))

    def _deps(self, E, reads, writes, same_ok):
        deps = {}
        for r in reads:
            for s, v in r.w.items():
                if deps.get(s, 0) < v:
                    deps[s] = v
        for w in writes:
            for s, v in w.w.items():
                if deps.get(s, 0) < v:
                    deps[s] = v
            for s, v in w.r.items():
                if deps.get(s, 0) < v:
                    deps[s] = v
        waits = []
        for s, v in deps.items():
            if s == E.sid and same_ok:
                continue
            if E.waited.get(s, 0) >= v:
                continue
            E.waited[s] = v
            waits.append((s, v))
        return waits

    def op(self, eng, fn, reads=(), writes=()):
        E = self.engs[eng]
        waits = self._deps(E, reads, writes, same_ok=(eng == "pe"))
        E.cnt += 1
        c = E.cnt
        E.prog.append((waits, fn, E.sid, 1))
        for r in reads:
            r.r[E.sid] = c
        for w in writes:
            w.w[E.sid] = c

    def dma(self, q, out, in_, reads=(), writes=(), chan_res=None, **kw):
        E = self.engs[q]
        waits = self._deps(E, reads, writes, same_ok=False)
        cr = chan_res
        if cr.cin is None:
            if self.free_ch:
                cr.cin = self.free_ch.pop()
            else:
                cr.cin = Chan(self.newsem("c"))
                self.chans.append(cr.cin)
            if self.pes is not None:
                self.phase_ch.append(cr.cin)
        ch = cr.cin
        ch.cnt += 16
        c = ch.cnt
        E.prog.append((waits, (lambda e, out=out, in_=in_, kw=kw: e.dma_start(out=out, in_=in_, **kw)), ch.sid, 16))
        for r in reads:
            r.r[ch.sid] = c
        for w in writes:
            w.w[ch.sid] = c

    def finish_wait(self, eng, resources):
        E = self.engs[eng]
        waits = self._deps(E, resources, (), same_ok=False)
        E.prog.append((waits, None, None, 0))

    def emit(self):
        self.emit_block()
        self.es.close()

    def emit_block(self):
        nc = self.nc
        sems = self.sems
        with nc.Block() as block:
            def run(E):
                def body(e):
                    for waits, fn, sid, inc in E.prog:
                        for s, v in waits:
                            e.wait_ge(sems[s], v)
                        if fn is not None:
                            ins = fn(e)
                            ins.then_inc(sems[sid], inc)
                return body
            block.tensor(run(self.engs["pe"]))
            block.scalar(run(self.engs["act"]))
            block.vector(run(self.engs["dve"]))
            block.gpsimd(run(self.engs["pool"]))
            block.sync(run(self.engs["sp"]))
        for E in self.engs.values():
            E.prog = []


D = 2048
KC = 16
DIN = 3904
EPS = 1e-6


class Pool:
    def __init__(self, kb, name, shape, dt, n, space="sbuf"):
        self.items = []
        for i in range(n):
            t = (kb.sbuf if space == "sbuf" else kb.psum)("%s%d" % (name, i), shape, dt)
            self.items.append((t, kb.res("%s%d" % (name, i))))
        self.i = 0

    def next(self):
        it = self.items[self.i % len(self.items)]
        self.i += 1
        return it


class Ctx:
    pass


def build(S, L, consts, debug=False, upto=99):
    nc = bass.Bass("TRN2", target_bir_lowering=False)
    kb = KB(nc)
    NT = S // 128
    NB = S // 512
    dbgkind = "ExternalOutput" if debug else "Internal"

    def din(name, shape, dt=F32):
        return nc.dram_tensor(name, list(shape), dt, kind="ExternalInput").ap()

    def dscr(name, shape, dt=BF16, kind=None):
        return nc.dram_tensor(name, list(shape), dt, kind=kind or dbgkind).ap()

    x_in = din("x", [S, D])
    w_in = din("w_in", [L, D, DIN])
    gains = din("gains", [L, 128, 64])
    ident_in = din("ident", [128, 128])
    cs_in = din("cossin", [2, 64, S])
    y_out = nc.dram_tensor("y", [S, D], F32, kind="ExternalOutput").ap()

    HT = dscr("HT", [31, 128, S])
    VA = dscr("VA", [S, 2, 129])
    VB = dscr("VB", [S, 4, 129])
    r_HT = kb.res("HT"); r_VA = kb.res("VA"); r_VB = kb.res("VB")
    r_x = kb.res("x_in"); r_w_in = kb.res("w_in"); r_const = kb.res("const")

    ident32 = kb.sbuf("ident32", [128, 128], F32)
    ident = kb.sbuf("ident", [128, 128], BF16)
    gn = kb.sbuf("gn", [128, L, 64], F32)
    gneg = kb.sbuf("gneg", [128, L, 64], F32)
    epsb = kb.sbuf("epsb", [128, 1], F32)
    r_id = kb.res("ident"); r_gn = kb.res("gn")
    kb.dma("sp", ident32[:], ident_in, reads=[r_const], writes=[r_id], chan_res=r_id)
    kb.op("dve", lambda e: e.tensor_copy(out=ident[:], in_=ident32[:]), reads=[r_id], writes=[r_id])
    kb.dma("sp", gn[:], gains.rearrange("l p c -> p l c"), reads=[r_const], writes=[r_gn], chan_res=r_gn)
    kb.op("dve", lambda e: e.tensor_scalar(out=gneg[:], in0=gn[:], scalar1=-1.0, scalar2=None, op0=ALU.mult),
          reads=[r_gn], writes=[r_gn])
    kb.op("dve", lambda e: e.memset(epsb[:], EPS), writes=[r_gn])

    NTAB = consts["_ntab"]
    schedB = consts["_schedB"]
    rot_in = din("rotR", [64, 64])
    tabA_in = din("tabA", [128, 3072])
    maskB_in = din("maskB", [128, NTAB, 128])
    biasB_in = din("biasB", [L, 128, NTAB, 4, 128])
    sink_in = din("sinkb", [128, L, 8])
    w_uq = din("c_w_uq", [L, 512, 768])
    w_ukv = din("c_w_ukv", [L, 256, 1024])
    w_o = din("w_o", [L, D, D])
    fnorm_in = din("fnorm", [128, D])
    w_pq = din("peer_w_q", [L, D, 1024])
    sk_in = din("sk", [L, 128, 256])
    peer_u = din("peer_u", [L, 16384, D])
    peer_v = din("peer_v", [L, 16384, D])
    ln2b_in = din("ln2b", [L, 128, D])
    TBT_MAX = 8
    r_win2 = kb.res("win2")

    QCN = dscr("QCN", [4, 128, S]); QCP = dscr("QCP", [4, 64, S]); KCN = dscr("KCN", [4, 128, S]); KCP = dscr("KCP", [64, S])
    VC = dscr("VC", [S, 4, 129])
    O = dscr("O", [S, D], F32)
    XA = dscr("XA", [S, D], F32); XB = dscr("XB", [S, D], F32)
    XN2T = dscr("XN2T", [KC, 128, S])
    QT = dscr("QT", [8, 128, S])
    GT = dscr("GT", [NT, 128, 128, 128])
    r_QT = kb.res("QT"); r_GT = kb.res("GT"); r_y = kb.res("y")
    r_QCN = kb.res("QCN"); r_QCP = kb.res("QCP"); r_KCN = kb.res("KCN"); r_KCP = kb.res("KCP"); r_VC = kb.res("VC")
    r_O = kb.res("O"); r_XA = kb.res("XA"); r_XB = kb.res("XB"); r_XN2T = kb.res("XN2T")

    rot32 = kb.sbuf("rot32", [64, 64], F32)
    rotb = kb.sbuf("rotb", [64, 64], BF16)
    onesb = kb.sbuf("onesb", [128, 128], BF16)
    esink = kb.sbuf("esink", [128, L, 8], F32)
    kb.dma("sp", rot32[:], rot_in, reads=[r_const], writes=[r_id], chan_res=r_id)
    kb.op("dve", lambda e: e.tensor_copy(out=rotb[:], in_=rot32[:]), reads=[r_id], writes=[r_id])
    kb.op("dve", lambda e: e.memset(onesb[:], 1.0), writes=[r_id])
    kb.dma("sp", esink[:], sink_in, reads=[r_const], writes=[r_gn], chan_res=r_gn)
    kb.op("act", lambda e: e.activation(out=esink[:], in_=esink[:], func=AF.Exp), reads=[r_gn], writes=[r_gn])

    SC_A = 128 ** -0.5
    SC_C = 192 ** -0.5

    def mk_norm_helpers(st_pool, sq_pool, ps_tp):
        def rmsnorm_tile(xt, r_xt, groups, xn, r_xn, gain=None):
            stt, r_st = st_pool.next()
            sq, r_sq = sq_pool.next()
            ng = len(groups)
            for g, (s0, wd) in enumerate(groups):
                kb.op("act", lambda e, s0=s0, wd=wd, g=g: e.activation(out=sq[:, s0:s0 + wd], in_=xt[:, s0:s0 + wd], func=AF.Square,
                                                                       accum_out=stt[:, g:g + 1]),
                      reads=[r_xt], writes=[r_sq, r_st])
            for g, (s0, wd) in enumerate(groups):
                kb.op("act", lambda e, g=g, wd=wd: e.activation(out=stt[:, g:g + 1], in_=stt[:, g:g + 1], func=AF.Ln,
                                                                bias=epsb[:, 0:1], scale=1.0 / wd),
                      reads=[r_st, r_gn], writes=[r_st])
            kb.op("act", lambda e: e.activation(out=stt[:, 0:ng], in_=stt[:, 0:ng], func=AF.Exp, scale=-0.5),
                  reads=[r_st], writes=[r_st])
            for g, (s0, wd) in enumerate(groups):
                if gain is None:
                    kb.op("dve", lambda e, s0=s0, wd=wd, g=g: e.tensor_scalar(out=xn[:, s0:s0 + wd], in0=xt[:, s0:s0 + wd],
                                                                              scalar1=stt[:, g:g + 1], scalar2=None, op0=ALU.mult),
                          reads=[r_xt, r_st], writes=[r_xn])
                else:
                    gt_, r_gt_ = gain
                    kb.op("dve", lambda e, s0=s0, wd=wd, g=g: e.scalar_tensor_tensor(out=xn[:, s0:s0 + wd], in0=xt[:, s0:s0 + wd],
                                                                                     scalar=stt[:, g:g + 1], in1=gt_[:, s0:s0 + wd],
                                                                                     op0=ALU.mult, op1=ALU.mult),
                          reads=[r_xt, r_st, r_gt_], writes=[r_xn])

        def transpose_tile(xn, r_xn, dst, r_dst, j):
            pt, r_pt = ps_tp.next()
            for k in range(KC):
                kb.op("pe", lambda e, k=k: e.transpose(out=pt[:, k, :], in_=xn[:, k * 128:(k + 1) * 128], identity=ident[:]),
                      reads=[r_xn, r_id], writes=[r_pt])
            kb.op("act", lambda e: e.copy(out=dst[:, :, j * 128:(j + 1) * 128], in_=pt[:]), reads=[r_pt], writes=[r_dst])
        return rmsnorm_tile, transpose_tile

    def load_w(wst_pool, src_ap, gcol, ncols, l, dst, r_dst, r_src):
        nk = src_ap.shape[0] // 128
        for k in range(nk):
            st, r_st = wst_pool.next()
            kb.dma("sp", st[:, :ncols], src_ap[k * 128:(k + 1) * 128, :], reads=[r_src], writes=[r_st], chan_res=r_st)
            if gcol is None:
                kb.op("act", lambda e, k=k, st=st: e.copy(out=dst[:, k, :ncols], in_=st[:, :ncols]), reads=[r_st], writes=[r_dst])
            else:
                kb.op("act", lambda e, k=k, st=st: e.activation(out=dst[:, k, :ncols], in_=st[:, :ncols], func=AF.Copy,
                                                                scale=gn[:, l, gcol + k:gcol + k + 1]),
                      reads=[r_st, r_gn], writes=[r_dst])

    def phase1(l, x_src, r_xsrc):
        kb.begin_phase()
        xt_pool = Pool(kb, "xt", [128, D], F32, 2)
        xn_pool = Pool(kb, "xn", [128, D], BF16, 2)
        sq_pool = Pool(kb, "sq", [128, D], BF16, 1)
        st_pool = Pool(kb, "stat", [128, 4], F32, 4)
        xnT_pool = Pool(kb, "xnT", [128, KC, 512], BF16, 2)
        ps_tp = Pool(kb, "tp", [128, KC, 128], BF16, 2, space="psum")
        ps_mm = Pool(kb, "mm", [128, 512], F32, 4, space="psum")
        wst_pool = Pool(kb, "wst", [128, 2048], F32, 2)
        wbf = kb.sbuf("wbf", [128, KC, 2048], BF16)
        r_wbf = kb.res("wbf")
        ev_pool = Pool(kb, "ev", [128, 512], BF16, 4)
        vst_pool = Pool(kb, "vst", [128, 4, 129], BF16, 2)
        for t, r in vst_pool.items:
            kb.op("pool", lambda e, t=t: e.memset(t[:], 1.0), writes=[r])
        rmsnorm_tile, transpose_tile = mk_norm_helpers(st_pool, sq_pool, ps_tp)
        passes = [
            dict(c0=0, fm=[(c, c * 128, 128) for c in range(0, 10)] + [(c, c * 128, 128) for c in range(12, 16)],
                 tm=[(VA, r_VA, 1280, 2)]),
            dict(c0=2048, fm=[(c, c * 128 - 2048, 128) for c in range(16, 20)] + [(c, c * 128 - 2048, 128) for c in range(24, 30)]
                 + [(30, 30 * 128 - 2048, 64)],
                 tm=[(VB, r_VB, 2560 - 2048, 4)]),
        ]
        for ps in passes:
            c0 = ps["c0"]
            ncols = min(2048, DIN - c0)
            load_w(wst_pool, w_in[l, :, c0:c0 + ncols], 0, ncols, l, wbf, r_wbf, r_w_in)
            for b in range(NB):
                xnT, r_xnT = xnT_pool.next()
                for j in range(4):
                    tok = b * 512 + j * 128
                    xt, r_xt = xt_pool.next()
                    kb.dma("sp", xt[:], x_src[tok:tok + 128, :], reads=[r_xsrc], writes=[r_xt], chan_res=r_xt)
                    xn, r_xn = xn_pool.next()
                    rmsnorm_tile(xt, r_xt, [(0, D)], xn, r_xn)
                    transpose_tile(xn, r_xn, xnT, r_xnT, j)
                for (c, off, m) in ps["fm"]:
                    pm, r_pm = ps_mm.next()
                    for k in range(KC):
                        kb.op("pe", lambda e, k=k, off=off, m=m, pm=pm, xnT=xnT: e.matmul(
                            pm[0:m, :], lhsT=wbf[:, k, off:off + m], rhs=xnT[:, k, :], start=(k == 0), stop=(k == KC - 1)),
                            reads=[r_wbf, r_xnT], writes=[r_pm])
                    ev, r_ev = ev_pool.next()
                    kb.op("dve", lambda e, m=m, pm=pm, ev=ev: e.tensor_copy(out=ev[0:m, :], in_=pm[0:m, :]),
                          reads=[r_pm], writes=[r_ev])
                    kb.dma("pool", HT[c, 0:m, b * 512:(b + 1) * 512], ev[0:m, :], reads=[r_ev], writes=[r_HT], chan_res=r_ev)
                for (dst, r_dstd, off, nh) in ps["tm"]:
                    for j in range(4):
                        tok = b * 512 + j * 128
                        pm, r_pm = ps_mm.next()
                        for k in range(KC):
                            kb.op("pe", lambda e, k=k, off=off, nh=nh, pm=pm, xnT=xnT, j=j: e.matmul(
                                pm[:, 0:nh * 128], lhsT=xnT[:, k, j * 128:(j + 1) * 128], rhs=wbf[:, k, off:off + nh * 128],
                                start=(k == 0), stop=(k == KC - 1)),
                                reads=[r_wbf, r_xnT], writes=[r_pm])
                        vs, r_vs = vst_pool.next()
                        kb.op("act", lambda e, nh=nh, pm=pm, vs=vs: e.copy(
                            out=vs[:, 0:nh, 0:128], in_=pm[:, 0:nh * 128].rearrange("p (h d) -> p h d", d=128)),
                            reads=[r_pm], writes=[r_vs])
                        kb.dma("pool", dst[tok:tok + 128, :, :], vs[:, 0:nh, :], reads=[r_vs], writes=[r_dstd], chan_res=r_vs)
        kb.end_phase()

    def phase1c(l):
        kb.begin_phase()
        wst_pool = Pool(kb, "wst", [128, 1024], F32, 2)
        wuq = kb.sbuf("wuq", [128, 4, 768], BF16)
        wukv = kb.sbuf("wukv", [128, 2, 1024], BF16)
        r_wu = kb.res("wu")
        load_w(wst_pool, w_uq[l], 16, 768, l, wuq, r_wu, r_win2)
        load_w(wst_pool, w_ukv[l], 20, 1024, l, wukv, r_wu, r_win2)
        cin_pool = Pool(kb, "cin", [128, 6, 512], BF16, 2)
        krin_pool = Pool(kb, "krin", [64, 512], BF16, 2)
        cs_pool = Pool(kb, "cs", [64, 2, 512], F32, 2)
        sqc_pool = Pool(kb, "sqc", [128, 6, 512], BF16, 1)
        rstd_pool = Pool(kb, "rstd", [128, 2, 512], F32, 1)
        cn_pool = Pool(kb, "cn", [128, 6, 512], BF16, 2)
        ps_mm = Pool(kb, "mm", [128, 512], F32, 6, space="psum")
        ev_pool = Pool(kb, "ev", [128, 512], BF16, 4)
        qpe_pool = Pool(kb, "qpe", [64, 512], BF16, 2)
        t1_pool = Pool(kb, "t1", [64, 512], F32, 2)
        t2_pool = Pool(kb, "t2", [64, 512], F32, 2)
        vst_pool = Pool(kb, "vst", [128, 4, 129], BF16, 2)
        for t, r in vst_pool.items:
            kb.op("pool", lambda e, t=t: e.memset(t[:], 1.0), writes=[r])

        def rope_store(src, r_src, cs, r_cs, dst_ap, r_dst):
            pr, r_pr = ps_mm.next()
            kb.op("pe", lambda e: e.matmul(pr[0:64, :], lhsT=rotb[:], rhs=src[:], start=True, stop=True),
                  reads=[r_src, r_id], writes=[r_pr])
            t1, r_t1 = t1_pool.next()
            t2, r_t2 = t2_pool.next()
            kb.op("dve", lambda e: e.tensor_tensor(out=t1[:], in0=src[:], in1=cs[:, 0, :], op=ALU.mult),
                  reads=[r_src, r_cs], writes=[r_t1])
            kb.op("dve", lambda e: e.tensor_tensor(out=t2[:], in0=pr[0:64, :], in1=cs[:, 1, :], op=ALU.mult),
                  reads=[r_pr, r_cs], writes=[r_t2])
            ev, r_ev = ev_pool.next()
            kb.op("dve", lambda e: e.tensor_tensor(out=ev[0:64, :], in0=t1[:], in1=t2[:], op=ALU.add),
                  reads=[r_t1, r_t2], writes=[r_ev])
            kb.dma("pool", dst_ap, ev[0:64, :], reads=[r_ev], writes=[r_dst], chan_res=r_ev)

        for b in range(NB):
            bs = slice(b * 512, (b + 1) * 512)
            cin, r_cin = cin_pool.next()
            kb.dma("sp", cin[:], HT[24:30, :, bs].rearrange("c p t -> p c t"), reads=[r_HT], writes=[r_cin], chan_res=r_cin)
            krin, r_krin = krin_pool.next()
            kb.dma("sp", krin[:], HT[30, 0:64, bs], reads=[r_HT], writes=[r_krin], chan_res=r_krin)
            cs, r_cs = cs_pool.next()
            kb.dma("sp", cs[:], cs_in[:, :, bs].rearrange("c p t -> p c t"), reads=[r_const], writes=[r_cs], chan_res=r_cs)
            sqc, r_sqc = sqc_pool.next()
            kb.op("act", lambda e, sqc=sqc, cin=cin: e.activation(out=sqc[:], in_=cin[:], func=AF.Square), reads=[r_cin], writes=[r_sqc])
            rstd, r_rstd = rstd_pool.next()
            cn, r_cn = cn_pool.next()
            for gi, (c0, nchunk) in enumerate([(0, 4), (4, 2)]):
                pss, r_pss = ps_mm.next()
                for k in range(nchunk):
                    kb.op("pe", lambda e, k=k, c0=c0, nchunk=nchunk, pss=pss, sqc=sqc: e.matmul(
                        pss[:], lhsT=onesb[:], rhs=sqc[:, c0 + k, :], start=(k == 0), stop=(k == nchunk - 1)),
                        reads=[r_sqc, r_id], writes=[r_pss])
                kb.op("act", lambda e, gi=gi, nchunk=nchunk, pss=pss, rstd=rstd: e.activation(
                    out=rstd[:, gi, :], in_=pss[:], func=AF.Ln, bias=epsb[:, 0:1], scale=1.0 / (nchunk * 128)),
                    reads=[r_pss, r_gn], writes=[r_rstd])
                kb.op("act", lambda e, gi=gi, rstd=rstd: e.activation(out=rstd[:, gi, :], in_=rstd[:, gi, :], func=AF.Exp, scale=-0.5),
                      reads=[r_rstd], writes=[r_rstd])
                for k in range(nchunk):
                    kb.op("dve", lambda e, k=k, c0=c0, gi=gi, cn=cn, cin=cin, rstd=rstd: e.tensor_tensor(
                        out=cn[:, c0 + k, :], in0=cin[:, c0 + k, :], in1=rstd[:, gi, :], op=ALU.mult),
                        reads=[r_cin, r_rstd], writes=[r_cn])
            for h in range(4):
                pm, r_pm = ps_mm.next()
                for k in range(4):
                    kb.op("pe", lambda e, k=k, h=h, pm=pm, cn=cn: e.matmul(
                        pm[:], lhsT=wuq[:, k, h * 192:h * 192 + 128], rhs=cn[:, k, :], start=(k == 0), stop=(k == 3)),
                        reads=[r_wu, r_cn], writes=[r_pm])
                ev, r_ev = ev_pool.next()
                kb.op("act", lambda e, pm=pm, ev=ev: e.copy(out=ev[:], in_=pm[:]), reads=[r_pm], writes=[r_ev])
                kb.dma("pool", QCN[h, :, bs], ev[:], reads=[r_ev], writes=[r_QCN], chan_res=r_ev)
                pm, r_pm = ps_mm.next()
                for k in range(4):
                    kb.op("pe", lambda e, k=k, h=h, pm=pm, cn=cn: e.matmul(
                        pm[0:64, :], lhsT=wuq[:, k, h * 192 + 128:h * 192 + 192], rhs=cn[:, k, :], start=(k == 0), stop=(k == 3)),
                        reads=[r_wu, r_cn], writes=[r_pm])
                qpe, r_qpe = qpe_pool.next()
                kb.op("act", lambda e, pm=pm, qpe=qpe: e.copy(out=qpe[:], in_=pm[0:64, :]), reads=[r_pm], writes=[r_qpe])
                rope_store(qpe, r_qpe, cs, r_cs, QCP[h, :, bs], r_QCP)
                pm, r_pm = ps_mm.next()
                for k in range(2):
                    kb.op("pe", lambda e, k=k, h=h, pm=pm, cn=cn: e.matmul(
                        pm[:], lhsT=wukv[:, k, h * 256:h * 256 + 128], rhs=cn[:, 4 + k, :], start=(k == 0), stop=(k == 1)),
                        reads=[r_wu, r_cn], writes=[r_pm])
                ev, r_ev = ev_pool.next()
                kb.op("act", lambda e, pm=pm, ev=ev: e.copy(out=ev[:], in_=pm[:]), reads=[r_pm], writes=[r_ev])
                kb.dma("pool", KCN[h, :, bs], ev[:], reads=[r_ev], writes=[r_KCN], chan_res=r_ev)
            for j in range(4):
                tok = b * 512 + j * 128
                pm, r_pm = ps_mm.next()
                for k in range(2):
                    kb.op("pe", lambda e, k=k, j=j, pm=pm, cn=cn: e.matmul(
                        pm[:].rearrange("p (h d) -> p h d", d=128), lhsT=cn[:, 4 + k, j * 128:(j + 1) * 128],
                        rhs=wukv[:, k, :].rearrange("p (h c) -> p h c", c=256)[:, :, 128:256], start=(k == 0), stop=(k == 1)),
                        reads=[r_wu, r_cn], writes=[r_pm])
                vs, r_vs = vst_pool.next()
                kb.op("act", lambda e, pm=pm, vs=vs: e.copy(out=vs[:, :, 0:128], in_=pm[:].rearrange("p (h d) -> p h d", d=128)),
                      reads=[r_pm], writes=[r_vs])
                kb.dma("pool", VC[tok:tok + 128, :, :], vs[:], reads=[r_vs], writes=[r_VC], chan_res=r_vs)
            rope_store(krin, r_krin, cs, r_cs, KCP[:, bs], r_KCP)
        kb.end_phase()

    def attn_block(P, kts, s_mm, tab, v_rhs, scale, finals):
        nk = len(kts)
        for i, kt in enumerate(kts):
            ps, r_ps = P.ps_s.next()
            s_mm(kt, ps, r_ps)
            pt, r_pt = P.pt_pool.next()
            tb = tab(kt)
            if tb is None:
                kb.op("act", lambda e, ps=ps, pt=pt: e.activation(out=pt[:], in_=ps[:], func=AF.Exp, scale=scale),
                      reads=[r_ps], writes=[r_pt])
            else:
                ex, r_ex = P.ex_pool.next()
                kb.op("act", lambda e, ps=ps, ex=ex: e.activation(out=ex[:], in_=ps[:], func=AF.Exp, scale=scale),
                      reads=[r_ps], writes=[r_ex])
                kb.op("dve", lambda e, ex=ex, pt=pt, tb=tb: e.tensor_tensor(out=pt[:], in0=ex[:], in1=tb, op=ALU.mult),
                      reads=[r_ex, P.r_tab], writes=[r_pt])
            for g in range(4):
                po, r_po = P.ps_o[g]
                kb.op("pe", lambda e, g=g, po=po, pt=pt, kt=kt, i=i: e.matmul(
                    po[:, 0:129], lhsT=pt[:, g * 128:(g + 1) * 128], rhs=v_rhs(kt, g), start=(i == 0), stop=(i == nk - 1)),
                    reads=[r_pt, P.r_v], writes=[r_po])
        for (g, out_ap, r_out, extra) in finals:
            po, r_po = P.ps_o[g]
            dn, r_dn = P.den_pool.next()
            if extra is not None:
                kb.op("dve", lambda e, po=po, dn=dn, extra=extra: e.tensor_scalar(out=dn[:, 0:1], in0=po[:, 128:129], scalar1=extra,
                                                                                 scalar2=None, op0=ALU.add),
                      reads=[r_po, r_gn], writes=[r_dn])
                kb.op("dve", lambda e, dn=dn: e.reciprocal(out=dn[:, 1:2], in_=dn[:, 0:1]), reads=[r_dn], writes=[r_dn])
            else:
                kb.op("dve", lambda e, po=po, dn=dn: e.reciprocal(out=dn[:, 1:2], in_=po[:, 128:129]), reads=[r_po], writes=[r_dn])
            kb.op("dve", lambda e, po=po, dn=dn, out_ap=out_ap: e.tensor_scalar(out=out_ap, in0=po[:, 0:128], scalar1=dn[:, 1:2],
                                                                               scalar2=None, op0=ALU.mult),
                  reads=[r_po, r_dn], writes=[r_out])

    def attn_pools():
        P = Ctx()
        P.ps_s = Pool(kb, "pss", [128, 512], F32, 2, space="psum")
        P.ps_o = [(kb.psum("pso%d" % g, [128, 512], F32), kb.res("pso%d" % g)) for g in range(4)]
        P.pt_pool = Pool(kb, "pt", [128, 512], BF16, 3)
        P.ex_pool = Pool(kb, "ex", [128, 512], BF16, 2)
        P.den_pool = Pool(kb, "den", [128, 2], F32, 4)
        P.r_tab = kb.res("tab")
        P.r_v = kb.res("vres")
        return P

    def phase2a(l):
        kb.begin_phase()
        P = attn_pools()
        QA = kb.sbuf("QA", [128, 8, S], BF16)
        KA = kb.sbuf("KA", [128, 2, S], BF16)
        VAs = kb.sbuf("VAs", [128, NT, 2, 129], BF16)
        tab32 = kb.sbuf("tab32", [128, 3072], F32)
        tabA = kb.sbuf("tabAb", [128, 3, 8, 128], BF16)
        r_q = kb.res("QAres")
        kb.dma("sp", QA[:], HT[0:8, :, :].rearrange("c p t -> p c t"), reads=[r_HT], writes=[r_q], chan_res=r_q)
        kb.dma("sp", KA[:], HT[8:10, :, :].rearrange("c p t -> p c t"), reads=[r_HT], writes=[P.r_v], chan_res=P.r_v)
        kb.dma("sp", VAs[:], VA.rearrange("(n p) g d -> p n g d", p=128), reads=[r_VA], writes=[P.r_v], chan_res=P.r_v)
        kb.dma("sp", tab32[:], tabA_in, reads=[r_const], writes=[P.r_tab], chan_res=P.r_tab)
        kb.op("dve", lambda e: e.tensor_copy(out=tabA[:].rearrange("p a h q -> p (a h q)"), in_=tab32[:]), reads=[P.r_tab], writes=[P.r_tab])
        o_pool = Pool(kb, "ost", [128, 1024], F32, 2)
        for n in range(NT):
            ost, r_ost = o_pool.next()
            for kg in range(2):
                kts = [m for m in (n - 1, n, n + 1) if 0 <= m < NT]

                def s_mm(kt, ps, r_ps, kg=kg, n=n):
                    kb.op("pe", lambda e: e.matmul(ps[:].rearrange("p (h q) -> p h q", q=128), lhsT=KA[:, kg, kt * 128:(kt + 1) * 128],
                                                   rhs=QA[:, 4 * kg:4 * kg + 4, n * 128:(n + 1) * 128], start=True, stop=True),
                          reads=[P.r_v, r_q], writes=[r_ps])
                attn_block(P, kts, s_mm,
                           lambda kt, kg=kg, n=n: tabA[:, kt - n + 1, 4 * kg:4 * kg + 4, :].rearrange("p h q -> p (h q)"),
                           lambda kt, g, kg=kg: VAs[:, kt, kg, :], SC_A,
                           [(g, ost[:, (4 * kg + g) * 128:(4 * kg + g + 1) * 128], r_ost, esink[:, l, 4 * kg + g:4 * kg + g + 1]) for g in range(4)])
            kb.dma("pool", O[n * 128:(n + 1) * 128, 0:1024], ost[:], reads=[r_ost], writes=[r_O], chan_res=r_ost)
        kb.end_phase()

    def phase2b(l):
        kb.begin_phase()
        P = attn_pools()
        QB = kb.sbuf("QB", [128, 4, S], BF16)
        KBs = kb.sbuf("KBs", [128, 4, S], BF16)
        VBs = kb.sbuf("VBs", [128, NT, 4, 129], BF16)
        tabB = kb.sbuf("tabB", [128, NTAB, 4, 128], BF16)
        mk32 = kb.sbuf("mk32", [128, NTAB, 128], F32)
        bst_pool = Pool(kb, "bst", [128, 4, 128], F32, 2)
        r_q = kb.res("QBres")
        kb.dma("sp", QB[:], HT[12:16, :, :].rearrange("c p t -> p c t"), reads=[r_HT], writes=[r_q], chan_res=r_q)
        kb.dma("sp", KBs[:], HT[16:20, :, :].rearrange("c p t -> p c t"), reads=[r_HT], writes=[P.r_v], chan_res=P.r_v)
        kb.dma("sp", VBs[:], VB.rearrange("(n p) g d -> p n g d", p=128), reads=[r_VB], writes=[P.r_v], chan_res=P.r_v)
        kb.dma("sp", mk32[:], maskB_in, reads=[r_const], writes=[P.r_tab], chan_res=P.r_tab)
        for ti in range(NTAB):
            bst, r_bst = bst_pool.next()
            kb.dma("sp", bst[:], biasB_in[l, :, ti, :, :], reads=[r_const], writes=[r_bst], chan_res=r_bst)
            kb.op("act", lambda e, bst=bst: e.activation(out=bst[:], in_=bst[:], func=AF.Exp), reads=[r_bst], writes=[r_bst])
            for h in range(4):
                kb.op("dve", lambda e, bst=bst, ti=ti, h=h: e.tensor_tensor(out=tabB[:, ti, h, :], in0=bst[:, h, :], in1=mk32[:, ti, :], op=ALU.mult),
                      reads=[r_bst, P.r_tab], writes=[P.r_tab])
        o_pool = Pool(kb, "ost", [128, 512], F32, 2)
        for n in range(NT):
            ost, r_ost = o_pool.next()
            lst = schedB[n]
            tmap = dict(lst)

            def s_mm(kt, ps, r_ps, n=n):
                for h in range(4):
                    kb.op("pe", lambda e, h=h: e.matmul(ps[:, h * 128:(h + 1) * 128], lhsT=KBs[:, h, kt * 128:(kt + 1) * 128],
                                                        rhs=QB[:, h, n * 128:(n + 1) * 128], start=True, stop=True),
                          reads=[P.r_v, r_q], writes=[r_ps])
            attn_block(P, [m for m, _ in lst], s_mm,
                       lambda kt, tmap=tmap: tabB[:, tmap[kt], :, :].rearrange("p h q -> p (h q)"),
                       lambda kt, g: VBs[:, kt, g, :], SC_A,
                       [(g, ost[:, g * 128:(g + 1) * 128], r_ost, None) for g in range(4)])
            kb.dma("pool", O[n * 128:(n + 1) * 128, 1024:1536], ost[:], reads=[r_ost], writes=[r_O], chan_res=r_ost)
        kb.end_phase()

    def phase2c(l):
        kb.begin_phase()
        P = attn_pools()
        KN = kb.sbuf("KN", [128, 4, S], BF16)
        KP = kb.sbuf("KP", [64, S], BF16)
        VCs = kb.sbuf("VCs", [128, NT, 4, 129], BF16)
        kb.dma("sp", KN[:], KCN.rearrange("c p t -> p c t"), reads=[r_KCN], writes=[P.r_v], chan_res=P.r_v)
        kb.dma("sp", KP[:], KCP, reads=[r_KCP], writes=[P.r_v], chan_res=P.r_v)
        kb.dma("sp", VCs[:], VC.rearrange("(n p) g d -> p n g d", p=128), reads=[r_VC], writes=[P.r_v], chan_res=P.r_v)
        qn_pool = Pool(kb, "qn", [128, 4, 512], BF16, 2)
        qp_pool = Pool(kb, "qp", [64, 4, 512], BF16, 2)
        o_pool = Pool(kb, "ost", [128, 4, 512], F32, 2)
        for b in range(NB):
            bs = slice(b * 512, (b + 1) * 512)
            qn, r_qn = qn_pool.next()
            qp, r_qp = qp_pool.next()
            kb.dma("sp", qn[:], QCN[:, :, bs].rearrange("c p t -> p c t"), reads=[r_QCN], writes=[r_qn], chan_res=r_qn)
            kb.dma("sp", qp[:], QCP[:, :, bs].rearrange("c p t -> p c t"), reads=[r_QCP], writes=[r_qp], chan_res=r_qp)
            ost, r_ost = o_pool.next()
            for h in range(4):
                def s_mm(kt, ps, r_ps, h=h, qn=qn, qp=qp, r_qn=r_qn, r_qp=r_qp):
                    kb.op("pe", lambda e: e.matmul(ps[:], lhsT=KN[:, h, kt * 128:(kt + 1) * 128], rhs=qn[:, h, :], start=True, stop=False),
                          reads=[P.r_v, r_qn], writes=[r_ps])
                    kb.op("pe", lambda e: e.matmul(ps[:], lhsT=KP[:, kt * 128:(kt + 1) * 128], rhs=qp[:, h, :], start=False, stop=True),
                          reads=[P.r_v, r_qp], writes=[r_ps])
                attn_block(P, list(range(NT)), s_mm, lambda kt: None, lambda kt, g, h=h: VCs[:, kt, h, :], SC_C,
                           [(g, ost[:, g, h * 128:(h + 1) * 128], r_ost, None) for g in range(4)])
            kb.dma("pool", O[bs, 1536:2048].rearrange("(g p) d -> p g d", p=128), ost[:], reads=[r_ost], writes=[r_O], chan_res=r_ost)
        kb.end_phase()

    def phase3(l, x_src, r_xsrc, X1, r_X1):
        kb.begin_phase()
        wst_pool = Pool(kb, "wst", [128, 2048], F32, 2)
        wo = kb.sbuf("wo", [128, KC, D], BF16)
        r_wo = kb.res("wo")
        load_w(wst_pool, w_o[l], 22, D, l, wo, r_wo, r_win2)
        g2 = kb.sbuf("g2", [128, D], F32); r_g2 = kb.res("g2")
        kb.dma("sp", g2[:], ln2b_in[l], reads=[r_const], writes=[r_g2], chan_res=r_g2)
        ot_pool = Pool(kb, "ot", [128, D], F32, 2)
        xt_pool = Pool(kb, "xt", [128, D], F32, 2)
        x1_pool = Pool(kb, "x1", [128, D], F32, 2)
        xn_pool = Pool(kb, "xn", [128, D], BF16, 2)
        sq_pool = Pool(kb, "sq", [128, D], BF16, 1)
        st_pool = Pool(kb, "stat", [128, 4], F32, 4)
        oT_pool = Pool(kb, "oT", [128, KC, 128], BF16, 2)
        xn2T_pool = Pool(kb, "xn2T", [128, KC, 512], BF16, 1)
        ps_tp = Pool(kb, "tp", [128, KC, 128], BF16, 2, space="psum")
        ps_mm = Pool(kb, "mm", [128, 512], F32, 4, space="psum")
        rmsnorm_tile, transpose_tile = mk_norm_helpers(st_pool, sq_pool, ps_tp)
        for b in range(NB):
            xn2T, r_xn2T = xn2T_pool.next()
            for j in range(4):
                tok = b * 512 + j * 128
                ot, r_ot = ot_pool.next()
                kb.dma("sp", ot[:], O[tok:tok + 128, :], reads=[r_O], writes=[r_ot], chan_res=r_ot)
                xt, r_xt = xt_pool.next()
                kb.dma("sp", xt[:], x_src[tok:tok + 128, :], reads=[r_xsrc], writes=[r_xt], chan_res=r_xt)
                on, r_on = xn_pool.next()
                rmsnorm_tile(ot, r_ot, [(0, 1024), (1024, 512), (1536, 512)], on, r_on)
                oT, r_oT = oT_pool.next()
                transpose_tile(on, r_on, oT, r_oT, 0)
                x1, r_x1 = x1_pool.next()
                for nb in range(4):
                    pm, r_pm = ps_mm.next()
                    for k in range(KC):
                        kb.op("pe", lambda e, k=k, nb=nb, pm=pm, oT=oT: e.matmul(
                            pm[:], lhsT=oT[:, k, :], rhs=wo[:, k, nb * 512:(nb + 1) * 512], start=(k == 0), stop=(k == KC - 1)),
                            reads=[r_wo, r_oT], writes=[r_pm])
                    kb.op("dve", lambda e, nb=nb, pm=pm, x1=x1, xt=xt: e.tensor_tensor(
                        out=x1[:, nb * 512:(nb + 1) * 512], in0=pm[:], in1=xt[:, nb * 512:(nb + 1) * 512], op=ALU.add),
                        reads=[r_pm, r_xt], writes=[r_x1])
                kb.dma("pool", X1[tok:tok + 128, :], x1[:], reads=[r_x1], writes=[r_X1], chan_res=r_x1)
                xn, r_xn = xn_pool.next()
                rmsnorm_tile(x1, r_x1, [(0, D)], xn, r_xn, gain=(g2, r_g2))
                transpose_tile(xn, r_xn, xn2T, r_xn2T, j)
            kb.dma("pool", XN2T[:, :, b * 512:(b + 1) * 512].rearrange("c p t -> p c t"), xn2T[:], reads=[r_xn2T], writes=[r_XN2T],
                   chan_res=r_xn2T)
        kb.end_phase()

    def phase4a(l):
        kb.begin_phase()
        wst_pool = Pool(kb, "wst", [128, 1024], F32, 2)
        wq = kb.sbuf("wq", [128, KC, 1024], BF16)
        r_wq = kb.res("wq")
        load_w(wst_pool, w_pq[l], None, 1024, l, wq, r_wq, r_win2)
        xb_pool = Pool(kb, "xb", [128, KC, 512], BF16, 2)
        ps_mm = Pool(kb, "mm", [128, 512], F32, 4, space="psum")
        ev_pool = Pool(kb, "ev", [128, 512], BF16, 4)
        for b in range(NB):
            bs = slice(b * 512, (b + 1) * 512)
            xb, r_xb = xb_pool.next()
            kb.dma("sp", xb[:], XN2T[:, :, bs].rearrange("c p t -> p c t"), reads=[r_XN2T], writes=[r_xb], chan_res=r_xb)
            for h in range(8):
                pm, r_pm = ps_mm.next()
                for k in range(KC):
                    kb.op("pe", lambda e, k=k, h=h, pm=pm, xb=xb: e.matmul(
                        pm[:], lhsT=wq[:, k, h * 128:(h + 1) * 128], rhs=xb[:, k, :], start=(k == 0), stop=(k == KC - 1)),
                        reads=[r_wq, r_xb], writes=[r_pm])
                ev, r_ev = ev_pool.next()
                kb.op("act", lambda e, pm=pm, ev=ev: e.copy(out=ev[:], in_=pm[:]), reads=[r_pm], writes=[r_ev])
                kb.dma("pool", QT[h, :, bs], ev[:], reads=[r_ev], writes=[r_QT], chan_res=r_ev)
        kb.end_phase()

    DELTA = 1e-5

    def phase4b(l):
        kb.begin_phase()
        sk32 = kb.sbuf("sk32", [128, 256], F32)
        skb = kb.sbuf("skb", [128, 256], BF16)
        r_sk = kb.res("sk")
        kb.dma("sp", sk32[:], sk_in[l], reads=[r_const], writes=[r_sk], chan_res=r_sk)
        kb.op("dve", lambda e: e.tensor_copy(out=skb[:], in_=sk32[:]), reads=[r_sk], writes=[r_sk])
        q_pool = Pool(kb, "qt", [128, 8, 128], BF16, 2)
        s_pool = Pool(kb, "s", [128, 8, 2, 128], F32, 2)
        top_pool = Pool(kb, "top", [128, 8, 2, 16], F32, 2)
        tmp = kb.sbuf("tmpm", [128, 256], F32); r_tmp = kb.res("tmpm")
        cand = kb.sbuf("cand", [128, 8, 256], F32); r_cand = kb.res("cand")
        best = kb.sbuf("best", [128, 8, 16], F32); r_best = kb.res("best")
        dd = kb.sbuf("dd", [128, 8, 16], F32)
        zz = kb.sbuf("zz", [128, 4, 8], F32)
        cc = kb.sbuf("cc", [128, 8, 16], F32)
        cb = kb.sbuf("cb", [128, 8, 16], BF16)
        thr = kb.sbuf("thr", [128, 8, 16], F32)
        r_sm = kb.res("small")
        E2 = kb.sbuf("E2", [128, 8, 128], BF16); r_E2 = kb.res("E2")
        A = kb.sbuf("A", [128, 8, 16, 128], BF16); r_A = kb.res("A")
        Bm = kb.sbuf("Bm", [128, 8, 16, 64], BF16); r_Bm = kb.res("Bm")
        AT = kb.sbuf("AT", [128, 128, 128], BF16); r_AT = kb.res("AT")
        BT = kb.sbuf("BT", [128, 64, 128], BF16); r_BT = kb.res("BT")
        cT = kb.sbuf("cT", [128, 128], BF16); r_cT = kb.res("cT")
        g_pool = Pool(kb, "gt", [128, 64, 128], BF16, 2)
        ps_s = Pool(kb, "pss", [128, 2, 256], F32, 1, space="psum")
        ps_tp = Pool(kb, "tp", [128, 16, 128], BF16, 2, space="psum")
        ps_g = Pool(kb, "pg", [128, 8, 64], F32, 3, space="psum")
        A2 = A[:].rearrange("p h a i -> p (h a) i")
        B2 = Bm[:].rearrange("p h a j -> p (h a) j")
        evtog = [0]
        for n in range(NT):
            ts = slice(n * 128, (n + 1) * 128)
            qt, r_qt = q_pool.next()
            kb.dma("sp", qt[:], QT[:, :, ts].rearrange("h p t -> p h t"), reads=[r_QT], writes=[r_qt], chan_res=r_qt)
            s, r_s = s_pool.next()
            for hp in range(4):
                ps, r_ps = ps_s.next()
                for hh in range(2):
                    kb.op("pe", lambda e, hp=hp, hh=hh, ps=ps, qt=qt: e.matmul(ps[:, hh, :], lhsT=qt[:, 2 * hp + hh, :], rhs=skb[:], start=True, stop=True),
                          reads=[r_qt, r_sk], writes=[r_ps])
                kb.op("act", lambda e, hp=hp, ps=ps, s=s: e.copy(out=s[:, 2 * hp:2 * hp + 2, :, :].rearrange("p h c n -> p h (c n)"), in_=ps[:]),
                      reads=[r_ps], writes=[r_s])
            top, r_top = top_pool.next()
            for h in range(8):
                for c in range(2):
                    kb.op("dve", lambda e, h=h, c=c, top=top, s=s: e.max(out=top[:, h, c, 0:8], in_=s[:, h, c, :]), reads=[r_s], writes=[r_top])
                    kb.op("dve", lambda e, h=h, c=c, top=top, s=s: e.match_replace(out=tmp[:, 0:128], in_to_replace=top[:, h, c, 0:8],
                                                                                   in_values=s[:, h, c, :], imm_value=-1e30),
                          reads=[r_s, r_top], writes=[r_tmp])
                    kb.op("dve", lambda e, h=h, c=c, top=top: e.max(out=top[:, h, c, 8:16], in_=tmp[:, 0:128]), reads=[r_tmp], writes=[r_top])
            kb.op("dve", lambda e, top=top: e.tensor_tensor(out=cand[:].rearrange("p h (a b) -> p h a b", a=16),
                                                            in0=top[:, :, 0, :].unsqueeze(3).to_broadcast([128, 8, 16, 16]),
                                                            in1=top[:, :, 1, :].unsqueeze(2).to_broadcast([128, 8, 16, 16]), op=ALU.add),
                  reads=[r_top], writes=[r_cand])
            for h in range(8):
                kb.op("dve", lambda e, h=h: e.max(out=best[:, h, 0:8], in_=cand[:, h, :]), reads=[r_cand], writes=[r_best])
                kb.op("dve", lambda e, h=h: e.match_replace(out=tmp[:, 0:256], in_to_replace=best[:, h, 0:8], in_values=cand[:, h, :], imm_value=-1e30),
                      reads=[r_cand, r_best], writes=[r_tmp])
                kb.op("dve", lambda e, h=h: e.max(out=best[:, h, 8:16], in_=tmp[:, 0:256]), reads=[r_tmp], writes=[r_best])
            kb.op("dve", lambda e: e.tensor_tensor(out=dd[:], in0=best[:], in1=best[:, :, 0:1].to_broadcast([128, 8, 16]), op=ALU.subtract),
                  reads=[r_best], writes=[r_sm])
            kb.op("act", lambda e: e.activation(out=dd[:], in_=dd[:], func=AF.Exp), reads=[r_sm], writes=[r_sm])
            kb.op("dve", lambda e: e.tensor_reduce(out=zz[:, 0, :], in_=dd[:], axis=AX.X, op=ALU.add), reads=[r_sm], writes=[r_sm])
            kb.op("act", lambda e: e.activation(out=zz[:, 1, :], in_=zz[:, 0, :], func=AF.Ln), reads=[r_sm], writes=[r_sm])
            kb.op("dve", lambda e: e.tensor_tensor(out=zz[:, 2, :], in0=zz[:, 1, :], in1=best[:, :, 0], op=ALU.add), reads=[r_sm, r_best], writes=[r_sm])
            kb.op("dve", lambda e, top=top: e.tensor_tensor(out=cc[:], in0=top[:, :, 0, :], in1=zz[:, 2, :].unsqueeze(2).to_broadcast([128, 8, 16]),
                                                            op=ALU.subtract), reads=[r_sm, r_top], writes=[r_sm])
            kb.op("act", lambda e: e.activation(out=cb[:], in_=cc[:], func=AF.Exp), reads=[r_sm], writes=[r_sm])
            kb.op("dve", lambda e: e.tensor_scalar(out=zz[:, 3, :], in0=best[:, :, 15], scalar1=-DELTA, scalar2=None, op0=ALU.add),
                  reads=[r_best], writes=[r_sm])
            kb.op("dve", lambda e, top=top: e.tensor_tensor(out=thr[:], in0=zz[:, 3, :].unsqueeze(2).to_broadcast([128, 8, 16]), in1=top[:, :, 0, :],
                                                            op=ALU.subtract), reads=[r_sm, r_top], writes=[r_sm])
            kb.op("act", lambda e, s=s: e.activation(out=E2[:], in_=s[:, :, 1, :], func=AF.Exp), reads=[r_s], writes=[r_E2])
            kb.op("dve", lambda e, s=s, top=top: e.tensor_tensor(out=A[:], in0=s[:, :, 0, :].unsqueeze(2).to_broadcast([128, 8, 16, 128]),
                                                                 in1=top[:, :, 0, :].unsqueeze(3).to_broadcast([128, 8, 16, 128]), op=ALU.is_equal),
                  reads=[r_s, r_top], writes=[r_A])
            pt, r_pt = ps_tp.next()
            kb.op("pe", lambda e, pt=pt: e.transpose(out=pt[:, 0, :], in_=cb[:].rearrange("p h a -> p (h a)"), identity=ident[:]),
                  reads=[r_sm, r_id], writes=[r_pt])
            kb.op("act", lambda e, pt=pt: e.copy(out=cT[:], in_=pt[:, 0, :]), reads=[r_pt], writes=[r_cT])
            for i0 in range(0, 128, 16):
                pt, r_pt = ps_tp.next()
                for ii in range(16):
                    kb.op("pe", lambda e, pt=pt, ii=ii, i0=i0: e.transpose(out=pt[:, ii, :], in_=A2[:, :, i0 + ii], identity=ident[:]),
                          reads=[r_A, r_id], writes=[r_pt])
                kb.op("dve", lambda e, pt=pt, i0=i0: e.tensor_tensor(out=AT[:, i0:i0 + 16, :], in0=pt[:],
                                                                     in1=cT[:].unsqueeze(1).to_broadcast([128, 16, 128]), op=ALU.mult),
                      reads=[r_pt, r_cT], writes=[r_AT])
            for jh in range(2):
                js = slice(jh * 64, (jh + 1) * 64)
                kb.op("dve", lambda e, s=s, js=js: e.tensor_tensor(out=Bm[:], in0=s[:, :, 1, js].unsqueeze(2).to_broadcast([128, 8, 16, 64]),
                                                                   in1=thr[:].unsqueeze(3).to_broadcast([128, 8, 16, 64]), op=ALU.is_ge),
                      reads=[r_s, r_sm], writes=[r_Bm])
                kb.op("dve", lambda e, js=js: e.tensor_tensor(out=Bm[:], in0=Bm[:], in1=E2[:, :, js].unsqueeze(2).to_broadcast([128, 8, 16, 64]),
                                                              op=ALU.mult), reads=[r_E2, r_Bm], writes=[r_Bm])
                for j0 in range(0, 64, 16):
                    pt, r_pt = ps_tp.next()
                    for jj in range(16):
                        kb.op("pe", lambda e, pt=pt, jj=jj, j0=j0: e.transpose(out=pt[:, jj, :], in_=B2[:, :, j0 + jj], identity=ident[:]),
                              reads=[r_Bm, r_id], writes=[r_pt])
                    kb.op("act", lambda e, pt=pt, j0=j0: e.copy(out=BT[:, j0:j0 + 16, :], in_=pt[:]), reads=[r_pt], writes=[r_BT])
                gt, r_gt = g_pool.next()
                for t0 in range(0, 128, 8):
                    pg, r_pg = ps_g.next()
                    for tt in range(8):
                        kb.op("pe", lambda e, pg=pg, tt=tt, t0=t0: e.matmul(pg[:, tt, :], lhsT=AT[:, :, t0 + tt], rhs=BT[:, :, t0 + tt], start=True, stop=True),
                              reads=[r_AT, r_BT], writes=[r_pg])
                    evtog[0] ^= 1
                    if evtog[0]:
                        kb.op("act", lambda e, pg=pg, gt=gt, t0=t0: e.copy(out=gt[:, :, t0:t0 + 8], in_=pg[:].rearrange("p t j -> p j t")),
                              reads=[r_pg], writes=[r_gt])
                    else:
                        kb.op("dve", lambda e, pg=pg, gt=gt, t0=t0: e.tensor_copy(out=gt[:, :, t0:t0 + 8], in_=pg[:].rearrange("p t j -> p j t")),
                              reads=[r_pg], writes=[r_gt])
                kb.dma("pool", GT[n, :, js, :], gt[:], reads=[r_gt], writes=[r_GT], chan_res=r_gt)
        kb.end_phase()

    def phase5(l, X1, r_X1, X2, r_X2):
        kb.begin_phase()
        TBT = min(TBT_MAX, NT)
        TB = TBT * 128
        nblk = NT // TBT
        xblk = kb.sbuf("xblk", [128, KC, TB], BF16); r_xblk = kb.res("xblk")
        y_sb = kb.sbuf("ysb", [128, TBT, D], F32); r_ysb = kb.res("ysb")
        un_pool = Pool(kb, "un", [128, 2, D], BF16, 2)
        v_pool = Pool(kb, "vv", [128, 2, D], BF16, 2)
        uT_pool = Pool(kb, "uT", [128, KC, 128], BF16, 2)
        g_pool = Pool(kb, "gg", [128, TBT, 2, 128], BF16, 2)
        ge_pool = Pool(kb, "ge", [128, 512], BF16, 2)
        hs_pool = Pool(kb, "hs", [128, TB], BF16, 4)
        xt_pool = Pool(kb, "xt", [128, D], F32, 2)
        ps_tp = Pool(kb, "tp", [128, KC, 128], BF16, 1, space="psum")
        ps_a = Pool(kb, "pa", [128, 512], F32, 2, space="psum")
        ps_y = Pool(kb, "py", [128, 512], F32, 4, space="psum")
        U3 = peer_u[l].rearrange("(i j) d -> i j d", j=128)
        V3 = peer_v[l].rearrange("(i j) d -> i j d", j=128)
        for blk in range(nblk):
            t0 = blk * TB
            kb.dma("sp", xblk[:], XN2T[:, :, t0:t0 + TB].rearrange("c p t -> p c t"), reads=[r_XN2T], writes=[r_xblk], chan_res=r_xblk)
            for sc in range(64):
                j0 = 2 * sc
                un, r_un = un_pool.next()
                vv, r_vv = v_pool.next()
                gg, r_gg = g_pool.next()
                kb.dma("pool", un[:], U3[:, j0:j0 + 2, :], reads=[r_const], writes=[r_un], chan_res=r_un)
                kb.dma("pool", vv[:], V3[:, j0:j0 + 2, :], reads=[r_const], writes=[r_vv], chan_res=r_vv)
                kb.dma("sp", gg[:], GT[blk * TBT:(blk + 1) * TBT, :, j0:j0 + 2, :].rearrange("n i j t -> i n j t"), reads=[r_GT], writes=[r_gg],
                       chan_res=r_gg)
                hs = []
                for jj in range(2):
                    pt, r_pt = ps_tp.next()
                    for k in range(KC):
                        kb.op("pe", lambda e, k=k, jj=jj, pt=pt, un=un: e.transpose(out=pt[:, k, :], in_=un[:, jj, k * 128:(k + 1) * 128], identity=ident[:]),
                              reads=[r_un, r_id], writes=[r_pt])
                    uT, r_uT = uT_pool.next()
                    kb.op("act", lambda e, pt=pt, uT=uT: e.copy(out=uT[:], in_=pt[:]), reads=[r_pt], writes=[r_uT])
                    h_t, r_ht = hs_pool.next()
                    for half in range(TB // 512):
                        pa, r_pa = ps_a.next()
                        for k in range(KC):
                            kb.op("pe", lambda e, k=k, pa=pa, uT=uT, half=half: e.matmul(pa[:], lhsT=uT[:, k, :], rhs=xblk[:, k, half * 512:(half + 1) * 512],
                                                                                          start=(k == 0), stop=(k == KC - 1)),
                                  reads=[r_uT, r_xblk], writes=[r_pa])
                        ge, r_ge = ge_pool.next()
                        kb.op("act", lambda e, pa=pa, ge=ge: e.activation(out=ge[:], in_=pa[:], func=AF.Gelu_apprx_tanh), reads=[r_pa], writes=[r_ge])
                        kb.op("dve", lambda e, ge=ge, h_t=h_t, gg=gg, jj=jj, half=half: e.tensor_tensor(
                            out=h_t[:, half * 512:(half + 1) * 512].rearrange("p (n t) -> p n t", t=128), in0=ge[:].rearrange("p (n t) -> p n t", t=128),
                            in1=gg[:, half * 4:(half + 1) * 4, jj, :], op=ALU.mult), reads=[r_ge, r_gg], writes=[r_ht])
                    hs.append((h_t, r_ht))
                for tile in range(TBT):
                    for nb in range(4):
                        py, r_py = ps_y.next()
                        for jj in range(2):
                            h_t, r_ht = hs[jj]
                            kb.op("pe", lambda e, jj=jj, py=py, h_t=h_t, vv=vv, tile=tile, nb=nb: e.matmul(
                                py[:], lhsT=h_t[:, tile * 128:(tile + 1) * 128], rhs=vv[:, jj, nb * 512:(nb + 1) * 512], start=(jj == 0), stop=(jj == 1)),
                                reads=[r_ht, r_vv], writes=[r_py])
                        if sc == 0:
                            kb.op("dve", lambda e, py=py, tile=tile, nb=nb: e.tensor_copy(out=y_sb[:, tile, nb * 512:(nb + 1) * 512], in_=py[:]),
                                  reads=[r_py], writes=[r_ysb])
                        else:
                            kb.op("dve", lambda e, py=py, tile=tile, nb=nb: e.tensor_tensor(out=y_sb[:, tile, nb * 512:(nb + 1) * 512], in0=py[:],
                                                                                            in1=y_sb[:, tile, nb * 512:(nb + 1) * 512], op=ALU.add),
                                  reads=[r_py, r_ysb], writes=[r_ysb])
            for tile in range(TBT):
                tok = t0 + tile * 128
                xt, r_xt = xt_pool.next()
                kb.dma("sp", xt[:], X1[tok:tok + 128, :], reads=[r_X1], writes=[r_xt], chan_res=r_xt)
                kb.op("pool", lambda e, xt=xt, tile=tile: e.tensor_tensor(out=xt[:], in0=xt[:], in1=y_sb[:, tile, :], op=ALU.add),
                      reads=[r_xt, r_ysb], writes=[r_xt])
                kb.dma("pool", X2[tok:tok + 128, :], xt[:], reads=[r_xt], writes=[r_X2], chan_res=r_xt)
        kb.end_phase()

    def phasef(x_src, r_xsrc):
        kb.begin_phase()
        fn = kb.sbuf("fn", [128, D], F32); r_fn = kb.res("fn")
        kb.dma("sp", fn[:], fnorm_in, reads=[r_const], writes=[r_fn], chan_res=r_fn)
        xt_pool = Pool(kb, "xt", [128, D], F32, 2)
        yo_pool = Pool(kb, "yo", [128, D], F32, 2)
        sq_pool = Pool(kb, "sq", [128, D], BF16, 1)
        st_pool = Pool(kb, "stat", [128, 4], F32, 4)
        for n in range(NT):
            xt, r_xt = xt_pool.next()
            kb.dma("sp", xt[:], x_src[n * 128:(n + 1) * 128, :], reads=[r_xsrc], writes=[r_xt], chan_res=r_xt)
            stt, r_st = st_pool.next()
            sq, r_sq = sq_pool.next()
            kb.op("act", lambda e, xt=xt, stt=stt, sq=sq: e.activation(out=sq[:], in_=xt[:], func=AF.Square, accum_out=stt[:, 0:1]),
                  reads=[r_xt], writes=[r_sq, r_st])
            kb.op("act", lambda e, stt=stt: e.activation(out=stt[:, 0:1], in_=stt[:, 0:1], func=AF.Ln, bias=epsb[:, 0:1], scale=1.0 / D),
                  reads=[r_st, r_gn], writes=[r_st])
            kb.op("act", lambda e, stt=stt: e.activation(out=stt[:, 0:1], in_=stt[:, 0:1], func=AF.Exp, scale=-0.5), reads=[r_st], writes=[r_st])
            yo, r_yo = yo_pool.next()
            kb.op("dve", lambda e, xt=xt, stt=stt, yo=yo: e.scalar_tensor_tensor(out=yo[:], in0=xt[:], scalar=stt[:, 0:1], in1=fn[:],
                                                                                 op0=ALU.mult, op1=ALU.mult),
                  reads=[r_xt, r_st, r_fn], writes=[r_yo])
            kb.dma("pool", y_out[n * 128:(n + 1) * 128, :], yo[:], reads=[r_yo], writes=[r_y], chan_res=r_yo)
        kb.end_phase()

    for l in range(L):
        x_src, r_xsrc = (x_in, r_x) if l == 0 else (XB, r_XB)
        phase1(l, x_src, r_xsrc)
        if upto >= 2:
            phase1c(l)
        if upto >= 3:
            phase2a(l)
            phase2b(l)
            phase2c(l)
        if upto >= 4:
            phase3(l, x_src, r_xsrc, XA, r_XA)
        if upto >= 5:
            phase4a(l)
            phase4b(l)
        if upto >= 6:
            phase5(l, XA, r_XA, XB, r_XB)
    if upto >= 7:
        phasef(XB, r_XB)
    outs = [r_HT, r_VA, r_VB, r_QCN, r_QCP, r_KCN, r_KCP, r_VC, r_O, r_XA, r_XB, r_XN2T, r_QT, r_GT, r_y]
    kb.finish_wait("pool", outs)
    kb.emit()
    return nc


def host_consts(S):
    c = {}
    c["ident"] = np.eye(128, dtype=np.float32)
    R = np.zeros((64, 64), np.float32)
    for m in range(32):
        R[m + 32, m] = -1.0
        R[m, m + 32] = 1.0
    c["rotR"] = R
    inv = 10000.0 ** (-(np.arange(32, dtype=np.float64)) / 32.0)
    ang = np.arange(S, dtype=np.float64)[None, :] * np.concatenate([inv, inv])[:, None]
    c["cossin"] = np.stack([np.cos(ang), np.sin(ang)]).astype(np.float32)
    ki = np.arange(128)[:, None]
    qi = np.arange(128)[None, :]
    slopes = np.array([2.0 ** (-8.0 * (h + 1) / 8) for h in range(8)], np.float64)
    tabA = np.zeros((128, 3, 8, 128), np.float64)
    for d, (dist, valid) in enumerate([(128 + qi - ki, qi <= ki), (np.abs(qi - ki), np.ones((128, 128), bool)),
                                       (128 + ki - qi, ki <= qi)]):
        for h in range(8):
            tabA[:, d, h, :] = np.where(valid, np.exp(-slopes[h] * dist), 0.0)
    c["tabA"] = tabA.astype(np.float32).reshape(128, 3 * 8 * 128)
    rows = S // 64
    kr = min(8, rows)
    NT = S // 128
    tabs = {}
    sched = []
    masks, dridx, dcidx = [], [], []
    for n in range(NT):
        qrow = 2 * n + np.arange(128) // 64
        qcol = np.arange(128) % 64
        rstart = np.clip(qrow - kr // 2, 0, rows - kr)
        cstart = np.clip(qcol - 8, 0, 64 - 16)
        lst = []
        for m in range(NT):
            krow = 2 * m + np.arange(128) // 64
            kcol = np.arange(128) % 64
            valid = ((krow[:, None] >= rstart[None, :]) & (krow[:, None] < rstart[None, :] + kr)
                     & (kcol[:, None] >= cstart[None, :]) & (kcol[:, None] < cstart[None, :] + 16))
            if not valid.any():
                continue
            dr = np.clip(krow[:, None] - qrow[None, :] + 7, 0, 14)
            dc = np.clip(kcol[:, None] - qcol[None, :] + 15, 0, 30)
            dr = np.where(valid, dr, 0)
            dc = np.where(valid, dc, 0)
            key = (valid.tobytes(), dr.tobytes(), dc.tobytes())
            if key not in tabs:
                tabs[key] = len(tabs)
                masks.append(valid.astype(np.float32))
                dridx.append(dr)
                dcidx.append(dc)
            lst.append((m, tabs[key]))
        sched.append(lst)
    c["maskB"] = np.ascontiguousarray(np.stack(masks, 1))
    c["_dr"] = np.stack(dridx, 1)
    c["_dc"] = np.stack(dcidx, 1)
    c["_schedB"] = sched
    c["_ntab"] = len(masks)
    return c


def gather_biasB(b_rel_bias_l, c):
    g = b_rel_bias_l[:, c["_dr"], c["_dc"]]
    return np.ascontiguousarray(np.transpose(g, (1, 2, 0, 3))).astype(np.float32)


def gains_pack(inp, L):
    g = np.zeros((L, 128, 64), np.float32)
    for l in range(L):
        g[l, :, 0:16] = inp["ln1"][l].reshape(16, 128).T
        g[l, :, 16:20] = inp["c_q_norm"][l].reshape(4, 128).T
        g[l, :, 20:22] = inp["c_kv_norm"][l].reshape(2, 128).T
        g[l, :, 22:38] = inp["out_norm"][l].reshape(16, 128).T
        g[l, :, 38:54] = inp["ln2"][l].reshape(16, 128).T
    return g


def sk_pack(inp, L):
    sk = np.zeros((L, 128, 256), np.float32)
    for l in range(L):
        sk[l, 0:64, 0:128] = inp["peer_sub_keys"][l, 0].T
        sk[l, 64:128, 128:256] = inp["peer_sub_keys"][l, 1].T
    return sk


def core_inputs(inp, xb, S, L, consts):
    m = dict(x=np.ascontiguousarray(xb), w_in=inp["w_in"][:L], gains=gains_pack(inp, L), ident=consts["ident"],
             cossin=consts["cossin"], rotR=consts["rotR"], tabA=consts["tabA"], maskB=consts["maskB"],
             biasB=np.stack([gather_biasB(inp["b_rel_bias"][l], consts) for l in range(L)]),
             sinkb=np.ascontiguousarray(np.broadcast_to(inp["a_sink"][:L][None], (128, L, 8))).astype(np.float32),
             c_w_uq=inp["c_w_uq"][:L], c_w_ukv=inp["c_w_ukv"][:L], w_o=inp["w_o"][:L],
             peer_w_q=inp["peer_w_q"][:L], peer_u=inp["peer_u"][:L], peer_v=inp["peer_v"][:L],
             sk=sk_pack(inp, L),
             ln2b=np.ascontiguousarray(np.broadcast_to(inp["ln2"][:L][:, None, :], (L, 128, 2048))).astype(np.float32),
             fnorm=np.ascontiguousarray(np.broadcast_to(inp["final_norm"][None], (128, 2048))).astype(np.float32))
    return m


N_CORES = 4
_CACHE = {}


def kernel(**inputs):
    inp = {k: np.asarray(v) for k, v in inputs.items()}
    B, S, _ = inp["x"].shape
    L = inp["w_in"].shape[0]
    key = (S, L)
    if key not in _CACHE:
        consts = host_consts(S)
        _CACHE[key] = (consts, build(S, L, consts, debug=False))
    consts, nc = _CACHE[key]
    in_maps = []
    for b in range(B):
        m = core_inputs(inp, inp["x"][b], S, L, consts)
        in_maps.append({k: np.ascontiguousarray(v, dtype=np.float32) for k, v in m.items()})
    res = run_bass_kernel_spmd(nc, in_maps, core_ids=list(range(B)))
    out = np.stack([np.asarray(res.results[b]["y"], dtype=np.float32) for b in range(B)], axis=0)
    return out
```

```python
import numpy as np
from concourse.bass_utils import run_bass_kernel_spmd
from contextlib import ExitStack
import concourse.bass as bass
import concourse.mybir as mybir

F32 = mybir.dt.float32
BF16 = mybir.dt.bfloat16
AF = mybir.ActivationFunctionType
ALU = mybir.AluOpType
AX = mybir.AxisListType


class Res:
    __slots__ = ("name", "w", "r", "cin", "cout")

    def __init__(self, name):
        self.name = name
        self.w = {}
        self.r = {}
        self.cin = None
        self.cout = None


class Chan:
    __slots__ = ("sid", "cnt")

    def __init__(self, sid):
        self.sid = sid
        self.cnt = 0


class Eng:
    def __init__(self, name, sid):
        self.name = name
        self.sid = sid
        self.cnt = 0
        self.waited = {}
        self.prog = []


class KB:
    def __init__(self, nc):
        self.nc = nc
        self.es = ExitStack()
        self.sems = []
        self.engs = {}
        self.chans = []
        for k in ("pe", "act", "dve", "pool", "sp"):
            self.engs[k] = Eng(k, self.newsem("e_" + k))
        self.nres = 0
        self.pes = None
        self.ntens = 0
        self.free_ch = []
        self.phase_ch = []

    def newsem(self, name):
        s = self.es.enter_context(self.nc.semaphore("%s_%d" % (name, len(self.sems))))
        self.sems.append(s)
        return len(self.sems) - 1

    def res(self, name=None):
        self.nres += 1
        return Res(name or ("r%d" % self.nres))

    def sbuf(self, name, shape, dt):
        st = self.pes if self.pes is not None else self.es
        self.ntens += 1
        return st.enter_context(self.nc.sbuf_tensor("sb%d_%s" % (self.ntens, name), list(shape), dt))

    def psum(self, name, shape, dt):
        st = self.pes if self.pes is not None else self.es
        self.ntens += 1
        return st.enter_context(self.nc.psum_tensor("ps%d_%s" % (self.ntens, name), list(shape), dt))

    def begin_phase(self):
        self.pes = ExitStack()

    def end_phase(self):
        self.barrier()
        self.emit_block()
        self.pes.close()
        self.pes = None
        self.free_ch.extend(self.phase_ch)
        self.phase_ch = []

    def totals(self):
        t = {}
        for E in self.engs.values():
            t[E.sid] = E.cnt
        for ch in self.chans:
            t[ch.sid] = ch.cnt
        return t

    def barrier(self):
        tot = self.totals()
        for E in self.engs.values():
            waits = []
            for s, v in tot.items():
                if v > 0 and E.waited.get(s, 0) < v:
                    E.waited[s] = v
                    waits.append((s, v))
            E.prog.append((waits, None, None, 0))

    def _deps(self, E, reads, writes, same_ok):
        deps = {}
        for r in reads:
            for s, v in r.w.items():
                if deps.get(s, 0) < v:
                    deps[s] = v
        for w in writes:
            for s, v in w.w.items():
                if deps.get(s, 0) < v:
                    deps[s] = v
            for s, v in w.r.items():
                if deps.get(s, 0) < v:
                    deps[s] = v
        waits = []
        for s, v in deps.items():
            if s == E.sid and same_ok:
                continue
            if E.waited.get(s, 0) >= v:
                continue
            E.waited[s] = v
            waits.append((s, v))
        return waits

    def op(self, eng, fn, reads=(), writes=()):
        E = self.engs[eng]
        waits = self._deps(E, reads, writes, same_ok=(eng == "pe"))
        E.cnt += 1
        c = E.cnt
        E.prog.append((waits, fn, E.sid, 1))
        for r in reads:
            r.r[E.sid] = c
        for w in writes:
            w.w[E.sid] = c

    def dma(self, q, out, in_, reads=(), writes=(), chan_res=None, **kw):
        E = self.engs[q]
        waits = self._deps(E, reads, writes, same_ok=False)
        cr = chan_res
        if cr.cin is None:
            if self.free_ch:
                cr.cin = self.free_ch.pop()
            else:
                cr.cin = Chan(self.newsem("c"))
                self.chans.append(cr.cin)
            if self.pes is not None:
                self.phase_ch.append(cr.cin)
        ch = cr.cin
        ch.cnt += 16
        c = ch.cnt
        E.prog.append((waits, (lambda e, out=out, in_=in_, kw=kw: e.dma_start(out=out, in_=in_, **kw)), ch.sid, 16))
        for r in reads:
            r.r[ch.sid] = c
        for w in writes:
            w.w[ch.sid] = c

    def finish_wait(self, eng, resources):
        E = self.engs[eng]
        waits = self._deps(E, resources, (), same_ok=False)
        E.prog.append((waits, None, None, 0))

    def emit(self):
        self.emit_block()
        self.es.close()

    def emit_block(self):
        nc = self.nc
        sems = self.sems
        with nc.Block() as block:
            def run(E):
                def body(e):
                    for waits, fn, sid, inc in E.prog:
                        for s, v in waits:
                            e.wait_ge(sems[s], v)
                        if fn is not None:
                            ins = fn(e)
                            ins.then_inc(sems[sid], inc)
                return body
            block.tensor(run(self.engs["pe"]))
            block.scalar(run(self.engs["act"]))
            block.vector(run(self.engs["dve"]))
            block.gpsimd(run(self.engs["pool"]))
            block.sync(run(self.engs["sp"]))
        for E in self.engs.values():
            E.prog = []


D = 2048
KC = 16
DIN = 3904
EPS = 1e-6


class Pool:
    def __init__(self, kb, name, shape, dt, n, space="sbuf"):
        self.items = []
        for i in range(n):
            t = (kb.sbuf if space == "sbuf" else kb.psum)("%s%d" % (name, i), shape, dt)
            self.items.append((t, kb.res("%s%d" % (name, i))))
        self.i = 0

    def next(self):
        it = self.items[self.i % len(self.items)]
        self.i += 1
        return it


class Ctx:
    pass


def build(S, L, consts, debug=False, upto=99):
    nc = bass.Bass("TRN2", target_bir_lowering=False)
    kb = KB(nc)
    NT = S // 128
    NB = S // 512
    dbgkind = "ExternalOutput" if debug else "Internal"

    def din(name, shape, dt=F32):
        return nc.dram_tensor(name, list(shape), dt, kind="ExternalInput").ap()

    def dscr(name, shape, dt=BF16, kind=None):
        return nc.dram_tensor(name, list(shape), dt, kind=kind or dbgkind).ap()

    x_in = din("x", [S, D])
    w_in = din("w_in", [L, D, DIN])
    gains = din("gains", [L, 128, 64])
    ident_in = din("ident", [128, 128])
    cs_in = din("cossin", [2, 64, S])
    y_out = nc.dram_tensor("y", [S, D], F32, kind="ExternalOutput").ap()

    HT = dscr("HT", [31, 128, S])
    VA = dscr("VA", [S, 2, 129])
    VB = dscr("VB", [S, 4, 129])
    r_HT = kb.res("HT"); r_VA = kb.res("VA"); r_VB = kb.res("VB")
    r_x = kb.res("x_in"); r_w_in = kb.res("w_in"); r_const = kb.res("const")

    ident32 = kb.sbuf("ident32", [128, 128], F32)
    ident = kb.sbuf("ident", [128, 128], BF16)
    gn = kb.sbuf("gn", [128, L, 64], F32)
    gneg = kb.sbuf("gneg", [128, L, 64], F32)
    epsb = kb.sbuf("epsb", [128, 1], F32)
    r_id = kb.res("ident"); r_gn = kb.res("gn")
    kb.dma("sp", ident32[:], ident_in, reads=[r_const], writes=[r_id], chan_res=r_id)
    kb.op("dve", lambda e: e.tensor_copy(out=ident[:], in_=ident32[:]), reads=[r_id], writes=[r_id])
    kb.dma("sp", gn[:], gains.rearrange("l p c -> p l c"), reads=[r_const], writes=[r_gn], chan_res=r_gn)
    kb.op("dve", lambda e: e.tensor_scalar(out=gneg[:], in0=gn[:], scalar1=-1.0, scalar2=None, op0=ALU.mult),
          reads=[r_gn], writes=[r_gn])
    kb.op("dve", lambda e: e.memset(epsb[:], EPS), writes=[r_gn])

    NTAB = consts["_ntab"]
    schedB = consts["_schedB"]
    rot_in = din("rotR", [64, 64])
    tabA_in = din("tabA", [128, 3072])
    maskB_in = din("maskB", [128, NTAB, 128])
    biasB_in = din("biasB", [L, 128, NTAB, 4, 128])
    sink_in = din("sinkb", [128, L, 8])
    w_uq = din("c_w_uq", [L, 512, 768])
    w_ukv = din("c_w_ukv", [L, 256, 1024])
    w_o = din("w_o", [L, D, D])
    fnorm_in = din("fnorm", [128, D])
    w_pq = din("peer_w_q", [L, D, 1024])
    sk_in = din("sk", [L, 128, 256])
    peer_u = din("peer_u", [L, 16384, D])
    peer_v = din("peer_v", [L, 16384, D])
    ln2b_in = din("ln2b", [L, 128, D])
    TBT_MAX = 8
    r_win2 = kb.res("win2")

    QCN = dscr("QCN", [4, 128, S]); QCP = dscr("QCP", [4, 64, S]); KCN = dscr("KCN", [4, 128, S]); KCP = dscr("KCP", [64, S])
    VC = dscr("VC", [S, 4, 129])
    O = dscr("O", [S, D], F32)
    XA = dscr("XA", [S, D], F32); XB = dscr("XB", [S, D], F32)
    XN2T = dscr("XN2T", [KC, 128, S])
    QT = dscr("QT", [8, 128, S])
    GT = dscr("GT", [NT, 128, 128, 128])
    r_QT = kb.res("QT"); r_GT = kb.res("GT"); r_y = kb.res("y")
    r_QCN = kb.res("QCN"); r_QCP = kb.res("QCP"); r_KCN = kb.res("KCN"); r_KCP = kb.res("KCP"); r_VC = kb.res("VC")
    r_O = kb.res("O"); r_XA = kb.res("XA"); r_XB = kb.res("XB"); r_XN2T = kb.res("XN2T")

    rot32 = kb.sbuf("rot32", [64, 64], F32)
    rotb = kb.sbuf("rotb", [64, 64], BF16)
    onesb = kb.sbuf("onesb", [128, 128], BF16)
    esink = kb.sbuf("esink", [128, L, 8], F32)
    kb.dma("sp", rot32[:], rot_in, reads=[r_const], writes=[r_id], chan_res=r_id)
    kb.op("dve", lambda e: e.tensor_copy(out=rotb[:], in_=rot32[:]), reads=[r_id], writes=[r_id])
    kb.op("dve", lambda e: e.memset(onesb[:], 1.0), writes=[r_id])
    kb.dma("sp", esink[:], sink_in, reads=[r_const], writes=[r_gn], chan_res=r_gn)
    kb.op("act", lambda e: e.activation(out=esink[:], in_=esink[:], func=AF.Exp), reads=[r_gn], writes=[r_gn])

    SC_A = 128 ** -0.5
    SC_C = 192 ** -0.5

    def mk_norm_helpers(st_pool, sq_pool, ps_tp):
        def rmsnorm_tile(xt, r_xt, groups, xn, r_xn, gain=None):
            stt, r_st = st_pool.next()
            sq, r_sq = sq_pool.next()
            ng = len(groups)
            for g, (s0, wd) in enumerate(groups):
                kb.op("act", lambda e, s0=s0, wd=wd, g=g: e.activation(out=sq[:, s0:s0 + wd], in_=xt[:, s0:s0 + wd], func=AF.Square,
                                                                       accum_out=stt[:, g:g + 1]),
                      reads=[r_xt], writes=[r_sq, r_st])
            for g, (s0, wd) in enumerate(groups):
                kb.op("act", lambda e, g=g, wd=wd: e.activation(out=stt[:, g:g + 1], in_=stt[:, g:g + 1], func=AF.Ln,
                                                                bias=epsb[:, 0:1], scale=1.0 / wd),
                      reads=[r_st, r_gn], writes=[r_st])
            kb.op("act", lambda e: e.activation(out=stt[:, 0:ng], in_=stt[:, 0:ng], func=AF.Exp, scale=-0.5),
                  reads=[r_st], writes=[r_st])
            for g, (s0, wd) in enumerate(groups):
                if gain is None:
                    kb.op("dve", lambda e, s0=s0, wd=wd, g=g: e.tensor_scalar(out=xn[:, s0:s0 + wd], in0=xt[:, s0:s0 + wd],
                                                                              scalar1=stt[:, g:g + 1], scalar2=None, op0=ALU.mult),
                          reads=[r_xt, r_st], writes=[r_xn])
                else:
                    gt_, r_gt_ = gain
                    kb.op("dve", lambda e, s0=s0, wd=wd, g=g: e.scalar_tensor_tensor(out=xn[:, s0:s0 + wd], in0=xt[:, s0:s0 + wd],
                                                                                     scalar=stt[:, g:g + 1], in1=gt_[:, s0:s0 + wd],
                                                                                     op0=ALU.mult, op1=ALU.mult),
                          reads=[r_xt, r_st, r_gt_], writes=[r_xn])

        def transpose_tile(xn, r_xn, dst, r_dst, j):
            pt, r_pt = ps_tp.next()
            for k in range(KC):
                kb.op("pe", lambda e, k=k: e.transpose(out=pt[:, k, :], in_=xn[:, k * 128:(k + 1) * 128], identity=ident[:]),
                      reads=[r_xn, r_id], writes=[r_pt])
            kb.op("act", lambda e: e.copy(out=dst[:, :, j * 128:(j + 1) * 128], in_=pt[:]), reads=[r_pt], writes=[r_dst])
        return rmsnorm_tile, transpose_tile

    def load_w(wst_pool, src_ap, gcol, ncols, l, dst, r_dst, r_src):
        nk = src_ap.shape[0] // 128
        for k in range(nk):
            st, r_st = wst_pool.next()
            kb.dma("sp", st[:, :ncols], src_ap[k * 128:(k + 1) * 128, :], reads=[r_src], writes=[r_st], chan_res=r_st)
            if gcol is None:
                kb.op("act", lambda e, k=k, st=st: e.copy(out=dst[:, k, :ncols], in_=st[:, :ncols]), reads=[r_st], writes=[r_dst])
            else:
                kb.op("act", lambda e, k=k, st=st: e.activation(out=dst[:, k, :ncols], in_=st[:, :ncols], func=AF.Copy,
                                                                scale=gn[:, l, gcol + k:gcol + k + 1]),
                      reads=[r_st, r_gn], writes=[r_dst])

    def phase1(l, x_src, r_xsrc):
        kb.begin_phase()
        xt_pool = Pool(kb, "xt", [128, D], F32, 2)
        xn_pool = Pool(kb, "xn", [128, D], BF16, 2)
        sq_pool = Pool(kb, "sq", [128, D], BF16, 1)
        st_pool = Pool(kb, "stat", [128, 4], F32, 4)
        xnT_pool = Pool(kb, "xnT", [128, KC, 512], BF16, 2)
        ps_tp = Pool(kb, "tp", [128, KC, 128], BF16, 2, space="psum")
        ps_mm = Pool(kb, "mm", [128, 512], F32, 4, space="psum")
        wst_pool = Pool(kb, "wst", [128, 2048], F32, 2)
        wbf = kb.sbuf("wbf", [128, KC, 2048], BF16)
        r_wbf = kb.res("wbf")
        ev_pool = Pool(kb, "ev", [128, 512], BF16, 4)
        vst_pool = Pool(kb, "vst", [128, 4, 129], BF16, 2)
        for t, r in vst_pool.items:
            kb.op("pool", lambda e, t=t: e.memset(t[:], 1.0), writes=[r])
        rmsnorm_tile, transpose_tile = mk_norm_helpers(st_pool, sq_pool, ps_tp)
        passes = [
            dict(c0=0, fm=[(c, c * 128, 128) for c in range(0, 10)] + [(c, c * 128, 128) for c in range(12, 16)],
                 tm=[(VA, r_VA, 1280, 2)]),
            dict(c0=2048, fm=[(c, c * 128 - 2048, 128) for c in range(16, 20)] + [(c, c * 128 - 2048, 128) for c in range(24, 30)]
                 + [(30, 30 * 128 - 2048, 64)],
                 tm=[(VB, r_VB, 2560 - 2048, 4)]),
        ]
        for ps in passes:
            c0 = ps["c0"]
            ncols = min(2048, DIN - c0)
            load_w(wst_pool, w_in[l, :, c0:c0 + ncols], 0, ncols, l, wbf, r_wbf, r_w_in)
            for b in range(NB):
                xnT, r_xnT = xnT_pool.next()
                for j in range(4):
                    tok = b * 512 + j * 128
                    xt, r_xt = xt_pool.next()
                    kb.dma("sp", xt[:], x_src[tok:tok + 128, :], reads=[r_xsrc], writes=[r_xt], chan_res=r_xt)
                    xn, r_xn = xn_pool.next()
                    rmsnorm_tile(xt, r_xt, [(0, D)], xn, r_xn)
                    transpose_tile(xn, r_xn, xnT, r_xnT, j)
                for (c, off, m) in ps["fm"]:
                    pm, r_pm = ps_mm.next()
                    for k in range(KC):
                        kb.op("pe", lambda e, k=k, off=off, m=m, pm=pm, xnT=xnT: e.matmul(
                            pm[0:m, :], lhsT=wbf[:, k, off:off + m], rhs=xnT[:, k, :], start=(k == 0), stop=(k == KC - 1)),
                            reads=[r_wbf, r_xnT], writes=[r_pm])
                    ev, r_ev = ev_pool.next()
                    kb.op("dve", lambda e, m=m, pm=pm, ev=ev: e.tensor_copy(out=ev[0:m, :], in_=pm[0:m, :]),
                          reads=[r_pm], writes=[r_ev])
                    kb.dma("pool", HT[c, 0:m, b * 512:(b + 1) * 512], ev[0:m, :], reads=[r_ev], writes=[r_HT], chan_res=r_ev)
                for (dst, r_dstd, off, nh) in ps["tm"]:
                    for j in range(4):
                        tok = b * 512 + j * 128
                        pm, r_pm = ps_mm.next()
                        for k in range(KC):
                            kb.op("pe", lambda e, k=k, off=off, nh=nh, pm=pm, xnT=xnT, j=j: e.matmul(
                                pm[:, 0:nh * 128], lhsT=xnT[:, k, j * 128:(j + 1) * 128], rhs=wbf[:, k, off:off + nh * 128],
                                start=(k == 0), stop=(k == KC - 1)),
                                reads=[r_wbf, r_xnT], writes=[r_pm])
                        vs, r_vs = vst_pool.next()
                        kb.op("act", lambda e, nh=nh, pm=pm, vs=vs: e.copy(
                            out=vs[:, 0:nh, 0:128], in_=pm[:, 0:nh * 128].rearrange("p (h d) -> p h d", d=128)),
                            reads=[r_pm], writes=[r_vs])
                        kb.dma("pool", dst[tok:tok + 128, :, :], vs[:, 0:nh, :], reads=[r_vs], writes=[r_dstd], chan_res=r_vs)
        kb.end_phase()

    def phase1c(l):
        kb.begin_phase()
        wst_pool = Pool(kb, "wst", [128, 1024], F32, 2)
        wuq = kb.sbuf("wuq", [128, 4, 768], BF16)
        wukv = kb.sbuf("wukv", [128, 2, 1024], BF16)
        r_wu = kb.res("wu")
        load_w(wst_pool, w_uq[l], 16, 768, l, wuq, r_wu, r_win2)
        load_w(wst_pool, w_ukv[l], 20, 1024, l, wukv, r_wu, r_win2)
        cin_pool = Pool(kb, "cin", [128, 6, 512], BF16, 2)
        krin_pool = Pool(kb, "krin", [64, 512], BF16, 2)
        cs_pool = Pool(kb, "cs", [64, 2, 512], F32, 2)
        sqc_pool = Pool(kb, "sqc", [128, 6, 512], BF16, 1)
        rstd_pool = Pool(kb, "rstd", [128, 2, 512], F32, 1)
        cn_pool = Pool(kb, "cn", [128, 6, 512], BF16, 2)
        ps_mm = Pool(kb, "mm", [128, 512], F32, 6, space="psum")
        ev_pool = Pool(kb, "ev", [128, 512], BF16, 4)
        qpe_pool = Pool(kb, "qpe", [64, 512], BF16, 2)
        t1_pool = Pool(kb, "t1", [64, 512], F32, 2)
        t2_pool = Pool(kb, "t2", [64, 512], F32, 2)
        vst_pool = Pool(kb, "vst", [128, 4, 129], BF16, 2)
        for t, r in vst_pool.items:
            kb.op("pool", lambda e, t=t: e.memset(t[:], 1.0), writes=[r])

        def rope_store(src, r_src, cs, r_cs, dst_ap, r_dst):
            pr, r_pr = ps_mm.next()
            kb.op("pe", lambda e: e.matmul(pr[0:64, :], lhsT=rotb[:], rhs=src[:], start=True, stop=True),
                  reads=[r_src, r_id], writes=[r_pr])
            t1, r_t1 = t1_pool.next()
            t2, r_t2 = t2_pool.next()
            kb.op("dve", lambda e: e.tensor_tensor(out=t1[:], in0=src[:], in1=cs[:, 0, :], op=ALU.mult),
                  reads=[r_src, r_cs], writes=[r_t1])
            kb.op("dve", lambda e: e.tensor_tensor(out=t2[:], in0=pr[0:64, :], in1=cs[:, 1, :], op=ALU.mult),
                  reads=[r_pr, r_cs], writes=[r_t2])
            ev, r_ev = ev_pool.next()
            kb.op("dve", lambda e: e.tensor_tensor(out=ev[0:64, :], in0=t1[:], in1=t2[:], op=ALU.add),
                  reads=[r_t1, r_t2], writes=[r_ev])
            kb.dma("pool", dst_ap, ev[0:64, :], reads=[r_ev], writes=[r_dst], chan_res=r_ev)

        for b in range(NB):
            bs = slice(b * 512, (b + 1) * 512)
            cin, r_cin = cin_pool.next()
            kb.dma("sp", cin[:], HT[24:30, :, bs].rearrange("c p t -> p c t"), reads=[r_HT], writes=[r_cin], chan_res=r_cin)
            krin, r_krin = krin_pool.next()
            kb.dma("sp", krin[:], HT[30, 0:64, bs], reads=[r_HT], writes=[r_krin], chan_res=r_krin)
            cs, r_cs = cs_pool.next()
            kb.dma("sp", cs[:], cs_in[:, :, bs].rearrange("c p t -> p c t"), reads=[r_const], writes=[r_cs], chan_res=r_cs)
            sqc, r_sqc = sqc_pool.next()
            kb.op("act", lambda e, sqc=sqc, cin=cin: e.activation(out=sqc[:], in_=cin[:], func=AF.Square), reads=[r_cin], writes=[r_sqc])
            rstd, r_rstd = rstd_pool.next()
            cn, r_cn = cn_pool.next()
            for gi, (c0, nchunk) in enumerate([(0, 4), (4, 2)]):
                pss, r_pss = ps_mm.next()
                for k in range(nchunk):
                    kb.op("pe", lambda e, k=k, c0=c0, nchunk=nchunk, pss=pss, sqc=sqc: e.matmul(
                        pss[:], lhsT=onesb[:], rhs=sqc[:, c0 + k, :], start=(k == 0), stop=(k == nchunk - 1)),
                        reads=[r_sqc, r_id], writes=[r_pss])
                kb.op("act", lambda e, gi=gi, nchunk=nchunk, pss=pss, rstd=rstd: e.activation(
                    out=rstd[:, gi, :], in_=pss[:], func=AF.Ln, bias=epsb[:, 0:1], scale=1.0 / (nchunk * 128)),
                    reads=[r_pss, r_gn], writes=[r_rstd])
                kb.op("act", lambda e, gi=gi, rstd=rstd: e.activation(out=rstd[:, gi, :], in_=rstd[:, gi, :], func=AF.Exp, scale=-0.5),
                      reads=[r_rstd], writes=[r_rstd])
                for k in range(nchunk):
                    kb.op("dve", lambda e, k=k, c0=c0, gi=gi, cn=cn, cin=cin, rstd=rstd: e.tensor_tensor(
                        out=cn[:, c0 + k, :], in0=cin[:, c0 + k, :], in1=rstd[:, gi, :], op=ALU.mult),
                        reads=[r_cin, r_rstd], writes=[r_cn])
            for h in range(4):
                pm, r_pm = ps_mm.next()
                for k in range(4):
                    kb.op("pe", lambda e, k=k, h=h, pm=pm, cn=cn: e.matmul(
                        pm[:], lhsT=wuq[:, k, h * 192:h * 192 + 128], rhs=cn[:, k, :], start=(k == 0), stop=(k == 3)),
                        reads=[r_wu, r_cn], writes=[r_pm])
                ev, r_ev = ev_pool.next()
                kb.op("act", lambda e, pm=pm, ev=ev: e.copy(out=ev[:], in_=pm[:]), reads=[r_pm], writes=[r_ev])
                kb.dma("pool", QCN[h, :, bs], ev[:], reads=[r_ev], writes=[r_QCN], chan_res=r_ev)
                pm, r_pm = ps_mm.next()
                for k in range(4):
                    kb.op("pe", lambda e, k=k, h=h, pm=pm, cn=cn: e.matmul(
                        pm[0:64, :], lhsT=wuq[:, k, h * 192 + 128:h * 192 + 192], rhs=cn[:, k, :], start=(k == 0), stop=(k == 3)),
                        reads=[r_wu, r_cn], writes=[r_pm])
                qpe, r_qpe = qpe_pool.next()
                kb.op("act", lambda e, pm=pm, qpe=qpe: e.copy(out=qpe[:], in_=pm[0:64, :]), reads=[r_pm], writes=[r_qpe])
                rope_store(qpe, r_qpe, cs, r_cs, QCP[h, :, bs], r_QCP)
                pm, r_pm = ps_mm.next()
                for k in range(2):
                    kb.op("pe", lambda e, k=k, h=h, pm=pm, cn=cn: e.matmul(
                        pm[:], lhsT=wukv[:, k, h * 256:h * 256 + 128], rhs=cn[:, 4 + k, :], start=(k == 0), stop=(k == 1)),
                        reads=[r_wu, r_cn], writes=[r_pm])
                ev, r_ev = ev_pool.next()
                kb.op("act", lambda e, pm=pm, ev=ev: e.copy(out=ev[:], in_=pm[:]), reads=[r_pm], writes=[r_ev])
                kb.dma("pool", KCN[h, :, bs], ev[:], reads=[r_ev], writes=[r_KCN], chan_res=r_ev)
            for j in range(4):
                tok = b * 512 + j * 128
                pm, r_pm = ps_mm.next()
                for k in range(2):
                    kb.op("pe", lambda e, k=k, j=j, pm=pm, cn=cn: e.matmul(
                        pm[:].rearrange("p (h d) -> p h d", d=128), lhsT=cn[:, 4 + k, j * 128:(j + 1) * 128],
                        rhs=wukv[:, k, :].rearrange("p (h c) -> p h c", c=256)[:, :, 128:256], start=(k == 0), stop=(k == 1)),
                        reads=[r_wu, r_cn], writes=[r_pm])
                vs, r_vs = vst_pool.next()
                kb.op("act", lambda e, pm=pm, vs=vs: e.copy(out=vs[:, :, 0:128], in_=pm[:].rearrange("p (h d) -> p h d", d=128)),
                      reads=[r_pm], writes=[r_vs])
                kb.dma("pool", VC[tok:tok + 128, :, :], vs[:], reads=[r_vs], writes=[r_VC], chan_res=r_vs)
            rope_store(krin, r_krin, cs, r_cs, KCP[:, bs], r_KCP)
        kb.end_phase()

    def attn_block(P, kts, s_mm, tab, v_rhs, scale, finals):
        nk = len(kts)
        for i, kt in enumerate(kts):
            ps, r_ps = P.ps_s.next()
            s_mm(kt, ps, r_ps)
            pt, r_pt = P.pt_pool.next()
            tb = tab(kt)
            if tb is None:
                kb.op("act", lambda e, ps=ps, pt=pt: e.activation(out=pt[:], in_=ps[:], func=AF.Exp, scale=scale),
                      reads=[r_ps], writes=[r_pt])
            else:
                ex, r_ex = P.ex_pool.next()
                kb.op("act", lambda e, ps=ps, ex=ex: e.activation(out=ex[:], in_=ps[:], func=AF.Exp, scale=scale),
                      reads=[r_ps], writes=[r_ex])
                kb.op("dve", lambda e, ex=ex, pt=pt, tb=tb: e.tensor_tensor(out=pt[:], in0=ex[:], in1=tb, op=ALU.mult),
                      reads=[r_ex, P.r_tab], writes=[r_pt])
            for g in range(4):
                po, r_po = P.ps_o[g]
                kb.op("pe", lambda e, g=g, po=po, pt=pt, kt=kt, i=i: e.matmul(
                    po[:, 0:129], lhsT=pt[:, g * 128:(g + 1) * 128], rhs=v_rhs(kt, g), start=(i == 0), stop=(i == nk - 1)),
                    reads=[r_pt, P.r_v], writes=[r_po])
        for (g, out_ap, r_out, extra) in finals:
            po, r_po = P.ps_o[g]
            dn, r_dn = P.den_pool.next()
            if extra is not None:
                kb.op("dve", lambda e, po=po, dn=dn, extra=extra: e.tensor_scalar(out=dn[:, 0:1], in0=po[:, 128:129], scalar1=extra,
                                                                                 scalar2=None, op0=ALU.add),
                      reads=[r_po, r_gn], writes=[r_dn])
                kb.op("dve", lambda e, dn=dn: e.reciprocal(out=dn[:, 1:2], in_=dn[:, 0:1]), reads=[r_dn], writes=[r_dn])
            else:
                kb.op("dve", lambda e, po=po, dn=dn: e.reciprocal(out=dn[:, 1:2], in_=po[:, 128:129]), reads=[r_po], writes=[r_dn])
            kb.op("dve", lambda e, po=po, dn=dn, out_ap=out_ap: e.tensor_scalar(out=out_ap, in0=po[:, 0:128], scalar1=dn[:, 1:2],
                                                                               scalar2=None, op0=ALU.mult),
                  reads=[r_po, r_dn], writes=[r_out])

    def attn_pools():
        P = Ctx()
        P.ps_s = Pool(kb, "pss", [128, 512], F32, 2, space="psum")
        P.ps_o = [(kb.psum("pso%d" % g, [128, 512], F32), kb.res("pso%d" % g)) for g in range(4)]
        P.pt_pool = Pool(kb, "pt", [128, 512], BF16, 3)
        P.ex_pool = Pool(kb, "ex", [128, 512], BF16, 2)
        P.den_pool = Pool(kb, "den", [128, 2], F32, 4)
        P.r_tab = kb.res("tab")
        P.r_v = kb.res("vres")
        return P

    def phase2a(l):
        kb.begin_phase()
        P = attn_pools()
        QA = kb.sbuf("QA", [128, 8, S], BF16)
        KA = kb.sbuf("KA", [128, 2, S], BF16)
        VAs = kb.sbuf("VAs", [128, NT, 2, 129], BF16)
        tab32 = kb.sbuf("tab32", [128, 3072], F32)
        tabA = kb.sbuf("tabAb", [128, 3, 8, 128], BF16)
        r_q = kb.res("QAres")
        kb.dma("sp", QA[:], HT[0:8, :, :].rearrange("c p t -> p c t"), reads=[r_HT], writes=[r_q], chan_res=r_q)
        kb.dma("sp", KA[:], HT[8:10, :, :].rearrange("c p t -> p c t"), reads=[r_HT], writes=[P.r_v], chan_res=P.r_v)
        kb.dma("sp", VAs[:], VA.rearrange("(n p) g d -> p n g d", p=128), reads=[r_VA], writes=[P.r_v], chan_res=P.r_v)
        kb.dma("sp", tab32[:], tabA_in, reads=[r_const], writes=[P.r_tab], chan_res=P.r_tab)
        kb.op("dve", lambda e: e.tensor_copy(out=tabA[:].rearrange("p a h q -> p (a h q)"), in_=tab32[:]), reads=[P.r_tab], writes=[P.r_tab])
        o_pool = Pool(kb, "ost", [128, 1024], F32, 2)
        for n in range(NT):
            ost, r_ost = o_pool.next()
            for kg in range(2):
                kts = [m for m in (n - 1, n, n + 1) if 0 <= m < NT]

                def s_mm(kt, ps, r_ps, kg=kg, n=n):
                    kb.op("pe", lambda e: e.matmul(ps[:].rearrange("p (h q) -> p h q", q=128), lhsT=KA[:, kg, kt * 128:(kt + 1) * 128],
                                                   rhs=QA[:, 4 * kg:4 * kg + 4, n * 128:(n + 1) * 128], start=True, stop=True),
                          reads=[P.r_v, r_q], writes=[r_ps])
                attn_block(P, kts, s_mm,
                           lambda kt, kg=kg, n=n: tabA[:, kt - n + 1, 4 * kg:4 * kg + 4, :].rearrange("p h q -> p (h q)"),
                           lambda kt, g, kg=kg: VAs[:, kt, kg, :], SC_A,
                           [(g, ost[:, (4 * kg + g) * 128:(4 * kg + g + 1) * 128], r_ost, esink[:, l, 4 * kg + g:4 * kg + g + 1]) for g in range(4)])
            kb.dma("pool", O[n * 128:(n + 1) * 128, 0:1024], ost[:], reads=[r_ost], writes=[r_O], chan_res=r_ost)
        kb.end_phase()

    def phase2b(l):
        kb.begin_phase()
        P = attn_pools()
        QB = kb.sbuf("QB", [128, 4, S], BF16)
        KBs = kb.sbuf("KBs", [128, 4, S], BF16)
        VBs = kb.sbuf("VBs", [128, NT, 4, 129], BF16)
        tabB = kb.sbuf("tabB", [128, NTAB, 4, 128], BF16)
        mk32 = kb.sbuf("mk32", [128, NTAB, 128], F32)
        bst_pool = Pool(kb, "bst", [128, 4, 128], F32, 2)
        r_q = kb.res("QBres")
        kb.dma("sp", QB[:], HT[12:16, :, :].rearrange("c p t -> p c t"), reads=[r_HT], writes=[r_q], chan_res=r_q)
        kb.dma("sp", KBs[:], HT[16:20, :, :].rearrange("c p t -> p c t"), reads=[r_HT], writes=[P.r_v], chan_res=P.r_v)
        kb.dma("sp", VBs[:], VB.rearrange("(n p) g d -> p n g d", p=128), reads=[r_VB], writes=[P.r_v], chan_res=P.r_v)
        kb.dma("sp", mk32[:], maskB_in, reads=[r_const], writes=[P.r_tab], chan_res=P.r_tab)
        for ti in range(NTAB):
            bst, r_bst = bst_pool.next()
            kb.dma("sp", bst[:], biasB_in[l, :, ti, :, :], reads=[r_const], writes=[r_bst], chan_res=r_bst)
            kb.op("act", lambda e, bst=bst: e.activation(out=bst[:], in_=bst[:], func=AF.Exp), reads=[r_bst], writes=[r_bst])
            for h in range(4):
                kb.op("dve", lambda e, bst=bst, ti=ti, h=h: e.tensor_tensor(out=tabB[:, ti, h, :], in0=bst[:, h, :], in1=mk32[:, ti, :], op=ALU.mult),
                      reads=[r_bst, P.r_tab], writes=[P.r_tab])
        o_pool = Pool(kb, "ost", [128, 512], F32, 2)
        for n in range(NT):
            ost, r_ost = o_pool.next()
            lst = schedB[n]
            tmap = dict(lst)

            def s_mm(kt, ps, r_ps, n=n):
                for h in range(4):
                    kb.op("pe", lambda e, h=h: e.matmul(ps[:, h * 128:(h + 1) * 128], lhsT=KBs[:, h, kt * 128:(kt + 1) * 128],
                                                        rhs=QB[:, h, n * 128:(n + 1) * 128], start=True, stop=True),
                          reads=[P.r_v, r_q], writes=[r_ps])
            attn_block(P, [m for m, _ in lst], s_mm,
                       lambda kt, tmap=tmap: tabB[:, tmap[kt], :, :].rearrange("p h q -> p (h q)"),
                       lambda kt, g: VBs[:, kt, g, :], SC_A,
                       [(g, ost[:, g * 128:(g + 1) * 128], r_ost, None) for g in range(4)])
            kb.dma("pool", O[n * 128:(n + 1) * 128, 1024:1536], ost[:], reads=[r_ost], writes=[r_O], chan_res=r_ost)
        kb.end_phase()

    def phase2c(l):
        kb.begin_phase()
        P = attn_pools()
        KN = kb.sbuf("KN", [128, 4, S], BF16)
        KP = kb.sbuf("KP", [64, S], BF16)
        VCs = kb.sbuf("VCs", [128, NT, 4, 129], BF16)
        kb.dma("sp", KN[:], KCN.rearrange("c p t -> p c t"), reads=[r_KCN], writes=[P.r_v], chan_res=P.r_v)
        kb.dma("sp", KP[:], KCP, reads=[r_KCP], writes=[P.r_v], chan_res=P.r_v)
        kb.dma("sp", VCs[:], VC.rearrange("(n p) g d -> p n g d", p=128), reads=[r_VC], writes=[P.r_v], chan_res=P.r_v)
        qn_pool = Pool(kb, "qn", [128, 4, 512], BF16, 2)
        qp_pool = Pool(kb, "qp", [64, 4, 512], BF16, 2)
        o_pool = Pool(kb, "ost", [128, 4, 512], F32, 2)
        for b in range(NB):
            bs = slice(b * 512, (b + 1) * 512)
            qn, r_qn = qn_pool.next()
            qp, r_qp = qp_pool.next()
            kb.dma("sp", qn[:], QCN[:, :, bs].rearrange("c p t -> p c t"), reads=[r_QCN], writes=[r_qn], chan_res=r_qn)
            kb.dma("sp", qp[:], QCP[:, :, bs].rearrange("c p t -> p c t"), reads=[r_QCP], writes=[r_qp], chan_res=r_qp)
            ost, r_ost = o_pool.next()
            for h in range(4):
                def s_mm(kt, ps, r_ps, h=h, qn=qn, qp=qp, r_qn=r_qn, r_qp=r_qp):
                    kb.op("pe", lambda e: e.matmul(ps[:], lhsT=KN[:, h, kt * 128:(kt + 1) * 128], rhs=qn[:, h, :], start=True, stop=False),
                          reads=[P.r_v, r_qn], writes=[r_ps])
                    kb.op("pe", lambda e: e.matmul(ps[:], lhsT=KP[:, kt * 128:(kt + 1) * 128], rhs=qp[:, h, :], start=False, stop=True),
                          reads=[P.r_v, r_qp], writes=[r_ps])
                attn_block(P, list(range(NT)), s_mm, lambda kt: None, lambda kt, g, h=h: VCs[:, kt, h, :], SC_C,
                           [(g, ost[:, g, h * 128:(h + 1) * 128], r_ost, None) for g in range(4)])
            kb.dma("pool", O[bs, 1536:2048].rearrange("(g p) d -> p g d", p=128), ost[:], reads=[r_ost], writes=[r_O], chan_res=r_ost)
        kb.end_phase()

    def phase3(l, x_src, r_xsrc, X1, r_X1):
        kb.begin_phase()
        wst_pool = Pool(kb, "wst", [128, 2048], F32, 2)
        wo = kb.sbuf("wo", [128, KC, D], BF16)
        r_wo = kb.res("wo")
        load_w(wst_pool, w_o[l], 22, D, l, wo, r_wo, r_win2)
        g2 = kb.sbuf("g2", [128, D], F32); r_g2 = kb.res("g2")
        kb.dma("sp", g2[:], ln2b_in[l], reads=[r_const], writes=[r_g2], chan_res=r_g2)
        ot_pool = Pool(kb, "ot", [128, D], F32, 2)
        xt_pool = Pool(kb, "xt", [128, D], F32, 2)
        x1_pool = Pool(kb, "x1", [128, D], F32, 2)
        xn_pool = Pool(kb, "xn", [128, D], BF16, 2)
        sq_pool = Pool(kb, "sq", [128, D], BF16, 1)
        st_pool = Pool(kb, "stat", [128, 4], F32, 4)
        oT_pool = Pool(kb, "oT", [128, KC, 128], BF16, 2)
        xn2T_pool = Pool(kb, "xn2T", [128, KC, 512], BF16, 1)
        ps_tp = Pool(kb, "tp", [128, KC, 128], BF16, 2, space="psum")
        ps_mm = Pool(kb, "mm", [128, 512], F32, 4, space="psum")
        rmsnorm_tile, transpose_tile = mk_norm_helpers(st_pool, sq_pool, ps_tp)
        for b in range(NB):
            xn2T, r_xn2T = xn2T_pool.next()
            for j in range(4):
                tok = b * 512 + j * 128
                ot, r_ot = ot_pool.next()
                kb.dma("sp", ot[:], O[tok:tok + 128, :], reads=[r_O], writes=[r_ot], chan_res=r_ot)
                xt, r_xt = xt_pool.next()
                kb.dma("sp", xt[:], x_src[tok:tok + 128, :], reads=[r_xsrc], writes=[r_xt], chan_res=r_xt)
                on, r_on = xn_pool.next()
                rmsnorm_tile(ot, r_ot, [(0, 1024), (1024, 512), (1536, 512)], on, r_on)
                oT, r_oT = oT_pool.next()
                transpose_tile(on, r_on, oT, r_oT, 0)
                x1, r_x1 = x1_pool.next()
                for nb in range(4):
                    pm, r_pm = ps_mm.next()
                    for k in range(KC):
                        kb.op("pe", lambda e, k=k, nb=nb, pm=pm, oT=oT: e.matmul(
                            pm[:], lhsT=oT[:, k, :], rhs=wo[:, k, nb * 512:(nb + 1) * 512], start=(k == 0), stop=(k == KC - 1)),
                            reads=[r_wo, r_oT], writes=[r_pm])
                    kb.op("dve", lambda e, nb=nb, pm=pm, x1=x1, xt=xt: e.tensor_tensor(
                        out=x1[:, nb * 512:(nb + 1) * 512], in0=pm[:], in1=xt[:, nb * 512:(nb + 1) * 512], op=ALU.add),
                        reads=[r_pm, r_xt], writes=[r_x1])
                kb.dma("pool", X1[tok:tok + 128, :], x1[:], reads=[r_x1], writes=[r_X1], chan_res=r_x1)
                xn, r_xn = xn_pool.next()
                rmsnorm_tile(x1, r_x1, [(0, D)], xn, r_xn, gain=(g2, r_g2))
                transpose_tile(xn, r_xn, xn2T, r_xn2T, j)
            kb.dma("pool", XN2T[:, :, b * 512:(b + 1) * 512].rearrange("c p t -> p c t"), xn2T[:], reads=[r_xn2T], writes=[r_XN2T],
                   chan_res=r_xn2T)
        kb.end_phase()

    def phase4a(l):
        kb.begin_phase()
        wst_pool = Pool(kb, "wst", [128, 1024], F32, 2)
        wq = kb.sbuf("wq", [128, KC, 1024], BF16)
        r_wq = kb.res("wq")
        load_w(wst_pool, w_pq[l], None, 1024, l, wq, r_wq, r_win2)
        xb_pool = Pool(kb, "xb", [128, KC, 512], BF16, 2)
        ps_mm = Pool(kb, "mm", [128, 512], F32, 4, space="psum")
        ev_pool = Pool(kb, "ev", [128, 512], BF16, 4)
        for b in range(NB):
            bs = slice(b * 512, (b + 1) * 512)
            xb, r_xb = xb_pool.next()
            kb.dma("sp", xb[:], XN2T[:, :, bs].rearrange("c p t -> p c t"), reads=[r_XN2T], writes=[r_xb], chan_res=r_xb)
            for h in range(8):
                pm, r_pm = ps_mm.next()
                for k in range(KC):
                    kb.op("pe", lambda e, k=k, h=h, pm=pm, xb=xb: e.matmul(
                        pm[:], lhsT=wq[:, k, h * 128:(h + 1) * 128], rhs=xb[:, k, :], start=(k == 0), stop=(k == KC - 1)),
                        reads=[r_wq, r_xb], writes=[r_pm])
                ev, r_ev = ev_pool.next()
                kb.op("act", lambda e, pm=pm, ev=ev: e.copy(out=ev[:], in_=pm[:]), reads=[r_pm], writes=[r_ev])
                kb.dma("pool", QT[h, :, bs], ev[:], reads=[r_ev], writes=[r_QT], chan_res=r_ev)
        kb.end_phase()

    DELTA = 1e-5

    def phase4b(l):
        kb.begin_phase()
        sk32 = kb.sbuf("sk32", [128, 256], F32)
        skb = kb.sbuf("skb", [128, 256], BF16)
        r_sk = kb.res("sk")
        kb.dma("sp", sk32[:], sk_in[l], reads=[r_const], writes=[r_sk], chan_res=r_sk)
        kb.op("dve", lambda e: e.tensor_copy(out=skb[:], in_=sk32[:]), reads=[r_sk], writes=[r_sk])
        q_pool = Pool(kb, "qt", [128, 8, 128], BF16, 2)
        s_pool = Pool(kb, "s", [128, 8, 2, 128], F32, 2)
        top_pool = Pool(kb, "top", [128, 8, 2, 16], F32, 2)
        tmp = kb.sbuf("tmpm", [128, 16, 128], F32)
        r_tmpc = [kb.res("tmpc%d" % i) for i in range(16)]
        cand = kb.sbuf("cand", [128, 8, 256], F32); r_cand = kb.res("cand")
        best = kb.sbuf("best", [128, 8, 16], F32); r_best = kb.res("best")
        dd = kb.sbuf("dd", [128, 8, 16], F32)
        zz = kb.sbuf("zz", [128, 4, 8], F32)
        cc = kb.sbuf("cc", [128, 8, 16], F32)
        cb = kb.sbuf("cb", [128, 8, 16], BF16)
        thr = kb.sbuf("thr", [128, 8, 16], F32)
        r_sm = kb.res("small")
        E2 = kb.sbuf("E2", [128, 8, 128], BF16); r_E2 = kb.res("E2")
        A = kb.sbuf("A", [128, 8, 16, 128], BF16); r_A = kb.res("A")
        Bms = [(kb.sbuf("Bm%d" % i, [128, 8, 16, 64], BF16), kb.res("Bm%d" % i)) for i in range(2)]
        AT = kb.sbuf("AT", [128, 128, 128], BF16); r_AT = kb.res("AT")
        BT = kb.sbuf("BT", [128, 64, 128], BF16); r_BT = kb.res("BT")
        cT = kb.sbuf("cT", [128, 128], BF16); r_cT = kb.res("cT")
        g_pool = Pool(kb, "gt", [128, 64, 128], BF16, 2)
        ps_s = Pool(kb, "pss", [128, 2, 256], F32, 1, space="psum")
        ps_tp = Pool(kb, "tp", [128, 16, 128], BF16, 2, space="psum")
        ps_g = Pool(kb, "pg", [128, 8, 64], F32, 3, space="psum")
        A2 = A[:].rearrange("p h a i -> p (h a) i")
        evtog = [0]
        for n in range(NT):
            ts = slice(n * 128, (n + 1) * 128)
            qt, r_qt = q_pool.next()
            kb.dma("sp", qt[:], QT[:, :, ts].rearrange("h p t -> p h t"), reads=[r_QT], writes=[r_qt], chan_res=r_qt)
            s, r_s = s_pool.next()
            for hp in range(4):
                ps, r_ps = ps_s.next()
                for hh in range(2):
                    kb.op("pe", lambda e, hp=hp, hh=hh, ps=ps, qt=qt: e.matmul(ps[:, hh, :], lhsT=qt[:, 2 * hp + hh, :], rhs=skb[:], start=True, stop=True),
                          reads=[r_qt, r_sk], writes=[r_ps])
                kb.op("act", lambda e, hp=hp, ps=ps, s=s: e.copy(out=s[:, 2 * hp:2 * hp + 2, :, :].rearrange("p h c n -> p h (c n)"), in_=ps[:]),
                      reads=[r_ps], writes=[r_s])
            top, _r_top_unused = top_pool.next()
            r_tc = [kb.res("topc") for _ in range(16)]
            r_top = kb.res("topall")
            for h in range(8):
                for c in range(2):
                    kb.op("dve", lambda e, h=h, c=c, top=top, s=s: e.max(out=top[:, h, c, 0:8], in_=s[:, h, c, :]), reads=[r_s], writes=[r_tc[2 * h + c]])
            for h in range(8):
                for c in range(2):
                    kb.op("dve", lambda e, h=h, c=c, top=top, s=s: e.match_replace(out=tmp[:, 2 * h + c, :], in_to_replace=top[:, h, c, 0:8],
                                                                                   in_values=s[:, h, c, :], imm_value=-1e30),
                          reads=[r_s, r_tc[2 * h + c]], writes=[r_tmpc[2 * h + c]])
            for h in range(8):
                for c in range(2):
                    kb.op("dve", lambda e, h=h, c=c, top=top: e.max(out=top[:, h, c, 8:16], in_=tmp[:, 2 * h + c, :]),
                          reads=[r_tmpc[2 * h + c]], writes=[r_tc[2 * h + c], r_top] if (h == 7 and c == 1) else [r_tc[2 * h + c]])
            kb.op("dve", lambda e, top=top: e.tensor_tensor(out=cand[:].rearrange("p h (a b) -> p h a b", a=16),
                                                            in0=top[:, :, 0, :].unsqueeze(3).to_broadcast([128, 8, 16, 16]),
                                                            in1=top[:, :, 1, :].unsqueeze(2).to_broadcast([128, 8, 16, 16]), op=ALU.add),
                  reads=r_tc, writes=[r_cand, r_top])
            r_bc = [kb.res("bestc") for _ in range(8)]
            tmp2 = tmp[:].rearrange("p (h c) n -> p h (c n)", c=2)
            for h in range(8):
                kb.op("dve", lambda e, h=h: e.max(out=best[:, h, 0:8], in_=cand[:, h, :]), reads=[r_cand], writes=[r_bc[h]])
            for h in range(8):
                kb.op("dve", lambda e, h=h: e.match_replace(out=tmp2[:, h, :], in_to_replace=best[:, h, 0:8], in_values=cand[:, h, :], imm_value=-1e30),
                      reads=[r_cand, r_bc[h]], writes=[r_tmpc[2 * h], r_tmpc[2 * h + 1]])
            for h in range(8):
                kb.op("dve", lambda e, h=h: e.max(out=best[:, h, 8:16], in_=tmp2[:, h, :]), reads=[r_tmpc[2 * h], r_tmpc[2 * h + 1]],
                      writes=[r_bc[h], r_best] if h == 7 else [r_bc[h]])
            kb.op("dve", lambda e: e.tensor_copy(out=dd[:, 0, 0:1], in_=best[:, 0, 0:1]), reads=r_bc, writes=[r_best, r_sm])
            kb.op("dve", lambda e: e.tensor_tensor(out=dd[:], in0=best[:], in1=best[:, :, 0:1].to_broadcast([128, 8, 16]), op=ALU.subtract),
                  reads=[r_best], writes=[r_sm])
            kb.op("act", lambda e: e.activation(out=dd[:], in_=dd[:], func=AF.Exp), reads=[r_sm], writes=[r_sm])
            kb.op("dve", lambda e: e.tensor_reduce(out=zz[:, 0, :], in_=dd[:], axis=AX.X, op=ALU.add), reads=[r_sm], writes=[r_sm])
            kb.op("act", lambda e: e.activation(out=zz[:, 1, :], in_=zz[:, 0, :], func=AF.Ln), reads=[r_sm], writes=[r_sm])
            kb.op("dve", lambda e: e.tensor_tensor(out=zz[:, 2, :], in0=zz[:, 1, :], in1=best[:, :, 0], op=ALU.add), reads=[r_sm, r_best], writes=[r_sm])
            kb.op("dve", lambda e, top=top: e.tensor_tensor(out=cc[:], in0=top[:, :, 0, :], in1=zz[:, 2, :].unsqueeze(2).to_broadcast([128, 8, 16]),
                                                            op=ALU.subtract), reads=[r_sm, r_top], writes=[r_sm])
            kb.op("act", lambda e: e.activation(out=cb[:], in_=cc[:], func=AF.Exp), reads=[r_sm], writes=[r_sm])
            kb.op("dve", lambda e: e.tensor_scalar(out=zz[:, 3, :], in0=best[:, :, 15], scalar1=-DELTA, scalar2=None, op0=ALU.add),
                  reads=[r_best], writes=[r_sm])
            kb.op("dve", lambda e, top=top: e.tensor_tensor(out=thr[:], in0=zz[:, 3, :].unsqueeze(2).to_broadcast([128, 8, 16]), in1=top[:, :, 0, :],
                                                            op=ALU.subtract), reads=[r_sm, r_top], writes=[r_sm])
            kb.op("act", lambda e, s=s: e.activation(out=E2[:], in_=s[:, :, 1, :], func=AF.Exp), reads=[r_s], writes=[r_E2])
            kb.op("dve", lambda e, s=s, top=top: e.tensor_tensor(out=A[:], in0=s[:, :, 0, :].unsqueeze(2).to_broadcast([128, 8, 16, 128]),
                                                                 in1=top[:, :, 0, :].unsqueeze(3).to_broadcast([128, 8, 16, 128]), op=ALU.is_equal),
                  reads=[r_s, r_top], writes=[r_A])
            pt, r_pt = ps_tp.next()
            kb.op("pe", lambda e, pt=pt: e.transpose(out=pt[:, 0, :], in_=cb[:].rearrange("p h a -> p (h a)"), identity=ident[:]),
                  reads=[r_sm, r_id], writes=[r_pt])
            kb.op("act", lambda e, pt=pt: e.copy(out=cT[:], in_=pt[:, 0, :]), reads=[r_pt], writes=[r_cT])
            def build_B(jh):
                Bm_, r_Bm_ = Bms[jh]
                js = slice(jh * 64, (jh + 1) * 64)
                kb.op("dve", lambda e, s=s, js=js, Bm_=Bm_: e.tensor_tensor(out=Bm_[:], in0=s[:, :, 1, js].unsqueeze(2).to_broadcast([128, 8, 16, 64]),
                                                                            in1=thr[:].unsqueeze(3).to_broadcast([128, 8, 16, 64]), op=ALU.is_ge),
                      reads=[r_s, r_sm], writes=[r_Bm_])
                kb.op("pool", lambda e, js=js, Bm_=Bm_: e.tensor_tensor(out=Bm_[:], in0=Bm_[:], in1=E2[:, :, js].unsqueeze(2).to_broadcast([128, 8, 16, 64]),
                                                                        op=ALU.mult), reads=[r_E2, r_Bm_], writes=[r_Bm_])

            a_pts = []
            for i0 in range(0, 128, 16):
                pt, r_pt = ps_tp.next()
                for ii in range(16):
                    kb.op("pe", lambda e, pt=pt, ii=ii, i0=i0: e.transpose(out=pt[:, ii, :], in_=A2[:, :, i0 + ii], identity=ident[:]),
                          reads=[r_A, r_id], writes=[r_pt])
                if i0 == 0:
                    build_B(0)
                kb.op("dve", lambda e, pt=pt, i0=i0: e.tensor_tensor(out=AT[:, i0:i0 + 16, :], in0=pt[:],
                                                                     in1=cT[:].unsqueeze(1).to_broadcast([128, 16, 128]), op=ALU.mult),
                      reads=[r_pt, r_cT], writes=[r_AT])
            build_B(1)
            for jh in range(2):
                Bm_, r_Bm_ = Bms[jh]
                B2_ = Bm_[:].rearrange("p h a j -> p (h a) j")
                js = slice(jh * 64, (jh + 1) * 64)
                for j0 in range(0, 64, 16):
                    pt, r_pt = ps_tp.next()
                    for jj in range(16):
                        kb.op("pe", lambda e, pt=pt, jj=jj, j0=j0, B2_=B2_: e.transpose(out=pt[:, jj, :], in_=B2_[:, :, j0 + jj], identity=ident[:]),
                              reads=[r_Bm_, r_id], writes=[r_pt])
                    kb.op("act", lambda e, pt=pt, j0=j0: e.copy(out=BT[:, j0:j0 + 16, :], in_=pt[:]), reads=[r_pt], writes=[r_BT])
                gt, r_gt = g_pool.next()
                for t0 in range(0, 128, 8):
                    pg, r_pg = ps_g.next()
                    for tt in range(8):
                        kb.op("pe", lambda e, pg=pg, tt=tt, t0=t0: e.matmul(pg[:, tt, :], lhsT=AT[:, :, t0 + tt], rhs=BT[:, :, t0 + tt], start=True, stop=True),
                              reads=[r_AT, r_BT], writes=[r_pg])
                    kb.op("act", lambda e, pg=pg, gt=gt, t0=t0: e.copy(out=gt[:, :, t0:t0 + 8], in_=pg[:].rearrange("p t j -> p j t")),
                          reads=[r_pg], writes=[r_gt])
                kb.dma("pool", GT[n, :, js, :], gt[:], reads=[r_gt], writes=[r_GT], chan_res=r_gt)
        kb.end_phase()

    def phase5(l, X1, r_X1, X2, r_X2):
        kb.begin_phase()
        TBT = min(TBT_MAX, NT)
        TB = TBT * 128
        nblk = NT // TBT
        NSC = 64
        xblk = kb.sbuf("xblk", [128, KC, TB], BF16); r_xblk = kb.res("xblk")
        y_sb = kb.sbuf("ysb", [128, TBT, D], F32)
        r_ysb = [[kb.res("ysb") for _ in range(4)] for _ in range(TBT)]
        ty_pool = Pool(kb, "ty", [128, 512], F32, 2)
        un_pool = Pool(kb, "un", [128, 2, D], BF16, 2)
        v_pool = Pool(kb, "vv", [128, 2, D], BF16, 3)
        uT_pool = Pool(kb, "uT", [128, KC, 128], BF16, 4)
        g_pool = Pool(kb, "gg", [128, TBT, 2, 128], BF16, 2)
        ge_pool = Pool(kb, "ge", [128, 512], BF16, 2)
        hs_pool = Pool(kb, "hs", [128, TB], BF16, 4)
        ps_tp = Pool(kb, "tp", [128, KC, 128], BF16, 2, space="psum")
        ps_a = Pool(kb, "pa", [128, 512], F32, 2, space="psum")
        ps_y = Pool(kb, "py", [128, 512], F32, 2, space="psum")
        U3 = peer_u[l].rearrange("(i j) d -> i j d", j=128)
        V3 = peer_v[l].rearrange("(i j) d -> i j d", j=128)
        for blk in range(nblk):
            t0 = blk * TB
            kb.dma("sp", xblk[:], XN2T[:, :, t0:t0 + TB].rearrange("c p t -> p c t"), reads=[r_XN2T], writes=[r_xblk], chan_res=r_xblk)
            ctx = {}
            for tile in range(TBT):
                tok = t0 + tile * 128
                kb.dma("sp", y_sb[:, tile, :], X1[tok:tok + 128, :], reads=[r_X1], writes=r_ysb[tile], chan_res=r_ysb[tile][1])

            def do_L(sc):
                j0 = 2 * sc
                c = Ctx()
                c.un, c.r_un = un_pool.next()
                kb.dma("pool", c.un[:], U3[:, j0:j0 + 2, :], reads=[r_const], writes=[c.r_un], chan_res=c.r_un)
                ctx[sc] = c

            def do_L2(sc):
                j0 = 2 * sc
                c = ctx[sc]
                c.gg, c.r_gg = g_pool.next()
                c.vv, c.r_vv = v_pool.next()
                kb.dma("sp", c.gg[:], GT[blk * TBT:(blk + 1) * TBT, :, j0:j0 + 2, :].rearrange("n i j t -> i n j t"), reads=[r_GT],
                       writes=[c.r_gg], chan_res=c.r_gg)
                kb.dma("pool", c.vv[:], V3[:, j0:j0 + 2, :], reads=[r_const], writes=[c.r_vv], chan_res=c.r_vv)

            def do_T(sc):
                c = ctx[sc]
                c.uT = []
                for jj in range(2):
                    pt, r_pt = ps_tp.next()
                    for k in range(KC):
                        kb.op("pe", lambda e, k=k, jj=jj, pt=pt, un=c.un: e.transpose(out=pt[:, k, :], in_=un[:, jj, k * 128:(k + 1) * 128], identity=ident[:]),
                              reads=[c.r_un, r_id], writes=[r_pt])
                    uT, r_uT = uT_pool.next()
                    kb.op("act", lambda e, pt=pt, uT=uT: e.copy(out=uT[:], in_=pt[:]), reads=[r_pt], writes=[r_uT])
                    c.uT.append((uT, r_uT))

            def do_A(sc):
                c = ctx[sc]
                c.hs = []
                for jj in range(2):
                    uT, r_uT = c.uT[jj]
                    h_t, r_ht = hs_pool.next()
                    for half in range(TB // 512):
                        pa, r_pa = ps_a.next()
                        for k in range(KC):
                            kb.op("pe", lambda e, k=k, pa=pa, uT=uT, half=half: e.matmul(pa[:], lhsT=uT[:, k, :], rhs=xblk[:, k, half * 512:(half + 1) * 512],
                                                                                          start=(k == 0), stop=(k == KC - 1)),
                                  reads=[r_uT, r_xblk], writes=[r_pa])
                        ge, r_ge = ge_pool.next()
                        kb.op("act", lambda e, pa=pa, ge=ge: e.activation(out=ge[:], in_=pa[:], func=AF.Gelu_apprx_tanh), reads=[r_pa], writes=[r_ge])
                        kb.op("dve", lambda e, ge=ge, h_t=h_t, gg=c.gg, jj=jj, half=half: e.tensor_tensor(
                            out=h_t[:, half * 512:(half + 1) * 512].rearrange("p (n t) -> p n t", t=128), in0=ge[:].rearrange("p (n t) -> p n t", t=128),
                            in1=gg[:, half * 4:(half + 1) * 4, jj, :], op=ALU.mult), reads=[r_ge, c.r_gg], writes=[r_ht])
                    c.hs.append((h_t, r_ht))

            def do_Y(sc):
                c = ctx[sc]
                for tile in range(TBT):
                    for nb in range(4):
                        py, r_py = ps_y.next()
                        for jj in range(2):
                            h_t, r_ht = c.hs[jj]
                            kb.op("pe", lambda e, jj=jj, py=py, h_t=h_t, vv=c.vv, tile=tile, nb=nb: e.matmul(
                                py[:], lhsT=h_t[:, tile * 128:(tile + 1) * 128], rhs=vv[:, jj, nb * 512:(nb + 1) * 512], start=(jj == 0), stop=(jj == 1)),
                                reads=[r_ht, c.r_vv], writes=[r_py])
                        ysl = y_sb[:, tile, nb * 512:(nb + 1) * 512]
                        if (tile * 4 + nb) % 3 == 2:
                            ty, r_ty = ty_pool.next()
                            kb.op("act", lambda e, py=py, ty=ty: e.copy(out=ty[:], in_=py[:]), reads=[r_py], writes=[r_ty])
                            kb.op("pool", lambda e, ty=ty, ysl=ysl: e.tensor_tensor(out=ysl, in0=ty[:], in1=ysl, op=ALU.add),
                                  reads=[r_ty, r_ysb[tile][nb]], writes=[r_ysb[tile][nb]])
                        else:
                            kb.op("dve", lambda e, py=py, ysl=ysl: e.tensor_tensor(out=ysl, in0=py[:], in1=ysl, op=ALU.add),
                                  reads=[r_py, r_ysb[tile][nb]], writes=[r_ysb[tile][nb]])
                del ctx[sc]

            do_L(0)
            do_L2(0)
            do_L(1)
            do_L2(1)
            do_T(0)
            for i in range(NSC):
                if i + 2 < NSC:
                    do_L(i + 2)
                if i + 1 < NSC:
                    do_T(i + 1)
                do_A(i)
                if i >= 1:
                    do_Y(i - 1)
                if i + 2 < NSC:
                    do_L2(i + 2)
            do_Y(NSC - 1)
            for tile in range(TBT):
                tok = t0 + tile * 128
                kb.dma("pool", X2[tok:tok + 128, :], y_sb[:, tile, :], reads=r_ysb[tile], writes=[r_X2], chan_res=r_ysb[tile][0])
        kb.end_phase()

    def phasef(x_src, r_xsrc):
        kb.begin_phase()
        fn = kb.sbuf("fn", [128, D], F32); r_fn = kb.res("fn")
        kb.dma("sp", fn[:], fnorm_in, reads=[r_const], writes=[r_fn], chan_res=r_fn)
        xt_pool = Pool(kb, "xt", [128, D], F32, 2)
        yo_pool = Pool(kb, "yo", [128, D], F32, 2)
        sq_pool = Pool(kb, "sq", [128, D], BF16, 1)
        st_pool = Pool(kb, "stat", [128, 4], F32, 4)
        for n in range(NT):
            xt, r_xt = xt_pool.next()
            kb.dma("sp", xt[:], x_src[n * 128:(n + 1) * 128, :], reads=[r_xsrc], writes=[r_xt], chan_res=r_xt)
            stt, r_st = st_pool.next()
            sq, r_sq = sq_pool.next()
            kb.op("act", lambda e, xt=xt, stt=stt, sq=sq: e.activation(out=sq[:], in_=xt[:], func=AF.Square, accum_out=stt[:, 0:1]),
                  reads=[r_xt], writes=[r_sq, r_st])
            kb.op("act", lambda e, stt=stt: e.activation(out=stt[:, 0:1], in_=stt[:, 0:1], func=AF.Ln, bias=epsb[:, 0:1], scale=1.0 / D),
                  reads=[r_st, r_gn], writes=[r_st])
            kb.op("act", lambda e, stt=stt: e.activation(out=stt[:, 0:1], in_=stt[:, 0:1], func=AF.Exp, scale=-0.5), reads=[r_st], writes=[r_st])
            yo, r_yo = yo_pool.next()
            kb.op("dve", lambda e, xt=xt, stt=stt, yo=yo: e.scalar_tensor_tensor(out=yo[:], in0=xt[:], scalar=stt[:, 0:1], in1=fn[:],
                                                                                 op0=ALU.mult, op1=ALU.mult),
                  reads=[r_xt, r_st, r_fn], writes=[r_yo])
            kb.dma("pool", y_out[n * 128:(n + 1) * 128, :], yo[:], reads=[r_yo], writes=[r_y], chan_res=r_yo)
        kb.end_phase()

    for l in range(L):
        x_src, r_xsrc = (x_in, r_x) if l == 0 else (XB, r_XB)
        phase1(l, x_src, r_xsrc)
        if upto >= 2:
            phase1c(l)
        if upto >= 3:
            phase2a(l)
            phase2b(l)
            phase2c(l)
        if upto >= 4:
            phase3(l, x_src, r_xsrc, XA, r_XA)
        if upto >= 5:
            phase4a(l)
            phase4b(l)
        if upto >= 6:
            phase5(l, XA, r_XA, XB, r_XB)
    if upto >= 7:
        phasef(XB, r_XB)
    outs = [r_HT, r_VA, r_VB, r_QCN, r_QCP, r_KCN, r_KCP, r_VC, r_O, r_XA, r_XB, r_XN2T, r_QT, r_GT, r_y]
    kb.finish_wait("pool", outs)
    kb.emit()
    return nc


def host_consts(S):
    c = {}
    c["ident"] = np.eye(128, dtype=np.float32)
    R = np.zeros((64, 64), np.float32)
    for m in range(32):
        R[m + 32, m] = -1.0
        R[m, m + 32] = 1.0
    c["rotR"] = R
    inv = 10000.0 ** (-(np.arange(32, dtype=np.float64)) / 32.0)
    ang = np.arange(S, dtype=np.float64)[None, :] * np.concatenate([inv, inv])[:, None]
    c["cossin"] = np.stack([np.cos(ang), np.sin(ang)]).astype(np.float32)
    ki = np.arange(128)[:, None]
    qi = np.arange(128)[None, :]
    slopes = np.array([2.0 ** (-8.0 * (h + 1) / 8) for h in range(8)], np.float64)
    tabA = np.zeros((128, 3, 8, 128), np.float64)
    for d, (dist, valid) in enumerate([(128 + qi - ki, qi <= ki), (np.abs(qi - ki), np.ones((128, 128), bool)),
                                       (128 + ki - qi, ki <= qi)]):
        for h in range(8):
            tabA[:, d, h, :] = np.where(valid, np.exp(-slopes[h] * dist), 0.0)
    c["tabA"] = tabA.astype(np.float32).reshape(128, 3 * 8 * 128)
    rows = S // 64
    kr = min(8, rows)
    NT = S // 128
    tabs = {}
    sched = []
    masks, dridx, dcidx = [], [], []
    for n in range(NT):
        qrow = 2 * n + np.arange(128) // 64
        qcol = np.arange(128) % 64
        rstart = np.clip(qrow - kr // 2, 0, rows - kr)
        cstart = np.clip(qcol - 8, 0, 64 - 16)
        lst = []
        for m in range(NT):
            krow = 2 * m + np.arange(128) // 64
            kcol = np.arange(128) % 64
            valid = ((krow[:, None] >= rstart[None, :]) & (krow[:, None] < rstart[None, :] + kr)
                     & (kcol[:, None] >= cstart[None, :]) & (kcol[:, None] < cstart[None, :] + 16))
            if not valid.any():
                continue
            dr = np.clip(krow[:, None] - qrow[None, :] + 7, 0, 14)
            dc = np.clip(kcol[:, None] - qcol[None, :] + 15, 0, 30)
            dr = np.where(valid, dr, 0)
            dc = np.where(valid, dc, 0)
            key = (valid.tobytes(), dr.tobytes(), dc.tobytes())
            if key not in tabs:
                tabs[key] = len(tabs)
                masks.append(valid.astype(np.float32))
                dridx.append(dr)
                dcidx.append(dc)
            lst.append((m, tabs[key]))
        sched.append(lst)
    c["maskB"] = np.ascontiguousarray(np.stack(masks, 1))
    c["_dr"] = np.stack(dridx, 1)
    c["_dc"] = np.stack(dcidx, 1)
    c["_schedB"] = sched
    c["_ntab"] = len(masks)
    return c


def gather_biasB(b_rel_bias_l, c):
    g = b_rel_bias_l[:, c["_dr"], c["_dc"]]
    return np.ascontiguousarray(np.transpose(g, (1, 2, 0, 3))).astype(np.float32)


def gains_pack(inp, L):
    g = np.zeros((L, 128, 64), np.float32)
    for l in range(L):
        g[l, :, 0:16] = inp["ln1"][l].reshape(16, 128).T
        g[l, :, 16:20] = inp["c_q_norm"][l].reshape(4, 128).T
        g[l, :, 20:22] = inp["c_kv_norm"][l].reshape(2, 128).T
        g[l, :, 22:38] = inp["out_norm"][l].reshape(16, 128).T
        g[l, :, 38:54] = inp["ln2"][l].reshape(16, 128).T
    return g


def sk_pack(inp, L):
    sk = np.zeros((L, 128, 256), np.float32)
    for l in range(L):
        sk[l, 0:64, 0:128] = inp["peer_sub_keys"][l, 0].T
        sk[l, 64:128, 128:256] = inp["peer_sub_keys"][l, 1].T
    return sk


def core_inputs(inp, xb, S, L, consts):
    m = dict(x=np.ascontiguousarray(xb), w_in=inp["w_in"][:L], gains=gains_pack(inp, L), ident=consts["ident"],
             cossin=consts["cossin"], rotR=consts["rotR"], tabA=consts["tabA"], maskB=consts["maskB"],
             biasB=np.stack([gather_biasB(inp["b_rel_bias"][l], consts) for l in range(L)]),
             sinkb=np.ascontiguousarray(np.broadcast_to(inp["a_sink"][:L][None], (128, L, 8))).astype(np.float32),
             c_w_uq=inp["c_w_uq"][:L], c_w_ukv=inp["c_w_ukv"][:L], w_o=inp["w_o"][:L],
             peer_w_q=inp["peer_w_q"][:L], peer_u=inp["peer_u"][:L], peer_v=inp["peer_v"][:L],
             sk=sk_pack(inp, L),
             ln2b=np.ascontiguousarray(np.broadcast_to(inp["ln2"][:L][:, None, :], (L, 128, 2048))).astype(np.float32),
             fnorm=np.ascontiguousarray(np.broadcast_to(inp["final_norm"][None], (128, 2048))).astype(np.float32))
    return m


N_CORES = 4
_CACHE = {}


def kernel(**inputs):
    inp = {k: np.asarray(v) for k, v in inputs.items()}
    B, S, _ = inp["x"].shape
    L = inp["w_in"].shape[0]
    key = (S, L)
    if key not in _CACHE:
        consts = host_consts(S)
        _CACHE[key] = (consts, build(S, L, consts, debug=False))
    consts, nc = _CACHE[key]
    in_maps = []
    for b in range(B):
        m = core_inputs(inp, inp["x"][b], S, L, consts)
        in_maps.append({k: np.ascontiguousarray(v, dtype=np.float32) for k, v in m.items()})
    res = run_bass_kernel_spmd(nc, in_maps, core_ids=list(range(B)))
    out = np.stack([np.asarray(res.results[b]["y"], dtype=np.float32) for b in range(B)], axis=0)
    return out
```

```python
import numpy as np
from concourse.bass_utils import run_bass_kernel_spmd
from contextlib import ExitStack
import concourse.bass as bass
import concourse.mybir as mybir

F32 = mybir.dt.float32
BF16 = mybir.dt.bfloat16
AF = mybir.ActivationFunctionType
ALU = mybir.AluOpType
AX = mybir.AxisListType


class Res:
    __slots__ = ("name", "w", "r", "cin", "cout")

    def __init__(self, name):
        self.name = name
        self.w = {}
        self.r = {}
        self.cin = None
        self.cout = None


class Chan:
    __slots__ = ("sid", "cnt")

    def __init__(self, sid):
        self.sid = sid
        self.cnt = 0


class Eng:
    def __init__(self, name, sid):
        self.name = name
        self.sid = sid
        self.cnt = 0
        self.waited = {}
        self.prog = []


class KB:
    def __init__(self, nc):
        self.nc = nc
        self.es = ExitStack()
        self.sems = []
        self.engs = {}
        self.chans = []
        for k in ("pe", "act", "dve", "pool", "sp"):
            self.engs[k] = Eng(k, self.newsem("e_" + k))
        self.nres = 0
        self.pes = None
        self.ntens = 0
        self.free_ch = []
        self.phase_ch = []

    def newsem(self, name):
        s = self.es.enter_context(self.nc.semaphore("%s_%d" % (name, len(self.sems))))
        self.sems.append(s)
        return len(self.sems) - 1

    def res(self, name=None):
        self.nres += 1
        return Res(name or ("r%d" % self.nres))

    def sbuf(self, name, shape, dt):
        st = self.pes if self.pes is not None else self.es
        self.ntens += 1
        return st.enter_context(self.nc.sbuf_tensor("sb%d_%s" % (self.ntens, name), list(shape), dt))

    def psum(self, name, shape, dt):
        st = self.pes if self.pes is not None else self.es
        self.ntens += 1
        return st.enter_context(self.nc.psum_tensor("ps%d_%s" % (self.ntens, name), list(shape), dt))

    def begin_phase(self):
        self.pes = ExitStack()

    def end_phase(self):
        self.barrier()
        self.emit_block()
        self.pes.close()
        self.pes = None
        self.free_ch.extend(self.phase_ch)
        self.phase_ch = []

    def totals(self):
        t = {}
        for E in self.engs.values():
            t[E.sid] = E.cnt
        for ch in self.chans:
            t[ch.sid] = ch.cnt
        return t

    def barrier(self):
        tot = self.totals()
        for E in self.engs.values():
            waits = []
            for s, v in tot.items():
                if v > 0 and E.waited.get(s, 0) < v:
                    E.waited[s] = v
                    waits.append((s, v))
            E.prog.append((waits, None, None, 0))

    def _deps(self, E, reads, writes, same_ok):
        deps = {}
        for r in reads:
            for s, v in r.w.items():
                if deps.get(s, 0) < v:
                    deps[s] = v
        for w in writes:
            for s, v in w.w.items():
                if deps.get(s, 0) < v:
                    deps[s] = v
            for s, v in w.r.items():
                if deps.get(s, 0) < v:
                    deps[s] = v
        waits = []
        for s, v in deps.items():
            if s == E.sid and same_ok:
                continue
            if E.waited.get(s, 0) >= v:
                continue
            E.waited[s] = v
            waits.append((s, v))
        return waits

    def op(self, eng, fn, reads=(), writes=()):
        E = self.engs[eng]
        waits = self._deps(E, reads, writes, same_ok=(eng == "pe"))
        E.cnt += 1
        c = E.cnt
        E.prog.append((waits, fn, E.sid, 1))
        for r in reads:
            r.r[E.sid] = c
        for w in writes:
            w.w[E.sid] = c

    def dma(self, q, out, in_, reads=(), writes=(), chan_res=None, **kw):
        E = self.engs[q]
        waits = self._deps(E, reads, writes, same_ok=False)
        cr = chan_res
        if cr.cin is None:
            if self.free_ch:
                cr.cin = self.free_ch.pop()
            else:
                cr.cin = Chan(self.newsem("c"))
                self.chans.append(cr.cin)
            if self.pes is not None:
                self.phase_ch.append(cr.cin)
        ch = cr.cin
        ch.cnt += 16
        c = ch.cnt
        E.prog.append((waits, (lambda e, out=out, in_=in_, kw=kw: e.dma_start(out=out, in_=in_, **kw)), ch.sid, 16))
        for r in reads:
            r.r[ch.sid] = c
        for w in writes:
            w.w[ch.sid] = c

    def finish_wait(self, eng, resources):
        E = self.engs[eng]
        waits = self._deps(E, resources, (), same_ok=False)
        E.prog.append((waits, None, None, 0))

    def emit(self):
        self.emit_block()
        self.es.close()

    def emit_block(self):
        nc = self.nc
        sems = self.sems
        with nc.Block() as block:
            def run(E):
                def body(e):
                    for waits, fn, sid, inc in E.prog:
                        for s, v in waits:
                            e.wait_ge(sems[s], v)
                        if fn is not None:
                            ins = fn(e)
                            ins.then_inc(sems[sid], inc)
                return body
            block.tensor(run(self.engs["pe"]))
            block.scalar(run(self.engs["act"]))
            block.vector(run(self.engs["dve"]))
            block.gpsimd(run(self.engs["pool"]))
            block.sync(run(self.engs["sp"]))
        for E in self.engs.values():
            E.prog = []


D = 2048
KC = 16
DIN = 3904
EPS = 1e-6


class Pool:
    def __init__(self, kb, name, shape, dt, n, space="sbuf"):
        self.items = []
        for i in range(n):
            t = (kb.sbuf if space == "sbuf" else kb.psum)("%s%d" % (name, i), shape, dt)
            self.items.append((t, kb.res("%s%d" % (name, i))))
        self.i = 0

    def next(self):
        it = self.items[self.i % len(self.items)]
        self.i += 1
        return it


class Ctx:
    pass


def build(S, L, consts, debug=False, upto=99):
    nc = bass.Bass("TRN2", target_bir_lowering=False)
    kb = KB(nc)
    NT = S // 128
    NB = S // 512
    dbgkind = "ExternalOutput" if debug else "Internal"

    def din(name, shape, dt=F32):
        return nc.dram_tensor(name, list(shape), dt, kind="ExternalInput").ap()

    def dscr(name, shape, dt=BF16, kind=None):
        return nc.dram_tensor(name, list(shape), dt, kind=kind or dbgkind).ap()

    x_in = din("x", [S, D])
    w_in = din("w_in", [L, D, DIN])
    gains = din("gains", [L, 128, 64])
    ident_in = din("ident", [128, 128])
    cs_in = din("cossin", [2, 64, S])
    y_out = nc.dram_tensor("y", [S, D], F32, kind="ExternalOutput").ap()

    HT = dscr("HT", [31, 128, S])
    VA = dscr("VA", [S, 2, 129])
    VB = dscr("VB", [S, 4, 129])
    r_HT = kb.res("HT"); r_VA = kb.res("VA"); r_VB = kb.res("VB")
    r_x = kb.res("x_in"); r_w_in = kb.res("w_in"); r_const = kb.res("const")

    ident32 = kb.sbuf("ident32", [128, 128], F32)
    ident = kb.sbuf("ident", [128, 128], BF16)
    gn = kb.sbuf("gn", [128, L, 64], F32)
    gneg = kb.sbuf("gneg", [128, L, 64], F32)
    epsb = kb.sbuf("epsb", [128, 1], F32)
    r_id = kb.res("ident"); r_gn = kb.res("gn")
    kb.dma("sp", ident32[:], ident_in, reads=[r_const], writes=[r_id], chan_res=r_id)
    kb.op("dve", lambda e: e.tensor_copy(out=ident[:], in_=ident32[:]), reads=[r_id], writes=[r_id])
    kb.dma("sp", gn[:], gains.rearrange("l p c -> p l c"), reads=[r_const], writes=[r_gn], chan_res=r_gn)
    kb.op("dve", lambda e: e.tensor_scalar(out=gneg[:], in0=gn[:], scalar1=-1.0, scalar2=None, op0=ALU.mult),
          reads=[r_gn], writes=[r_gn])
    kb.op("dve", lambda e: e.memset(epsb[:], EPS), writes=[r_gn])

    NTAB = consts["_ntab"]
    schedB = consts["_schedB"]
    rot_in = din("rotR", [64, 64])
    tabA_in = din("tabA", [128, 3072])
    maskB_in = din("maskB", [128, NTAB, 128])
    biasB_in = din("biasB", [L, 128, NTAB, 4, 128])
    sink_in = din("sinkb", [128, L, 8])
    w_uq = din("c_w_uq", [L, 512, 768])
    w_ukv = din("c_w_ukv", [L, 256, 1024])
    w_o = din("w_o", [L, D, D])
    fnorm_in = din("fnorm", [128, D])
    w_pq = din("peer_w_q", [L, D, 1024])
    sk_in = din("sk", [L, 128, 256])
    peer_u = din("peer_u", [L, 16384, D])
    peer_v = din("peer_v", [L, 16384, D])
    ln2b_in = din("ln2b", [L, 128, D])
    TBT_MAX = 8
    r_win2 = kb.res("win2")

    QCN = dscr("QCN", [4, 128, S]); QCP = dscr("QCP", [4, 64, S]); KCN = dscr("KCN", [4, 128, S]); KCP = dscr("KCP", [64, S])
    VC = dscr("VC", [S, 4, 129])
    O = dscr("O", [S, D], F32)
    XA = dscr("XA", [S, D], F32); XB = dscr("XB", [S, D], F32)
    XN2T = dscr("XN2T", [KC, 128, S])
    QT = dscr("QT", [8, 128, S])
    GT = dscr("GT", [NT, 128, 128, 128])
    r_QT = kb.res("QT"); r_GT = kb.res("GT"); r_y = kb.res("y")
    r_QCN = kb.res("QCN"); r_QCP = kb.res("QCP"); r_KCN = kb.res("KCN"); r_KCP = kb.res("KCP"); r_VC = kb.res("VC")
    r_O = kb.res("O"); r_XA = kb.res("XA"); r_XB = kb.res("XB"); r_XN2T = kb.res("XN2T")

    rot32 = kb.sbuf("rot32", [64, 64], F32)
    rotb = kb.sbuf("rotb", [64, 64], BF16)
    onesb = kb.sbuf("onesb", [128, 128], BF16)
    esink = kb.sbuf("esink", [128, L, 8], F32)
    kb.dma("sp", rot32[:], rot_in, reads=[r_const], writes=[r_id], chan_res=r_id)
    kb.op("dve", lambda e: e.tensor_copy(out=rotb[:], in_=rot32[:]), reads=[r_id], writes=[r_id])
    kb.op("dve", lambda e: e.memset(onesb[:], 1.0), writes=[r_id])
    kb.dma("sp", esink[:], sink_in, reads=[r_const], writes=[r_gn], chan_res=r_gn)
    kb.op("act", lambda e: e.activation(out=esink[:], in_=esink[:], func=AF.Exp), reads=[r_gn], writes=[r_gn])

    SC_A = 128 ** -0.5
    SC_C = 192 ** -0.5

    def mk_norm_helpers(st_pool, sq_pool, ps_tp):
        def rmsnorm_tile(xt, r_xt, groups, xn, r_xn, gain=None):
            stt, r_st = st_pool.next()
            sq, r_sq = sq_pool.next()
            ng = len(groups)
            for g, (s0, wd) in enumerate(groups):
                kb.op("act", lambda e, s0=s0, wd=wd, g=g: e.activation(out=sq[:, s0:s0 + wd], in_=xt[:, s0:s0 + wd], func=AF.Square,
                                                                       accum_out=stt[:, g:g + 1]),
                      reads=[r_xt], writes=[r_sq, r_st])
            for g, (s0, wd) in enumerate(groups):
                kb.op("act", lambda e, g=g, wd=wd: e.activation(out=stt[:, g:g + 1], in_=stt[:, g:g + 1], func=AF.Ln,
                                                                bias=epsb[:, 0:1], scale=1.0 / wd),
                      reads=[r_st, r_gn], writes=[r_st])
            kb.op("act", lambda e: e.activation(out=stt[:, 0:ng], in_=stt[:, 0:ng], func=AF.Exp, scale=-0.5),
                  reads=[r_st], writes=[r_st])
            for g, (s0, wd) in enumerate(groups):
                if gain is None:
                    kb.op("dve", lambda e, s0=s0, wd=wd, g=g: e.tensor_scalar(out=xn[:, s0:s0 + wd], in0=xt[:, s0:s0 + wd],
                                                                              scalar1=stt[:, g:g + 1], scalar2=None, op0=ALU.mult),
                          reads=[r_xt, r_st], writes=[r_xn])
                else:
                    gt_, r_gt_ = gain
                    kb.op("dve", lambda e, s0=s0, wd=wd, g=g: e.scalar_tensor_tensor(out=xn[:, s0:s0 + wd], in0=xt[:, s0:s0 + wd],
                                                                                     scalar=stt[:, g:g + 1], in1=gt_[:, s0:s0 + wd],
                                                                                     op0=ALU.mult, op1=ALU.mult),
                          reads=[r_xt, r_st, r_gt_], writes=[r_xn])

        def transpose_tile(xn, r_xn, dst, r_dst, j):
            pt, r_pt = ps_tp.next()
            for k in range(KC):
                kb.op("pe", lambda e, k=k: e.transpose(out=pt[:, k, :], in_=xn[:, k * 128:(k + 1) * 128], identity=ident[:]),
                      reads=[r_xn, r_id], writes=[r_pt])
            kb.op("act", lambda e: e.copy(out=dst[:, :, j * 128:(j + 1) * 128], in_=pt[:]), reads=[r_pt], writes=[r_dst])
        return rmsnorm_tile, transpose_tile

    def load_w(wst_pool, src_ap, gcol, ncols, l, dst, r_dst, r_src):
        nk = src_ap.shape[0] // 128
        for k in range(nk):
            st, r_st = wst_pool.next()
            kb.dma("sp", st[:, :ncols], src_ap[k * 128:(k + 1) * 128, :], reads=[r_src], writes=[r_st], chan_res=r_st)
            if gcol is None:
                kb.op("act", lambda e, k=k, st=st: e.copy(out=dst[:, k, :ncols], in_=st[:, :ncols]), reads=[r_st], writes=[r_dst])
            else:
                kb.op("act", lambda e, k=k, st=st: e.activation(out=dst[:, k, :ncols], in_=st[:, :ncols], func=AF.Copy,
                                                                scale=gn[:, l, gcol + k:gcol + k + 1]),
                      reads=[r_st, r_gn], writes=[r_dst])

    def phase1(l, x_src, r_xsrc):
        kb.begin_phase()
        xt_pool = Pool(kb, "xt", [128, D], F32, 2)
        xn_pool = Pool(kb, "xn", [128, D], BF16, 2)
        sq_pool = Pool(kb, "sq", [128, D], BF16, 1)
        st_pool = Pool(kb, "stat", [128, 4], F32, 4)
        xnT_pool = Pool(kb, "xnT", [128, KC, 512], BF16, 2)
        ps_tp = Pool(kb, "tp", [128, KC, 128], BF16, 2, space="psum")
        ps_mm = Pool(kb, "mm", [128, 512], F32, 4, space="psum")
        wst_pool = Pool(kb, "wst", [128, 2048], F32, 2)
        wbf = kb.sbuf("wbf", [128, KC, 2048], BF16)
        r_wbf = kb.res("wbf")
        ev_pool = Pool(kb, "ev", [128, 512], BF16, 4)
        vst_pool = Pool(kb, "vst", [128, 4, 129], BF16, 2)
        for t, r in vst_pool.items:
            kb.op("pool", lambda e, t=t: e.memset(t[:], 1.0), writes=[r])
        rmsnorm_tile, transpose_tile = mk_norm_helpers(st_pool, sq_pool, ps_tp)
        passes = [
            dict(c0=0, fm=[(c, c * 128, 128) for c in range(0, 10)] + [(c, c * 128, 128) for c in range(12, 16)],
                 tm=[(VA, r_VA, 1280, 2)]),
            dict(c0=2048, fm=[(c, c * 128 - 2048, 128) for c in range(16, 20)] + [(c, c * 128 - 2048, 128) for c in range(24, 30)]
                 + [(30, 30 * 128 - 2048, 64)],
                 tm=[(VB, r_VB, 2560 - 2048, 4)]),
        ]
        for ps in passes:
            c0 = ps["c0"]
            ncols = min(2048, DIN - c0)
            load_w(wst_pool, w_in[l, :, c0:c0 + ncols], 0, ncols, l, wbf, r_wbf, r_w_in)
            for b in range(NB):
                xnT, r_xnT = xnT_pool.next()
                for j in range(4):
                    tok = b * 512 + j * 128
                    xt, r_xt = xt_pool.next()
                    kb.dma("sp", xt[:], x_src[tok:tok + 128, :], reads=[r_xsrc], writes=[r_xt], chan_res=r_xt)
                    xn, r_xn = xn_pool.next()
                    rmsnorm_tile(xt, r_xt, [(0, D)], xn, r_xn)
                    transpose_tile(xn, r_xn, xnT, r_xnT, j)
                for (c, off, m) in ps["fm"]:
                    pm, r_pm = ps_mm.next()
                    for k in range(KC):
                        kb.op("pe", lambda e, k=k, off=off, m=m, pm=pm, xnT=xnT: e.matmul(
                            pm[0:m, :], lhsT=wbf[:, k, off:off + m], rhs=xnT[:, k, :], start=(k == 0), stop=(k == KC - 1)),
                            reads=[r_wbf, r_xnT], writes=[r_pm])
                    ev, r_ev = ev_pool.next()
                    kb.op("dve", lambda e, m=m, pm=pm, ev=ev: e.tensor_copy(out=ev[0:m, :], in_=pm[0:m, :]),
                          reads=[r_pm], writes=[r_ev])
                    kb.dma("pool", HT[c, 0:m, b * 512:(b + 1) * 512], ev[0:m, :], reads=[r_ev], writes=[r_HT], chan_res=r_ev)
                for (dst, r_dstd, off, nh) in ps["tm"]:
                    for j in range(4):
                        tok = b * 512 + j * 128
                        pm, r_pm = ps_mm.next()
                        for k in range(KC):
                            kb.op("pe", lambda e, k=k, off=off, nh=nh, pm=pm, xnT=xnT, j=j: e.matmul(
                                pm[:, 0:nh * 128], lhsT=xnT[:, k, j * 128:(j + 1) * 128], rhs=wbf[:, k, off:off + nh * 128],
                                start=(k == 0), stop=(k == KC - 1)),
                                reads=[r_wbf, r_xnT], writes=[r_pm])
                        vs, r_vs = vst_pool.next()
                        kb.op("act", lambda e, nh=nh, pm=pm, vs=vs: e.copy(
                            out=vs[:, 0:nh, 0:128], in_=pm[:, 0:nh * 128].rearrange("p (h d) -> p h d", d=128)),
                            reads=[r_pm], writes=[r_vs])
                        kb.dma("pool", dst[tok:tok + 128, :, :], vs[:, 0:nh, :], reads=[r_vs], writes=[r_dstd], chan_res=r_vs)
        kb.end_phase()

    def phase1c(l):
        kb.begin_phase()
        wst_pool = Pool(kb, "wst", [128, 1024], F32, 2)
        wuq = kb.sbuf("wuq", [128, 4, 768], BF16)
        wukv = kb.sbuf("wukv", [128, 2, 1024], BF16)
        r_wu = kb.res("wu")
        load_w(wst_pool, w_uq[l], 16, 768, l, wuq, r_wu, r_win2)
        load_w(wst_pool, w_ukv[l], 20, 1024, l, wukv, r_wu, r_win2)
        cin_pool = Pool(kb, "cin", [128, 6, 512], BF16, 2)
        krin_pool = Pool(kb, "krin", [64, 512], BF16, 2)
        cs_pool = Pool(kb, "cs", [64, 2, 512], F32, 2)
        sqc_pool = Pool(kb, "sqc", [128, 6, 512], BF16, 1)
        rstd_pool = Pool(kb, "rstd", [128, 2, 512], F32, 1)
        cn_pool = Pool(kb, "cn", [128, 6, 512], BF16, 2)
        ps_mm = Pool(kb, "mm", [128, 512], F32, 6, space="psum")
        ev_pool = Pool(kb, "ev", [128, 512], BF16, 4)
        qpe_pool = Pool(kb, "qpe", [64, 512], BF16, 2)
        t1_pool = Pool(kb, "t1", [64, 512], F32, 2)
        t2_pool = Pool(kb, "t2", [64, 512], F32, 2)
        vst_pool = Pool(kb, "vst", [128, 4, 129], BF16, 2)
        for t, r in vst_pool.items:
            kb.op("pool", lambda e, t=t: e.memset(t[:], 1.0), writes=[r])

        def rope_store(src, r_src, cs, r_cs, dst_ap, r_dst):
            pr, r_pr = ps_mm.next()
            kb.op("pe", lambda e: e.matmul(pr[0:64, :], lhsT=rotb[:], rhs=src[:], start=True, stop=True),
                  reads=[r_src, r_id], writes=[r_pr])
            t1, r_t1 = t1_pool.next()
            t2, r_t2 = t2_pool.next()
            kb.op("dve", lambda e: e.tensor_tensor(out=t1[:], in0=src[:], in1=cs[:, 0, :], op=ALU.mult),
                  reads=[r_src, r_cs], writes=[r_t1])
            kb.op("dve", lambda e: e.tensor_tensor(out=t2[:], in0=pr[0:64, :], in1=cs[:, 1, :], op=ALU.mult),
                  reads=[r_pr, r_cs], writes=[r_t2])
            ev, r_ev = ev_pool.next()
            kb.op("dve", lambda e: e.tensor_tensor(out=ev[0:64, :], in0=t1[:], in1=t2[:], op=ALU.add),
                  reads=[r_t1, r_t2], writes=[r_ev])
            kb.dma("pool", dst_ap, ev[0:64, :], reads=[r_ev], writes=[r_dst], chan_res=r_ev)

        for b in range(NB):
            bs = slice(b * 512, (b + 1) * 512)
            cin, r_cin = cin_pool.next()
            kb.dma("sp", cin[:], HT[24:30, :, bs].rearrange("c p t -> p c t"), reads=[r_HT], writes=[r_cin], chan_res=r_cin)
            krin, r_krin = krin_pool.next()
            kb.dma("sp", krin[:], HT[30, 0:64, bs], reads=[r_HT], writes=[r_krin], chan_res=r_krin)
            cs, r_cs = cs_pool.next()
            kb.dma("sp", cs[:], cs_in[:, :, bs].rearrange("c p t -> p c t"), reads=[r_const], writes=[r_cs], chan_res=r_cs)
            sqc, r_sqc = sqc_pool.next()
            kb.op("act", lambda e, sqc=sqc, cin=cin: e.activation(out=sqc[:], in_=cin[:], func=AF.Square), reads=[r_cin], writes=[r_sqc])
            rstd, r_rstd = rstd_pool.next()
            cn, r_cn = cn_pool.next()
            for gi, (c0, nchunk) in enumerate([(0, 4), (4, 2)]):
                pss, r_pss = ps_mm.next()
                for k in range(nchunk):
                    kb.op("pe", lambda e, k=k, c0=c0, nchunk=nchunk, pss=pss, sqc=sqc: e.matmul(
                        pss[:], lhsT=onesb[:], rhs=sqc[:, c0 + k, :], start=(k == 0), stop=(k == nchunk - 1)),
                        reads=[r_sqc, r_id], writes=[r_pss])
                kb.op("act", lambda e, gi=gi, nchunk=nchunk, pss=pss, rstd=rstd: e.activation(
                    out=rstd[:, gi, :], in_=pss[:], func=AF.Ln, bias=epsb[:, 0:1], scale=1.0 / (nchunk * 128)),
                    reads=[r_pss, r_gn], writes=[r_rstd])
                kb.op("act", lambda e, gi=gi, rstd=rstd: e.activation(out=rstd[:, gi, :], in_=rstd[:, gi, :], func=AF.Exp, scale=-0.5),
                      reads=[r_rstd], writes=[r_rstd])
                for k in range(nchunk):
                    kb.op("dve", lambda e, k=k, c0=c0, gi=gi, cn=cn, cin=cin, rstd=rstd: e.tensor_tensor(
                        out=cn[:, c0 + k, :], in0=cin[:, c0 + k, :], in1=rstd[:, gi, :], op=ALU.mult),
                        reads=[r_cin, r_rstd], writes=[r_cn])
            for h in range(4):
                pm, r_pm = ps_mm.next()
                for k in range(4):
                    kb.op("pe", lambda e, k=k, h=h, pm=pm, cn=cn: e.matmul(
                        pm[:], lhsT=wuq[:, k, h * 192:h * 192 + 128], rhs=cn[:, k, :], start=(k == 0), stop=(k == 3)),
                        reads=[r_wu, r_cn], writes=[r_pm])
                ev, r_ev = ev_pool.next()
                kb.op("act", lambda e, pm=pm, ev=ev: e.copy(out=ev[:], in_=pm[:]), reads=[r_pm], writes=[r_ev])
                kb.dma("pool", QCN[h, :, bs], ev[:], reads=[r_ev], writes=[r_QCN], chan_res=r_ev)
                pm, r_pm = ps_mm.next()
                for k in range(4):
                    kb.op("pe", lambda e, k=k, h=h, pm=pm, cn=cn: e.matmul(
                        pm[0:64, :], lhsT=wuq[:, k, h * 192 + 128:h * 192 + 192], rhs=cn[:, k, :], start=(k == 0), stop=(k == 3)),
                        reads=[r_wu, r_cn], writes=[r_pm])
                qpe, r_qpe = qpe_pool.next()
                kb.op("act", lambda e, pm=pm, qpe=qpe: e.copy(out=qpe[:], in_=pm[0:64, :]), reads=[r_pm], writes=[r_qpe])
                rope_store(qpe, r_qpe, cs, r_cs, QCP[h, :, bs], r_QCP)
                pm, r_pm = ps_mm.next()
                for k in range(2):
                    kb.op("pe", lambda e, k=k, h=h, pm=pm, cn=cn: e.matmul(
                        pm[:], lhsT=wukv[:, k, h * 256:h * 256 + 128], rhs=cn[:, 4 + k, :], start=(k == 0), stop=(k == 1)),
                        reads=[r_wu, r_cn], writes=[r_pm])
                ev, r_ev = ev_pool.next()
                kb.op("act", lambda e, pm=pm, ev=ev: e.copy(out=ev[:], in_=pm[:]), reads=[r_pm], writes=[r_ev])
                kb.dma("pool", KCN[h, :, bs], ev[:], reads=[r_ev], writes=[r_KCN], chan_res=r_ev)
            for j in range(4):
                tok = b * 512 + j * 128
                pm, r_pm = ps_mm.next()
                for k in range(2):
                    kb.op("pe", lambda e, k=k, j=j, pm=pm, cn=cn: e.matmul(
                        pm[:].rearrange("p (h d) -> p h d", d=128), lhsT=cn[:, 4 + k, j * 128:(j + 1) * 128],
                        rhs=wukv[:, k, :].rearrange("p (h c) -> p h c", c=256)[:, :, 128:256], start=(k == 0), stop=(k == 1)),
                        reads=[r_wu, r_cn], writes=[r_pm])
                vs, r_vs = vst_pool.next()
                kb.op("act", lambda e, pm=pm, vs=vs: e.copy(out=vs[:, :, 0:128], in_=pm[:].rearrange("p (h d) -> p h d", d=128)),
                      reads=[r_pm], writes=[r_vs])
                kb.dma("pool", VC[tok:tok + 128, :, :], vs[:], reads=[r_vs], writes=[r_VC], chan_res=r_vs)
            rope_store(krin, r_krin, cs, r_cs, KCP[:, bs], r_KCP)
        kb.end_phase()

    def attn_block(P, kts, s_mm, tab, v_rhs, scale, finals):
        nk = len(kts)
        pss = {}
        pss[0] = P.ps_s.next()
        s_mm(kts[0], pss[0][0], pss[0][1])
        for i, kt in enumerate(kts):
            ps, r_ps = pss.pop(i)
            if i + 1 < nk:
                pss[i + 1] = P.ps_s.next()
                s_mm(kts[i + 1], pss[i + 1][0], pss[i + 1][1])
            pt, r_pt = P.pt_pool.next()
            tb = tab(kt)
            if tb is None:
                kb.op("act", lambda e, ps=ps, pt=pt: e.activation(out=pt[:], in_=ps[:], func=AF.Exp, scale=scale),
                      reads=[r_ps], writes=[r_pt])
            else:
                ex, r_ex = P.ex_pool.next()
                kb.op("act", lambda e, ps=ps, ex=ex: e.activation(out=ex[:], in_=ps[:], func=AF.Exp, scale=scale),
                      reads=[r_ps], writes=[r_ex])
                kb.op("dve", lambda e, ex=ex, pt=pt, tb=tb: e.tensor_tensor(out=pt[:], in0=ex[:], in1=tb, op=ALU.mult),
                      reads=[r_ex, P.r_tab], writes=[r_pt])
            for g in range(4):
                po, r_po = P.ps_o[g]
                kb.op("pe", lambda e, g=g, po=po, pt=pt, kt=kt, i=i: e.matmul(
                    po[:, 0:129], lhsT=pt[:, g * 128:(g + 1) * 128], rhs=v_rhs(kt, g), start=(i == 0), stop=(i == nk - 1)),
                    reads=[r_pt, P.r_v], writes=[r_po])
        for (g, out_ap, r_out, extra) in finals:
            po, r_po = P.ps_o[g]
            dn, r_dn = P.den_pool.next()
            if extra is not None:
                kb.op("dve", lambda e, po=po, dn=dn, extra=extra: e.tensor_scalar(out=dn[:, 0:1], in0=po[:, 128:129], scalar1=extra,
                                                                                 scalar2=None, op0=ALU.add),
                      reads=[r_po, r_gn], writes=[r_dn])
                kb.op("dve", lambda e, dn=dn: e.reciprocal(out=dn[:, 1:2], in_=dn[:, 0:1]), reads=[r_dn], writes=[r_dn])
            else:
                kb.op("dve", lambda e, po=po, dn=dn: e.reciprocal(out=dn[:, 1:2], in_=po[:, 128:129]), reads=[r_po], writes=[r_dn])
            kb.op("dve", lambda e, po=po, dn=dn, out_ap=out_ap: e.tensor_scalar(out=out_ap, in0=po[:, 0:128], scalar1=dn[:, 1:2],
                                                                               scalar2=None, op0=ALU.mult),
                  reads=[r_po, r_dn], writes=[r_out])

    def attn_pools():
        P = Ctx()
        P.ps_s = Pool(kb, "pss", [128, 512], F32, 2, space="psum")
        P.ps_o = [(kb.psum("pso%d" % g, [128, 512], F32), kb.res("pso%d" % g)) for g in range(4)]
        P.pt_pool = Pool(kb, "pt", [128, 512], BF16, 3)
        P.ex_pool = Pool(kb, "ex", [128, 512], BF16, 2)
        P.den_pool = Pool(kb, "den", [128, 2], F32, 4)
        P.r_tab = kb.res("tab")
        P.r_v = kb.res("vres")
        return P

    def phase2a(l):
        kb.begin_phase()
        P = attn_pools()
        QA = kb.sbuf("QA", [128, 8, S], BF16)
        KA = kb.sbuf("KA", [128, 2, S], BF16)
        VAs = kb.sbuf("VAs", [128, NT, 2, 129], BF16)
        tab32 = kb.sbuf("tab32", [128, 3072], F32)
        tabA = kb.sbuf("tabAb", [128, 3, 8, 128], BF16)
        r_q = kb.res("QAres")
        kb.dma("sp", QA[:], HT[0:8, :, :].rearrange("c p t -> p c t"), reads=[r_HT], writes=[r_q], chan_res=r_q)
        kb.dma("sp", KA[:], HT[8:10, :, :].rearrange("c p t -> p c t"), reads=[r_HT], writes=[P.r_v], chan_res=P.r_v)
        kb.dma("sp", VAs[:], VA.rearrange("(n p) g d -> p n g d", p=128), reads=[r_VA], writes=[P.r_v], chan_res=P.r_v)
        kb.dma("sp", tab32[:], tabA_in, reads=[r_const], writes=[P.r_tab], chan_res=P.r_tab)
        kb.op("dve", lambda e: e.tensor_copy(out=tabA[:].rearrange("p a h q -> p (a h q)"), in_=tab32[:]), reads=[P.r_tab], writes=[P.r_tab])
        o_pool = Pool(kb, "ost", [128, 1024], F32, 2)
        for n in range(NT):
            ost, r_ost = o_pool.next()
            for kg in range(2):
                kts = [m for m in (n - 1, n, n + 1) if 0 <= m < NT]

                def s_mm(kt, ps, r_ps, kg=kg, n=n):
                    kb.op("pe", lambda e: e.matmul(ps[:].rearrange("p (h q) -> p h q", q=128), lhsT=KA[:, kg, kt * 128:(kt + 1) * 128],
                                                   rhs=QA[:, 4 * kg:4 * kg + 4, n * 128:(n + 1) * 128], start=True, stop=True),
                          reads=[P.r_v, r_q], writes=[r_ps])
                attn_block(P, kts, s_mm,
                           lambda kt, kg=kg, n=n: tabA[:, kt - n + 1, 4 * kg:4 * kg + 4, :].rearrange("p h q -> p (h q)"),
                           lambda kt, g, kg=kg: VAs[:, kt, kg, :], SC_A,
                           [(g, ost[:, (4 * kg + g) * 128:(4 * kg + g + 1) * 128], r_ost, esink[:, l, 4 * kg + g:4 * kg + g + 1]) for g in range(4)])
            kb.dma("pool", O[n * 128:(n + 1) * 128, 0:1024], ost[:], reads=[r_ost], writes=[r_O], chan_res=r_ost)
        kb.end_phase()

    def phase2b(l):
        kb.begin_phase()
        P = attn_pools()
        QB = kb.sbuf("QB", [128, 4, S], BF16)
        KBs = kb.sbuf("KBs", [128, 4, S], BF16)
        VBs = kb.sbuf("VBs", [128, NT, 4, 129], BF16)
        tabB = kb.sbuf("tabB", [128, NTAB, 4, 128], BF16)
        mk32 = kb.sbuf("mk32", [128, NTAB, 128], F32)
        bst_pool = Pool(kb, "bst", [128, 4, 128], F32, 2)
        r_q = kb.res("QBres")
        kb.dma("sp", QB[:], HT[12:16, :, :].rearrange("c p t -> p c t"), reads=[r_HT], writes=[r_q], chan_res=r_q)
        kb.dma("sp", KBs[:], HT[16:20, :, :].rearrange("c p t -> p c t"), reads=[r_HT], writes=[P.r_v], chan_res=P.r_v)
        kb.dma("sp", VBs[:], VB.rearrange("(n p) g d -> p n g d", p=128), reads=[r_VB], writes=[P.r_v], chan_res=P.r_v)
        kb.dma("sp", mk32[:], maskB_in, reads=[r_const], writes=[P.r_tab], chan_res=P.r_tab)
        for ti in range(NTAB):
            bst, r_bst = bst_pool.next()
            kb.dma("sp", bst[:], biasB_in[l, :, ti, :, :], reads=[r_const], writes=[r_bst], chan_res=r_bst)
            kb.op("act", lambda e, bst=bst: e.activation(out=bst[:], in_=bst[:], func=AF.Exp), reads=[r_bst], writes=[r_bst])
            for h in range(4):
                kb.op("dve", lambda e, bst=bst, ti=ti, h=h: e.tensor_tensor(out=tabB[:, ti, h, :], in0=bst[:, h, :], in1=mk32[:, ti, :], op=ALU.mult),
                      reads=[r_bst, P.r_tab], writes=[P.r_tab])
        o_pool = Pool(kb, "ost", [128, 512], F32, 2)
        for n in range(NT):
            ost, r_ost = o_pool.next()
            lst = schedB[n]
            tmap = dict(lst)

            def s_mm(kt, ps, r_ps, n=n):
                for h in range(4):
                    kb.op("pe", lambda e, h=h: e.matmul(ps[:, h * 128:(h + 1) * 128], lhsT=KBs[:, h, kt * 128:(kt + 1) * 128],
                                                        rhs=QB[:, h, n * 128:(n + 1) * 128], start=True, stop=True),
                          reads=[P.r_v, r_q], writes=[r_ps])
            attn_block(P, [m for m, _ in lst], s_mm,
                       lambda kt, tmap=tmap: tabB[:, tmap[kt], :, :].rearrange("p h q -> p (h q)"),
                       lambda kt, g: VBs[:, kt, g, :], SC_A,
                       [(g, ost[:, g * 128:(g + 1) * 128], r_ost, None) for g in range(4)])
            kb.dma("pool", O[n * 128:(n + 1) * 128, 1024:1536], ost[:], reads=[r_ost], writes=[r_O], chan_res=r_ost)
        kb.end_phase()

    def phase2c(l):
        kb.begin_phase()
        P = attn_pools()
        KN = kb.sbuf("KN", [128, 4, S], BF16)
        KP = kb.sbuf("KP", [64, S], BF16)
        VCs = kb.sbuf("VCs", [128, NT, 4, 129], BF16)
        kb.dma("sp", KN[:], KCN.rearrange("c p t -> p c t"), reads=[r_KCN], writes=[P.r_v], chan_res=P.r_v)
        kb.dma("sp", KP[:], KCP, reads=[r_KCP], writes=[P.r_v], chan_res=P.r_v)
        kb.dma("sp", VCs[:], VC.rearrange("(n p) g d -> p n g d", p=128), reads=[r_VC], writes=[P.r_v], chan_res=P.r_v)
        qn_pool = Pool(kb, "qn", [128, 4, 512], BF16, 2)
        qp_pool = Pool(kb, "qp", [64, 4, 512], BF16, 2)
        o_pool = Pool(kb, "ost", [128, 4, 512], F32, 2)
        for b in range(NB):
            bs = slice(b * 512, (b + 1) * 512)
            qn, r_qn = qn_pool.next()
            qp, r_qp = qp_pool.next()
            kb.dma("sp", qn[:], QCN[:, :, bs].rearrange("c p t -> p c t"), reads=[r_QCN], writes=[r_qn], chan_res=r_qn)
            kb.dma("sp", qp[:], QCP[:, :, bs].rearrange("c p t -> p c t"), reads=[r_QCP], writes=[r_qp], chan_res=r_qp)
            ost, r_ost = o_pool.next()
            for h in range(4):
                def s_mm(kt, ps, r_ps, h=h, qn=qn, qp=qp, r_qn=r_qn, r_qp=r_qp):
                    kb.op("pe", lambda e: e.matmul(ps[:], lhsT=KN[:, h, kt * 128:(kt + 1) * 128], rhs=qn[:, h, :], start=True, stop=False),
                          reads=[P.r_v, r_qn], writes=[r_ps])
                    kb.op("pe", lambda e: e.matmul(ps[:], lhsT=KP[:, kt * 128:(kt + 1) * 128], rhs=qp[:, h, :], start=False, stop=True),
                          reads=[P.r_v, r_qp], writes=[r_ps])
                attn_block(P, list(range(NT)), s_mm, lambda kt: None, lambda kt, g, h=h: VCs[:, kt, h, :], SC_C,
                           [(g, ost[:, g, h * 128:(h + 1) * 128], r_ost, None) for g in range(4)])
            kb.dma("pool", O[bs, 1536:2048].rearrange("(g p) d -> p g d", p=128), ost[:], reads=[r_ost], writes=[r_O], chan_res=r_ost)
        kb.end_phase()

    def phase3(l, x_src, r_xsrc, X1, r_X1):
        kb.begin_phase()
        wst_pool = Pool(kb, "wst", [128, 2048], F32, 2)
        wo = kb.sbuf("wo", [128, KC, D], BF16)
        r_wo = kb.res("wo")
        load_w(wst_pool, w_o[l], 22, D, l, wo, r_wo, r_win2)
        g2 = kb.sbuf("g2", [128, D], F32); r_g2 = kb.res("g2")
        kb.dma("sp", g2[:], ln2b_in[l], reads=[r_const], writes=[r_g2], chan_res=r_g2)
        ot_pool = Pool(kb, "ot", [128, D], F32, 2)
        xt_pool = Pool(kb, "xt", [128, D], F32, 2)
        x1_pool = Pool(kb, "x1", [128, D], F32, 2)
        xn_pool = Pool(kb, "xn", [128, D], BF16, 2)
        sq_pool = Pool(kb, "sq", [128, D], BF16, 1)
        st_pool = Pool(kb, "stat", [128, 4], F32, 4)
        oT_pool = Pool(kb, "oT", [128, KC, 128], BF16, 2)
        xn2T_pool = Pool(kb, "xn2T", [128, KC, 512], BF16, 1)
        ps_tp = Pool(kb, "tp", [128, KC, 128], BF16, 2, space="psum")
        ps_mm = Pool(kb, "mm", [128, 512], F32, 4, space="psum")
        rmsnorm_tile, transpose_tile = mk_norm_helpers(st_pool, sq_pool, ps_tp)
        for b in range(NB):
            xn2T, r_xn2T = xn2T_pool.next()
            for j in range(4):
                tok = b * 512 + j * 128
                ot, r_ot = ot_pool.next()
                kb.dma("sp", ot[:], O[tok:tok + 128, :], reads=[r_O], writes=[r_ot], chan_res=r_ot)
                xt, r_xt = xt_pool.next()
                kb.dma("sp", xt[:], x_src[tok:tok + 128, :], reads=[r_xsrc], writes=[r_xt], chan_res=r_xt)
                on, r_on = xn_pool.next()
                rmsnorm_tile(ot, r_ot, [(0, 1024), (1024, 512), (1536, 512)], on, r_on)
                oT, r_oT = oT_pool.next()
                transpose_tile(on, r_on, oT, r_oT, 0)
                x1, r_x1 = x1_pool.next()
                for nb in range(4):
                    pm, r_pm = ps_mm.next()
                    for k in range(KC):
                        kb.op("pe", lambda e, k=k, nb=nb, pm=pm, oT=oT: e.matmul(
                            pm[:], lhsT=oT[:, k, :], rhs=wo[:, k, nb * 512:(nb + 1) * 512], start=(k == 0), stop=(k == KC - 1)),
                            reads=[r_wo, r_oT], writes=[r_pm])
                    kb.op("dve", lambda e, nb=nb, pm=pm, x1=x1, xt=xt: e.tensor_tensor(
                        out=x1[:, nb * 512:(nb + 1) * 512], in0=pm[:], in1=xt[:, nb * 512:(nb + 1) * 512], op=ALU.add),
                        reads=[r_pm, r_xt], writes=[r_x1])
                kb.dma("pool", X1[tok:tok + 128, :], x1[:], reads=[r_x1], writes=[r_X1], chan_res=r_x1)
                xn, r_xn = xn_pool.next()
                rmsnorm_tile(x1, r_x1, [(0, D)], xn, r_xn, gain=(g2, r_g2))
                transpose_tile(xn, r_xn, xn2T, r_xn2T, j)
            kb.dma("pool", XN2T[:, :, b * 512:(b + 1) * 512].rearrange("c p t -> p c t"), xn2T[:], reads=[r_xn2T], writes=[r_XN2T],
                   chan_res=r_xn2T)
        kb.end_phase()

    def phase4a(l):
        kb.begin_phase()
        wst_pool = Pool(kb, "wst", [128, 1024], F32, 2)
        wq = kb.sbuf("wq", [128, KC, 1024], BF16)
        r_wq = kb.res("wq")
        load_w(wst_pool, w_pq[l], None, 1024, l, wq, r_wq, r_win2)
        xb_pool = Pool(kb, "xb", [128, KC, 512], BF16, 2)
        ps_mm = Pool(kb, "mm", [128, 512], F32, 4, space="psum")
        ev_pool = Pool(kb, "ev", [128, 512], BF16, 4)
        for b in range(NB):
            bs = slice(b * 512, (b + 1) * 512)
            xb, r_xb = xb_pool.next()
            kb.dma("sp", xb[:], XN2T[:, :, bs].rearrange("c p t -> p c t"), reads=[r_XN2T], writes=[r_xb], chan_res=r_xb)
            for h in range(8):
                pm, r_pm = ps_mm.next()
                for k in range(KC):
                    kb.op("pe", lambda e, k=k, h=h, pm=pm, xb=xb: e.matmul(
                        pm[:], lhsT=wq[:, k, h * 128:(h + 1) * 128], rhs=xb[:, k, :], start=(k == 0), stop=(k == KC - 1)),
                        reads=[r_wq, r_xb], writes=[r_pm])
                ev, r_ev = ev_pool.next()
                kb.op("act", lambda e, pm=pm, ev=ev: e.copy(out=ev[:], in_=pm[:]), reads=[r_pm], writes=[r_ev])
                kb.dma("pool", QT[h, :, bs], ev[:], reads=[r_ev], writes=[r_QT], chan_res=r_ev)
        kb.end_phase()

    DELTA = 1e-5

    def phase4b(l):
        kb.begin_phase()
        sk32 = kb.sbuf("sk32", [128, 256], F32)
        skb = kb.sbuf("skb", [128, 256], BF16)
        r_sk = kb.res("sk")
        kb.dma("sp", sk32[:], sk_in[l], reads=[r_const], writes=[r_sk], chan_res=r_sk)
        kb.op("dve", lambda e: e.tensor_copy(out=skb[:], in_=sk32[:]), reads=[r_sk], writes=[r_sk])
        q_pool = Pool(kb, "qt", [128, 8, 128], BF16, 2)
        s_pool = Pool(kb, "s", [128, 8, 2, 128], F32, 2)
        top_pool = Pool(kb, "top", [128, 8, 2, 16], F32, 2)
        tmp = kb.sbuf("tmpm", [128, 16, 128], F32)
        r_tmpc = [kb.res("tmpc%d" % i) for i in range(16)]
        cand = kb.sbuf("cand", [128, 8, 256], F32); r_cand = kb.res("cand")
        best = kb.sbuf("best", [128, 8, 16], F32); r_best = kb.res("best")
        dd = kb.sbuf("dd", [128, 8, 16], F32)
        zz = kb.sbuf("zz", [128, 4, 8], F32)
        cc = kb.sbuf("cc", [128, 8, 16], F32)
        cb = kb.sbuf("cb", [128, 8, 16], BF16)
        thr = kb.sbuf("thr", [128, 8, 16], F32)
        r_sm = kb.res("small")
        E2 = kb.sbuf("E2", [128, 8, 128], BF16); r_E2 = kb.res("E2")
        A = kb.sbuf("A", [128, 8, 16, 128], BF16); r_A = kb.res("A")
        Bms = [(kb.sbuf("Bm%d" % i, [128, 8, 16, 64], BF16), kb.res("Bm%d" % i)) for i in range(2)]
        AT = kb.sbuf("AT", [128, 128, 128], BF16); r_AT = kb.res("AT")
        BT = kb.sbuf("BT", [128, 64, 128], BF16); r_BT = kb.res("BT")
        cT = kb.sbuf("cT", [128, 128], BF16); r_cT = kb.res("cT")
        g_pool = Pool(kb, "gt", [128, 64, 128], BF16, 2)
        ps_s = Pool(kb, "pss", [128, 2, 256], F32, 1, space="psum")
        ps_tp = Pool(kb, "tp", [128, 16, 128], BF16, 2, space="psum")
        ps_g = Pool(kb, "pg", [128, 8, 64], F32, 3, space="psum")
        A2 = A[:].rearrange("p h a i -> p (h a) i")
        evtog = [0]
        def front1(n):
            ts = slice(n * 128, (n + 1) * 128)
            qt, r_qt = q_pool.next()
            kb.dma("sp", qt[:], QT[:, :, ts].rearrange("h p t -> p h t"), reads=[r_QT], writes=[r_qt], chan_res=r_qt)
            s, r_s = s_pool.next()
            for hp in range(4):
                ps, r_ps = ps_s.next()
                for hh in range(2):
                    kb.op("pe", lambda e, hp=hp, hh=hh, ps=ps, qt=qt: e.matmul(ps[:, hh, :], lhsT=qt[:, 2 * hp + hh, :], rhs=skb[:], start=True, stop=True),
                          reads=[r_qt, r_sk], writes=[r_ps])
                kb.op("act", lambda e, hp=hp, ps=ps, s=s: e.copy(out=s[:, 2 * hp:2 * hp + 2, :, :].rearrange("p h c n -> p h (c n)"), in_=ps[:]),
                      reads=[r_ps], writes=[r_s])
            return s, r_s

        nxt = front1(0)
        for n in range(NT):
            s, r_s = nxt
            top, _r_top_unused = top_pool.next()
            r_tc = [kb.res("topc") for _ in range(16)]
            r_top = kb.res("topall")
            for h in range(8):
                for c in range(2):
                    kb.op("dve", lambda e, h=h, c=c, top=top, s=s: e.max(out=top[:, h, c, 0:8], in_=s[:, h, c, :]), reads=[r_s], writes=[r_tc[2 * h + c]])
            for h in range(8):
                for c in range(2):
                    kb.op("dve", lambda e, h=h, c=c, top=top, s=s: e.match_replace(out=tmp[:, 2 * h + c, :], in_to_replace=top[:, h, c, 0:8],
                                                                                   in_values=s[:, h, c, :], imm_value=-1e30),
                          reads=[r_s, r_tc[2 * h + c]], writes=[r_tmpc[2 * h + c]])
            for h in range(8):
                for c in range(2):
                    kb.op("dve", lambda e, h=h, c=c, top=top: e.max(out=top[:, h, c, 8:16], in_=tmp[:, 2 * h + c, :]),
                          reads=[r_tmpc[2 * h + c]], writes=[r_tc[2 * h + c], r_top] if (h == 7 and c == 1) else [r_tc[2 * h + c]])
            kb.op("dve", lambda e, top=top: e.tensor_tensor(out=cand[:].rearrange("p h (a b) -> p h a b", a=16),
                                                            in0=top[:, :, 0, :].unsqueeze(3).to_broadcast([128, 8, 16, 16]),
                                                            in1=top[:, :, 1, :].unsqueeze(2).to_broadcast([128, 8, 16, 16]), op=ALU.add),
                  reads=r_tc, writes=[r_cand, r_top])
            r_bc = [kb.res("bestc") for _ in range(8)]
            tmp2 = tmp[:].rearrange("p (h c) n -> p h (c n)", c=2)
            for h in range(8):
                kb.op("dve", lambda e, h=h: e.max(out=best[:, h, 0:8], in_=cand[:, h, :]), reads=[r_cand], writes=[r_bc[h]])
            for h in range(8):
                kb.op("dve", lambda e, h=h: e.match_replace(out=tmp2[:, h, :], in_to_replace=best[:, h, 0:8], in_values=cand[:, h, :], imm_value=-1e30),
                      reads=[r_cand, r_bc[h]], writes=[r_tmpc[2 * h], r_tmpc[2 * h + 1]])
            for h in range(8):
                kb.op("dve", lambda e, h=h: e.max(out=best[:, h, 8:16], in_=tmp2[:, h, :]), reads=[r_tmpc[2 * h], r_tmpc[2 * h + 1]],
                      writes=[r_bc[h], r_best] if h == 7 else [r_bc[h]])
            kb.op("dve", lambda e: e.tensor_copy(out=dd[:, 0, 0:1], in_=best[:, 0, 0:1]), reads=r_bc, writes=[r_best, r_sm])
            kb.op("dve", lambda e: e.tensor_tensor(out=dd[:], in0=best[:], in1=best[:, :, 0:1].to_broadcast([128, 8, 16]), op=ALU.subtract),
                  reads=[r_best], writes=[r_sm])
            kb.op("act", lambda e: e.activation(out=dd[:], in_=dd[:], func=AF.Exp), reads=[r_sm], writes=[r_sm])
            kb.op("dve", lambda e: e.tensor_reduce(out=zz[:, 0, :], in_=dd[:], axis=AX.X, op=ALU.add), reads=[r_sm], writes=[r_sm])
            kb.op("act", lambda e: e.activation(out=zz[:, 1, :], in_=zz[:, 0, :], func=AF.Ln), reads=[r_sm], writes=[r_sm])
            kb.op("dve", lambda e: e.tensor_tensor(out=zz[:, 2, :], in0=zz[:, 1, :], in1=best[:, :, 0], op=ALU.add), reads=[r_sm, r_best], writes=[r_sm])
            kb.op("dve", lambda e, top=top: e.tensor_tensor(out=cc[:], in0=top[:, :, 0, :], in1=zz[:, 2, :].unsqueeze(2).to_broadcast([128, 8, 16]),
                                                            op=ALU.subtract), reads=[r_sm, r_top], writes=[r_sm])
            kb.op("act", lambda e: e.activation(out=cb[:], in_=cc[:], func=AF.Exp), reads=[r_sm], writes=[r_sm])
            kb.op("dve", lambda e: e.tensor_scalar(out=zz[:, 3, :], in0=best[:, :, 15], scalar1=-DELTA, scalar2=None, op0=ALU.add),
                  reads=[r_best], writes=[r_sm])
            kb.op("dve", lambda e, top=top: e.tensor_tensor(out=thr[:], in0=zz[:, 3, :].unsqueeze(2).to_broadcast([128, 8, 16]), in1=top[:, :, 0, :],
                                                            op=ALU.subtract), reads=[r_sm, r_top], writes=[r_sm])
            kb.op("act", lambda e, s=s: e.activation(out=E2[:], in_=s[:, :, 1, :], func=AF.Exp), reads=[r_s], writes=[r_E2])
            kb.op("dve", lambda e, s=s, top=top: e.tensor_tensor(out=A[:], in0=s[:, :, 0, :].unsqueeze(2).to_broadcast([128, 8, 16, 128]),
                                                                 in1=top[:, :, 0, :].unsqueeze(3).to_broadcast([128, 8, 16, 128]), op=ALU.is_equal),
                  reads=[r_s, r_top], writes=[r_A])
            pt, r_pt = ps_tp.next()
            kb.op("pe", lambda e, pt=pt: e.transpose(out=pt[:, 0, :], in_=cb[:].rearrange("p h a -> p (h a)"), identity=ident[:]),
                  reads=[r_sm, r_id], writes=[r_pt])
            kb.op("act", lambda e, pt=pt: e.copy(out=cT[:], in_=pt[:, 0, :]), reads=[r_pt], writes=[r_cT])
            def build_B(jh):
                Bm_, r_Bm_ = Bms[jh]
                js = slice(jh * 64, (jh + 1) * 64)
                kb.op("dve", lambda e, s=s, js=js, Bm_=Bm_: e.tensor_tensor(out=Bm_[:], in0=s[:, :, 1, js].unsqueeze(2).to_broadcast([128, 8, 16, 64]),
                                                                            in1=thr[:].unsqueeze(3).to_broadcast([128, 8, 16, 64]), op=ALU.is_ge),
                      reads=[r_s, r_sm], writes=[r_Bm_])
                kb.op("pool", lambda e, js=js, Bm_=Bm_: e.tensor_tensor(out=Bm_[:], in0=Bm_[:], in1=E2[:, :, js].unsqueeze(2).to_broadcast([128, 8, 16, 64]),
                                                                        op=ALU.mult), reads=[r_E2, r_Bm_], writes=[r_Bm_])

            a_pts = []
            for i0 in range(0, 128, 16):
                pt, r_pt = ps_tp.next()
                for ii in range(16):
                    kb.op("pe", lambda e, pt=pt, ii=ii, i0=i0: e.transpose(out=pt[:, ii, :], in_=A2[:, :, i0 + ii], identity=ident[:]),
                          reads=[r_A, r_id], writes=[r_pt])
                if i0 == 0:
                    build_B(0)
                kb.op("dve", lambda e, pt=pt, i0=i0: e.tensor_tensor(out=AT[:, i0:i0 + 16, :], in0=pt[:],
                                                                     in1=cT[:].unsqueeze(1).to_broadcast([128, 16, 128]), op=ALU.mult),
                      reads=[r_pt, r_cT], writes=[r_AT])
            build_B(1)
            if n + 1 < NT:
                nxt = front1(n + 1)
            for jh in range(2):
                Bm_, r_Bm_ = Bms[jh]
                B2_ = Bm_[:].rearrange("p h a j -> p (h a) j")
                js = slice(jh * 64, (jh + 1) * 64)
                for j0 in range(0, 64, 16):
                    pt, r_pt = ps_tp.next()
                    for jj in range(16):
                        kb.op("pe", lambda e, pt=pt, jj=jj, j0=j0, B2_=B2_: e.transpose(out=pt[:, jj, :], in_=B2_[:, :, j0 + jj], identity=ident[:]),
                              reads=[r_Bm_, r_id], writes=[r_pt])
                    kb.op("act", lambda e, pt=pt, j0=j0: e.copy(out=BT[:, j0:j0 + 16, :], in_=pt[:]), reads=[r_pt], writes=[r_BT])
                gt, r_gt = g_pool.next()
                for t0 in range(0, 128, 8):
                    pg, r_pg = ps_g.next()
                    for tt in range(8):
                        kb.op("pe", lambda e, pg=pg, tt=tt, t0=t0: e.matmul(pg[:, tt, :], lhsT=AT[:, :, t0 + tt], rhs=BT[:, :, t0 + tt], start=True, stop=True),
                              reads=[r_AT, r_BT], writes=[r_pg])
                    kb.op("act", lambda e, pg=pg, gt=gt, t0=t0: e.copy(out=gt[:, :, t0:t0 + 8], in_=pg[:].rearrange("p t j -> p j t")),
                          reads=[r_pg], writes=[r_gt])
                kb.dma("pool", GT[n, :, js, :], gt[:], reads=[r_gt], writes=[r_GT], chan_res=r_gt)
        kb.end_phase()

    def phase5(l, X1, r_X1, X2, r_X2):
        kb.begin_phase()
        TBT = min(TBT_MAX, NT)
        TB = TBT * 128
        nblk = NT // TBT
        NSC = 64
        xblk = kb.sbuf("xblk", [128, KC, TB], BF16); r_xblk = kb.res("xblk")
        y_sb = kb.sbuf("ysb", [128, TBT, D], F32)
        r_ysb = [[kb.res("ysb") for _ in range(4)] for _ in range(TBT)]
        ty_pool = Pool(kb, "ty", [128, 512], F32, 2)
        un_pool = Pool(kb, "un", [128, 2, D], BF16, 2)
        v_pool = Pool(kb, "vv", [128, 2, D], BF16, 3)
        uT_pool = Pool(kb, "uT", [128, KC, 128], BF16, 4)
        g_pool = Pool(kb, "gg", [128, TBT, 2, 128], BF16, 2)
        ge_pool = Pool(kb, "ge", [128, 512], BF16, 2)
        hs_pool = Pool(kb, "hs", [128, TB], BF16, 4)
        ps_tp = Pool(kb, "tp", [128, KC, 128], BF16, 2, space="psum")
        ps_a = Pool(kb, "pa", [128, 512], F32, 2, space="psum")
        ps_y = Pool(kb, "py", [128, 512], F32, 2, space="psum")
        U3 = peer_u[l].rearrange("(i j) d -> i j d", j=128)
        V3 = peer_v[l].rearrange("(i j) d -> i j d", j=128)
        for blk in range(nblk):
            t0 = blk * TB
            kb.dma("sp", xblk[:], XN2T[:, :, t0:t0 + TB].rearrange("c p t -> p c t"), reads=[r_XN2T], writes=[r_xblk], chan_res=r_xblk)
            ctx = {}
            for tile in range(TBT):
                tok = t0 + tile * 128
                kb.dma("sp", y_sb[:, tile, :], X1[tok:tok + 128, :], reads=[r_X1], writes=r_ysb[tile], chan_res=r_ysb[tile][1])

            def do_L(sc):
                j0 = 2 * sc
                c = Ctx()
                c.un, c.r_un = un_pool.next()
                kb.dma("pool", c.un[:], U3[:, j0:j0 + 2, :], reads=[r_const], writes=[c.r_un], chan_res=c.r_un)
                ctx[sc] = c

            def do_L2(sc):
                j0 = 2 * sc
                c = ctx[sc]
                c.gg, c.r_gg = g_pool.next()
                c.vv, c.r_vv = v_pool.next()
                kb.dma("sp", c.gg[:], GT[blk * TBT:(blk + 1) * TBT, :, j0:j0 + 2, :].rearrange("n i j t -> i n j t"), reads=[r_GT],
                       writes=[c.r_gg], chan_res=c.r_gg)
                kb.dma("pool", c.vv[:], V3[:, j0:j0 + 2, :], reads=[r_const], writes=[c.r_vv], chan_res=c.r_vv)

            def do_T(sc):
                c = ctx[sc]
                c.uT = []
                for jj in range(2):
                    pt, r_pt = ps_tp.next()
                    for k in range(KC):
                        kb.op("pe", lambda e, k=k, jj=jj, pt=pt, un=c.un: e.transpose(out=pt[:, k, :], in_=un[:, jj, k * 128:(k + 1) * 128], identity=ident[:]),
                              reads=[c.r_un, r_id], writes=[r_pt])
                    uT, r_uT = uT_pool.next()
                    kb.op("act", lambda e, pt=pt, uT=uT: e.copy(out=uT[:], in_=pt[:]), reads=[r_pt], writes=[r_uT])
                    c.uT.append((uT, r_uT))

            def do_A(sc):
                c = ctx[sc]
                c.hs = []
                for jj in range(2):
                    uT, r_uT = c.uT[jj]
                    h_t, r_ht = hs_pool.next()
                    for half in range(TB // 512):
                        pa, r_pa = ps_a.next()
                        for k in range(KC):
                            kb.op("pe", lambda e, k=k, pa=pa, uT=uT, half=half: e.matmul(pa[:], lhsT=uT[:, k, :], rhs=xblk[:, k, half * 512:(half + 1) * 512],
                                                                                          start=(k == 0), stop=(k == KC - 1)),
                                  reads=[r_uT, r_xblk], writes=[r_pa])
                        ge, r_ge = ge_pool.next()
                        kb.op("act", lambda e, pa=pa, ge=ge: e.activation(out=ge[:], in_=pa[:], func=AF.Gelu_apprx_tanh), reads=[r_pa], writes=[r_ge])
                        kb.op("dve", lambda e, ge=ge, h_t=h_t, gg=c.gg, jj=jj, half=half: e.tensor_tensor(
                            out=h_t[:, half * 512:(half + 1) * 512].rearrange("p (n t) -> p n t", t=128), in0=ge[:].rearrange("p (n t) -> p n t", t=128),
                            in1=gg[:, half * 4:(half + 1) * 4, jj, :], op=ALU.mult), reads=[r_ge, c.r_gg], writes=[r_ht])
                    c.hs.append((h_t, r_ht))

            def do_Y(sc):
                c = ctx[sc]
                for tile in range(TBT):
                    for nb in range(4):
                        py, r_py = ps_y.next()
                        for jj in range(2):
                            h_t, r_ht = c.hs[jj]
                            kb.op("pe", lambda e, jj=jj, py=py, h_t=h_t, vv=c.vv, tile=tile, nb=nb: e.matmul(
                                py[:], lhsT=h_t[:, tile * 128:(tile + 1) * 128], rhs=vv[:, jj, nb * 512:(nb + 1) * 512], start=(jj == 0), stop=(jj == 1)),
                                reads=[r_ht, c.r_vv], writes=[r_py])
                        ysl = y_sb[:, tile, nb * 512:(nb + 1) * 512]
                        if (tile * 4 + nb) % 3 == 2:
                            ty, r_ty = ty_pool.next()
                            kb.op("act", lambda e, py=py, ty=ty: e.copy(out=ty[:], in_=py[:]), reads=[r_py], writes=[r_ty])
                            kb.op("pool", lambda e, ty=ty, ysl=ysl: e.tensor_tensor(out=ysl, in0=ty[:], in1=ysl, op=ALU.add),
                                  reads=[r_ty, r_ysb[tile][nb]], writes=[r_ysb[tile][nb]])
                        else:
                            kb.op("dve", lambda e, py=py, ysl=ysl: e.tensor_tensor(out=ysl, in0=py[:], in1=ysl, op=ALU.add),
                                  reads=[r_py, r_ysb[tile][nb]], writes=[r_ysb[tile][nb]])
                del ctx[sc]

            do_L(0)
            do_L2(0)
            do_L(1)
            do_L2(1)
            do_T(0)
            for i in range(NSC):
                if i + 2 < NSC:
                    do_L(i + 2)
                if i + 1 < NSC:
                    do_T(i + 1)
                do_A(i)
                if i >= 1:
                    do_Y(i - 1)
                if i + 2 < NSC:
                    do_L2(i + 2)
            do_Y(NSC - 1)
            for tile in range(TBT):
                tok = t0 + tile * 128
                kb.dma("pool", X2[tok:tok + 128, :], y_sb[:, tile, :], reads=r_ysb[tile], writes=[r_X2], chan_res=r_ysb[tile][0])
        kb.end_phase()

    def phasef(x_src, r_xsrc):
        kb.begin_phase()
        fn = kb.sbuf("fn", [128, D], F32); r_fn = kb.res("fn")
        kb.dma("sp", fn[:], fnorm_in, reads=[r_const], writes=[r_fn], chan_res=r_fn)
        xt_pool = Pool(kb, "xt", [128, D], F32, 2)
        yo_pool = Pool(kb, "yo", [128, D], F32, 2)
        sq_pool = Pool(kb, "sq", [128, D], BF16, 1)
        st_pool = Pool(kb, "stat", [128, 4], F32, 4)
        for n in range(NT):
            xt, r_xt = xt_pool.next()
            kb.dma("sp", xt[:], x_src[n * 128:(n + 1) * 128, :], reads=[r_xsrc], writes=[r_xt], chan_res=r_xt)
            stt, r_st = st_pool.next()
            sq, r_sq = sq_pool.next()
            kb.op("act", lambda e, xt=xt, stt=stt, sq=sq: e.activation(out=sq[:], in_=xt[:], func=AF.Square, accum_out=stt[:, 0:1]),
                  reads=[r_xt], writes=[r_sq, r_st])
            kb.op("act", lambda e, stt=stt: e.activation(out=stt[:, 0:1], in_=stt[:, 0:1], func=AF.Ln, bias=epsb[:, 0:1], scale=1.0 / D),
                  reads=[r_st, r_gn], writes=[r_st])
            kb.op("act", lambda e, stt=stt: e.activation(out=stt[:, 0:1], in_=stt[:, 0:1], func=AF.Exp, scale=-0.5), reads=[r_st], writes=[r_st])
            yo, r_yo = yo_pool.next()
            kb.op("dve", lambda e, xt=xt, stt=stt, yo=yo: e.scalar_tensor_tensor(out=yo[:], in0=xt[:], scalar=stt[:, 0:1], in1=fn[:],
                                                                                 op0=ALU.mult, op1=ALU.mult),
                  reads=[r_xt, r_st, r_fn], writes=[r_yo])
            kb.dma("pool", y_out[n * 128:(n + 1) * 128, :], yo[:], reads=[r_yo], writes=[r_y], chan_res=r_yo)
        kb.end_phase()

    for l in range(L):
        x_src, r_xsrc = (x_in, r_x) if l == 0 else (XB, r_XB)
        phase1(l, x_src, r_xsrc)
        if upto >= 2:
            phase1c(l)
        if upto >= 3:
            phase2a(l)
            phase2b(l)
            phase2c(l)
        if upto >= 4:
            phase3(l, x_src, r_xsrc, XA, r_XA)
        if upto >= 5:
            phase4a(l)
            phase4b(l)
        if upto >= 6:
            phase5(l, XA, r_XA, XB, r_XB)
    if upto >= 7:
        phasef(XB, r_XB)
    outs = [r_HT, r_VA, r_VB, r_QCN, r_QCP, r_KCN, r_KCP, r_VC, r_O, r_XA, r_XB, r_XN2T, r_QT, r_GT, r_y]
    kb.finish_wait("pool", outs)
    kb.emit()
    return nc


def host_consts(S):
    c = {}
    c["ident"] = np.eye(128, dtype=np.float32)
    R = np.zeros((64, 64), np.float32)
    for m in range(32):
        R[m + 32, m] = -1.0
        R[m, m + 32] = 1.0
    c["rotR"] = R
    inv = 10000.0 ** (-(np.arange(32, dtype=np.float64)) / 32.0)
    ang = np.arange(S, dtype=np.float64)[None, :] * np.concatenate([inv, inv])[:, None]
    c["cossin"] = np.stack([np.cos(ang), np.sin(ang)]).astype(np.float32)
    ki = np.arange(128)[:, None]
    qi = np.arange(128)[None, :]
    slopes = np.array([2.0 ** (-8.0 * (h + 1) / 8) for h in range(8)], np.float64)
    tabA = np.zeros((128, 3, 8, 128), np.float64)
    for d, (dist, valid) in enumerate([(128 + qi - ki, qi <= ki), (np.abs(qi - ki), np.ones((128, 128), bool)),
                                       (128 + ki - qi, ki <= qi)]):
        for h in range(8):
            tabA[:, d, h, :] = np.where(valid, np.exp(-slopes[h] * dist), 0.0)
    c["tabA"] = tabA.astype(np.float32).reshape(128, 3 * 8 * 128)
    rows = S // 64
    kr = min(8, rows)
    NT = S // 128
    tabs = {}
    sched = []
    masks, dridx, dcidx = [], [], []
    for n in range(NT):
        qrow = 2 * n + np.arange(128) // 64
        qcol = np.arange(128) % 64
        rstart = np.clip(qrow - kr // 2, 0, rows - kr)
        cstart = np.clip(qcol - 8, 0, 64 - 16)
        lst = []
        for m in range(NT):
            krow = 2 * m + np.arange(128) // 64
            kcol = np.arange(128) % 64
            valid = ((krow[:, None] >= rstart[None, :]) & (krow[:, None] < rstart[None, :] + kr)
                     & (kcol[:, None] >= cstart[None, :]) & (kcol[:, None] < cstart[None, :] + 16))
            if not valid.any():
                continue
            dr = np.clip(krow[:, None] - qrow[None, :] + 7, 0, 14)
            dc = np.clip(kcol[:, None] - qcol[None, :] + 15, 0, 30)
            dr = np.where(valid, dr, 0)
            dc = np.where(valid, dc, 0)
            key = (valid.tobytes(), dr.tobytes(), dc.tobytes())
            if key not in tabs:
                tabs[key] = len(tabs)
                masks.append(valid.astype(np.float32))
                dridx.append(dr)
                dcidx.append(dc)
            lst.append((m, tabs[key]))
        sched.append(lst)
    c["maskB"] = np.ascontiguousarray(np.stack(masks, 1))
    c["_dr"] = np.stack(dridx, 1)
    c["_dc"] = np.stack(dcidx, 1)
    c["_schedB"] = sched
    c["_ntab"] = len(masks)
    return c


def gather_biasB(b_rel_bias_l, c):
    g = b_rel_bias_l[:, c["_dr"], c["_dc"]]
    return np.ascontiguousarray(np.transpose(g, (1, 2, 0, 3))).astype(np.float32)


def gains_pack(inp, L):
    g = np.zeros((L, 128, 64), np.float32)
    for l in range(L):
        g[l, :, 0:16] = inp["ln1"][l].reshape(16, 128).T
        g[l, :, 16:20] = inp["c_q_norm"][l].reshape(4, 128).T
        g[l, :, 20:22] = inp["c_kv_norm"][l].reshape(2, 128).T
        g[l, :, 22:38] = inp["out_norm"][l].reshape(16, 128).T
        g[l, :, 38:54] = inp["ln2"][l].reshape(16, 128).T
    return g


def sk_pack(inp, L):
    sk = np.zeros((L, 128, 256), np.float32)
    for l in range(L):
        sk[l, 0:64, 0:128] = inp["peer_sub_keys"][l, 0].T
        sk[l, 64:128, 128:256] = inp["peer_sub_keys"][l, 1].T
    return sk


def core_inputs(inp, xb, S, L, consts):
    m = dict(x=np.ascontiguousarray(xb), w_in=inp["w_in"][:L], gains=gains_pack(inp, L), ident=consts["ident"],
             cossin=consts["cossin"], rotR=consts["rotR"], tabA=consts["tabA"], maskB=consts["maskB"],
             biasB=np.stack([gather_biasB(inp["b_rel_bias"][l], consts) for l in range(L)]),
             sinkb=np.ascontiguousarray(np.broadcast_to(inp["a_sink"][:L][None], (128, L, 8))).astype(np.float32),
             c_w_uq=inp["c_w_uq"][:L], c_w_ukv=inp["c_w_ukv"][:L], w_o=inp["w_o"][:L],
             peer_w_q=inp["peer_w_q"][:L], peer_u=inp["peer_u"][:L], peer_v=inp["peer_v"][:L],
             sk=sk_pack(inp, L),
             ln2b=np.ascontiguousarray(np.broadcast_to(inp["ln2"][:L][:, None, :], (L, 128, 2048))).astype(np.float32),
             fnorm=np.ascontiguousarray(np.broadcast_to(inp["final_norm"][None], (128, 2048))).astype(np.float32))
    return m


N_CORES = 4
_CACHE = {}


def kernel(**inputs):
    inp = {k: np.asarray(v) for k, v in inputs.items()}
    B, S, _ = inp["x"].shape
    L = inp["w_in"].shape[0]
    key = (S, L)
    if key not in _CACHE:
        consts = host_consts(S)
        _CACHE[key] = (consts, build(S, L, consts, debug=False))
    consts, nc = _CACHE[key]
    in_maps = []
    for b in range(B):
        m = core_inputs(inp, inp["x"][b], S, L, consts)
        in_maps.append({k: np.ascontiguousarray(v, dtype=np.float32) for k, v in m.items()})
    res = run_bass_kernel_spmd(nc, in_maps, core_ids=list(range(B)))
    out = np.stack([np.asarray(res.results[b]["y"], dtype=np.float32) for b in range(B)], axis=0)
    return out
```

```python
import numpy as np
from concourse.bass_utils import run_bass_kernel_spmd
from contextlib import ExitStack
import concourse.bass as bass
import concourse.mybir as mybir

F32 = mybir.dt.float32
BF16 = mybir.dt.bfloat16
AF = mybir.ActivationFunctionType
ALU = mybir.AluOpType
AX = mybir.AxisListType


class Res:
    __slots__ = ("name", "w", "r", "cin", "cout")

    def __init__(self, name):
        self.name = name
        self.w = {}
        self.r = {}
        self.cin = None
        self.cout = None


class Chan:
    __slots__ = ("sid", "cnt")

    def __init__(self, sid):
        self.sid = sid
        self.cnt = 0


class Eng:
    def __init__(self, name, sid):
        self.name = name
        self.sid = sid
        self.cnt = 0
        self.waited = {}
        self.prog = []


class KB:
    def __init__(self, nc):
        self.nc = nc
        self.es = ExitStack()
        self.sems = []
        self.engs = {}
        self.chans = []
        for k in ("pe", "act", "dve", "pool", "sp"):
            self.engs[k] = Eng(k, self.newsem("e_" + k))
        self.nres = 0
        self.pes = None
        self.ntens = 0
        self.free_ch = []
        self.phase_ch = []

    def newsem(self, name):
        s = self.es.enter_context(self.nc.semaphore("%s_%d" % (name, len(self.sems))))
        self.sems.append(s)
        return len(self.sems) - 1

    def res(self, name=None):
        self.nres += 1
        return Res(name or ("r%d" % self.nres))

    def sbuf(self, name, shape, dt):
        st = self.pes if self.pes is not None else self.es
        self.ntens += 1
        return st.enter_context(self.nc.sbuf_tensor("sb%d_%s" % (self.ntens, name), list(shape), dt))

    def psum(self, name, shape, dt):
        st = self.pes if self.pes is not None else self.es
        self.ntens += 1
        return st.enter_context(self.nc.psum_tensor("ps%d_%s" % (self.ntens, name), list(shape), dt))

    def begin_phase(self):
        self.pes = ExitStack()

    def end_phase(self):
        self.barrier()
        self.emit_block()
        self.pes.close()
        self.pes = None
        self.free_ch.extend(self.phase_ch)
        self.phase_ch = []

    def totals(self):
        t = {}
        for E in self.engs.values():
            t[E.sid] = E.cnt
        for ch in self.chans:
            t[ch.sid] = ch.cnt
        return t

    def barrier(self):
        tot = self.totals()
        for E in self.engs.values():
            waits = []
            for s, v in tot.items():
                if v > 0 and E.waited.get(s, 0) < v:
                    E.waited[s] = v
                    waits.append((s, v))
            E.prog.append((waits, None, None, 0))

    def _deps(self, E, reads, writes, same_ok):
        deps = {}
        for r in reads:
            for s, v in r.w.items():
                if deps.get(s, 0) < v:
                    deps[s] = v
        for w in writes:
            for s, v in w.w.items():
                if deps.get(s, 0) < v:
                    deps[s] = v
            for s, v in w.r.items():
                if deps.get(s, 0) < v:
                    deps[s] = v
        waits = []
        for s, v in deps.items():
            if s == E.sid and same_ok:
                continue
            if E.waited.get(s, 0) >= v:
                continue
            E.waited[s] = v
            waits.append((s, v))
        return waits

    def op(self, eng, fn, reads=(), writes=()):
        E = self.engs[eng]
        waits = self._deps(E, reads, writes, same_ok=(eng == "pe"))
        E.cnt += 1
        c = E.cnt
        E.prog.append((waits, fn, E.sid, 1))
        for r in reads:
            r.r[E.sid] = c
        for w in writes:
            w.w[E.sid] = c

    def dma(self, q, out, in_, reads=(), writes=(), chan_res=None, **kw):
        E = self.engs[q]
        waits = self._deps(E, reads, writes, same_ok=False)
        cr = chan_res
        if cr.cin is None:
            if self.free_ch:
                cr.cin = self.free_ch.pop()
            else:
                cr.cin = Chan(self.newsem("c"))
                self.chans.append(cr.cin)
            if self.pes is not None:
                self.phase_ch.append(cr.cin)
        ch = cr.cin
        ch.cnt += 16
        c = ch.cnt
        E.prog.append((waits, (lambda e, out=out, in_=in_, kw=kw: e.dma_start(out=out, in_=in_, **kw)), ch.sid, 16))
        for r in reads:
            r.r[ch.sid] = c
        for w in writes:
            w.w[ch.sid] = c

    def finish_wait(self, eng, resources):
        E = self.engs[eng]
        waits = self._deps(E, resources, (), same_ok=False)
        E.prog.append((waits, None, None, 0))

    def emit(self):
        self.emit_block()
        self.es.close()

    def emit_block(self):
        nc = self.nc
        sems = self.sems
        with nc.Block() as block:
            def run(E):
                def body(e):
                    for waits, fn, sid, inc in E.prog:
                        for s, v in waits:
                            e.wait_ge(sems[s], v)
                        if fn is not None:
                            ins = fn(e)
                            ins.then_inc(sems[sid], inc)
                return body
            block.tensor(run(self.engs["pe"]))
            block.scalar(run(self.engs["act"]))
            block.vector(run(self.engs["dve"]))
            block.gpsimd(run(self.engs["pool"]))
            block.sync(run(self.engs["sp"]))
        for E in self.engs.values():
            E.prog = []


D = 2048
KC = 16
DIN = 3904
EPS = 1e-6


class Pool:
    def __init__(self, kb, name, shape, dt, n, space="sbuf"):
        self.items = []
        for i in range(n):
            t = (kb.sbuf if space == "sbuf" else kb.psum)("%s%d" % (name, i), shape, dt)
            self.items.append((t, kb.res("%s%d" % (name, i))))
        self.i = 0

    def next(self):
        it = self.items[self.i % len(self.items)]
        self.i += 1
        return it


class Ctx:
    pass


def build(S, L, consts, debug=False, upto=99):
    nc = bass.Bass("TRN2", target_bir_lowering=False)
    kb = KB(nc)
    NT = S // 128
    NB = S // 512
    dbgkind = "ExternalOutput" if debug else "Internal"

    def din(name, shape, dt=F32):
        return nc.dram_tensor(name, list(shape), dt, kind="ExternalInput").ap()

    def dscr(name, shape, dt=BF16, kind=None):
        return nc.dram_tensor(name, list(shape), dt, kind=kind or dbgkind).ap()

    x_in = din("x", [S, D])
    w_in = din("w_in", [L, D, DIN])
    gains = din("gains", [L, 128, 64])
    ident_in = din("ident", [128, 128])
    cs_in = din("cossin", [2, 64, S])
    y_out = nc.dram_tensor("y", [S, D], F32, kind="ExternalOutput").ap()

    HT = dscr("HT", [31, 128, S])
    VA = dscr("VA", [S, 2, 129])
    VB = dscr("VB", [S, 4, 129])
    r_HT = kb.res("HT"); r_VA = kb.res("VA"); r_VB = kb.res("VB")
    r_x = kb.res("x_in"); r_w_in = kb.res("w_in"); r_const = kb.res("const")

    ident32 = kb.sbuf("ident32", [128, 128], F32)
    ident = kb.sbuf("ident", [128, 128], BF16)
    gn = kb.sbuf("gn", [128, L, 64], F32)
    gneg = kb.sbuf("gneg", [128, L, 64], F32)
    epsb = kb.sbuf("epsb", [128, 1], F32)
    r_id = kb.res("ident"); r_gn = kb.res("gn")
    kb.dma("sp", ident32[:], ident_in, reads=[r_const], writes=[r_id], chan_res=r_id)
    kb.op("dve", lambda e: e.tensor_copy(out=ident[:], in_=ident32[:]), reads=[r_id], writes=[r_id])
    kb.dma("sp", gn[:], gains.rearrange("l p c -> p l c"), reads=[r_const], writes=[r_gn], chan_res=r_gn)
    kb.op("dve", lambda e: e.tensor_scalar(out=gneg[:], in0=gn[:], scalar1=-1.0, scalar2=None, op0=ALU.mult),
          reads=[r_gn], writes=[r_gn])
    kb.op("dve", lambda e: e.memset(epsb[:], EPS), writes=[r_gn])

    NTAB = consts["_ntab"]
    schedB = consts["_schedB"]
    rot_in = din("rotR", [64, 64])
    tabA_in = din("tabA", [128, 3072])
    maskB_in = din("maskB", [128, NTAB, 128])
    biasB_in = din("biasB", [L, 128, NTAB, 4, 128])
    sink_in = din("sinkb", [128, L, 8])
    w_uq = din("c_w_uq", [L, 512, 768])
    w_ukv = din("c_w_ukv", [L, 256, 1024])
    w_o = din("w_o", [L, D, D])
    fnorm_in = din("fnorm", [128, D])
    w_pq = din("peer_w_q", [L, D, 1024])
    sk_in = din("sk", [L, 128, 256])
    peer_u = din("peer_u", [L, 16384, D])
    peer_v = din("peer_v", [L, 16384, D])
    ln2b_in = din("ln2b", [L, 128, D])
    TBT_MAX = 8
    r_win2 = kb.res("win2")

    QCN = dscr("QCN", [4, 128, S]); QCP = dscr("QCP", [4, 64, S]); KCN = dscr("KCN", [4, 128, S]); KCP = dscr("KCP", [64, S])
    VC = dscr("VC", [S, 4, 129])
    O = dscr("O", [S, D], F32)
    XA = dscr("XA", [S, D], F32); XB = dscr("XB", [S, D], F32)
    XN2T = dscr("XN2T", [KC, 128, S])
    QT = dscr("QT", [8, 128, S])
    GT = dscr("GT", [NT, 128, 128, 128])
    r_QT = kb.res("QT"); r_GT = kb.res("GT"); r_y = kb.res("y")
    r_QCN = kb.res("QCN"); r_QCP = kb.res("QCP"); r_KCN = kb.res("KCN"); r_KCP = kb.res("KCP"); r_VC = kb.res("VC")
    r_O = kb.res("O"); r_XA = kb.res("XA"); r_XB = kb.res("XB"); r_XN2T = kb.res("XN2T")

    rot32 = kb.sbuf("rot32", [64, 64], F32)
    rotb = kb.sbuf("rotb", [64, 64], BF16)
    onesb = kb.sbuf("onesb", [128, 128], BF16)
    esink = kb.sbuf("esink", [128, L, 8], F32)
    kb.dma("sp", rot32[:], rot_in, reads=[r_const], writes=[r_id], chan_res=r_id)
    kb.op("dve", lambda e: e.tensor_copy(out=rotb[:], in_=rot32[:]), reads=[r_id], writes=[r_id])
    kb.op("dve", lambda e: e.memset(onesb[:], 1.0), writes=[r_id])
    kb.dma("sp", esink[:], sink_in, reads=[r_const], writes=[r_gn], chan_res=r_gn)
    kb.op("act", lambda e: e.activation(out=esink[:], in_=esink[:], func=AF.Exp), reads=[r_gn], writes=[r_gn])

    SC_A = 128 ** -0.5
    SC_C = 192 ** -0.5

    def mk_norm_helpers(st_pool, sq_pool, ps_tp):
        def rmsnorm_tile(xt, r_xt, groups, xn, r_xn, gain=None):
            stt, r_st = st_pool.next()
            sq, r_sq = sq_pool.next()
            ng = len(groups)
            for g, (s0, wd) in enumerate(groups):
                kb.op("act", lambda e, s0=s0, wd=wd, g=g: e.activation(out=sq[:, s0:s0 + wd], in_=xt[:, s0:s0 + wd], func=AF.Square,
                                                                       accum_out=stt[:, g:g + 1]),
                      reads=[r_xt], writes=[r_sq, r_st])
            for g, (s0, wd) in enumerate(groups):
                kb.op("act", lambda e, g=g, wd=wd: e.activation(out=stt[:, g:g + 1], in_=stt[:, g:g + 1], func=AF.Ln,
                                                                bias=epsb[:, 0:1], scale=1.0 / wd),
                      reads=[r_st, r_gn], writes=[r_st])
            kb.op("act", lambda e: e.activation(out=stt[:, 0:ng], in_=stt[:, 0:ng], func=AF.Exp, scale=-0.5),
                  reads=[r_st], writes=[r_st])
            for g, (s0, wd) in enumerate(groups):
                if gain is None:
                    kb.op("dve", lambda e, s0=s0, wd=wd, g=g: e.tensor_scalar(out=xn[:, s0:s0 + wd], in0=xt[:, s0:s0 + wd],
                                                                              scalar1=stt[:, g:g + 1], scalar2=None, op0=ALU.mult),
                          reads=[r_xt, r_st], writes=[r_xn])
                else:
                    gt_, r_gt_ = gain
                    kb.op("dve", lambda e, s0=s0, wd=wd, g=g: e.scalar_tensor_tensor(out=xn[:, s0:s0 + wd], in0=xt[:, s0:s0 + wd],
                                                                                     scalar=stt[:, g:g + 1], in1=gt_[:, s0:s0 + wd],
                                                                                     op0=ALU.mult, op1=ALU.mult),
                          reads=[r_xt, r_st, r_gt_], writes=[r_xn])

        def transpose_tile(xn, r_xn, dst, r_dst, j):
            pt, r_pt = ps_tp.next()
            for k in range(KC):
                kb.op("pe", lambda e, k=k: e.transpose(out=pt[:, k, :], in_=xn[:, k * 128:(k + 1) * 128], identity=ident[:]),
                      reads=[r_xn, r_id], writes=[r_pt])
            kb.op("act", lambda e: e.copy(out=dst[:, :, j * 128:(j + 1) * 128], in_=pt[:]), reads=[r_pt], writes=[r_dst])
        return rmsnorm_tile, transpose_tile

    def load_w(wst_pool, src_ap, gcol, ncols, l, dst, r_dst, r_src):
        nk = src_ap.shape[0] // 128
        for k in range(nk):
            st, r_st = wst_pool.next()
            kb.dma("sp", st[:, :ncols], src_ap[k * 128:(k + 1) * 128, :], reads=[r_src], writes=[r_st], chan_res=r_st)
            if gcol is None:
                kb.op("act", lambda e, k=k, st=st: e.copy(out=dst[:, k, :ncols], in_=st[:, :ncols]), reads=[r_st], writes=[r_dst])
            else:
                kb.op("act", lambda e, k=k, st=st: e.activation(out=dst[:, k, :ncols], in_=st[:, :ncols], func=AF.Copy,
                                                                scale=gn[:, l, gcol + k:gcol + k + 1]),
                      reads=[r_st, r_gn], writes=[r_dst])

    def phase1(l, x_src, r_xsrc):
        kb.begin_phase()
        xt_pool = Pool(kb, "xt", [128, D], F32, 2)
        xn_pool = Pool(kb, "xn", [128, D], BF16, 2)
        sq_pool = Pool(kb, "sq", [128, D], BF16, 1)
        st_pool = Pool(kb, "stat", [128, 4], F32, 4)
        xnT_pool = Pool(kb, "xnT", [128, KC, 512], BF16, 2)
        ps_tp = Pool(kb, "tp", [128, KC, 128], BF16, 2, space="psum")
        ps_mm = Pool(kb, "mm", [128, 512], F32, 4, space="psum")
        wst_pool = Pool(kb, "wst", [128, 2048], F32, 2)
        wbf = kb.sbuf("wbf", [128, KC, 2048], BF16)
        r_wbf = kb.res("wbf")
        ev_pool = Pool(kb, "ev", [128, 512], BF16, 4)
        vst_pool = Pool(kb, "vst", [128, 4, 129], BF16, 2)
        for t, r in vst_pool.items:
            kb.op("pool", lambda e, t=t: e.memset(t[:], 1.0), writes=[r])
        rmsnorm_tile, transpose_tile = mk_norm_helpers(st_pool, sq_pool, ps_tp)
        passes = [
            dict(c0=0, fm=[(c, c * 128, 128) for c in range(0, 10)] + [(c, c * 128, 128) for c in range(12, 16)],
                 tm=[(VA, r_VA, 1280, 2)]),
            dict(c0=2048, fm=[(c, c * 128 - 2048, 128) for c in range(16, 20)] + [(c, c * 128 - 2048, 128) for c in range(24, 30)]
                 + [(30, 30 * 128 - 2048, 64)],
                 tm=[(VB, r_VB, 2560 - 2048, 4)]),
        ]
        for ps in passes:
            c0 = ps["c0"]
            ncols = min(2048, DIN - c0)
            load_w(wst_pool, w_in[l, :, c0:c0 + ncols], 0, ncols, l, wbf, r_wbf, r_w_in)
            for b in range(NB):
                xnT, r_xnT = xnT_pool.next()
                for j in range(4):
                    tok = b * 512 + j * 128
                    xt, r_xt = xt_pool.next()
                    kb.dma("sp", xt[:], x_src[tok:tok + 128, :], reads=[r_xsrc], writes=[r_xt], chan_res=r_xt)
                    xn, r_xn = xn_pool.next()
                    rmsnorm_tile(xt, r_xt, [(0, D)], xn, r_xn)
                    transpose_tile(xn, r_xn, xnT, r_xnT, j)
                for (c, off, m) in ps["fm"]:
                    pm, r_pm = ps_mm.next()
                    for k in range(KC):
                        kb.op("pe", lambda e, k=k, off=off, m=m, pm=pm, xnT=xnT: e.matmul(
                            pm[0:m, :], lhsT=wbf[:, k, off:off + m], rhs=xnT[:, k, :], start=(k == 0), stop=(k == KC - 1)),
                            reads=[r_wbf, r_xnT], writes=[r_pm])
                    ev, r_ev = ev_pool.next()
                    kb.op("dve", lambda e, m=m, pm=pm, ev=ev: e.tensor_copy(out=ev[0:m, :], in_=pm[0:m, :]),
                          reads=[r_pm], writes=[r_ev])
                    kb.dma("pool", HT[c, 0:m, b * 512:(b + 1) * 512], ev[0:m, :], reads=[r_ev], writes=[r_HT], chan_res=r_ev)
                for (dst, r_dstd, off, nh) in ps["tm"]:
                    for j in range(4):
                        tok = b * 512 + j * 128
                        pm, r_pm = ps_mm.next()
                        for k in range(KC):
                            kb.op("pe", lambda e, k=k, off=off, nh=nh, pm=pm, xnT=xnT, j=j: e.matmul(
                                pm[:, 0:nh * 128], lhsT=xnT[:, k, j * 128:(j + 1) * 128], rhs=wbf[:, k, off:off + nh * 128],
                                start=(k == 0), stop=(k == KC - 1)),
                                reads=[r_wbf, r_xnT], writes=[r_pm])
                        vs, r_vs = vst_pool.next()
                        kb.op("act", lambda e, nh=nh, pm=pm, vs=vs: e.copy(
                            out=vs[:, 0:nh, 0:128], in_=pm[:, 0:nh * 128].rearrange("p (h d) -> p h d", d=128)),
                            reads=[r_pm], writes=[r_vs])
                        kb.dma("pool", dst[tok:tok + 128, :, :], vs[:, 0:nh, :], reads=[r_vs], writes=[r_dstd], chan_res=r_vs)
        kb.end_phase()

    def phase1c(l):
        kb.begin_phase()
        wst_pool = Pool(kb, "wst", [128, 1024], F32, 2)
        wuq = kb.sbuf("wuq", [128, 4, 768], BF16)
        wukv = kb.sbuf("wukv", [128, 2, 1024], BF16)
        r_wu = kb.res("wu")
        load_w(wst_pool, w_uq[l], 16, 768, l, wuq, r_wu, r_win2)
        load_w(wst_pool, w_ukv[l], 20, 1024, l, wukv, r_wu, r_win2)
        cin_pool = Pool(kb, "cin", [128, 6, 512], BF16, 2)
        krin_pool = Pool(kb, "krin", [64, 512], BF16, 2)
        cs_pool = Pool(kb, "cs", [64, 2, 512], F32, 2)
        sqc_pool = Pool(kb, "sqc", [128, 6, 512], BF16, 1)
        rstd_pool = Pool(kb, "rstd", [128, 2, 512], F32, 1)
        cn_pool = Pool(kb, "cn", [128, 6, 512], BF16, 2)
        ps_mm = Pool(kb, "mm", [128, 512], F32, 6, space="psum")
        ev_pool = Pool(kb, "ev", [128, 512], BF16, 4)
        qpe_pool = Pool(kb, "qpe", [64, 512], BF16, 2)
        t1_pool = Pool(kb, "t1", [64, 512], F32, 2)
        t2_pool = Pool(kb, "t2", [64, 512], F32, 2)
        vst_pool = Pool(kb, "vst", [128, 4, 129], BF16, 2)
        for t, r in vst_pool.items:
            kb.op("pool", lambda e, t=t: e.memset(t[:], 1.0), writes=[r])

        def rope_store(src, r_src, cs, r_cs, dst_ap, r_dst):
            pr, r_pr = ps_mm.next()
            kb.op("pe", lambda e: e.matmul(pr[0:64, :], lhsT=rotb[:], rhs=src[:], start=True, stop=True),
                  reads=[r_src, r_id], writes=[r_pr])
            t1, r_t1 = t1_pool.next()
            t2, r_t2 = t2_pool.next()
            kb.op("dve", lambda e: e.tensor_tensor(out=t1[:], in0=src[:], in1=cs[:, 0, :], op=ALU.mult),
                  reads=[r_src, r_cs], writes=[r_t1])
            kb.op("dve", lambda e: e.tensor_tensor(out=t2[:], in0=pr[0:64, :], in1=cs[:, 1, :], op=ALU.mult),
                  reads=[r_pr, r_cs], writes=[r_t2])
            ev, r_ev = ev_pool.next()
            kb.op("dve", lambda e: e.tensor_tensor(out=ev[0:64, :], in0=t1[:], in1=t2[:], op=ALU.add),
                  reads=[r_t1, r_t2], writes=[r_ev])
            kb.dma("pool", dst_ap, ev[0:64, :], reads=[r_ev], writes=[r_dst], chan_res=r_ev)

        for b in range(NB):
            bs = slice(b * 512, (b + 1) * 512)
            cin, r_cin = cin_pool.next()
            kb.dma("sp", cin[:], HT[24:30, :, bs].rearrange("c p t -> p c t"), reads=[r_HT], writes=[r_cin], chan_res=r_cin)
            krin, r_krin = krin_pool.next()
            kb.dma("sp", krin[:], HT[30, 0:64, bs], reads=[r_HT], writes=[r_krin], chan_res=r_krin)
            cs, r_cs = cs_pool.next()
            kb.dma("sp", cs[:], cs_in[:, :, bs].rearrange("c p t -> p c t"), reads=[r_const], writes=[r_cs], chan_res=r_cs)
            sqc, r_sqc = sqc_pool.next()
            kb.op("act", lambda e, sqc=sqc, cin=cin: e.activation(out=sqc[:], in_=cin[:], func=AF.Square), reads=[r_cin], writes=[r_sqc])
            rstd, r_rstd = rstd_pool.next()
            cn, r_cn = cn_pool.next()
            for gi, (c0, nchunk) in enumerate([(0, 4), (4, 2)]):
                pss, r_pss = ps_mm.next()
                for k in range(nchunk):
                    kb.op("pe", lambda e, k=k, c0=c0, nchunk=nchunk, pss=pss, sqc=sqc: e.matmul(
                        pss[:], lhsT=onesb[:], rhs=sqc[:, c0 + k, :], start=(k == 0), stop=(k == nchunk - 1)),
                        reads=[r_sqc, r_id], writes=[r_pss])
                kb.op("act", lambda e, gi=gi, nchunk=nchunk, pss=pss, rstd=rstd: e.activation(
                    out=rstd[:, gi, :], in_=pss[:], func=AF.Ln, bias=epsb[:, 0:1], scale=1.0 / (nchunk * 128)),
                    reads=[r_pss, r_gn], writes=[r_rstd])
                kb.op("act", lambda e, gi=gi, rstd=rstd: e.activation(out=rstd[:, gi, :], in_=rstd[:, gi, :], func=AF.Exp, scale=-0.5),
                      reads=[r_rstd], writes=[r_rstd])
                for k in range(nchunk):
                    kb.op("dve", lambda e, k=k, c0=c0, gi=gi, cn=cn, cin=cin, rstd=rstd: e.tensor_tensor(
                        out=cn[:, c0 + k, :], in0=cin[:, c0 + k, :], in1=rstd[:, gi, :], op=ALU.mult),
                        reads=[r_cin, r_rstd], writes=[r_cn])
            for h in range(4):
                pm, r_pm = ps_mm.next()
                for k in range(4):
                    kb.op("pe", lambda e, k=k, h=h, pm=pm, cn=cn: e.matmul(
                        pm[:], lhsT=wuq[:, k, h * 192:h * 192 + 128], rhs=cn[:, k, :], start=(k == 0), stop=(k == 3)),
                        reads=[r_wu, r_cn], writes=[r_pm])
                ev, r_ev = ev_pool.next()
                kb.op("act", lambda e, pm=pm, ev=ev: e.copy(out=ev[:], in_=pm[:]), reads=[r_pm], writes=[r_ev])
                kb.dma("pool", QCN[h, :, bs], ev[:], reads=[r_ev], writes=[r_QCN], chan_res=r_ev)
                pm, r_pm = ps_mm.next()
                for k in range(4):
                    kb.op("pe", lambda e, k=k, h=h, pm=pm, cn=cn: e.matmul(
                        pm[0:64, :], lhsT=wuq[:, k, h * 192 + 128:h * 192 + 192], rhs=cn[:, k, :], start=(k == 0), stop=(k == 3)),
                        reads=[r_wu, r_cn], writes=[r_pm])
                qpe, r_qpe = qpe_pool.next()
                kb.op("act", lambda e, pm=pm, qpe=qpe: e.copy(out=qpe[:], in_=pm[0:64, :]), reads=[r_pm], writes=[r_qpe])
                rope_store(qpe, r_qpe, cs, r_cs, QCP[h, :, bs], r_QCP)
                pm, r_pm = ps_mm.next()
                for k in range(2):
                    kb.op("pe", lambda e, k=k, h=h, pm=pm, cn=cn: e.matmul(
                        pm[:], lhsT=wukv[:, k, h * 256:h * 256 + 128], rhs=cn[:, 4 + k, :], start=(k == 0), stop=(k == 1)),
                        reads=[r_wu, r_cn], writes=[r_pm])
                ev, r_ev = ev_pool.next()
                kb.op("act", lambda e, pm=pm, ev=ev: e.copy(out=ev[:], in_=pm[:]), reads=[r_pm], writes=[r_ev])
                kb.dma("pool", KCN[h, :, bs], ev[:], reads=[r_ev], writes=[r_KCN], chan_res=r_ev)
            for j in range(4):
                tok = b * 512 + j * 128
                pm, r_pm = ps_mm.next()
                for k in range(2):
                    kb.op("pe", lambda e, k=k, j=j, pm=pm, cn=cn: e.matmul(
                        pm[:].rearrange("p (h d) -> p h d", d=128), lhsT=cn[:, 4 + k, j * 128:(j + 1) * 128],
                        rhs=wukv[:, k, :].rearrange("p (h c) -> p h c", c=256)[:, :, 128:256], start=(k == 0), stop=(k == 1)),
                        reads=[r_wu, r_cn], writes=[r_pm])
                vs, r_vs = vst_pool.next()
                kb.op("act", lambda e, pm=pm, vs=vs: e.copy(out=vs[:, :, 0:128], in_=pm[:].rearrange("p (h d) -> p h d", d=128)),
                      reads=[r_pm], writes=[r_vs])
                kb.dma("pool", VC[tok:tok + 128, :, :], vs[:], reads=[r_vs], writes=[r_VC], chan_res=r_vs)
            rope_store(krin, r_krin, cs, r_cs, KCP[:, bs], r_KCP)
        kb.end_phase()

    def attn_block(P, kts, s_mm, tab, v_rhs, scale, finals):
        nk = len(kts)
        pss = {}
        AHEAD = 2
        for a in range(min(AHEAD, nk)):
            pss[a] = P.ps_s.next()
            s_mm(kts[a], pss[a][0], pss[a][1])
        for i, kt in enumerate(kts):
            ps, r_ps = pss.pop(i)
            if i + AHEAD < nk:
                pss[i + AHEAD] = P.ps_s.next()
                s_mm(kts[i + AHEAD], pss[i + AHEAD][0], pss[i + AHEAD][1])
            pt, r_pt = P.pt_pool.next()
            tb = tab(kt)
            if tb is None:
                kb.op("act", lambda e, ps=ps, pt=pt: e.activation(out=pt[:], in_=ps[:], func=AF.Exp, scale=scale),
                      reads=[r_ps], writes=[r_pt])
            else:
                ex, r_ex = P.ex_pool.next()
                kb.op("act", lambda e, ps=ps, ex=ex: e.activation(out=ex[:], in_=ps[:], func=AF.Exp, scale=scale),
                      reads=[r_ps], writes=[r_ex])
                kb.op("dve", lambda e, ex=ex, pt=pt, tb=tb: e.tensor_tensor(out=pt[:], in0=ex[:], in1=tb, op=ALU.mult),
                      reads=[r_ex, P.r_tab], writes=[r_pt])
            for g in range(4):
                po, r_po = P.ps_o[g]
                kb.op("pe", lambda e, g=g, po=po, pt=pt, kt=kt, i=i: e.matmul(
                    po[:, 0:129], lhsT=pt[:, g * 128:(g + 1) * 128], rhs=v_rhs(kt, g), start=(i == 0), stop=(i == nk - 1)),
                    reads=[r_pt, P.r_v], writes=[r_po])
        for (g, out_ap, r_out, extra) in finals:
            po, r_po = P.ps_o[g]
            dn, r_dn = P.den_pool.next()
            if extra is not None:
                kb.op("dve", lambda e, po=po, dn=dn, extra=extra: e.tensor_scalar(out=dn[:, 0:1], in0=po[:, 128:129], scalar1=extra,
                                                                                 scalar2=None, op0=ALU.add),
                      reads=[r_po, r_gn], writes=[r_dn])
                kb.op("dve", lambda e, dn=dn: e.reciprocal(out=dn[:, 1:2], in_=dn[:, 0:1]), reads=[r_dn], writes=[r_dn])
            else:
                kb.op("dve", lambda e, po=po, dn=dn: e.reciprocal(out=dn[:, 1:2], in_=po[:, 128:129]), reads=[r_po], writes=[r_dn])
            kb.op("dve", lambda e, po=po, dn=dn, out_ap=out_ap: e.tensor_scalar(out=out_ap, in0=po[:, 0:128], scalar1=dn[:, 1:2],
                                                                               scalar2=None, op0=ALU.mult),
                  reads=[r_po, r_dn], writes=[r_out])

    def attn_pools():
        P = Ctx()
        P.ps_s = Pool(kb, "pss", [128, 512], F32, 4, space="psum")
        P.ps_o = [(kb.psum("pso%d" % g, [128, 512], F32), kb.res("pso%d" % g)) for g in range(4)]
        P.pt_pool = Pool(kb, "pt", [128, 512], BF16, 4)
        P.ex_pool = Pool(kb, "ex", [128, 512], BF16, 3)
        P.den_pool = Pool(kb, "den", [128, 2], F32, 4)
        P.r_tab = kb.res("tab")
        P.r_v = kb.res("vres")
        return P

    def phase2a(l):
        kb.begin_phase()
        P = attn_pools()
        QA = kb.sbuf("QA", [128, 8, S], BF16)
        KA = kb.sbuf("KA", [128, 2, S], BF16)
        VAs = kb.sbuf("VAs", [128, NT, 2, 129], BF16)
        tab32 = kb.sbuf("tab32", [128, 3072], F32)
        tabA = kb.sbuf("tabAb", [128, 3, 8, 128], BF16)
        r_q = kb.res("QAres")
        kb.dma("sp", QA[:], HT[0:8, :, :].rearrange("c p t -> p c t"), reads=[r_HT], writes=[r_q], chan_res=r_q)
        kb.dma("sp", KA[:], HT[8:10, :, :].rearrange("c p t -> p c t"), reads=[r_HT], writes=[P.r_v], chan_res=P.r_v)
        kb.dma("sp", VAs[:], VA.rearrange("(n p) g d -> p n g d", p=128), reads=[r_VA], writes=[P.r_v], chan_res=P.r_v)
        kb.dma("sp", tab32[:], tabA_in, reads=[r_const], writes=[P.r_tab], chan_res=P.r_tab)
        kb.op("dve", lambda e: e.tensor_copy(out=tabA[:].rearrange("p a h q -> p (a h q)"), in_=tab32[:]), reads=[P.r_tab], writes=[P.r_tab])
        o_pool = Pool(kb, "ost", [128, 1024], F32, 2)
        for n in range(NT):
            ost, r_ost = o_pool.next()
            for kg in range(2):
                kts = [m for m in (n - 1, n, n + 1) if 0 <= m < NT]

                def s_mm(kt, ps, r_ps, kg=kg, n=n):
                    kb.op("pe", lambda e: e.matmul(ps[:].rearrange("p (h q) -> p h q", q=128), lhsT=KA[:, kg, kt * 128:(kt + 1) * 128],
                                                   rhs=QA[:, 4 * kg:4 * kg + 4, n * 128:(n + 1) * 128], start=True, stop=True),
                          reads=[P.r_v, r_q], writes=[r_ps])
                attn_block(P, kts, s_mm,
                           lambda kt, kg=kg, n=n: tabA[:, kt - n + 1, 4 * kg:4 * kg + 4, :].rearrange("p h q -> p (h q)"),
                           lambda kt, g, kg=kg: VAs[:, kt, kg, :], SC_A,
                           [(g, ost[:, (4 * kg + g) * 128:(4 * kg + g + 1) * 128], r_ost, esink[:, l, 4 * kg + g:4 * kg + g + 1]) for g in range(4)])
            kb.dma("pool", O[n * 128:(n + 1) * 128, 0:1024], ost[:], reads=[r_ost], writes=[r_O], chan_res=r_ost)
        kb.end_phase()

    def phase2b(l):
        kb.begin_phase()
        P = attn_pools()
        QB = kb.sbuf("QB", [128, 4, S], BF16)
        KBs = kb.sbuf("KBs", [128, 4, S], BF16)
        VBs = kb.sbuf("VBs", [128, NT, 4, 129], BF16)
        tabB = kb.sbuf("tabB", [128, NTAB, 4, 128], BF16)
        mk32 = kb.sbuf("mk32", [128, NTAB, 128], F32)
        bst_pool = Pool(kb, "bst", [128, 4, 128], F32, 2)
        r_q = kb.res("QBres")
        kb.dma("sp", QB[:], HT[12:16, :, :].rearrange("c p t -> p c t"), reads=[r_HT], writes=[r_q], chan_res=r_q)
        kb.dma("sp", KBs[:], HT[16:20, :, :].rearrange("c p t -> p c t"), reads=[r_HT], writes=[P.r_v], chan_res=P.r_v)
        kb.dma("sp", VBs[:], VB.rearrange("(n p) g d -> p n g d", p=128), reads=[r_VB], writes=[P.r_v], chan_res=P.r_v)
        kb.dma("sp", mk32[:], maskB_in, reads=[r_const], writes=[P.r_tab], chan_res=P.r_tab)
        for ti in range(NTAB):
            bst, r_bst = bst_pool.next()
            kb.dma("sp", bst[:], biasB_in[l, :, ti, :, :], reads=[r_const], writes=[r_bst], chan_res=r_bst)
            kb.op("act", lambda e, bst=bst: e.activation(out=bst[:], in_=bst[:], func=AF.Exp), reads=[r_bst], writes=[r_bst])
            for h in range(4):
                kb.op("dve", lambda e, bst=bst, ti=ti, h=h: e.tensor_tensor(out=tabB[:, ti, h, :], in0=bst[:, h, :], in1=mk32[:, ti, :], op=ALU.mult),
                      reads=[r_bst, P.r_tab], writes=[P.r_tab])
        o_pool = Pool(kb, "ost", [128, 512], F32, 2)
        for n in range(NT):
            ost, r_ost = o_pool.next()
            lst = schedB[n]
            tmap = dict(lst)

            def s_mm(kt, ps, r_ps, n=n):
                for h in range(4):
                    kb.op("pe", lambda e, h=h: e.matmul(ps[:, h * 128:(h + 1) * 128], lhsT=KBs[:, h, kt * 128:(kt + 1) * 128],
                                                        rhs=QB[:, h, n * 128:(n + 1) * 128], start=True, stop=True),
                          reads=[P.r_v, r_q], writes=[r_ps])
            attn_block(P, [m for m, _ in lst], s_mm,
                       lambda kt, tmap=tmap: tabB[:, tmap[kt], :, :].rearrange("p h q -> p (h q)"),
                       lambda kt, g: VBs[:, kt, g, :], SC_A,
                       [(g, ost[:, g * 128:(g + 1) * 128], r_ost, None) for g in range(4)])
            kb.dma("pool", O[n * 128:(n + 1) * 128, 1024:1536], ost[:], reads=[r_ost], writes=[r_O], chan_res=r_ost)
        kb.end_phase()

    def phase2c(l):
        kb.begin_phase()
        P = attn_pools()
        KN = kb.sbuf("KN", [128, 4, S], BF16)
        KP = kb.sbuf("KP", [64, S], BF16)
        VCs = kb.sbuf("VCs", [128, NT, 4, 129], BF16)
        kb.dma("sp", KN[:], KCN.rearrange("c p t -> p c t"), reads=[r_KCN], writes=[P.r_v], chan_res=P.r_v)
        kb.dma("sp", KP[:], KCP, reads=[r_KCP], writes=[P.r_v], chan_res=P.r_v)
        kb.dma("sp", VCs[:], VC.rearrange("(n p) g d -> p n g d", p=128), reads=[r_VC], writes=[P.r_v], chan_res=P.r_v)
        qn_pool = Pool(kb, "qn", [128, 4, 512], BF16, 2)
        qp_pool = Pool(kb, "qp", [64, 4, 512], BF16, 2)
        o_pool = Pool(kb, "ost", [128, 4, 512], F32, 2)
        for b in range(NB):
            bs = slice(b * 512, (b + 1) * 512)
            qn, r_qn = qn_pool.next()
            qp, r_qp = qp_pool.next()
            kb.dma("sp", qn[:], QCN[:, :, bs].rearrange("c p t -> p c t"), reads=[r_QCN], writes=[r_qn], chan_res=r_qn)
            kb.dma("sp", qp[:], QCP[:, :, bs].rearrange("c p t -> p c t"), reads=[r_QCP], writes=[r_qp], chan_res=r_qp)
            ost, r_ost = o_pool.next()
            for h in range(4):
                def s_mm(kt, ps, r_ps, h=h, qn=qn, qp=qp, r_qn=r_qn, r_qp=r_qp):
                    kb.op("pe", lambda e: e.matmul(ps[:], lhsT=KN[:, h, kt * 128:(kt + 1) * 128], rhs=qn[:, h, :], start=True, stop=False),
                          reads=[P.r_v, r_qn], writes=[r_ps])
                    kb.op("pe", lambda e: e.matmul(ps[:], lhsT=KP[:, kt * 128:(kt + 1) * 128], rhs=qp[:, h, :], start=False, stop=True),
                          reads=[P.r_v, r_qp], writes=[r_ps])
                attn_block(P, list(range(NT)), s_mm, lambda kt: None, lambda kt, g, h=h: VCs[:, kt, h, :], SC_C,
                           [(g, ost[:, g, h * 128:(h + 1) * 128], r_ost, None) for g in range(4)])
            kb.dma("pool", O[bs, 1536:2048].rearrange("(g p) d -> p g d", p=128), ost[:], reads=[r_ost], writes=[r_O], chan_res=r_ost)
        kb.end_phase()

    def phase3(l, x_src, r_xsrc, X1, r_X1):
        kb.begin_phase()
        wst_pool = Pool(kb, "wst", [128, 2048], F32, 2)
        wo = kb.sbuf("wo", [128, KC, D], BF16)
        r_wo = kb.res("wo")
        load_w(wst_pool, w_o[l], 22, D, l, wo, r_wo, r_win2)
        g2 = kb.sbuf("g2", [128, D], F32); r_g2 = kb.res("g2")
        kb.dma("sp", g2[:], ln2b_in[l], reads=[r_const], writes=[r_g2], chan_res=r_g2)
        ot_pool = Pool(kb, "ot", [128, D], F32, 2)
        xt_pool = Pool(kb, "xt", [128, D], F32, 2)
        x1_pool = Pool(kb, "x1", [128, D], F32, 2)
        xn_pool = Pool(kb, "xn", [128, D], BF16, 2)
        sq_pool = Pool(kb, "sq", [128, D], BF16, 1)
        st_pool = Pool(kb, "stat", [128, 4], F32, 4)
        oT_pool = Pool(kb, "oT", [128, KC, 128], BF16, 2)
        xn2T_pool = Pool(kb, "xn2T", [128, KC, 512], BF16, 1)
        ps_tp = Pool(kb, "tp", [128, KC, 128], BF16, 2, space="psum")
        ps_mm = Pool(kb, "mm", [128, 512], F32, 4, space="psum")
        rmsnorm_tile, transpose_tile = mk_norm_helpers(st_pool, sq_pool, ps_tp)
        for b in range(NB):
            xn2T, r_xn2T = xn2T_pool.next()
            for j in range(4):
                tok = b * 512 + j * 128
                ot, r_ot = ot_pool.next()
                kb.dma("sp", ot[:], O[tok:tok + 128, :], reads=[r_O], writes=[r_ot], chan_res=r_ot)
                xt, r_xt = xt_pool.next()
                kb.dma("sp", xt[:], x_src[tok:tok + 128, :], reads=[r_xsrc], writes=[r_xt], chan_res=r_xt)
                on, r_on = xn_pool.next()
                rmsnorm_tile(ot, r_ot, [(0, 1024), (1024, 512), (1536, 512)], on, r_on)
                oT, r_oT = oT_pool.next()
                transpose_tile(on, r_on, oT, r_oT, 0)
                x1, r_x1 = x1_pool.next()
                for nb in range(4):
                    pm, r_pm = ps_mm.next()
                    for k in range(KC):
                        kb.op("pe", lambda e, k=k, nb=nb, pm=pm, oT=oT: e.matmul(
                            pm[:], lhsT=oT[:, k, :], rhs=wo[:, k, nb * 512:(nb + 1) * 512], start=(k == 0), stop=(k == KC - 1)),
                            reads=[r_wo, r_oT], writes=[r_pm])
                    kb.op("dve", lambda e, nb=nb, pm=pm, x1=x1, xt=xt: e.tensor_tensor(
                        out=x1[:, nb * 512:(nb + 1) * 512], in0=pm[:], in1=xt[:, nb * 512:(nb + 1) * 512], op=ALU.add),
                        reads=[r_pm, r_xt], writes=[r_x1])
                kb.dma("pool", X1[tok:tok + 128, :], x1[:], reads=[r_x1], writes=[r_X1], chan_res=r_x1)
                xn, r_xn = xn_pool.next()
                rmsnorm_tile(x1, r_x1, [(0, D)], xn, r_xn, gain=(g2, r_g2))
                transpose_tile(xn, r_xn, xn2T, r_xn2T, j)
            kb.dma("pool", XN2T[:, :, b * 512:(b + 1) * 512].rearrange("c p t -> p c t"), xn2T[:], reads=[r_xn2T], writes=[r_XN2T],
                   chan_res=r_xn2T)
        kb.end_phase()

    def phase4a(l):
        kb.begin_phase()
        wst_pool = Pool(kb, "wst", [128, 1024], F32, 2)
        wq = kb.sbuf("wq", [128, KC, 1024], BF16)
        r_wq = kb.res("wq")
        load_w(wst_pool, w_pq[l], None, 1024, l, wq, r_wq, r_win2)
        xb_pool = Pool(kb, "xb", [128, KC, 512], BF16, 2)
        ps_mm = Pool(kb, "mm", [128, 512], F32, 4, space="psum")
        ev_pool = Pool(kb, "ev", [128, 512], BF16, 4)
        for b in range(NB):
            bs = slice(b * 512, (b + 1) * 512)
            xb, r_xb = xb_pool.next()
            kb.dma("sp", xb[:], XN2T[:, :, bs].rearrange("c p t -> p c t"), reads=[r_XN2T], writes=[r_xb], chan_res=r_xb)
            for h in range(8):
                pm, r_pm = ps_mm.next()
                for k in range(KC):
                    kb.op("pe", lambda e, k=k, h=h, pm=pm, xb=xb: e.matmul(
                        pm[:], lhsT=wq[:, k, h * 128:(h + 1) * 128], rhs=xb[:, k, :], start=(k == 0), stop=(k == KC - 1)),
                        reads=[r_wq, r_xb], writes=[r_pm])
                ev, r_ev = ev_pool.next()
                kb.op("act", lambda e, pm=pm, ev=ev: e.copy(out=ev[:], in_=pm[:]), reads=[r_pm], writes=[r_ev])
                kb.dma("pool", QT[h, :, bs], ev[:], reads=[r_ev], writes=[r_QT], chan_res=r_ev)
        kb.end_phase()

    DELTA = 1e-5

    def phase4b(l):
        kb.begin_phase()
        sk32 = kb.sbuf("sk32", [128, 256], F32)
        skb = kb.sbuf("skb", [128, 256], BF16)
        r_sk = kb.res("sk")
        kb.dma("sp", sk32[:], sk_in[l], reads=[r_const], writes=[r_sk], chan_res=r_sk)
        kb.op("dve", lambda e: e.tensor_copy(out=skb[:], in_=sk32[:]), reads=[r_sk], writes=[r_sk])
        q_pool = Pool(kb, "qt", [128, 8, 128], BF16, 2)
        s_pool = Pool(kb, "s", [128, 8, 2, 128], F32, 2)
        top_pool = Pool(kb, "top", [128, 8, 2, 16], F32, 2)
        tmp = kb.sbuf("tmpm", [128, 16, 128], F32)
        r_tmpc = [kb.res("tmpc%d" % i) for i in range(16)]
        cand = kb.sbuf("cand", [128, 8, 256], F32); r_cand = kb.res("cand")
        best = kb.sbuf("best", [128, 8, 16], F32); r_best = kb.res("best")
        dd = kb.sbuf("dd", [128, 8, 16], F32)
        zz = kb.sbuf("zz", [128, 4, 8], F32)
        cc = kb.sbuf("cc", [128, 8, 16], F32)
        cb = kb.sbuf("cb", [128, 8, 16], BF16)
        thr = kb.sbuf("thr", [128, 8, 16], F32)
        r_sm = kb.res("small")
        E2 = kb.sbuf("E2", [128, 8, 128], BF16); r_E2 = kb.res("E2")
        A = kb.sbuf("A", [128, 8, 16, 128], BF16); r_A = kb.res("A")
        Bms = [(kb.sbuf("Bm%d" % i, [128, 8, 16, 64], BF16), kb.res("Bm%d" % i)) for i in range(2)]
        AT = kb.sbuf("AT", [128, 128, 128], BF16); r_AT = kb.res("AT")
        BT = kb.sbuf("BT", [128, 64, 128], BF16); r_BT = kb.res("BT")
        cT = kb.sbuf("cT", [128, 128], BF16); r_cT = kb.res("cT")
        g_pool = Pool(kb, "gt", [128, 64, 128], BF16, 2)
        ps_s = Pool(kb, "pss", [128, 2, 256], F32, 1, space="psum")
        ps_tp = Pool(kb, "tp", [128, 16, 128], BF16, 2, space="psum")
        ps_g = Pool(kb, "pg", [128, 8, 64], F32, 3, space="psum")
        A2 = A[:].rearrange("p h a i -> p (h a) i")
        evtog = [0]
        def front1(n):
            ts = slice(n * 128, (n + 1) * 128)
            qt, r_qt = q_pool.next()
            kb.dma("sp", qt[:], QT[:, :, ts].rearrange("h p t -> p h t"), reads=[r_QT], writes=[r_qt], chan_res=r_qt)
            s, r_s = s_pool.next()
            for hp in range(4):
                ps, r_ps = ps_s.next()
                for hh in range(2):
                    kb.op("pe", lambda e, hp=hp, hh=hh, ps=ps, qt=qt: e.matmul(ps[:, hh, :], lhsT=qt[:, 2 * hp + hh, :], rhs=skb[:], start=True, stop=True),
                          reads=[r_qt, r_sk], writes=[r_ps])
                kb.op("act", lambda e, hp=hp, ps=ps, s=s: e.copy(out=s[:, 2 * hp:2 * hp + 2, :, :].rearrange("p h c n -> p h (c n)"), in_=ps[:]),
                      reads=[r_ps], writes=[r_s])
            return s, r_s

        nxt = front1(0)
        for n in range(NT):
            s, r_s = nxt
            top, _r_top_unused = top_pool.next()
            r_tc = [kb.res("topc") for _ in range(16)]
            r_top = kb.res("topall")
            for h in range(8):
                for c in range(2):
                    kb.op("dve", lambda e, h=h, c=c, top=top, s=s: e.max(out=top[:, h, c, 0:8], in_=s[:, h, c, :]), reads=[r_s], writes=[r_tc[2 * h + c]])
            for h in range(8):
                for c in range(2):
                    kb.op("dve", lambda e, h=h, c=c, top=top, s=s: e.match_replace(out=tmp[:, 2 * h + c, :], in_to_replace=top[:, h, c, 0:8],
                                                                                   in_values=s[:, h, c, :], imm_value=-1e30),
                          reads=[r_s, r_tc[2 * h + c]], writes=[r_tmpc[2 * h + c]])
            for h in range(8):
                for c in range(2):
                    kb.op("dve", lambda e, h=h, c=c, top=top: e.max(out=top[:, h, c, 8:16], in_=tmp[:, 2 * h + c, :]),
                          reads=[r_tmpc[2 * h + c]], writes=[r_tc[2 * h + c], r_top] if (h == 7 and c == 1) else [r_tc[2 * h + c]])
            kb.op("dve", lambda e, top=top: e.tensor_tensor(out=cand[:].rearrange("p h (a b) -> p h a b", a=16),
                                                            in0=top[:, :, 0, :].unsqueeze(3).to_broadcast([128, 8, 16, 16]),
                                                            in1=top[:, :, 1, :].unsqueeze(2).to_broadcast([128, 8, 16, 16]), op=ALU.add),
                  reads=r_tc, writes=[r_cand, r_top])
            r_bc = [kb.res("bestc") for _ in range(8)]
            tmp2 = tmp[:].rearrange("p (h c) n -> p h (c n)", c=2)
            for h in range(8):
                kb.op("dve", lambda e, h=h: e.max(out=best[:, h, 0:8], in_=cand[:, h, :]), reads=[r_cand], writes=[r_bc[h]])
            for h in range(8):
                kb.op("dve", lambda e, h=h: e.match_replace(out=tmp2[:, h, :], in_to_replace=best[:, h, 0:8], in_values=cand[:, h, :], imm_value=-1e30),
                      reads=[r_cand, r_bc[h]], writes=[r_tmpc[2 * h], r_tmpc[2 * h + 1]])
            for h in range(8):
                kb.op("dve", lambda e, h=h: e.max(out=best[:, h, 8:16], in_=tmp2[:, h, :]), reads=[r_tmpc[2 * h], r_tmpc[2 * h + 1]],
                      writes=[r_bc[h], r_best] if h == 7 else [r_bc[h]])
            kb.op("dve", lambda e: e.tensor_copy(out=dd[:, 0, 0:1], in_=best[:, 0, 0:1]), reads=r_bc, writes=[r_best, r_sm])
            kb.op("dve", lambda e: e.tensor_tensor(out=dd[:], in0=best[:], in1=best[:, :, 0:1].to_broadcast([128, 8, 16]), op=ALU.subtract),
                  reads=[r_best], writes=[r_sm])
            kb.op("act", lambda e: e.activation(out=dd[:], in_=dd[:], func=AF.Exp), reads=[r_sm], writes=[r_sm])
            kb.op("dve", lambda e: e.tensor_reduce(out=zz[:, 0, :], in_=dd[:], axis=AX.X, op=ALU.add), reads=[r_sm], writes=[r_sm])
            kb.op("act", lambda e: e.activation(out=zz[:, 1, :], in_=zz[:, 0, :], func=AF.Ln), reads=[r_sm], writes=[r_sm])
            kb.op("dve", lambda e: e.tensor_tensor(out=zz[:, 2, :], in0=zz[:, 1, :], in1=best[:, :, 0], op=ALU.add), reads=[r_sm, r_best], writes=[r_sm])
            kb.op("dve", lambda e, top=top: e.tensor_tensor(out=cc[:], in0=top[:, :, 0, :], in1=zz[:, 2, :].unsqueeze(2).to_broadcast([128, 8, 16]),
                                                            op=ALU.subtract), reads=[r_sm, r_top], writes=[r_sm])
            kb.op("act", lambda e: e.activation(out=cb[:], in_=cc[:], func=AF.Exp), reads=[r_sm], writes=[r_sm])
            kb.op("dve", lambda e: e.tensor_scalar(out=zz[:, 3, :], in0=best[:, :, 15], scalar1=-DELTA, scalar2=None, op0=ALU.add),
                  reads=[r_best], writes=[r_sm])
            kb.op("dve", lambda e, top=top: e.tensor_tensor(out=thr[:], in0=zz[:, 3, :].unsqueeze(2).to_broadcast([128, 8, 16]), in1=top[:, :, 0, :],
                                                            op=ALU.subtract), reads=[r_sm, r_top], writes=[r_sm])
            kb.op("act", lambda e, s=s: e.activation(out=E2[:], in_=s[:, :, 1, :], func=AF.Exp), reads=[r_s], writes=[r_E2])
            kb.op("dve", lambda e, s=s, top=top: e.tensor_tensor(out=A[:], in0=s[:, :, 0, :].unsqueeze(2).to_broadcast([128, 8, 16, 128]),
                                                                 in1=top[:, :, 0, :].unsqueeze(3).to_broadcast([128, 8, 16, 128]), op=ALU.is_equal),
                  reads=[r_s, r_top], writes=[r_A])
            pt, r_pt = ps_tp.next()
            kb.op("pe", lambda e, pt=pt: e.transpose(out=pt[:, 0, :], in_=cb[:].rearrange("p h a -> p (h a)"), identity=ident[:]),
                  reads=[r_sm, r_id], writes=[r_pt])
            kb.op("act", lambda e, pt=pt: e.copy(out=cT[:], in_=pt[:, 0, :]), reads=[r_pt], writes=[r_cT])
            def build_B(jh):
                Bm_, r_Bm_ = Bms[jh]
                js = slice(jh * 64, (jh + 1) * 64)
                kb.op("dve", lambda e, s=s, js=js, Bm_=Bm_: e.tensor_tensor(out=Bm_[:], in0=s[:, :, 1, js].unsqueeze(2).to_broadcast([128, 8, 16, 64]),
                                                                            in1=thr[:].unsqueeze(3).to_broadcast([128, 8, 16, 64]), op=ALU.is_ge),
                      reads=[r_s, r_sm], writes=[r_Bm_])
                kb.op("pool", lambda e, js=js, Bm_=Bm_: e.tensor_tensor(out=Bm_[:], in0=Bm_[:], in1=E2[:, :, js].unsqueeze(2).to_broadcast([128, 8, 16, 64]),
                                                                        op=ALU.mult), reads=[r_E2, r_Bm_], writes=[r_Bm_])

            a_pts = []
            for i0 in range(0, 128, 16):
                pt, r_pt = ps_tp.next()
                for ii in range(16):
                    kb.op("pe", lambda e, pt=pt, ii=ii, i0=i0: e.transpose(out=pt[:, ii, :], in_=A2[:, :, i0 + ii], identity=ident[:]),
                          reads=[r_A, r_id], writes=[r_pt])
                if i0 == 0:
                    build_B(0)
                kb.op("dve", lambda e, pt=pt, i0=i0: e.tensor_tensor(out=AT[:, i0:i0 + 16, :], in0=pt[:],
                                                                     in1=cT[:].unsqueeze(1).to_broadcast([128, 16, 128]), op=ALU.mult),
                      reads=[r_pt, r_cT], writes=[r_AT])
            build_B(1)
            if n + 1 < NT:
                nxt = front1(n + 1)
            for jh in range(2):
                Bm_, r_Bm_ = Bms[jh]
                B2_ = Bm_[:].rearrange("p h a j -> p (h a) j")
                js = slice(jh * 64, (jh + 1) * 64)
                for j0 in range(0, 64, 16):
                    pt, r_pt = ps_tp.next()
                    for jj in range(16):
                        kb.op("pe", lambda e, pt=pt, jj=jj, j0=j0, B2_=B2_: e.transpose(out=pt[:, jj, :], in_=B2_[:, :, j0 + jj], identity=ident[:]),
                              reads=[r_Bm_, r_id], writes=[r_pt])
                    kb.op("act", lambda e, pt=pt, j0=j0: e.copy(out=BT[:, j0:j0 + 16, :], in_=pt[:]), reads=[r_pt], writes=[r_BT])
                gt, r_gt = g_pool.next()
                for t0 in range(0, 128, 8):
                    pg, r_pg = ps_g.next()
                    for tt in range(8):
                        kb.op("pe", lambda e, pg=pg, tt=tt, t0=t0: e.matmul(pg[:, tt, :], lhsT=AT[:, :, t0 + tt], rhs=BT[:, :, t0 + tt], start=True, stop=True),
                              reads=[r_AT, r_BT], writes=[r_pg])
                    kb.op("act", lambda e, pg=pg, gt=gt, t0=t0: e.copy(out=gt[:, :, t0:t0 + 8], in_=pg[:].rearrange("p t j -> p j t")),
                          reads=[r_pg], writes=[r_gt])
                kb.dma("pool", GT[n, :, js, :], gt[:], reads=[r_gt], writes=[r_GT], chan_res=r_gt)
        kb.end_phase()

    def phase5(l, X1, r_X1, X2, r_X2):
        kb.begin_phase()
        TBT = min(TBT_MAX, NT)
        TB = TBT * 128
        nblk = NT // TBT
        NSC = 64
        xblk = kb.sbuf("xblk", [128, KC, TB], BF16); r_xblk = kb.res("xblk")
        y_sb = kb.sbuf("ysb", [128, TBT, D], F32)
        r_ysb = [[kb.res("ysb") for _ in range(4)] for _ in range(TBT)]
        ty_pool = Pool(kb, "ty", [128, 512], F32, 2)
        un_pool = Pool(kb, "un", [128, 2, D], BF16, 2)
        v_pool = Pool(kb, "vv", [128, 2, D], BF16, 3)
        uT_pool = Pool(kb, "uT", [128, KC, 128], BF16, 4)
        g_pool = Pool(kb, "gg", [128, TBT, 2, 128], BF16, 2)
        ge_pool = Pool(kb, "ge", [128, 512], BF16, 2)
        hs_pool = Pool(kb, "hs", [128, TB], BF16, 4)
        ps_tp = Pool(kb, "tp", [128, 8, 128], BF16, 3, space="psum")
        ps_a = Pool(kb, "pa", [128, 512], F32, 2, space="psum")
        ps_y = Pool(kb, "py", [128, 512], F32, 3, space="psum")
        U3 = peer_u[l].rearrange("(i j) d -> i j d", j=128)
        V3 = peer_v[l].rearrange("(i j) d -> i j d", j=128)
        for blk in range(nblk):
            t0 = blk * TB
            kb.dma("sp", xblk[:], XN2T[:, :, t0:t0 + TB].rearrange("c p t -> p c t"), reads=[r_XN2T], writes=[r_xblk], chan_res=r_xblk)
            ctx = {}
            for tile in range(TBT):
                tok = t0 + tile * 128
                kb.dma("sp", y_sb[:, tile, :], X1[tok:tok + 128, :], reads=[r_X1], writes=r_ysb[tile], chan_res=r_ysb[tile][1])

            def do_L(sc):
                j0 = 2 * sc
                c = Ctx()
                c.un, c.r_un = un_pool.next()
                kb.dma("pool", c.un[:], U3[:, j0:j0 + 2, :], reads=[r_const], writes=[c.r_un], chan_res=c.r_un)
                ctx[sc] = c

            def do_L2(sc):
                j0 = 2 * sc
                c = ctx[sc]
                c.gg, c.r_gg = g_pool.next()
                c.vv, c.r_vv = v_pool.next()
                kb.dma("sp", c.gg[:], GT[blk * TBT:(blk + 1) * TBT, :, j0:j0 + 2, :].rearrange("n i j t -> i n j t"), reads=[r_GT],
                       writes=[c.r_gg], chan_res=c.r_gg)
                kb.dma("pool", c.vv[:], V3[:, j0:j0 + 2, :], reads=[r_const], writes=[c.r_vv], chan_res=c.r_vv)

            def do_T(sc):
                c = ctx[sc]
                c.uT = []
                for jj in range(2):
                    uT, r_uT = uT_pool.next()
                    for hf in range(2):
                        pt, r_pt = ps_tp.next()
                        for k8 in range(8):
                            k = hf * 8 + k8
                            kb.op("pe", lambda e, k=k, k8=k8, jj=jj, pt=pt, un=c.un: e.transpose(out=pt[:, k8, :], in_=un[:, jj, k * 128:(k + 1) * 128],
                                                                                                identity=ident[:]),
                                  reads=[c.r_un, r_id], writes=[r_pt])
                        kb.op("act", lambda e, pt=pt, uT=uT, hf=hf: e.copy(out=uT[:, hf * 8:(hf + 1) * 8, :], in_=pt[:]), reads=[r_pt], writes=[r_uT])
                    c.uT.append((uT, r_uT))

            def do_A(sc):
                c = ctx[sc]
                c.hs = []
                for jj in range(2):
                    uT, r_uT = c.uT[jj]
                    h_t, r_ht = hs_pool.next()
                    for half in range(TB // 512):
                        pa, r_pa = ps_a.next()
                        for k in range(KC):
                            kb.op("pe", lambda e, k=k, pa=pa, uT=uT, half=half: e.matmul(pa[:], lhsT=uT[:, k, :], rhs=xblk[:, k, half * 512:(half + 1) * 512],
                                                                                          start=(k == 0), stop=(k == KC - 1)),
                                  reads=[r_uT, r_xblk], writes=[r_pa])
                        ge, r_ge = ge_pool.next()
                        kb.op("act", lambda e, pa=pa, ge=ge: e.activation(out=ge[:], in_=pa[:], func=AF.Gelu_apprx_tanh), reads=[r_pa], writes=[r_ge])
                        kb.op("dve", lambda e, ge=ge, h_t=h_t, gg=c.gg, jj=jj, half=half: e.tensor_tensor(
                            out=h_t[:, half * 512:(half + 1) * 512].rearrange("p (n t) -> p n t", t=128), in0=ge[:].rearrange("p (n t) -> p n t", t=128),
                            in1=gg[:, half * 4:(half + 1) * 4, jj, :], op=ALU.mult), reads=[r_ge, c.r_gg], writes=[r_ht])
                    c.hs.append((h_t, r_ht))

            def do_Y(sc):
                c = ctx[sc]
                for tile in range(TBT):
                    for nb in range(4):
                        py, r_py = ps_y.next()
                        for jj in range(2):
                            h_t, r_ht = c.hs[jj]
                            kb.op("pe", lambda e, jj=jj, py=py, h_t=h_t, vv=c.vv, tile=tile, nb=nb: e.matmul(
                                py[:], lhsT=h_t[:, tile * 128:(tile + 1) * 128], rhs=vv[:, jj, nb * 512:(nb + 1) * 512], start=(jj == 0), stop=(jj == 1)),
                                reads=[r_ht, c.r_vv], writes=[r_py])
                        ysl = y_sb[:, tile, nb * 512:(nb + 1) * 512]
                        if (tile * 4 + nb) % 3 == 2:
                            ty, r_ty = ty_pool.next()
                            kb.op("act", lambda e, py=py, ty=ty: e.copy(out=ty[:], in_=py[:]), reads=[r_py], writes=[r_ty])
                            kb.op("pool", lambda e, ty=ty, ysl=ysl: e.tensor_tensor(out=ysl, in0=ty[:], in1=ysl, op=ALU.add),
                                  reads=[r_ty, r_ysb[tile][nb]], writes=[r_ysb[tile][nb]])
                        else:
                            kb.op("dve", lambda e, py=py, ysl=ysl: e.tensor_tensor(out=ysl, in0=py[:], in1=ysl, op=ALU.add),
                                  reads=[r_py, r_ysb[tile][nb]], writes=[r_ysb[tile][nb]])
                del ctx[sc]

            do_L(0)
            do_L2(0)
            do_L(1)
            do_L2(1)
            do_T(0)
            for i in range(NSC):
                if i + 2 < NSC:
                    do_L(i + 2)
                if i + 1 < NSC:
                    do_T(i + 1)
                do_A(i)
                if i >= 1:
                    do_Y(i - 1)
                if i + 2 < NSC:
                    do_L2(i + 2)
            do_Y(NSC - 1)
            for tile in range(TBT):
                tok = t0 + tile * 128
                kb.dma("pool", X2[tok:tok + 128, :], y_sb[:, tile, :], reads=r_ysb[tile], writes=[r_X2], chan_res=r_ysb[tile][0])
        kb.end_phase()

    def phasef(x_src, r_xsrc):
        kb.begin_phase()
        fn = kb.sbuf("fn", [128, D], F32); r_fn = kb.res("fn")
        kb.dma("sp", fn[:], fnorm_in, reads=[r_const], writes=[r_fn], chan_res=r_fn)
        xt_pool = Pool(kb, "xt", [128, D], F32, 2)
        yo_pool = Pool(kb, "yo", [128, D], F32, 2)
        sq_pool = Pool(kb, "sq", [128, D], BF16, 1)
        st_pool = Pool(kb, "stat", [128, 4], F32, 4)
        for n in range(NT):
            xt, r_xt = xt_pool.next()
            kb.dma("sp", xt[:], x_src[n * 128:(n + 1) * 128, :], reads=[r_xsrc], writes=[r_xt], chan_res=r_xt)
            stt, r_st = st_pool.next()
            sq, r_sq = sq_pool.next()
            kb.op("act", lambda e, xt=xt, stt=stt, sq=sq: e.activation(out=sq[:], in_=xt[:], func=AF.Square, accum_out=stt[:, 0:1]),
                  reads=[r_xt], writes=[r_sq, r_st])
            kb.op("act", lambda e, stt=stt: e.activation(out=stt[:, 0:1], in_=stt[:, 0:1], func=AF.Ln, bias=epsb[:, 0:1], scale=1.0 / D),
                  reads=[r_st, r_gn], writes=[r_st])
            kb.op("act", lambda e, stt=stt: e.activation(out=stt[:, 0:1], in_=stt[:, 0:1], func=AF.Exp, scale=-0.5), reads=[r_st], writes=[r_st])
            yo, r_yo = yo_pool.next()
            kb.op("dve", lambda e, xt=xt, stt=stt, yo=yo: e.scalar_tensor_tensor(out=yo[:], in0=xt[:], scalar=stt[:, 0:1], in1=fn[:],
                                                                                 op0=ALU.mult, op1=ALU.mult),
                  reads=[r_xt, r_st, r_fn], writes=[r_yo])
            kb.dma("pool", y_out[n * 128:(n + 1) * 128, :], yo[:], reads=[r_yo], writes=[r_y], chan_res=r_yo)
        kb.end_phase()

    for l in range(L):
        x_src, r_xsrc = (x_in, r_x) if l == 0 else (XB, r_XB)
        phase1(l, x_src, r_xsrc)
        if upto >= 2:
            phase1c(l)
        if upto >= 3:
            phase2a(l)
            phase2b(l)
            phase2c(l)
        if upto >= 4:
            phase3(l, x_src, r_xsrc, XA, r_XA)
        if upto >= 5:
            phase4a(l)
            phase4b(l)
        if upto >= 6:
            phase5(l, XA, r_XA, XB, r_XB)
    if upto >= 7:
        phasef(XB, r_XB)
    outs = [r_HT, r_VA, r_VB, r_QCN, r_QCP, r_KCN, r_KCP, r_VC, r_O, r_XA, r_XB, r_XN2T, r_QT, r_GT, r_y]
    kb.finish_wait("pool", outs)
    kb.emit()
    return nc


def host_consts(S):
    c = {}
    c["ident"] = np.eye(128, dtype=np.float32)
    R = np.zeros((64, 64), np.float32)
    for m in range(32):
        R[m + 32, m] = -1.0
        R[m, m + 32] = 1.0
    c["rotR"] = R
    inv = 10000.0 ** (-(np.arange(32, dtype=np.float64)) / 32.0)
    ang = np.arange(S, dtype=np.float64)[None, :] * np.concatenate([inv, inv])[:, None]
    c["cossin"] = np.stack([np.cos(ang), np.sin(ang)]).astype(np.float32)
    ki = np.arange(128)[:, None]
    qi = np.arange(128)[None, :]
    slopes = np.array([2.0 ** (-8.0 * (h + 1) / 8) for h in range(8)], np.float64)
    tabA = np.zeros((128, 3, 8, 128), np.float64)
    for d, (dist, valid) in enumerate([(128 + qi - ki, qi <= ki), (np.abs(qi - ki), np.ones((128, 128), bool)),
                                       (128 + ki - qi, ki <= qi)]):
        for h in range(8):
            tabA[:, d, h, :] = np.where(valid, np.exp(-slopes[h] * dist), 0.0)
    c["tabA"] = tabA.astype(np.float32).reshape(128, 3 * 8 * 128)
    rows = S // 64
    kr = min(8, rows)
    NT = S // 128
    tabs = {}
    sched = []
    masks, dridx, dcidx = [], [], []
    for n in range(NT):
        qrow = 2 * n + np.arange(128) // 64
        qcol = np.arange(128) % 64
        rstart = np.clip(qrow - kr // 2, 0, rows - kr)
        cstart = np.clip(qcol - 8, 0, 64 - 16)
        lst = []
        for m in range(NT):
            krow = 2 * m + np.arange(128) // 64
            kcol = np.arange(128) % 64
            valid = ((krow[:, None] >= rstart[None, :]) & (krow[:, None] < rstart[None, :] + kr)
                     & (kcol[:, None] >= cstart[None, :]) & (kcol[:, None] < cstart[None, :] + 16))
            if not valid.any():
                continue
            dr = np.clip(krow[:, None] - qrow[None, :] + 7, 0, 14)
            dc = np.clip(kcol[:, None] - qcol[None, :] + 15, 0, 30)
            dr = np.where(valid, dr, 0)
            dc = np.where(valid, dc, 0)
            key = (valid.tobytes(), dr.tobytes(), dc.tobytes())
            if key not in tabs:
                tabs[key] = len(tabs)
                masks.append(valid.astype(np.float32))
                dridx.append(dr)
                dcidx.append(dc)
            lst.append((m, tabs[key]))
        sched.append(lst)
    c["maskB"] = np.ascontiguousarray(np.stack(masks, 1))
    c["_dr"] = np.stack(dridx, 1)
    c["_dc"] = np.stack(dcidx, 1)
    c["_schedB"] = sched
    c["_ntab"] = len(masks)
    return c


def gather_biasB(b_rel_bias_l, c):
    g = b_rel_bias_l[:, c["_dr"], c["_dc"]]
    return np.ascontiguousarray(np.transpose(g, (1, 2, 0, 3))).astype(np.float32)


def gains_pack(inp, L):
    g = np.zeros((L, 128, 64), np.float32)
    for l in range(L):
        g[l, :, 0:16] = inp["ln1"][l].reshape(16, 128).T
        g[l, :, 16:20] = inp["c_q_norm"][l].reshape(4, 128).T
        g[l, :, 20:22] = inp["c_kv_norm"][l].reshape(2, 128).T
        g[l, :, 22:38] = inp["out_norm"][l].reshape(16, 128).T
        g[l, :, 38:54] = inp["ln2"][l].reshape(16, 128).T
    return g


def sk_pack(inp, L):
    sk = np.zeros((L, 128, 256), np.float32)
    for l in range(L):
        sk[l, 0:64, 0:128] = inp["peer_sub_keys"][l, 0].T
        sk[l, 64:128, 128:256] = inp["peer_sub_keys"][l, 1].T
    return sk


def core_inputs(inp, xb, S, L, consts):
    m = dict(x=np.ascontiguousarray(xb), w_in=inp["w_in"][:L], gains=gains_pack(inp, L), ident=consts["ident"],
             cossin=consts["cossin"], rotR=consts["rotR"], tabA=consts["tabA"], maskB=consts["maskB"],
             biasB=np.stack([gather_biasB(inp["b_rel_bias"][l], consts) for l in range(L)]),
             sinkb=np.ascontiguousarray(np.broadcast_to(inp["a_sink"][:L][None], (128, L, 8))).astype(np.float32),
             c_w_uq=inp["c_w_uq"][:L], c_w_ukv=inp["c_w_ukv"][:L], w_o=inp["w_o"][:L],
             peer_w_q=inp["peer_w_q"][:L], peer_u=inp["peer_u"][:L], peer_v=inp["peer_v"][:L],
             sk=sk_pack(inp, L),
             ln2b=np.ascontiguousarray(np.broadcast_to(inp["ln2"][:L][:, None, :], (L, 128, 2048))).astype(np.float32),
             fnorm=np.ascontiguousarray(np.broadcast_to(inp["final_norm"][None], (128, 2048))).astype(np.float32))
    return m


N_CORES = 4
_CACHE = {}


def kernel(**inputs):
    inp = {k: np.asarray(v) for k, v in inputs.items()}
    B, S, _ = inp["x"].shape
    L = inp["w_in"].shape[0]
    key = (S, L)
    if key not in _CACHE:
        consts = host_consts(S)
        _CACHE[key] = (consts, build(S, L, consts, debug=False))
    consts, nc = _CACHE[key]
    in_maps = []
    for b in range(B):
        m = core_inputs(inp, inp["x"][b], S, L, consts)
        in_maps.append({k: np.ascontiguousarray(v, dtype=np.float32) for k, v in m.items()})
    res = run_bass_kernel_spmd(nc, in_maps, core_ids=list(range(B)))
    out = np.stack([np.asarray(res.results[b]["y"], dtype=np.float32) for b in range(B)], axis=0)
    return out
```

```python
import numpy as np
from concourse.bass_utils import run_bass_kernel_spmd
from contextlib import ExitStack
import concourse.bass as bass
import concourse.mybir as mybir

F32 = mybir.dt.float32
BF16 = mybir.dt.bfloat16
AF = mybir.ActivationFunctionType
ALU = mybir.AluOpType
AX = mybir.AxisListType


class Res:
    __slots__ = ("name", "w", "r", "cin", "cout")

    def __init__(self, name):
        self.name = name
        self.w = {}
        self.r = {}
        self.cin = None
        self.cout = None


class Chan:
    __slots__ = ("sid", "cnt")

    def __init__(self, sid):
        self.sid = sid
        self.cnt = 0


class Eng:
    def __init__(self, name, sid):
        self.name = name
        self.sid = sid
        self.cnt = 0
        self.waited = {}
        self.prog = []


class KB:
    def __init__(self, nc):
        self.nc = nc
        self.es = ExitStack()
        self.sems = []
        self.engs = {}
        self.chans = []
        for k in ("pe", "act", "dve", "pool", "sp"):
            self.engs[k] = Eng(k, self.newsem("e_" + k))
        self.nres = 0
        self.pes = None
        self.ntens = 0
        self.free_ch = []
        self.phase_ch = []

    def newsem(self, name):
        s = self.es.enter_context(self.nc.semaphore("%s_%d" % (name, len(self.sems))))
        self.sems.append(s)
        return len(self.sems) - 1

    def res(self, name=None):
        self.nres += 1
        return Res(name or ("r%d" % self.nres))

    def sbuf(self, name, shape, dt):
        st = self.pes if self.pes is not None else self.es
        self.ntens += 1
        return st.enter_context(self.nc.sbuf_tensor("sb%d_%s" % (self.ntens, name), list(shape), dt))

    def psum(self, name, shape, dt):
        st = self.pes if self.pes is not None else self.es
        self.ntens += 1
        return st.enter_context(self.nc.psum_tensor("ps%d_%s" % (self.ntens, name), list(shape), dt))

    def begin_phase(self):
        self.pes = ExitStack()

    def end_phase(self):
        self.barrier()
        self.emit_block()
        self.pes.close()
        self.pes = None
        self.free_ch.extend(self.phase_ch)
        self.phase_ch = []

    def totals(self):
        t = {}
        for E in self.engs.values():
            t[E.sid] = E.cnt
        for ch in self.chans:
            t[ch.sid] = ch.cnt
        return t

    def barrier(self):
        tot = self.totals()
        for E in self.engs.values():
            waits = []
            for s, v in tot.items():
                if v > 0 and E.waited.get(s, 0) < v:
                    E.waited[s] = v
                    waits.append((s, v))
            E.prog.append((waits, None, None, 0))

    def _deps(self, E, reads, writes, same_ok):
        deps = {}
        for r in reads:
            for s, v in r.w.items():
                if deps.get(s, 0) < v:
                    deps[s] = v
        for w in writes:
            for s, v in w.w.items():
                if deps.get(s, 0) < v:
                    deps[s] = v
            for s, v in w.r.items():
                if deps.get(s, 0) < v:
                    deps[s] = v
        waits = []
        for s, v in deps.items():
            if s == E.sid and same_ok:
                continue
            if E.waited.get(s, 0) >= v:
                continue
            E.waited[s] = v
            waits.append((s, v))
        return waits

    def op(self, eng, fn, reads=(), writes=()):
        E = self.engs[eng]
        waits = self._deps(E, reads, writes, same_ok=(eng == "pe"))
        E.cnt += 1
        c = E.cnt
        E.prog.append((waits, fn, E.sid, 1))
        for r in reads:
            r.r[E.sid] = c
        for w in writes:
            w.w[E.sid] = c

    def dma(self, q, out, in_, reads=(), writes=(), chan_res=None, **kw):
        E = self.engs[q]
        waits = self._deps(E, reads, writes, same_ok=False)
        cr = chan_res
        if cr.cin is None:
            if self.free_ch:
                cr.cin = self.free_ch.pop()
            else:
                cr.cin = Chan(self.newsem("c"))
                self.chans.append(cr.cin)
            if self.pes is not None:
                self.phase_ch.append(cr.cin)
        ch = cr.cin
        ch.cnt += 16
        c = ch.cnt
        E.prog.append((waits, (lambda e, out=out, in_=in_, kw=kw: e.dma_start(out=out, in_=in_, **kw)), ch.sid, 16))
        for r in reads:
            r.r[ch.sid] = c
        for w in writes:
            w.w[ch.sid] = c

    def finish_wait(self, eng, resources):
        E = self.engs[eng]
        waits = self._deps(E, resources, (), same_ok=False)
        E.prog.append((waits, None, None, 0))

    def emit(self):
        self.emit_block()
        self.es.close()

    def emit_block(self):
        nc = self.nc
        sems = self.sems
        with nc.Block() as block:
            def run(E):
                def body(e):
                    for waits, fn, sid, inc in E.prog:
                        for s, v in waits:
                            e.wait_ge(sems[s], v)
                        if fn is not None:
                            ins = fn(e)
                            ins.then_inc(sems[sid], inc)
                return body
            block.tensor(run(self.engs["pe"]))
            block.scalar(run(self.engs["act"]))
            block.vector(run(self.engs["dve"]))
            block.gpsimd(run(self.engs["pool"]))
            block.sync(run(self.engs["sp"]))
        for E in self.engs.values():
            E.prog = []


D = 2048
KC = 16
DIN = 3904
EPS = 1e-6


class Pool:
    def __init__(self, kb, name, shape, dt, n, space="sbuf"):
        self.items = []
        for i in range(n):
            t = (kb.sbuf if space == "sbuf" else kb.psum)("%s%d" % (name, i), shape, dt)
            self.items.append((t, kb.res("%s%d" % (name, i))))
        self.i = 0

    def next(self):
        it = self.items[self.i % len(self.items)]
        self.i += 1
        return it


class Ctx:
    pass


def build(S, L, consts, debug=False, upto=99):
    nc = bass.Bass("TRN2", target_bir_lowering=False)
    kb = KB(nc)
    NT = S // 128
    NB = S // 512
    dbgkind = "ExternalOutput" if debug else "Internal"

    def din(name, shape, dt=F32):
        return nc.dram_tensor(name, list(shape), dt, kind="ExternalInput").ap()

    def dscr(name, shape, dt=BF16, kind=None):
        return nc.dram_tensor(name, list(shape), dt, kind=kind or dbgkind).ap()

    x_in = din("x", [S, D])
    w_in = din("w_in", [L, D, DIN])
    gains = din("gains", [L, 128, 64])
    ident_in = din("ident", [128, 128])
    cs_in = din("cossin", [2, 64, S])
    y_out = nc.dram_tensor("y", [S, D], F32, kind="ExternalOutput").ap()

    HT = dscr("HT", [31, 128, S])
    VA = dscr("VA", [S, 2, 129])
    VB = dscr("VB", [S, 4, 129])
    r_HT = kb.res("HT"); r_VA = kb.res("VA"); r_VB = kb.res("VB")
    r_x = kb.res("x_in"); r_w_in = kb.res("w_in"); r_const = kb.res("const")

    ident32 = kb.sbuf("ident32", [128, 128], F32)
    ident = kb.sbuf("ident", [128, 128], BF16)
    gn = kb.sbuf("gn", [128, L, 64], F32)
    gneg = kb.sbuf("gneg", [128, L, 64], F32)
    epsb = kb.sbuf("epsb", [128, 1], F32)
    r_id = kb.res("ident"); r_gn = kb.res("gn")
    kb.dma("sp", ident32[:], ident_in, reads=[r_const], writes=[r_id], chan_res=r_id)
    kb.op("dve", lambda e: e.tensor_copy(out=ident[:], in_=ident32[:]), reads=[r_id], writes=[r_id])
    kb.dma("sp", gn[:], gains.rearrange("l p c -> p l c"), reads=[r_const], writes=[r_gn], chan_res=r_gn)
    kb.op("dve", lambda e: e.tensor_scalar(out=gneg[:], in0=gn[:], scalar1=-1.0, scalar2=None, op0=ALU.mult),
          reads=[r_gn], writes=[r_gn])
    kb.op("dve", lambda e: e.memset(epsb[:], EPS), writes=[r_gn])

    NTAB = consts["_ntab"]
    schedB = consts["_schedB"]
    rot_in = din("rotR", [64, 64])
    tabA_in = din("tabA", [128, 3072])
    maskB_in = din("maskB", [128, NTAB, 128])
    biasB_in = din("biasB", [L, 128, NTAB, 4, 128])
    sink_in = din("sinkb", [128, L, 8])
    w_uq = din("c_w_uq", [L, 512, 768])
    w_ukv = din("c_w_ukv", [L, 256, 1024])
    w_o = din("w_o", [L, D, D])
    fnorm_in = din("fnorm", [128, D])
    w_pq = din("peer_w_q", [L, D, 1024])
    sk_in = din("sk", [L, 128, 256])
    peer_u = din("peer_u", [L, 16384, D])
    peer_v = din("peer_v", [L, 16384, D])
    ln2b_in = din("ln2b", [L, 128, D])
    TBT_MAX = 8
    r_win2 = kb.res("win2")

    QCN = dscr("QCN", [4, 128, S]); QCP = dscr("QCP", [4, 64, S]); KCN = dscr("KCN", [4, 128, S]); KCP = dscr("KCP", [64, S])
    VC = dscr("VC", [S, 4, 129])
    O = dscr("O", [S, D], F32)
    XA = dscr("XA", [S, D], F32); XB = dscr("XB", [S, D], F32)
    XN2T = dscr("XN2T", [KC, 128, S])
    QT = dscr("QT", [8, 128, S])
    GT = dscr("GT", [NT, 128, 128, 128])
    r_QT = kb.res("QT"); r_GT = kb.res("GT"); r_y = kb.res("y")
    r_QCN = kb.res("QCN"); r_QCP = kb.res("QCP"); r_KCN = kb.res("KCN"); r_KCP = kb.res("KCP"); r_VC = kb.res("VC")
    r_O = kb.res("O"); r_XA = kb.res("XA"); r_XB = kb.res("XB"); r_XN2T = kb.res("XN2T")

    rot32 = kb.sbuf("rot32", [64, 64], F32)
    rotb = kb.sbuf("rotb", [64, 64], BF16)
    onesb = kb.sbuf("onesb", [128, 128], BF16)
    esink = kb.sbuf("esink", [128, L, 8], F32)
    kb.dma("sp", rot32[:], rot_in, reads=[r_const], writes=[r_id], chan_res=r_id)
    kb.op("dve", lambda e: e.tensor_copy(out=rotb[:], in_=rot32[:]), reads=[r_id], writes=[r_id])
    kb.op("dve", lambda e: e.memset(onesb[:], 1.0), writes=[r_id])
    kb.dma("sp", esink[:], sink_in, reads=[r_const], writes=[r_gn], chan_res=r_gn)
    kb.op("act", lambda e: e.activation(out=esink[:], in_=esink[:], func=AF.Exp), reads=[r_gn], writes=[r_gn])

    SC_A = 128 ** -0.5
    SC_C = 192 ** -0.5

    def mk_norm_helpers(st_pool, sq_pool, ps_tp):
        def rmsnorm_tile(xt, r_xt, groups, xn, r_xn, gain=None):
            stt, r_st = st_pool.next()
            sq, r_sq = sq_pool.next()
            ng = len(groups)
            for g, (s0, wd) in enumerate(groups):
                kb.op("act", lambda e, s0=s0, wd=wd, g=g: e.activation(out=sq[:, s0:s0 + wd], in_=xt[:, s0:s0 + wd], func=AF.Square,
                                                                       accum_out=stt[:, g:g + 1]),
                      reads=[r_xt], writes=[r_sq, r_st])
            for g, (s0, wd) in enumerate(groups):
                kb.op("act", lambda e, g=g, wd=wd: e.activation(out=stt[:, g:g + 1], in_=stt[:, g:g + 1], func=AF.Ln,
                                                                bias=epsb[:, 0:1], scale=1.0 / wd),
                      reads=[r_st, r_gn], writes=[r_st])
            kb.op("act", lambda e: e.activation(out=stt[:, 0:ng], in_=stt[:, 0:ng], func=AF.Exp, scale=-0.5),
                  reads=[r_st], writes=[r_st])
            for g, (s0, wd) in enumerate(groups):
                if gain is None:
                    kb.op("dve", lambda e, s0=s0, wd=wd, g=g: e.tensor_scalar(out=xn[:, s0:s0 + wd], in0=xt[:, s0:s0 + wd],
                                                                              scalar1=stt[:, g:g + 1], scalar2=None, op0=ALU.mult),
                          reads=[r_xt, r_st], writes=[r_xn])
                else:
                    gt_, r_gt_ = gain
                    kb.op("dve", lambda e, s0=s0, wd=wd, g=g: e.scalar_tensor_tensor(out=xn[:, s0:s0 + wd], in0=xt[:, s0:s0 + wd],
                                                                                     scalar=stt[:, g:g + 1], in1=gt_[:, s0:s0 + wd],
                                                                                     op0=ALU.mult, op1=ALU.mult),
                          reads=[r_xt, r_st, r_gt_], writes=[r_xn])

        def transpose_tile(xn, r_xn, dst, r_dst, j):
            pt, r_pt = ps_tp.next()
            for k in range(KC):
                kb.op("pe", lambda e, k=k: e.transpose(out=pt[:, k, :], in_=xn[:, k * 128:(k + 1) * 128], identity=ident[:]),
                      reads=[r_xn, r_id], writes=[r_pt])
            kb.op("act", lambda e: e.copy(out=dst[:, :, j * 128:(j + 1) * 128], in_=pt[:]), reads=[r_pt], writes=[r_dst])
        return rmsnorm_tile, transpose_tile

    def load_w(wst_pool, src_ap, gcol, ncols, l, dst, r_dst, r_src):
        nk = src_ap.shape[0] // 128
        for k in range(nk):
            st, r_st = wst_pool.next()
            kb.dma("sp", st[:, :ncols], src_ap[k * 128:(k + 1) * 128, :], reads=[r_src], writes=[r_st], chan_res=r_st)
            if gcol is None:
                kb.op("act", lambda e, k=k, st=st: e.copy(out=dst[:, k, :ncols], in_=st[:, :ncols]), reads=[r_st], writes=[r_dst])
            else:
                kb.op("act", lambda e, k=k, st=st: e.activation(out=dst[:, k, :ncols], in_=st[:, :ncols], func=AF.Copy,
                                                                scale=gn[:, l, gcol + k:gcol + k + 1]),
                      reads=[r_st, r_gn], writes=[r_dst])

    def phase1(l, x_src, r_xsrc):
        kb.begin_phase()
        xt_pool = Pool(kb, "xt", [128, D], F32, 4)
        xn_pool = Pool(kb, "xn", [128, D], BF16, 4)
        sq_pool = Pool(kb, "sq", [128, D], BF16, 1)
        st_pool = Pool(kb, "stat", [128, 4], F32, 4)
        xnT_pool = Pool(kb, "xnT", [128, KC, 512], BF16, 2)
        ps_tp = Pool(kb, "tp", [128, KC, 128], BF16, 2, space="psum")
        ps_mm = Pool(kb, "mm", [128, 512], F32, 4, space="psum")
        wst_pool = Pool(kb, "wst", [128, 2048], F32, 2)
        wbf = kb.sbuf("wbf", [128, KC, 2048], BF16)
        r_wbf = kb.res("wbf")
        ev_pool = Pool(kb, "ev", [128, 512], BF16, 4)
        vst_pool = Pool(kb, "vst", [128, 4, 129], BF16, 2)
        for t, r in vst_pool.items:
            kb.op("pool", lambda e, t=t: e.memset(t[:], 1.0), writes=[r])
        rmsnorm_tile, transpose_tile = mk_norm_helpers(st_pool, sq_pool, ps_tp)
        passes = [
            dict(c0=0, fm=[(c, c * 128, 128) for c in range(0, 10)] + [(c, c * 128, 128) for c in range(12, 16)],
                 tm=[(VA, r_VA, 1280, 2)]),
            dict(c0=2048, fm=[(c, c * 128 - 2048, 128) for c in range(16, 20)] + [(c, c * 128 - 2048, 128) for c in range(24, 30)]
                 + [(30, 30 * 128 - 2048, 64)],
                 tm=[(VB, r_VB, 2560 - 2048, 4)]),
        ]
        for ps in passes:
            c0 = ps["c0"]
            ncols = min(2048, DIN - c0)
            load_w(wst_pool, w_in[l, :, c0:c0 + ncols], 0, ncols, l, wbf, r_wbf, r_w_in)
            def front(b):
                xnT, r_xnT = xnT_pool.next()
                for j in range(4):
                    tok = b * 512 + j * 128
                    xt, r_xt = xt_pool.next()
                    kb.dma("sp", xt[:], x_src[tok:tok + 128, :], reads=[r_xsrc], writes=[r_xt], chan_res=r_xt)
                    xn, r_xn = xn_pool.next()
                    rmsnorm_tile(xt, r_xt, [(0, D)], xn, r_xn)
                    transpose_tile(xn, r_xn, xnT, r_xnT, j)
                return xnT, r_xnT

            nxt = front(0)
            for b in range(NB):
                xnT, r_xnT = nxt
                if b + 1 < NB:
                    nxt = front(b + 1)
                for (c, off, m) in ps["fm"]:
                    pm, r_pm = ps_mm.next()
                    for k in range(KC):
                        kb.op("pe", lambda e, k=k, off=off, m=m, pm=pm, xnT=xnT: e.matmul(
                            pm[0:m, :], lhsT=wbf[:, k, off:off + m], rhs=xnT[:, k, :], start=(k == 0), stop=(k == KC - 1)),
                            reads=[r_wbf, r_xnT], writes=[r_pm])
                    ev, r_ev = ev_pool.next()
                    kb.op("dve", lambda e, m=m, pm=pm, ev=ev: e.tensor_copy(out=ev[0:m, :], in_=pm[0:m, :]),
                          reads=[r_pm], writes=[r_ev])
                    kb.dma("pool", HT[c, 0:m, b * 512:(b + 1) * 512], ev[0:m, :], reads=[r_ev], writes=[r_HT], chan_res=r_ev)
                for (dst, r_dstd, off, nh) in ps["tm"]:
                    for j in range(4):
                        tok = b * 512 + j * 128
                        pm, r_pm = ps_mm.next()
                        for k in range(KC):
                            kb.op("pe", lambda e, k=k, off=off, nh=nh, pm=pm, xnT=xnT, j=j: e.matmul(
                                pm[:, 0:nh * 128], lhsT=xnT[:, k, j * 128:(j + 1) * 128], rhs=wbf[:, k, off:off + nh * 128],
                                start=(k == 0), stop=(k == KC - 1)),
                                reads=[r_wbf, r_xnT], writes=[r_pm])
                        vs, r_vs = vst_pool.next()
                        kb.op("act", lambda e, nh=nh, pm=pm, vs=vs: e.copy(
                            out=vs[:, 0:nh, 0:128], in_=pm[:, 0:nh * 128].rearrange("p (h d) -> p h d", d=128)),
                            reads=[r_pm], writes=[r_vs])
                        kb.dma("pool", dst[tok:tok + 128, :, :], vs[:, 0:nh, :], reads=[r_vs], writes=[r_dstd], chan_res=r_vs)
        kb.end_phase()

    def phase1c(l):
        kb.begin_phase()
        wst_pool = Pool(kb, "wst", [128, 1024], F32, 2)
        wuq = kb.sbuf("wuq", [128, 4, 768], BF16)
        wukv = kb.sbuf("wukv", [128, 2, 1024], BF16)
        r_wu = kb.res("wu")
        load_w(wst_pool, w_uq[l], 16, 768, l, wuq, r_wu, r_win2)
        load_w(wst_pool, w_ukv[l], 20, 1024, l, wukv, r_wu, r_win2)
        cin_pool = Pool(kb, "cin", [128, 6, 512], BF16, 2)
        krin_pool = Pool(kb, "krin", [64, 512], BF16, 2)
        cs_pool = Pool(kb, "cs", [64, 2, 512], F32, 2)
        sqc_pool = Pool(kb, "sqc", [128, 6, 512], BF16, 1)
        rstd_pool = Pool(kb, "rstd", [128, 2, 512], F32, 1)
        cn_pool = Pool(kb, "cn", [128, 6, 512], BF16, 2)
        ps_mm = Pool(kb, "mm", [128, 512], F32, 6, space="psum")
        ev_pool = Pool(kb, "ev", [128, 512], BF16, 4)
        qpe_pool = Pool(kb, "qpe", [64, 512], BF16, 2)
        t1_pool = Pool(kb, "t1", [64, 512], F32, 2)
        t2_pool = Pool(kb, "t2", [64, 512], F32, 2)
        vst_pool = Pool(kb, "vst", [128, 4, 129], BF16, 2)
        for t, r in vst_pool.items:
            kb.op("pool", lambda e, t=t: e.memset(t[:], 1.0), writes=[r])

        def rope_store(src, r_src, cs, r_cs, dst_ap, r_dst):
            pr, r_pr = ps_mm.next()
            kb.op("pe", lambda e: e.matmul(pr[0:64, :], lhsT=rotb[:], rhs=src[:], start=True, stop=True),
                  reads=[r_src, r_id], writes=[r_pr])
            t1, r_t1 = t1_pool.next()
            t2, r_t2 = t2_pool.next()
            kb.op("dve", lambda e: e.tensor_tensor(out=t1[:], in0=src[:], in1=cs[:, 0, :], op=ALU.mult),
                  reads=[r_src, r_cs], writes=[r_t1])
            kb.op("dve", lambda e: e.tensor_tensor(out=t2[:], in0=pr[0:64, :], in1=cs[:, 1, :], op=ALU.mult),
                  reads=[r_pr, r_cs], writes=[r_t2])
            ev, r_ev = ev_pool.next()
            kb.op("dve", lambda e: e.tensor_tensor(out=ev[0:64, :], in0=t1[:], in1=t2[:], op=ALU.add),
                  reads=[r_t1, r_t2], writes=[r_ev])
            kb.dma("pool", dst_ap, ev[0:64, :], reads=[r_ev], writes=[r_dst], chan_res=r_ev)

        for b in range(NB):
            bs = slice(b * 512, (b + 1) * 512)
            cin, r_cin = cin_pool.next()
            kb.dma("sp", cin[:], HT[24:30, :, bs].rearrange("c p t -> p c t"), reads=[r_HT], writes=[r_cin], chan_res=r_cin)
            krin, r_krin = krin_pool.next()
            kb.dma("sp", krin[:], HT[30, 0:64, bs], reads=[r_HT], writes=[r_krin], chan_res=r_krin)
            cs, r_cs = cs_pool.next()
            kb.dma("sp", cs[:], cs_in[:, :, bs].rearrange("c p t -> p c t"), reads=[r_const], writes=[r_cs], chan_res=r_cs)
            sqc, r_sqc = sqc_pool.next()
            kb.op("act", lambda e, sqc=sqc, cin=cin: e.activation(out=sqc[:], in_=cin[:], func=AF.Square), reads=[r_cin], writes=[r_sqc])
            rstd, r_rstd = rstd_pool.next()
            cn, r_cn = cn_pool.next()
            for gi, (c0, nchunk) in enumerate([(0, 4), (4, 2)]):
                pss, r_pss = ps_mm.next()
                for k in range(nchunk):
                    kb.op("pe", lambda e, k=k, c0=c0, nchunk=nchunk, pss=pss, sqc=sqc: e.matmul(
                        pss[:], lhsT=onesb[:], rhs=sqc[:, c0 + k, :], start=(k == 0), stop=(k == nchunk - 1)),
                        reads=[r_sqc, r_id], writes=[r_pss])
                kb.op("act", lambda e, gi=gi, nchunk=nchunk, pss=pss, rstd=rstd: e.activation(
                    out=rstd[:, gi, :], in_=pss[:], func=AF.Ln, bias=epsb[:, 0:1], scale=1.0 / (nchunk * 128)),
                    reads=[r_pss, r_gn], writes=[r_rstd])
                kb.op("act", lambda e, gi=gi, rstd=rstd: e.activation(out=rstd[:, gi, :], in_=rstd[:, gi, :], func=AF.Exp, scale=-0.5),
                      reads=[r_rstd], writes=[r_rstd])
                for k in range(nchunk):
                    kb.op("dve", lambda e, k=k, c0=c0, gi=gi, cn=cn, cin=cin, rstd=rstd: e.tensor_tensor(
                        out=cn[:, c0 + k, :], in0=cin[:, c0 + k, :], in1=rstd[:, gi, :], op=ALU.mult),
                        reads=[r_cin, r_rstd], writes=[r_cn])
            for h in range(4):
                pm, r_pm = ps_mm.next()
                for k in range(4):
                    kb.op("pe", lambda e, k=k, h=h, pm=pm, cn=cn: e.matmul(
                        pm[:], lhsT=wuq[:, k, h * 192:h * 192 + 128], rhs=cn[:, k, :], start=(k == 0), stop=(k == 3)),
                        reads=[r_wu, r_cn], writes=[r_pm])
                ev, r_ev = ev_pool.next()
                kb.op("act", lambda e, pm=pm, ev=ev: e.copy(out=ev[:], in_=pm[:]), reads=[r_pm], writes=[r_ev])
                kb.dma("pool", QCN[h, :, bs], ev[:], reads=[r_ev], writes=[r_QCN], chan_res=r_ev)
                pm, r_pm = ps_mm.next()
                for k in range(4):
                    kb.op("pe", lambda e, k=k, h=h, pm=pm, cn=cn: e.matmul(
                        pm[0:64, :], lhsT=wuq[:, k, h * 192 + 128:h * 192 + 192], rhs=cn[:, k, :], start=(k == 0), stop=(k == 3)),
                        reads=[r_wu, r_cn], writes=[r_pm])
                qpe, r_qpe = qpe_pool.next()
                kb.op("act", lambda e, pm=pm, qpe=qpe: e.copy(out=qpe[:], in_=pm[0:64, :]), reads=[r_pm], writes=[r_qpe])
                rope_store(qpe, r_qpe, cs, r_cs, QCP[h, :, bs], r_QCP)
                pm, r_pm = ps_mm.next()
                for k in range(2):
                    kb.op("pe", lambda e, k=k, h=h, pm=pm, cn=cn: e.matmul(
                        pm[:], lhsT=wukv[:, k, h * 256:h * 256 + 128], rhs=cn[:, 4 + k, :], start=(k == 0), stop=(k == 1)),
                        reads=[r_wu, r_cn], writes=[r_pm])
                ev, r_ev = ev_pool.next()
                kb.op("act", lambda e, pm=pm, ev=ev: e.copy(out=ev[:], in_=pm[:]), reads=[r_pm], writes=[r_ev])
                kb.dma("pool", KCN[h, :, bs], ev[:], reads=[r_ev], writes=[r_KCN], chan_res=r_ev)
            for j in range(4):
                tok = b * 512 + j * 128
                pm, r_pm = ps_mm.next()
                for k in range(2):
                    kb.op("pe", lambda e, k=k, j=j, pm=pm, cn=cn: e.matmul(
                        pm[:].rearrange("p (h d) -> p h d", d=128), lhsT=cn[:, 4 + k, j * 128:(j + 1) * 128],
                        rhs=wukv[:, k, :].rearrange("p (h c) -> p h c", c=256)[:, :, 128:256], start=(k == 0), stop=(k == 1)),
                        reads=[r_wu, r_cn], writes=[r_pm])
                vs, r_vs = vst_pool.next()
                kb.op("act", lambda e, pm=pm, vs=vs: e.copy(out=vs[:, :, 0:128], in_=pm[:].rearrange("p (h d) -> p h d", d=128)),
                      reads=[r_pm], writes=[r_vs])
                kb.dma("pool", VC[tok:tok + 128, :, :], vs[:], reads=[r_vs], writes=[r_VC], chan_res=r_vs)
            rope_store(krin, r_krin, cs, r_cs, KCP[:, bs], r_KCP)
        kb.end_phase()

    def attn_block(P, kts, s_mm, tab, v_rhs, scale, finals):
        nk = len(kts)
        pss = {}
        AHEAD = 2
        for a in range(min(AHEAD, nk)):
            pss[a] = P.ps_s.next()
            s_mm(kts[a], pss[a][0], pss[a][1])
        for i, kt in enumerate(kts):
            ps, r_ps = pss.pop(i)
            if i + AHEAD < nk:
                pss[i + AHEAD] = P.ps_s.next()
                s_mm(kts[i + AHEAD], pss[i + AHEAD][0], pss[i + AHEAD][1])
            pt, r_pt = P.pt_pool.next()
            tb = tab(kt)
            if tb is None:
                kb.op("act", lambda e, ps=ps, pt=pt: e.activation(out=pt[:], in_=ps[:], func=AF.Exp, scale=scale),
                      reads=[r_ps], writes=[r_pt])
            else:
                ex, r_ex = P.ex_pool.next()
                kb.op("act", lambda e, ps=ps, ex=ex: e.activation(out=ex[:], in_=ps[:], func=AF.Exp, scale=scale),
                      reads=[r_ps], writes=[r_ex])
                kb.op("dve", lambda e, ex=ex, pt=pt, tb=tb: e.tensor_tensor(out=pt[:], in0=ex[:], in1=tb, op=ALU.mult),
                      reads=[r_ex, P.r_tab], writes=[r_pt])
            for g in range(4):
                po, r_po = P.ps_o[g]
                kb.op("pe", lambda e, g=g, po=po, pt=pt, kt=kt, i=i: e.matmul(
                    po[:, 0:129], lhsT=pt[:, g * 128:(g + 1) * 128], rhs=v_rhs(kt, g), start=(i == 0), stop=(i == nk - 1)),
                    reads=[r_pt, P.r_v], writes=[r_po])
        for (g, out_ap, r_out, extra) in finals:
            po, r_po = P.ps_o[g]
            dn, r_dn = P.den_pool.next()
            if extra is not None:
                kb.op("dve", lambda e, po=po, dn=dn, extra=extra: e.tensor_scalar(out=dn[:, 0:1], in0=po[:, 128:129], scalar1=extra,
                                                                                 scalar2=None, op0=ALU.add),
                      reads=[r_po, r_gn], writes=[r_dn])
                kb.op("dve", lambda e, dn=dn: e.reciprocal(out=dn[:, 1:2], in_=dn[:, 0:1]), reads=[r_dn], writes=[r_dn])
            else:
                kb.op("dve", lambda e, po=po, dn=dn: e.reciprocal(out=dn[:, 1:2], in_=po[:, 128:129]), reads=[r_po], writes=[r_dn])
            kb.op("dve", lambda e, po=po, dn=dn, out_ap=out_ap: e.tensor_scalar(out=out_ap, in0=po[:, 0:128], scalar1=dn[:, 1:2],
                                                                               scalar2=None, op0=ALU.mult),
                  reads=[r_po, r_dn], writes=[r_out])

    def attn_pools():
        P = Ctx()
        P.ps_s = Pool(kb, "pss", [128, 512], F32, 4, space="psum")
        P.ps_o = [(kb.psum("pso%d" % g, [128, 512], F32), kb.res("pso%d" % g)) for g in range(4)]
        P.pt_pool = Pool(kb, "pt", [128, 512], BF16, 4)
        P.ex_pool = Pool(kb, "ex", [128, 512], BF16, 3)
        P.den_pool = Pool(kb, "den", [128, 2], F32, 4)
        P.r_tab = kb.res("tab")
        P.r_v = kb.res("vres")
        return P

    def phase2a(l):
        kb.begin_phase()
        P = attn_pools()
        QA = kb.sbuf("QA", [128, 8, S], BF16)
        KA = kb.sbuf("KA", [128, 2, S], BF16)
        VAs = kb.sbuf("VAs", [128, NT, 2, 129], BF16)
        tab32 = kb.sbuf("tab32", [128, 3072], F32)
        tabA = kb.sbuf("tabAb", [128, 3, 8, 128], BF16)
        r_q = kb.res("QAres")
        kb.dma("sp", QA[:], HT[0:8, :, :].rearrange("c p t -> p c t"), reads=[r_HT], writes=[r_q], chan_res=r_q)
        kb.dma("sp", KA[:], HT[8:10, :, :].rearrange("c p t -> p c t"), reads=[r_HT], writes=[P.r_v], chan_res=P.r_v)
        kb.dma("sp", VAs[:], VA.rearrange("(n p) g d -> p n g d", p=128), reads=[r_VA], writes=[P.r_v], chan_res=P.r_v)
        kb.dma("sp", tab32[:], tabA_in, reads=[r_const], writes=[P.r_tab], chan_res=P.r_tab)
        kb.op("dve", lambda e: e.tensor_copy(out=tabA[:].rearrange("p a h q -> p (a h q)"), in_=tab32[:]), reads=[P.r_tab], writes=[P.r_tab])
        o_pool = Pool(kb, "ost", [128, 1024], F32, 2)
        for n in range(NT):
            ost, r_ost = o_pool.next()
            for kg in range(2):
                kts = [m for m in (n - 1, n, n + 1) if 0 <= m < NT]

                def s_mm(kt, ps, r_ps, kg=kg, n=n):
                    kb.op("pe", lambda e: e.matmul(ps[:].rearrange("p (h q) -> p h q", q=128), lhsT=KA[:, kg, kt * 128:(kt + 1) * 128],
                                                   rhs=QA[:, 4 * kg:4 * kg + 4, n * 128:(n + 1) * 128], start=True, stop=True),
                          reads=[P.r_v, r_q], writes=[r_ps])
                attn_block(P, kts, s_mm,
                           lambda kt, kg=kg, n=n: tabA[:, kt - n + 1, 4 * kg:4 * kg + 4, :].rearrange("p h q -> p (h q)"),
                           lambda kt, g, kg=kg: VAs[:, kt, kg, :], SC_A,
                           [(g, ost[:, (4 * kg + g) * 128:(4 * kg + g + 1) * 128], r_ost, esink[:, l, 4 * kg + g:4 * kg + g + 1]) for g in range(4)])
            kb.dma("pool", O[n * 128:(n + 1) * 128, 0:1024], ost[:], reads=[r_ost], writes=[r_O], chan_res=r_ost)
        kb.end_phase()

    def phase2b(l):
        kb.begin_phase()
        P = attn_pools()
        QB = kb.sbuf("QB", [128, 4, S], BF16)
        KBs = kb.sbuf("KBs", [128, 4, S], BF16)
        VBs = kb.sbuf("VBs", [128, NT, 4, 129], BF16)
        tabB = kb.sbuf("tabB", [128, NTAB, 4, 128], BF16)
        mk32 = kb.sbuf("mk32", [128, NTAB, 128], F32)
        bst_pool = Pool(kb, "bst", [128, 4, 128], F32, 2)
        r_q = kb.res("QBres")
        kb.dma("sp", QB[:], HT[12:16, :, :].rearrange("c p t -> p c t"), reads=[r_HT], writes=[r_q], chan_res=r_q)
        kb.dma("sp", KBs[:], HT[16:20, :, :].rearrange("c p t -> p c t"), reads=[r_HT], writes=[P.r_v], chan_res=P.r_v)
        kb.dma("sp", VBs[:], VB.rearrange("(n p) g d -> p n g d", p=128), reads=[r_VB], writes=[P.r_v], chan_res=P.r_v)
        kb.dma("sp", mk32[:], maskB_in, reads=[r_const], writes=[P.r_tab], chan_res=P.r_tab)
        for ti in range(NTAB):
            bst, r_bst = bst_pool.next()
            kb.dma("sp", bst[:], biasB_in[l, :, ti, :, :], reads=[r_const], writes=[r_bst], chan_res=r_bst)
            kb.op("act", lambda e, bst=bst: e.activation(out=bst[:], in_=bst[:], func=AF.Exp), reads=[r_bst], writes=[r_bst])
            for h in range(4):
                kb.op("dve", lambda e, bst=bst, ti=ti, h=h: e.tensor_tensor(out=tabB[:, ti, h, :], in0=bst[:, h, :], in1=mk32[:, ti, :], op=ALU.mult),
                      reads=[r_bst, P.r_tab], writes=[P.r_tab])
        o_pool = Pool(kb, "ost", [128, 512], F32, 2)
        for n in range(NT):
            ost, r_ost = o_pool.next()
            lst = schedB[n]
            tmap = dict(lst)

            def s_mm(kt, ps, r_ps, n=n):
                for h in range(4):
                    kb.op("pe", lambda e, h=h: e.matmul(ps[:, h * 128:(h + 1) * 128], lhsT=KBs[:, h, kt * 128:(kt + 1) * 128],
                                                        rhs=QB[:, h, n * 128:(n + 1) * 128], start=True, stop=True),
                          reads=[P.r_v, r_q], writes=[r_ps])
            attn_block(P, [m for m, _ in lst], s_mm,
                       lambda kt, tmap=tmap: tabB[:, tmap[kt], :, :].rearrange("p h q -> p (h q)"),
                       lambda kt, g: VBs[:, kt, g, :], SC_A,
                       [(g, ost[:, g * 128:(g + 1) * 128], r_ost, None) for g in range(4)])
            kb.dma("pool", O[n * 128:(n + 1) * 128, 1024:1536], ost[:], reads=[r_ost], writes=[r_O], chan_res=r_ost)
        kb.end_phase()

    def phase2c(l):
        kb.begin_phase()
        P = attn_pools()
        KN = kb.sbuf("KN", [128, 4, S], BF16)
        KP = kb.sbuf("KP", [64, S], BF16)
        VCs = kb.sbuf("VCs", [128, NT, 4, 129], BF16)
        kb.dma("sp", KN[:], KCN.rearrange("c p t -> p c t"), reads=[r_KCN], writes=[P.r_v], chan_res=P.r_v)
        kb.dma("sp", KP[:], KCP, reads=[r_KCP], writes=[P.r_v], chan_res=P.r_v)
        kb.dma("sp", VCs[:], VC.rearrange("(n p) g d -> p n g d", p=128), reads=[r_VC], writes=[P.r_v], chan_res=P.r_v)
        qn_pool = Pool(kb, "qn", [128, 4, 512], BF16, 2)
        qp_pool = Pool(kb, "qp", [64, 4, 512], BF16, 2)
        o_pool = Pool(kb, "ost", [128, 4, 512], F32, 2)
        for b in range(NB):
            bs = slice(b * 512, (b + 1) * 512)
            qn, r_qn = qn_pool.next()
            qp, r_qp = qp_pool.next()
            kb.dma("sp", qn[:], QCN[:, :, bs].rearrange("c p t -> p c t"), reads=[r_QCN], writes=[r_qn], chan_res=r_qn)
            kb.dma("sp", qp[:], QCP[:, :, bs].rearrange("c p t -> p c t"), reads=[r_QCP], writes=[r_qp], chan_res=r_qp)
            ost, r_ost = o_pool.next()
            for h in range(4):
                def s_mm(kt, ps, r_ps, h=h, qn=qn, qp=qp, r_qn=r_qn, r_qp=r_qp):
                    kb.op("pe", lambda e: e.matmul(ps[:], lhsT=KN[:, h, kt * 128:(kt + 1) * 128], rhs=qn[:, h, :], start=True, stop=False),
                          reads=[P.r_v, r_qn], writes=[r_ps])
                    kb.op("pe", lambda e: e.matmul(ps[:], lhsT=KP[:, kt * 128:(kt + 1) * 128], rhs=qp[:, h, :], start=False, stop=True),
                          reads=[P.r_v, r_qp], writes=[r_ps])
                attn_block(P, list(range(NT)), s_mm, lambda kt: None, lambda kt, g, h=h: VCs[:, kt, h, :], SC_C,
                           [(g, ost[:, g, h * 128:(h + 1) * 128], r_ost, None) for g in range(4)])
            kb.dma("pool", O[bs, 1536:2048].rearrange("(g p) d -> p g d", p=128), ost[:], reads=[r_ost], writes=[r_O], chan_res=r_ost)
        kb.end_phase()

    def phase3(l, x_src, r_xsrc, X1, r_X1):
        kb.begin_phase()
        wst_pool = Pool(kb, "wst", [128, 2048], F32, 2)
        wo = kb.sbuf("wo", [128, KC, D], BF16)
        r_wo = kb.res("wo")
        load_w(wst_pool, w_o[l], 22, D, l, wo, r_wo, r_win2)
        g2 = kb.sbuf("g2", [128, D], F32); r_g2 = kb.res("g2")
        kb.dma("sp", g2[:], ln2b_in[l], reads=[r_const], writes=[r_g2], chan_res=r_g2)
        ot_pool = Pool(kb, "ot", [128, D], F32, 2)
        xt_pool = Pool(kb, "xt", [128, D], F32, 2)
        x1_pool = Pool(kb, "x1", [128, D], F32, 2)
        xn_pool = Pool(kb, "xn", [128, D], BF16, 2)
        sq_pool = Pool(kb, "sq", [128, D], BF16, 1)
        st_pool = Pool(kb, "stat", [128, 4], F32, 4)
        oT_pool = Pool(kb, "oT", [128, KC, 128], BF16, 2)
        xn2T_pool = Pool(kb, "xn2T", [128, KC, 512], BF16, 1)
        ps_tp = Pool(kb, "tp", [128, KC, 128], BF16, 2, space="psum")
        ps_mm = Pool(kb, "mm", [128, 512], F32, 4, space="psum")
        rmsnorm_tile, transpose_tile = mk_norm_helpers(st_pool, sq_pool, ps_tp)
        def part_a(n):
            tok = n * 128
            c = Ctx()
            ot, r_ot = ot_pool.next()
            kb.dma("sp", ot[:], O[tok:tok + 128, :], reads=[r_O], writes=[r_ot], chan_res=r_ot)
            c.xt, c.r_xt = xt_pool.next()
            kb.dma("sp", c.xt[:], x_src[tok:tok + 128, :], reads=[r_xsrc], writes=[c.r_xt], chan_res=c.r_xt)
            on, r_on = xn_pool.next()
            rmsnorm_tile(ot, r_ot, [(0, 1024), (1024, 512), (1536, 512)], on, r_on)
            c.oT, c.r_oT = oT_pool.next()
            transpose_tile(on, r_on, c.oT, c.r_oT, 0)
            return c

        def part_b(n, c):
            tok = n * 128
            c.x1, c.r_x1 = x1_pool.next()
            for nb in range(4):
                pm, r_pm = ps_mm.next()
                for k in range(KC):
                    kb.op("pe", lambda e, k=k, nb=nb, pm=pm, oT=c.oT: e.matmul(
                        pm[:], lhsT=oT[:, k, :], rhs=wo[:, k, nb * 512:(nb + 1) * 512], start=(k == 0), stop=(k == KC - 1)),
                        reads=[r_wo, c.r_oT], writes=[r_pm])
                kb.op("dve", lambda e, nb=nb, pm=pm, x1=c.x1, xt=c.xt: e.tensor_tensor(
                    out=x1[:, nb * 512:(nb + 1) * 512], in0=pm[:], in1=xt[:, nb * 512:(nb + 1) * 512], op=ALU.add),
                    reads=[r_pm, c.r_xt], writes=[c.r_x1])
            kb.dma("pool", X1[tok:tok + 128, :], c.x1[:], reads=[c.r_x1], writes=[r_X1], chan_res=c.r_x1)

        cur = [None]

        def part_c(n, c):
            b, j = n // 4, n % 4
            if j == 0:
                cur[0] = xn2T_pool.next()
            xn2T, r_xn2T = cur[0]
            xn, r_xn = xn_pool.next()
            rmsnorm_tile(c.x1, c.r_x1, [(0, D)], xn, r_xn, gain=(g2, r_g2))
            transpose_tile(xn, r_xn, xn2T, r_xn2T, j)
            if j == 3:
                kb.dma("pool", XN2T[:, :, b * 512:(b + 1) * 512].rearrange("c p t -> p c t"), xn2T[:], reads=[r_xn2T], writes=[r_XN2T],
                       chan_res=r_xn2T)

        ca = part_a(0)
        for n in range(NT):
            part_b(n, ca)
            cn = part_a(n + 1) if n + 1 < NT else None
            part_c(n, ca)
            ca = cn
        kb.end_phase()

    def phase4a(l):
        kb.begin_phase()
        wst_pool = Pool(kb, "wst", [128, 1024], F32, 2)
        wq = kb.sbuf("wq", [128, KC, 1024], BF16)
        r_wq = kb.res("wq")
        load_w(wst_pool, w_pq[l], None, 1024, l, wq, r_wq, r_win2)
        xb_pool = Pool(kb, "xb", [128, KC, 512], BF16, 2)
        ps_mm = Pool(kb, "mm", [128, 512], F32, 4, space="psum")
        ev_pool = Pool(kb, "ev", [128, 512], BF16, 4)
        for b in range(NB):
            bs = slice(b * 512, (b + 1) * 512)
            xb, r_xb = xb_pool.next()
            kb.dma("sp", xb[:], XN2T[:, :, bs].rearrange("c p t -> p c t"), reads=[r_XN2T], writes=[r_xb], chan_res=r_xb)
            for h in range(8):
                pm, r_pm = ps_mm.next()
                for k in range(KC):
                    kb.op("pe", lambda e, k=k, h=h, pm=pm, xb=xb: e.matmul(
                        pm[:], lhsT=wq[:, k, h * 128:(h + 1) * 128], rhs=xb[:, k, :], start=(k == 0), stop=(k == KC - 1)),
                        reads=[r_wq, r_xb], writes=[r_pm])
                ev, r_ev = ev_pool.next()
                kb.op("act", lambda e, pm=pm, ev=ev: e.copy(out=ev[:], in_=pm[:]), reads=[r_pm], writes=[r_ev])
                kb.dma("pool", QT[h, :, bs], ev[:], reads=[r_ev], writes=[r_QT], chan_res=r_ev)
        kb.end_phase()

    DELTA = 1e-5

    def phase4b(l):
        kb.begin_phase()
        sk32 = kb.sbuf("sk32", [128, 256], F32)
        skb = kb.sbuf("skb", [128, 256], BF16)
        r_sk = kb.res("sk")
        kb.dma("sp", sk32[:], sk_in[l], reads=[r_const], writes=[r_sk], chan_res=r_sk)
        kb.op("dve", lambda e: e.tensor_copy(out=skb[:], in_=sk32[:]), reads=[r_sk], writes=[r_sk])
        q_pool = Pool(kb, "qt", [128, 8, 128], BF16, 2)
        s_pool = Pool(kb, "s", [128, 8, 2, 128], F32, 2)
        top_pool = Pool(kb, "top", [128, 8, 2, 16], F32, 2)
        tmp = kb.sbuf("tmpm", [128, 16, 128], F32)
        r_tmpc = [kb.res("tmpc%d" % i) for i in range(16)]
        cand = kb.sbuf("cand", [128, 8, 256], F32); r_cand = kb.res("cand")
        best = kb.sbuf("best", [128, 8, 16], F32); r_best = kb.res("best")
        dd = kb.sbuf("dd", [128, 8, 16], F32)
        zz = kb.sbuf("zz", [128, 4, 8], F32)
        cc = kb.sbuf("cc", [128, 8, 16], F32)
        cb = kb.sbuf("cb", [128, 8, 16], BF16)
        thr = kb.sbuf("thr", [128, 8, 16], F32)
        r_sm = kb.res("small")
        E2 = kb.sbuf("E2", [128, 8, 128], BF16); r_E2 = kb.res("E2")
        A = kb.sbuf("A", [128, 8, 16, 128], BF16); r_A = kb.res("A")
        Bms = [(kb.sbuf("Bm%d" % i, [128, 8, 16, 64], BF16), kb.res("Bm%d" % i)) for i in range(2)]
        AT = kb.sbuf("AT", [128, 128, 128], BF16); r_AT = kb.res("AT")
        BT = kb.sbuf("BT", [128, 64, 128], BF16); r_BT = kb.res("BT")
        cT = kb.sbuf("cT", [128, 128], BF16); r_cT = kb.res("cT")
        g_pool = Pool(kb, "gt", [128, 64, 128], BF16, 2)
        ps_s = Pool(kb, "pss", [128, 2, 256], F32, 1, space="psum")
        ps_tp = Pool(kb, "tp", [128, 16, 128], BF16, 2, space="psum")
        ps_g = Pool(kb, "pg", [128, 8, 64], F32, 3, space="psum")
        A2 = A[:].rearrange("p h a i -> p (h a) i")
        evtog = [0]
        def front1(n):
            ts = slice(n * 128, (n + 1) * 128)
            qt, r_qt = q_pool.next()
            kb.dma("sp", qt[:], QT[:, :, ts].rearrange("h p t -> p h t"), reads=[r_QT], writes=[r_qt], chan_res=r_qt)
            s, r_s = s_pool.next()
            for hp in range(4):
                ps, r_ps = ps_s.next()
                for hh in range(2):
                    kb.op("pe", lambda e, hp=hp, hh=hh, ps=ps, qt=qt: e.matmul(ps[:, hh, :], lhsT=qt[:, 2 * hp + hh, :], rhs=skb[:], start=True, stop=True),
                          reads=[r_qt, r_sk], writes=[r_ps])
                kb.op("act", lambda e, hp=hp, ps=ps, s=s: e.copy(out=s[:, 2 * hp:2 * hp + 2, :, :].rearrange("p h c n -> p h (c n)"), in_=ps[:]),
                      reads=[r_ps], writes=[r_s])
            return s, r_s

        nxt = front1(0)
        for n in range(NT):
            s, r_s = nxt
            top, _r_top_unused = top_pool.next()
            r_tc = [kb.res("topc") for _ in range(16)]
            r_top = kb.res("topall")
            for h in range(8):
                for c in range(2):
                    kb.op("dve", lambda e, h=h, c=c, top=top, s=s: e.max(out=top[:, h, c, 0:8], in_=s[:, h, c, :]), reads=[r_s], writes=[r_tc[2 * h + c]])
            for h in range(8):
                for c in range(2):
                    kb.op("dve", lambda e, h=h, c=c, top=top, s=s: e.match_replace(out=tmp[:, 2 * h + c, :], in_to_replace=top[:, h, c, 0:8],
                                                                                   in_values=s[:, h, c, :], imm_value=-1e30),
                          reads=[r_s, r_tc[2 * h + c]], writes=[r_tmpc[2 * h + c]])
            for h in range(8):
                for c in range(2):
                    kb.op("dve", lambda e, h=h, c=c, top=top: e.max(out=top[:, h, c, 8:16], in_=tmp[:, 2 * h + c, :]),
                          reads=[r_tmpc[2 * h + c]], writes=[r_tc[2 * h + c], r_top] if (h == 7 and c == 1) else [r_tc[2 * h + c]])
            kb.op("dve", lambda e, top=top: e.tensor_tensor(out=cand[:].rearrange("p h (a b) -> p h a b", a=16),
                                                            in0=top[:, :, 0, :].unsqueeze(3).to_broadcast([128, 8, 16, 16]),
                                                            in1=top[:, :, 1, :].unsqueeze(2).to_broadcast([128, 8, 16, 16]), op=ALU.add),
                  reads=r_tc, writes=[r_cand, r_top])
            r_bc = [kb.res("bestc") for _ in range(8)]
            tmp2 = tmp[:].rearrange("p (h c) n -> p h (c n)", c=2)
            for h in range(8):
                kb.op("dve", lambda e, h=h: e.max(out=best[:, h, 0:8], in_=cand[:, h, :]), reads=[r_cand], writes=[r_bc[h]])
            for h in range(8):
                kb.op("dve", lambda e, h=h: e.match_replace(out=tmp2[:, h, :], in_to_replace=best[:, h, 0:8], in_values=cand[:, h, :], imm_value=-1e30),
                      reads=[r_cand, r_bc[h]], writes=[r_tmpc[2 * h], r_tmpc[2 * h + 1]])
            for h in range(8):
                kb.op("dve", lambda e, h=h: e.max(out=best[:, h, 8:16], in_=tmp2[:, h, :]), reads=[r_tmpc[2 * h], r_tmpc[2 * h + 1]],
                      writes=[r_bc[h], r_best] if h == 7 else [r_bc[h]])
            kb.op("dve", lambda e: e.tensor_copy(out=dd[:, 0, 0:1], in_=best[:, 0, 0:1]), reads=r_bc, writes=[r_best, r_sm])
            kb.op("dve", lambda e: e.tensor_tensor(out=dd[:], in0=best[:], in1=best[:, :, 0:1].to_broadcast([128, 8, 16]), op=ALU.subtract),
                  reads=[r_best], writes=[r_sm])
            kb.op("act", lambda e: e.activation(out=dd[:], in_=dd[:], func=AF.Exp), reads=[r_sm], writes=[r_sm])
            kb.op("dve", lambda e: e.tensor_reduce(out=zz[:, 0, :], in_=dd[:], axis=AX.X, op=ALU.add), reads=[r_sm], writes=[r_sm])
            kb.op("act", lambda e: e.activation(out=zz[:, 1, :], in_=zz[:, 0, :], func=AF.Ln), reads=[r_sm], writes=[r_sm])
            kb.op("dve", lambda e: e.tensor_tensor(out=zz[:, 2, :], in0=zz[:, 1, :], in1=best[:, :, 0], op=ALU.add), reads=[r_sm, r_best], writes=[r_sm])
            kb.op("dve", lambda e, top=top: e.tensor_tensor(out=cc[:], in0=top[:, :, 0, :], in1=zz[:, 2, :].unsqueeze(2).to_broadcast([128, 8, 16]),
                                                            op=ALU.subtract), reads=[r_sm, r_top], writes=[r_sm])
            kb.op("act", lambda e: e.activation(out=cb[:], in_=cc[:], func=AF.Exp), reads=[r_sm], writes=[r_sm])
            kb.op("dve", lambda e: e.tensor_scalar(out=zz[:, 3, :], in0=best[:, :, 15], scalar1=-DELTA, scalar2=None, op0=ALU.add),
                  reads=[r_best], writes=[r_sm])
            kb.op("dve", lambda e, top=top: e.tensor_tensor(out=thr[:], in0=zz[:, 3, :].unsqueeze(2).to_broadcast([128, 8, 16]), in1=top[:, :, 0, :],
                                                            op=ALU.subtract), reads=[r_sm, r_top], writes=[r_sm])
            kb.op("act", lambda e, s=s: e.activation(out=E2[:], in_=s[:, :, 1, :], func=AF.Exp), reads=[r_s], writes=[r_E2])
            kb.op("dve", lambda e, s=s, top=top: e.tensor_tensor(out=A[:], in0=s[:, :, 0, :].unsqueeze(2).to_broadcast([128, 8, 16, 128]),
                                                                 in1=top[:, :, 0, :].unsqueeze(3).to_broadcast([128, 8, 16, 128]), op=ALU.is_equal),
                  reads=[r_s, r_top], writes=[r_A])
            pt, r_pt = ps_tp.next()
            kb.op("pe", lambda e, pt=pt: e.transpose(out=pt[:, 0, :], in_=cb[:].rearrange("p h a -> p (h a)"), identity=ident[:]),
                  reads=[r_sm, r_id], writes=[r_pt])
            kb.op("act", lambda e, pt=pt: e.copy(out=cT[:], in_=pt[:, 0, :]), reads=[r_pt], writes=[r_cT])
            def build_B(jh):
                Bm_, r_Bm_ = Bms[jh]
                js = slice(jh * 64, (jh + 1) * 64)
                kb.op("dve", lambda e, s=s, js=js, Bm_=Bm_: e.tensor_tensor(out=Bm_[:], in0=s[:, :, 1, js].unsqueeze(2).to_broadcast([128, 8, 16, 64]),
                                                                            in1=thr[:].unsqueeze(3).to_broadcast([128, 8, 16, 64]), op=ALU.is_ge),
                      reads=[r_s, r_sm], writes=[r_Bm_])
                kb.op("pool", lambda e, js=js, Bm_=Bm_: e.tensor_tensor(out=Bm_[:], in0=Bm_[:], in1=E2[:, :, js].unsqueeze(2).to_broadcast([128, 8, 16, 64]),
                                                                        op=ALU.mult), reads=[r_E2, r_Bm_], writes=[r_Bm_])

            a_pts = []
            for i0 in range(0, 128, 16):
                pt, r_pt = ps_tp.next()
                for ii in range(16):
                    kb.op("pe", lambda e, pt=pt, ii=ii, i0=i0: e.transpose(out=pt[:, ii, :], in_=A2[:, :, i0 + ii], identity=ident[:]),
                          reads=[r_A, r_id], writes=[r_pt])
                if i0 == 0:
                    build_B(0)
                kb.op("dve", lambda e, pt=pt, i0=i0: e.tensor_tensor(out=AT[:, i0:i0 + 16, :], in0=pt[:],
                                                                     in1=cT[:].unsqueeze(1).to_broadcast([128, 16, 128]), op=ALU.mult),
                      reads=[r_pt, r_cT], writes=[r_AT])
            build_B(1)
            if n + 1 < NT:
                nxt = front1(n + 1)
            for jh in range(2):
                Bm_, r_Bm_ = Bms[jh]
                B2_ = Bm_[:].rearrange("p h a j -> p (h a) j")
                js = slice(jh * 64, (jh + 1) * 64)
                for j0 in range(0, 64, 16):
                    pt, r_pt = ps_tp.next()
                    for jj in range(16):
                        kb.op("pe", lambda e, pt=pt, jj=jj, j0=j0, B2_=B2_: e.transpose(out=pt[:, jj, :], in_=B2_[:, :, j0 + jj], identity=ident[:]),
                              reads=[r_Bm_, r_id], writes=[r_pt])
                    kb.op("act", lambda e, pt=pt, j0=j0: e.copy(out=BT[:, j0:j0 + 16, :], in_=pt[:]), reads=[r_pt], writes=[r_BT])
                gt, r_gt = g_pool.next()
                for t0 in range(0, 128, 8):
                    pg, r_pg = ps_g.next()
                    for tt in range(8):
                        kb.op("pe", lambda e, pg=pg, tt=tt, t0=t0: e.matmul(pg[:, tt, :], lhsT=AT[:, :, t0 + tt], rhs=BT[:, :, t0 + tt], start=True, stop=True),
                              reads=[r_AT, r_BT], writes=[r_pg])
                    kb.op("act", lambda e, pg=pg, gt=gt, t0=t0: e.copy(out=gt[:, :, t0:t0 + 8], in_=pg[:].rearrange("p t j -> p j t")),
                          reads=[r_pg], writes=[r_gt])
                kb.dma("pool", GT[n, :, js, :], gt[:], reads=[r_gt], writes=[r_GT], chan_res=r_gt)
        kb.end_phase()

    def phase5(l, X1, r_X1, X2, r_X2):
        kb.begin_phase()
        TBT = min(TBT_MAX, NT)
        TB = TBT * 128
        nblk = NT // TBT
        NSC = 64
        xblk = kb.sbuf("xblk", [128, KC, TB], BF16); r_xblk = kb.res("xblk")
        y_sb = kb.sbuf("ysb", [128, TBT, D], F32)
        r_ysb = [[kb.res("ysb") for _ in range(4)] for _ in range(TBT)]
        ty_pool = Pool(kb, "ty", [128, 512], F32, 2)
        un_pool = Pool(kb, "un", [128, 2, D], BF16, 2)
        v_pool = Pool(kb, "vv", [128, 2, D], BF16, 3)
        uT_pool = Pool(kb, "uT", [128, KC, 128], BF16, 4)
        g_pool = Pool(kb, "gg", [128, TBT, 2, 128], BF16, 2)
        ge_pool = Pool(kb, "ge", [128, 512], BF16, 2)
        hs_pool = Pool(kb, "hs", [128, TB], BF16, 4)
        ps_tp = Pool(kb, "tp", [128, 8, 128], BF16, 3, space="psum")
        ps_a = Pool(kb, "pa", [128, 512], F32, 2, space="psum")
        ps_y = Pool(kb, "py", [128, 512], F32, 3, space="psum")
        U3 = peer_u[l].rearrange("(i j) d -> i j d", j=128)
        V3 = peer_v[l].rearrange("(i j) d -> i j d", j=128)
        for blk in range(nblk):
            t0 = blk * TB
            kb.dma("sp", xblk[:], XN2T[:, :, t0:t0 + TB].rearrange("c p t -> p c t"), reads=[r_XN2T], writes=[r_xblk], chan_res=r_xblk)
            ctx = {}
            for tile in range(TBT):
                tok = t0 + tile * 128
                kb.dma("sp", y_sb[:, tile, :], X1[tok:tok + 128, :], reads=[r_X1], writes=r_ysb[tile], chan_res=r_ysb[tile][1])

            def do_L(sc):
                j0 = 2 * sc
                c = Ctx()
                c.un, c.r_un = un_pool.next()
                kb.dma("pool", c.un[:], U3[:, j0:j0 + 2, :], reads=[r_const], writes=[c.r_un], chan_res=c.r_un)
                ctx[sc] = c

            def do_L2(sc):
                j0 = 2 * sc
                c = ctx[sc]
                c.gg, c.r_gg = g_pool.next()
                c.vv, c.r_vv = v_pool.next()
                kb.dma("sp", c.gg[:], GT[blk * TBT:(blk + 1) * TBT, :, j0:j0 + 2, :].rearrange("n i j t -> i n j t"), reads=[r_GT],
                       writes=[c.r_gg], chan_res=c.r_gg)
                kb.dma("pool", c.vv[:], V3[:, j0:j0 + 2, :], reads=[r_const], writes=[c.r_vv], chan_res=c.r_vv)

            def do_T(sc):
                c = ctx[sc]
                c.uT = []
                for jj in range(2):
                    uT, r_uT = uT_pool.next()
                    for hf in range(2):
                        pt, r_pt = ps_tp.next()
                        for k8 in range(8):
                            k = hf * 8 + k8
                            kb.op("pe", lambda e, k=k, k8=k8, jj=jj, pt=pt, un=c.un: e.transpose(out=pt[:, k8, :], in_=un[:, jj, k * 128:(k + 1) * 128],
                                                                                                identity=ident[:]),
                                  reads=[c.r_un, r_id], writes=[r_pt])
                        kb.op("act", lambda e, pt=pt, uT=uT, hf=hf: e.copy(out=uT[:, hf * 8:(hf + 1) * 8, :], in_=pt[:]), reads=[r_pt], writes=[r_uT])
                    c.uT.append((uT, r_uT))

            def do_A(sc):
                c = ctx[sc]
                c.hs = []
                for jj in range(2):
                    uT, r_uT = c.uT[jj]
                    h_t, r_ht = hs_pool.next()
                    for half in range(TB // 512):
                        pa, r_pa = ps_a.next()
                        for k in range(KC):
                            kb.op("pe", lambda e, k=k, pa=pa, uT=uT, half=half: e.matmul(pa[:], lhsT=uT[:, k, :], rhs=xblk[:, k, half * 512:(half + 1) * 512],
                                                                                          start=(k == 0), stop=(k == KC - 1)),
                                  reads=[r_uT, r_xblk], writes=[r_pa])
                        ge, r_ge = ge_pool.next()
                        kb.op("act", lambda e, pa=pa, ge=ge: e.activation(out=ge[:], in_=pa[:], func=AF.Gelu_apprx_tanh), reads=[r_pa], writes=[r_ge])
                        kb.op("dve", lambda e, ge=ge, h_t=h_t, gg=c.gg, jj=jj, half=half: e.tensor_tensor(
                            out=h_t[:, half * 512:(half + 1) * 512].rearrange("p (n t) -> p n t", t=128), in0=ge[:].rearrange("p (n t) -> p n t", t=128),
                            in1=gg[:, half * 4:(half + 1) * 4, jj, :], op=ALU.mult), reads=[r_ge, c.r_gg], writes=[r_ht])
                    c.hs.append((h_t, r_ht))

            def do_Y(sc):
                c = ctx[sc]
                for tile in range(TBT):
                    for nb in range(4):
                        py, r_py = ps_y.next()
                        for jj in range(2):
                            h_t, r_ht = c.hs[jj]
                            kb.op("pe", lambda e, jj=jj, py=py, h_t=h_t, vv=c.vv, tile=tile, nb=nb: e.matmul(
                                py[:], lhsT=h_t[:, tile * 128:(tile + 1) * 128], rhs=vv[:, jj, nb * 512:(nb + 1) * 512], start=(jj == 0), stop=(jj == 1)),
                                reads=[r_ht, c.r_vv], writes=[r_py])
                        ysl = y_sb[:, tile, nb * 512:(nb + 1) * 512]
                        if (tile * 4 + nb) % 3 == 2:
                            ty, r_ty = ty_pool.next()
                            kb.op("act", lambda e, py=py, ty=ty: e.copy(out=ty[:], in_=py[:]), reads=[r_py], writes=[r_ty])
                            kb.op("pool", lambda e, ty=ty, ysl=ysl: e.tensor_tensor(out=ysl, in0=ty[:], in1=ysl, op=ALU.add),
                                  reads=[r_ty, r_ysb[tile][nb]], writes=[r_ysb[tile][nb]])
                        else:
                            kb.op("dve", lambda e, py=py, ysl=ysl: e.tensor_tensor(out=ysl, in0=py[:], in1=ysl, op=ALU.add),
                                  reads=[r_py, r_ysb[tile][nb]], writes=[r_ysb[tile][nb]])
                del ctx[sc]

            do_L(0)
            do_L2(0)
            do_L(1)
            do_L2(1)
            do_T(0)
            for i in range(NSC):
                if i + 2 < NSC:
                    do_L(i + 2)
                if i + 1 < NSC:
                    do_T(i + 1)
                do_A(i)
                if i >= 1:
                    do_Y(i - 1)
                if i + 2 < NSC:
                    do_L2(i + 2)
            do_Y(NSC - 1)
            for tile in range(TBT):
                tok = t0 + tile * 128
                kb.dma("pool", X2[tok:tok + 128, :], y_sb[:, tile, :], reads=r_ysb[tile], writes=[r_X2], chan_res=r_ysb[tile][0])
        kb.end_phase()

    def phasef(x_src, r_xsrc):
        kb.begin_phase()
        fn = kb.sbuf("fn", [128, D], F32); r_fn = kb.res("fn")
        kb.dma("sp", fn[:], fnorm_in, reads=[r_const], writes=[r_fn], chan_res=r_fn)
        xt_pool = Pool(kb, "xt", [128, D], F32, 2)
        yo_pool = Pool(kb, "yo", [128, D], F32, 2)
        sq_pool = Pool(kb, "sq", [128, D], BF16, 1)
        st_pool = Pool(kb, "stat", [128, 4], F32, 4)
        for n in range(NT):
            xt, r_xt = xt_pool.next()
            kb.dma("sp", xt[:], x_src[n * 128:(n + 1) * 128, :], reads=[r_xsrc], writes=[r_xt], chan_res=r_xt)
            stt, r_st = st_pool.next()
            sq, r_sq = sq_pool.next()
            kb.op("act", lambda e, xt=xt, stt=stt, sq=sq: e.activation(out=sq[:], in_=xt[:], func=AF.Square, accum_out=stt[:, 0:1]),
                  reads=[r_xt], writes=[r_sq, r_st])
            kb.op("act", lambda e, stt=stt: e.activation(out=stt[:, 0:1], in_=stt[:, 0:1], func=AF.Ln, bias=epsb[:, 0:1], scale=1.0 / D),
                  reads=[r_st, r_gn], writes=[r_st])
            kb.op("act", lambda e, stt=stt: e.activation(out=stt[:, 0:1], in_=stt[:, 0:1], func=AF.Exp, scale=-0.5), reads=[r_st], writes=[r_st])
            yo, r_yo = yo_pool.next()
            kb.op("dve", lambda e, xt=xt, stt=stt, yo=yo: e.scalar_tensor_tensor(out=yo[:], in0=xt[:], scalar=stt[:, 0:1], in1=fn[:],
                                                                                 op0=ALU.mult, op1=ALU.mult),
                  reads=[r_xt, r_st, r_fn], writes=[r_yo])
            kb.dma("pool", y_out[n * 128:(n + 1) * 128, :], yo[:], reads=[r_yo], writes=[r_y], chan_res=r_yo)
        kb.end_phase()

    for l in range(L):
        x_src, r_xsrc = (x_in, r_x) if l == 0 else (XB, r_XB)
        phase1(l, x_src, r_xsrc)
        if upto >= 2:
            phase1c(l)
        if upto >= 3:
            phase2a(l)
            phase2b(l)
            phase2c(l)
        if upto >= 4:
            phase3(l, x_src, r_xsrc, XA, r_XA)
        if upto >= 5:
            phase4a(l)
            phase4b(l)
        if upto >= 6:
            phase5(l, XA, r_XA, XB, r_XB)
    if upto >= 7:
        phasef(XB, r_XB)
    outs = [r_HT, r_VA, r_VB, r_QCN, r_QCP, r_KCN, r_KCP, r_VC, r_O, r_XA, r_XB, r_XN2T, r_QT, r_GT, r_y]
    kb.finish_wait("pool", outs)
    kb.emit()
    return nc


def host_consts(S):
    c = {}
    c["ident"] = np.eye(128, dtype=np.float32)
    R = np.zeros((64, 64), np.float32)
    for m in range(32):
        R[m + 32, m] = -1.0
        R[m, m + 32] = 1.0
    c["rotR"] = R
    inv = 10000.0 ** (-(np.arange(32, dtype=np.float64)) / 32.0)
    ang = np.arange(S, dtype=np.float64)[None, :] * np.concatenate([inv, inv])[:, None]
    c["cossin"] = np.stack([np.cos(ang), np.sin(ang)]).astype(np.float32)
    ki = np.arange(128)[:, None]
    qi = np.arange(128)[None, :]
    slopes = np.array([2.0 ** (-8.0 * (h + 1) / 8) for h in range(8)], np.float64)
    tabA = np.zeros((128, 3, 8, 128), np.float64)
    for d, (dist, valid) in enumerate([(128 + qi - ki, qi <= ki), (np.abs(qi - ki), np.ones((128, 128), bool)),
                                       (128 + ki - qi, ki <= qi)]):
        for h in range(8):
            tabA[:, d, h, :] = np.where(valid, np.exp(-slopes[h] * dist), 0.0)
    c["tabA"] = tabA.astype(np.float32).reshape(128, 3 * 8 * 128)
    rows = S // 64
    kr = min(8, rows)
    NT = S // 128
    tabs = {}
    sched = []
    masks, dridx, dcidx = [], [], []
    for n in range(NT):
        qrow = 2 * n + np.arange(128) // 64
        qcol = np.arange(128) % 64
        rstart = np.clip(qrow - kr // 2, 0, rows - kr)
        cstart = np.clip(qcol - 8, 0, 64 - 16)
        lst = []
        for m in range(NT):
            krow = 2 * m + np.arange(128) // 64
            kcol = np.arange(128) % 64
            valid = ((krow[:, None] >= rstart[None, :]) & (krow[:, None] < rstart[None, :] + kr)
                     & (kcol[:, None] >= cstart[None, :]) & (kcol[:, None] < cstart[None, :] + 16))
            if not valid.any():
                continue
            dr = np.clip(krow[:, None] - qrow[None, :] + 7, 0, 14)
            dc = np.clip(kcol[:, None] - qcol[None, :] + 15, 0, 30)
            dr = np.where(valid, dr, 0)
            dc = np.where(valid, dc, 0)
            key = (valid.tobytes(), dr.tobytes(), dc.tobytes())
            if key not in tabs:
                tabs[key] = len(tabs)
                masks.append(valid.astype(np.float32))
                dridx.append(dr)
                dcidx.append(dc)
            lst.append((m, tabs[key]))
        sched.append(lst)
    c["maskB"] = np.ascontiguousarray(np.stack(masks, 1))
    c["_dr"] = np.stack(dridx, 1)
    c["_dc"] = np.stack(dcidx, 1)
    c["_schedB"] = sched
    c["_ntab"] = len(masks)
    return c


def gather_biasB(b_rel_bias_l, c):
    g = b_rel_bias_l[:, c["_dr"], c["_dc"]]
    return np.ascontiguousarray(np.transpose(g, (1, 2, 0, 3))).astype(np.float32)


def gains_pack(inp, L):
    g = np.zeros((L, 128, 64), np.float32)
    for l in range(L):
        g[l, :, 0:16] = inp["ln1"][l].reshape(16, 128).T
        g[l, :, 16:20] = inp["c_q_norm"][l].reshape(4, 128).T
        g[l, :, 20:22] = inp["c_kv_norm"][l].reshape(2, 128).T
        g[l, :, 22:38] = inp["out_norm"][l].reshape(16, 128).T
        g[l, :, 38:54] = inp["ln2"][l].reshape(16, 128).T
    return g


def sk_pack(inp, L):
    sk = np.zeros((L, 128, 256), np.float32)
    for l in range(L):
        sk[l, 0:64, 0:128] = inp["peer_sub_keys"][l, 0].T
        sk[l, 64:128, 128:256] = inp["peer_sub_keys"][l, 1].T
    return sk


def core_inputs(inp, xb, S, L, consts):
    m = dict(x=np.ascontiguousarray(xb), w_in=inp["w_in"][:L], gains=gains_pack(inp, L), ident=consts["ident"],
             cossin=consts["cossin"], rotR=consts["rotR"], tabA=consts["tabA"], maskB=consts["maskB"],
             biasB=np.stack([gather_biasB(inp["b_rel_bias"][l], consts) for l in range(L)]),
             sinkb=np.ascontiguousarray(np.broadcast_to(inp["a_sink"][:L][None], (128, L, 8))).astype(np.float32),
             c_w_uq=inp["c_w_uq"][:L], c_w_ukv=inp["c_w_ukv"][:L], w_o=inp["w_o"][:L],
             peer_w_q=inp["peer_w_q"][:L], peer_u=inp["peer_u"][:L], peer_v=inp["peer_v"][:L],
             sk=sk_pack(inp, L),
             ln2b=np.ascontiguousarray(np.broadcast_to(inp["ln2"][:L][:, None, :], (L, 128, 2048))).astype(np.float32),
             fnorm=np.ascontiguousarray(np.broadcast_to(inp["final_norm"][None], (128, 2048))).astype(np.float32))
    return m


N_CORES = 4
_CACHE = {}


def kernel(**inputs):
    inp = {k: np.asarray(v) for k, v in inputs.items()}
    B, S, _ = inp["x"].shape
    L = inp["w_in"].shape[0]
    key = (S, L)
    if key not in _CACHE:
        consts = host_consts(S)
        _CACHE[key] = (consts, build(S, L, consts, debug=False))
    consts, nc = _CACHE[key]
    in_maps = []
    for b in range(B):
        m = core_inputs(inp, inp["x"][b], S, L, consts)
        in_maps.append({k: np.ascontiguousarray(v, dtype=np.float32) for k, v in m.items()})
    res = run_bass_kernel_spmd(nc, in_maps, core_ids=list(range(B)))
    out = np.stack([np.asarray(res.results[b]["y"], dtype=np.float32) for b in range(B)], axis=0)
    return out
```

```python
import numpy as np
from concourse.bass_utils import run_bass_kernel_spmd
from contextlib import ExitStack
import concourse.bass as bass
import concourse.mybir as mybir

F32 = mybir.dt.float32
BF16 = mybir.dt.bfloat16
AF = mybir.ActivationFunctionType
ALU = mybir.AluOpType
AX = mybir.AxisListType


class Res:
    __slots__ = ("name", "w", "r", "cin", "cout")

    def __init__(self, name):
        self.name = name
        self.w = {}
        self.r = {}
        self.cin = None
        self.cout = None


class Chan:
    __slots__ = ("sid", "cnt")

    def __init__(self, sid):
        self.sid = sid
        self.cnt = 0


class Eng:
    def __init__(self, name, sid):
        self.name = name
        self.sid = sid
        self.cnt = 0
        self.waited = {}
        self.prog = []


class KB:
    def __init__(self, nc):
        self.nc = nc
        self.es = ExitStack()
        self.sems = []
        self.engs = {}
        self.chans = []
        for k in ("pe", "act", "dve", "pool", "sp"):
            self.engs[k] = Eng(k, self.newsem("e_" + k))
        self.nres = 0
        self.pes = None
        self.ntens = 0
        self.free_ch = []
        self.phase_ch = []

    def newsem(self, name):
        s = self.es.enter_context(self.nc.semaphore("%s_%d" % (name, len(self.sems))))
        self.sems.append(s)
        return len(self.sems) - 1

    def res(self, name=None):
        self.nres += 1
        return Res(name or ("r%d" % self.nres))

    def sbuf(self, name, shape, dt):
        st = self.pes if self.pes is not None else self.es
        self.ntens += 1
        return st.enter_context(self.nc.sbuf_tensor("sb%d_%s" % (self.ntens, name), list(shape), dt))

    def psum(self, name, shape, dt):
        st = self.pes if self.pes is not None else self.es
        self.ntens += 1
        return st.enter_context(self.nc.psum_tensor("ps%d_%s" % (self.ntens, name), list(shape), dt))

    def begin_phase(self):
        self.pes = ExitStack()

    def end_phase(self):
        self.barrier()
        self.emit_block()
        self.pes.close()
        self.pes = None
        self.free_ch.extend(self.phase_ch)
        self.phase_ch = []

    def totals(self):
        t = {}
        for E in self.engs.values():
            t[E.sid] = E.cnt
        for ch in self.chans:
            t[ch.sid] = ch.cnt
        return t

    def barrier(self):
        tot = self.totals()
        for E in self.engs.values():
            waits = []
            for s, v in tot.items():
                if v > 0 and E.waited.get(s, 0) < v:
                    E.waited[s] = v
                    waits.append((s, v))
            E.prog.append((waits, None, None, 0))

    def _deps(self, E, reads, writes, same_ok):
        deps = {}
        for r in reads:
            for s, v in r.w.items():
                if deps.get(s, 0) < v:
                    deps[s] = v
        for w in writes:
            for s, v in w.w.items():
                if deps.get(s, 0) < v:
                    deps[s] = v
            for s, v in w.r.items():
                if deps.get(s, 0) < v:
                    deps[s] = v
        waits = []
        for s, v in deps.items():
            if s == E.sid and same_ok:
                continue
            if E.waited.get(s, 0) >= v:
                continue
            E.waited[s] = v
            waits.append((s, v))
        return waits

    def op(self, eng, fn, reads=(), writes=()):
        E = self.engs[eng]
        waits = self._deps(E, reads, writes, same_ok=(eng == "pe"))
        E.cnt += 1
        c = E.cnt
        E.prog.append((waits, fn, E.sid, 1))
        for r in reads:
            r.r[E.sid] = c
        for w in writes:
            w.w[E.sid] = c

    def dma(self, q, out, in_, reads=(), writes=(), chan_res=None, **kw):
        E = self.engs[q]
        waits = self._deps(E, reads, writes, same_ok=False)
        cr = chan_res
        if cr.cin is None:
            if self.free_ch:
                cr.cin = self.free_ch.pop()
            else:
                cr.cin = Chan(self.newsem("c"))
                self.chans.append(cr.cin)
            if self.pes is not None:
                self.phase_ch.append(cr.cin)
        ch = cr.cin
        ch.cnt += 16
        c = ch.cnt
        E.prog.append((waits, (lambda e, out=out, in_=in_, kw=kw: e.dma_start(out=out, in_=in_, **kw)), ch.sid, 16))
        for r in reads:
            r.r[ch.sid] = c
        for w in writes:
            w.w[ch.sid] = c

    def finish_wait(self, eng, resources):
        E = self.engs[eng]
        waits = self._deps(E, resources, (), same_ok=False)
        E.prog.append((waits, None, None, 0))

    def emit(self):
        self.emit_block()
        self.es.close()

    def emit_block(self):
        nc = self.nc
        sems = self.sems
        with nc.Block() as block:
            def run(E):
                def body(e):
                    for waits, fn, sid, inc in E.prog:
                        for s, v in waits:
                            e.wait_ge(sems[s], v)
                        if fn is not None:
                            ins = fn(e)
                            ins.then_inc(sems[sid], inc)
                return body
            block.tensor(run(self.engs["pe"]))
            block.scalar(run(self.engs["act"]))
            block.vector(run(self.engs["dve"]))
            block.gpsimd(run(self.engs["pool"]))
            block.sync(run(self.engs["sp"]))
        for E in self.engs.values():
            E.prog = []


D = 2048
KC = 16
DIN = 3904
EPS = 1e-6


class Pool:
    def __init__(self, kb, name, shape, dt, n, space="sbuf"):
        self.items = []
        for i in range(n):
            t = (kb.sbuf if space == "sbuf" else kb.psum)("%s%d" % (name, i), shape, dt)
            self.items.append((t, kb.res("%s%d" % (name, i))))
        self.i = 0

    def next(self):
        it = self.items[self.i % len(self.items)]
        self.i += 1
        return it


class Ctx:
    pass


def build(S, L, consts, debug=False, upto=99):
    nc = bass.Bass("TRN2", target_bir_lowering=False)
    kb = KB(nc)
    NT = S // 128
    NB = S // 512
    dbgkind = "ExternalOutput" if debug else "Internal"

    def din(name, shape, dt=F32):
        return nc.dram_tensor(name, list(shape), dt, kind="ExternalInput").ap()

    def dscr(name, shape, dt=BF16, kind=None):
        return nc.dram_tensor(name, list(shape), dt, kind=kind or dbgkind).ap()

    x_in = din("x", [S, D])
    w_in = din("w_in", [L, D, DIN])
    gains = din("gains", [L, 128, 64])
    ident_in = din("ident", [128, 128])
    cs_in = din("cossin", [2, 64, S])
    y_out = nc.dram_tensor("y", [S, D], F32, kind="ExternalOutput").ap()

    HT = dscr("HT", [31, 128, S])
    VA = dscr("VA", [S, 2, 129])
    VB = dscr("VB", [S, 4, 129])
    r_HT = kb.res("HT"); r_VA = kb.res("VA"); r_VB = kb.res("VB")
    r_x = kb.res("x_in"); r_w_in = kb.res("w_in"); r_const = kb.res("const")

    ident32 = kb.sbuf("ident32", [128, 128], F32)
    ident = kb.sbuf("ident", [128, 128], BF16)
    gn = kb.sbuf("gn", [128, L, 64], F32)
    gneg = kb.sbuf("gneg", [128, L, 64], F32)
    epsb = kb.sbuf("epsb", [128, 1], F32)
    r_id = kb.res("ident"); r_gn = kb.res("gn")
    kb.dma("sp", ident32[:], ident_in, reads=[r_const], writes=[r_id], chan_res=r_id)
    kb.op("dve", lambda e: e.tensor_copy(out=ident[:], in_=ident32[:]), reads=[r_id], writes=[r_id])
    kb.dma("sp", gn[:], gains.rearrange("l p c -> p l c"), reads=[r_const], writes=[r_gn], chan_res=r_gn)
    kb.op("dve", lambda e: e.tensor_scalar(out=gneg[:], in0=gn[:], scalar1=-1.0, scalar2=None, op0=ALU.mult),
          reads=[r_gn], writes=[r_gn])
    kb.op("dve", lambda e: e.memset(epsb[:], EPS), writes=[r_gn])

    NTAB = consts["_ntab"]
    schedB = consts["_schedB"]
    rot_in = din("rotR", [64, 64])
    tabA_in = din("tabA", [128, 3072])
    maskB_in = din("maskB", [128, NTAB, 128])
    biasB_in = din("biasB", [L, 128, NTAB, 4, 128])
    sink_in = din("sinkb", [128, L, 8])
    w_uq = din("c_w_uq", [L, 512, 768])
    w_ukv = din("c_w_ukv", [L, 256, 1024])
    w_o = din("w_o", [L, D, D])
    fnorm_in = din("fnorm", [128, D])
    w_pq = din("peer_w_q", [L, D, 1024])
    sk_in = din("sk", [L, 128, 256])
    peer_u = din("peer_u", [L, 16384, D])
    peer_v = din("peer_v", [L, 16384, D])
    ln2b_in = din("ln2b", [L, 128, D])
    TBT_MAX = 8
    r_win2 = kb.res("win2")

    QCN = dscr("QCN", [4, 128, S]); QCP = dscr("QCP", [4, 64, S]); KCN = dscr("KCN", [4, 128, S]); KCP = dscr("KCP", [64, S])
    VC = dscr("VC", [S, 4, 129])
    O = dscr("O", [S, D], F32)
    XA = dscr("XA", [S, D], F32); XB = dscr("XB", [S, D], F32)
    XN2T = dscr("XN2T", [KC, 128, S])
    QT = dscr("QT", [8, 128, S])
    GT = dscr("GT", [NT, 128, 128, 128])
    r_QT = kb.res("QT"); r_GT = kb.res("GT"); r_y = kb.res("y")
    r_QCN = kb.res("QCN"); r_QCP = kb.res("QCP"); r_KCN = kb.res("KCN"); r_KCP = kb.res("KCP"); r_VC = kb.res("VC")
    r_O = kb.res("O"); r_XA = kb.res("XA"); r_XB = kb.res("XB"); r_XN2T = kb.res("XN2T")

    rot32 = kb.sbuf("rot32", [64, 64], F32)
    rotb = kb.sbuf("rotb", [64, 64], BF16)
    onesb = kb.sbuf("onesb", [128, 128], BF16)
    esink = kb.sbuf("esink", [128, L, 8], F32)
    kb.dma("sp", rot32[:], rot_in, reads=[r_const], writes=[r_id], chan_res=r_id)
    kb.op("dve", lambda e: e.tensor_copy(out=rotb[:], in_=rot32[:]), reads=[r_id], writes=[r_id])
    kb.op("dve", lambda e: e.memset(onesb[:], 1.0), writes=[r_id])
    kb.dma("sp", esink[:], sink_in, reads=[r_const], writes=[r_gn], chan_res=r_gn)
    kb.op("act", lambda e: e.activation(out=esink[:], in_=esink[:], func=AF.Exp), reads=[r_gn], writes=[r_gn])

    SC_A = 128 ** -0.5
    SC_C = 192 ** -0.5

    def mk_norm_helpers(st_pool, sq_pool, ps_tp):
        def rmsnorm_tile(xt, r_xt, groups, xn, r_xn, gain=None):
            stt, r_st = st_pool.next()
            sq, r_sq = sq_pool.next()
            ng = len(groups)
            for g, (s0, wd) in enumerate(groups):
                kb.op("act", lambda e, s0=s0, wd=wd, g=g: e.activation(out=sq[:, s0:s0 + wd], in_=xt[:, s0:s0 + wd], func=AF.Square,
                                                                       accum_out=stt[:, g:g + 1]),
                      reads=[r_xt], writes=[r_sq, r_st])
            for g, (s0, wd) in enumerate(groups):
                kb.op("act", lambda e, g=g, wd=wd: e.activation(out=stt[:, g:g + 1], in_=stt[:, g:g + 1], func=AF.Ln,
                                                                bias=epsb[:, 0:1], scale=1.0 / wd),
                      reads=[r_st, r_gn], writes=[r_st])
            kb.op("act", lambda e: e.activation(out=stt[:, 0:ng], in_=stt[:, 0:ng], func=AF.Exp, scale=-0.5),
                  reads=[r_st], writes=[r_st])
            for g, (s0, wd) in enumerate(groups):
                if gain is None:
                    kb.op("dve", lambda e, s0=s0, wd=wd, g=g: e.tensor_scalar(out=xn[:, s0:s0 + wd], in0=xt[:, s0:s0 + wd],
                                                                              scalar1=stt[:, g:g + 1], scalar2=None, op0=ALU.mult),
                          reads=[r_xt, r_st], writes=[r_xn])
                else:
                    gt_, r_gt_ = gain
                    kb.op("dve", lambda e, s0=s0, wd=wd, g=g: e.scalar_tensor_tensor(out=xn[:, s0:s0 + wd], in0=xt[:, s0:s0 + wd],
                                                                                     scalar=stt[:, g:g + 1], in1=gt_[:, s0:s0 + wd],
                                                                                     op0=ALU.mult, op1=ALU.mult),
                          reads=[r_xt, r_st, r_gt_], writes=[r_xn])

        def transpose_tile(xn, r_xn, dst, r_dst, j):
            pt, r_pt = ps_tp.next()
            for k in range(KC):
                kb.op("pe", lambda e, k=k: e.transpose(out=pt[:, k, :], in_=xn[:, k * 128:(k + 1) * 128], identity=ident[:]),
                      reads=[r_xn, r_id], writes=[r_pt])
            kb.op("act", lambda e: e.copy(out=dst[:, :, j * 128:(j + 1) * 128], in_=pt[:]), reads=[r_pt], writes=[r_dst])
        return rmsnorm_tile, transpose_tile

    def load_w(wst_pool, src_ap, gcol, ncols, l, dst, r_dst, r_src):
        nk = src_ap.shape[0] // 128
        for k in range(nk):
            st, r_st = wst_pool.next()
            kb.dma("sp", st[:, :ncols], src_ap[k * 128:(k + 1) * 128, :], reads=[r_src], writes=[r_st], chan_res=r_st)
            if gcol is None:
                kb.op("act", lambda e, k=k, st=st: e.copy(out=dst[:, k, :ncols], in_=st[:, :ncols]), reads=[r_st], writes=[r_dst])
            else:
                kb.op("act", lambda e, k=k, st=st: e.activation(out=dst[:, k, :ncols], in_=st[:, :ncols], func=AF.Copy,
                                                                scale=gn[:, l, gcol + k:gcol + k + 1]),
                      reads=[r_st, r_gn], writes=[r_dst])

    def phase1(l, x_src, r_xsrc):
        kb.begin_phase()
        xt_pool = Pool(kb, "xt", [128, D], F32, 4)
        xn_pool = Pool(kb, "xn", [128, D], BF16, 4)
        sq_pool = Pool(kb, "sq", [128, D], BF16, 1)
        st_pool = Pool(kb, "stat", [128, 4], F32, 4)
        xnT_pool = Pool(kb, "xnT", [128, KC, 512], BF16, 2)
        ps_tp = Pool(kb, "tp", [128, KC, 128], BF16, 2, space="psum")
        ps_mm = Pool(kb, "mm", [128, 512], F32, 4, space="psum")
        wst_pool = Pool(kb, "wst", [128, 2048], F32, 2)
        wbf = kb.sbuf("wbf", [128, KC, 2048], BF16)
        r_wbf = kb.res("wbf")
        ev_pool = Pool(kb, "ev", [128, 512], BF16, 4)
        vst_pool = Pool(kb, "vst", [128, 4, 129], BF16, 2)
        for t, r in vst_pool.items:
            kb.op("pool", lambda e, t=t: e.memset(t[:], 1.0), writes=[r])
        rmsnorm_tile, transpose_tile = mk_norm_helpers(st_pool, sq_pool, ps_tp)
        passes = [
            dict(c0=0, fm=[(c, c * 128, 128) for c in range(0, 10)] + [(c, c * 128, 128) for c in range(12, 16)],
                 tm=[(VA, r_VA, 1280, 2)]),
            dict(c0=2048, fm=[(c, c * 128 - 2048, 128) for c in range(16, 20)] + [(c, c * 128 - 2048, 128) for c in range(24, 30)]
                 + [(30, 30 * 128 - 2048, 64)],
                 tm=[(VB, r_VB, 2560 - 2048, 4)]),
        ]
        for ps in passes:
            c0 = ps["c0"]
            ncols = min(2048, DIN - c0)
            load_w(wst_pool, w_in[l, :, c0:c0 + ncols], 0, ncols, l, wbf, r_wbf, r_w_in)
            def front(b):
                xnT, r_xnT = xnT_pool.next()
                for j in range(4):
                    tok = b * 512 + j * 128
                    xt, r_xt = xt_pool.next()
                    kb.dma("sp", xt[:], x_src[tok:tok + 128, :], reads=[r_xsrc], writes=[r_xt], chan_res=r_xt)
                    xn, r_xn = xn_pool.next()
                    rmsnorm_tile(xt, r_xt, [(0, D)], xn, r_xn)
                    transpose_tile(xn, r_xn, xnT, r_xnT, j)
                return xnT, r_xnT

            nxt = front(0)
            for b in range(NB):
                xnT, r_xnT = nxt
                if b + 1 < NB:
                    nxt = front(b + 1)
                for (c, off, m) in ps["fm"]:
                    pm, r_pm = ps_mm.next()
                    for k in range(KC):
                        kb.op("pe", lambda e, k=k, off=off, m=m, pm=pm, xnT=xnT: e.matmul(
                            pm[0:m, :], lhsT=wbf[:, k, off:off + m], rhs=xnT[:, k, :], start=(k == 0), stop=(k == KC - 1)),
                            reads=[r_wbf, r_xnT], writes=[r_pm])
                    ev, r_ev = ev_pool.next()
                    kb.op("dve", lambda e, m=m, pm=pm, ev=ev: e.tensor_copy(out=ev[0:m, :], in_=pm[0:m, :]),
                          reads=[r_pm], writes=[r_ev])
                    kb.dma("pool", HT[c, 0:m, b * 512:(b + 1) * 512], ev[0:m, :], reads=[r_ev], writes=[r_HT], chan_res=r_ev)
                for (dst, r_dstd, off, nh) in ps["tm"]:
                    for j in range(4):
                        tok = b * 512 + j * 128
                        pm, r_pm = ps_mm.next()
                        for k in range(KC):
                            kb.op("pe", lambda e, k=k, off=off, nh=nh, pm=pm, xnT=xnT, j=j: e.matmul(
                                pm[:, 0:nh * 128], lhsT=xnT[:, k, j * 128:(j + 1) * 128], rhs=wbf[:, k, off:off + nh * 128],
                                start=(k == 0), stop=(k == KC - 1)),
                                reads=[r_wbf, r_xnT], writes=[r_pm])
                        vs, r_vs = vst_pool.next()
                        kb.op("act", lambda e, nh=nh, pm=pm, vs=vs: e.copy(
                            out=vs[:, 0:nh, 0:128], in_=pm[:, 0:nh * 128].rearrange("p (h d) -> p h d", d=128)),
                            reads=[r_pm], writes=[r_vs])
                        kb.dma("pool", dst[tok:tok + 128, :, :], vs[:, 0:nh, :], reads=[r_vs], writes=[r_dstd], chan_res=r_vs)
        kb.end_phase()

    def phase1c(l):
        kb.begin_phase()
        wst_pool = Pool(kb, "wst", [128, 1024], F32, 2)
        wuq = kb.sbuf("wuq", [128, 4, 768], BF16)
        wukv = kb.sbuf("wukv", [128, 2, 1024], BF16)
        r_wu = kb.res("wu")
        load_w(wst_pool, w_uq[l], 16, 768, l, wuq, r_wu, r_win2)
        load_w(wst_pool, w_ukv[l], 20, 1024, l, wukv, r_wu, r_win2)
        cin_pool = Pool(kb, "cin", [128, 6, 512], BF16, 2)
        krin_pool = Pool(kb, "krin", [64, 512], BF16, 2)
        cs_pool = Pool(kb, "cs", [64, 2, 512], F32, 2)
        sqc_pool = Pool(kb, "sqc", [128, 6, 512], BF16, 1)
        rstd_pool = Pool(kb, "rstd", [128, 2, 512], F32, 1)
        cn_pool = Pool(kb, "cn", [128, 6, 512], BF16, 2)
        ps_mm = Pool(kb, "mm", [128, 512], F32, 6, space="psum")
        ev_pool = Pool(kb, "ev", [128, 512], BF16, 4)
        qpe_pool = Pool(kb, "qpe", [64, 512], BF16, 2)
        t1_pool = Pool(kb, "t1", [64, 512], F32, 2)
        t2_pool = Pool(kb, "t2", [64, 512], F32, 2)
        vst_pool = Pool(kb, "vst", [128, 4, 129], BF16, 2)
        for t, r in vst_pool.items:
            kb.op("pool", lambda e, t=t: e.memset(t[:], 1.0), writes=[r])

        def rope_store(src, r_src, cs, r_cs, dst_ap, r_dst):
            pr, r_pr = ps_mm.next()
            kb.op("pe", lambda e: e.matmul(pr[0:64, :], lhsT=rotb[:], rhs=src[:], start=True, stop=True),
                  reads=[r_src, r_id], writes=[r_pr])
            t1, r_t1 = t1_pool.next()
            t2, r_t2 = t2_pool.next()
            kb.op("dve", lambda e: e.tensor_tensor(out=t1[:], in0=src[:], in1=cs[:, 0, :], op=ALU.mult),
                  reads=[r_src, r_cs], writes=[r_t1])
            kb.op("dve", lambda e: e.tensor_tensor(out=t2[:], in0=pr[0:64, :], in1=cs[:, 1, :], op=ALU.mult),
                  reads=[r_pr, r_cs], writes=[r_t2])
            ev, r_ev = ev_pool.next()
            kb.op("dve", lambda e: e.tensor_tensor(out=ev[0:64, :], in0=t1[:], in1=t2[:], op=ALU.add),
                  reads=[r_t1, r_t2], writes=[r_ev])
            kb.dma("pool", dst_ap, ev[0:64, :], reads=[r_ev], writes=[r_dst], chan_res=r_ev)

        for b in range(NB):
            bs = slice(b * 512, (b + 1) * 512)
            cin, r_cin = cin_pool.next()
            kb.dma("sp", cin[:], HT[24:30, :, bs].rearrange("c p t -> p c t"), reads=[r_HT], writes=[r_cin], chan_res=r_cin)
            krin, r_krin = krin_pool.next()
            kb.dma("sp", krin[:], HT[30, 0:64, bs], reads=[r_HT], writes=[r_krin], chan_res=r_krin)
            cs, r_cs = cs_pool.next()
            kb.dma("sp", cs[:], cs_in[:, :, bs].rearrange("c p t -> p c t"), reads=[r_const], writes=[r_cs], chan_res=r_cs)
            sqc, r_sqc = sqc_pool.next()
            kb.op("act", lambda e, sqc=sqc, cin=cin: e.activation(out=sqc[:], in_=cin[:], func=AF.Square), reads=[r_cin], writes=[r_sqc])
            rstd, r_rstd = rstd_pool.next()
            cn, r_cn = cn_pool.next()
            for gi, (c0, nchunk) in enumerate([(0, 4), (4, 2)]):
                pss, r_pss = ps_mm.next()
                for k in range(nchunk):
                    kb.op("pe", lambda e, k=k, c0=c0, nchunk=nchunk, pss=pss, sqc=sqc: e.matmul(
                        pss[:], lhsT=onesb[:], rhs=sqc[:, c0 + k, :], start=(k == 0), stop=(k == nchunk - 1)),
                        reads=[r_sqc, r_id], writes=[r_pss])
                kb.op("act", lambda e, gi=gi, nchunk=nchunk, pss=pss, rstd=rstd: e.activation(
                    out=rstd[:, gi, :], in_=pss[:], func=AF.Ln, bias=epsb[:, 0:1], scale=1.0 / (nchunk * 128)),
                    reads=[r_pss, r_gn], writes=[r_rstd])
                kb.op("act", lambda e, gi=gi, rstd=rstd: e.activation(out=rstd[:, gi, :], in_=rstd[:, gi, :], func=AF.Exp, scale=-0.5),
                      reads=[r_rstd], writes=[r_rstd])
                for k in range(nchunk):
                    kb.op("dve", lambda e, k=k, c0=c0, gi=gi, cn=cn, cin=cin, rstd=rstd: e.tensor_tensor(
                        out=cn[:, c0 + k, :], in0=cin[:, c0 + k, :], in1=rstd[:, gi, :], op=ALU.mult),
                        reads=[r_cin, r_rstd], writes=[r_cn])
            for h in range(4):
                pm, r_pm = ps_mm.next()
                for k in range(4):
                    kb.op("pe", lambda e, k=k, h=h, pm=pm, cn=cn: e.matmul(
                        pm[:], lhsT=wuq[:, k, h * 192:h * 192 + 128], rhs=cn[:, k, :], start=(k == 0), stop=(k == 3)),
                        reads=[r_wu, r_cn], writes=[r_pm])
                ev, r_ev = ev_pool.next()
                kb.op("act", lambda e, pm=pm, ev=ev: e.copy(out=ev[:], in_=pm[:]), reads=[r_pm], writes=[r_ev])
                kb.dma("pool", QCN[h, :, bs], ev[:], reads=[r_ev], writes=[r_QCN], chan_res=r_ev)
                pm, r_pm = ps_mm.next()
                for k in range(4):
                    kb.op("pe", lambda e, k=k, h=h, pm=pm, cn=cn: e.matmul(
                        pm[0:64, :], lhsT=wuq[:, k, h * 192 + 128:h * 192 + 192], rhs=cn[:, k, :], start=(k == 0), stop=(k == 3)),
                        reads=[r_wu, r_cn], writes=[r_pm])
                qpe, r_qpe = qpe_pool.next()
                kb.op("act", lambda e, pm=pm, qpe=qpe: e.copy(out=qpe[:], in_=pm[0:64, :]), reads=[r_pm], writes=[r_qpe])
                rope_store(qpe, r_qpe, cs, r_cs, QCP[h, :, bs], r_QCP)
                pm, r_pm = ps_mm.next()
                for k in range(2):
                    kb.op("pe", lambda e, k=k, h=h, pm=pm, cn=cn: e.matmul(
                        pm[:], lhsT=wukv[:, k, h * 256:h * 256 + 128], rhs=cn[:, 4 + k, :], start=(k == 0), stop=(k == 1)),
                        reads=[r_wu, r_cn], writes=[r_pm])
                ev, r_ev = ev_pool.next()
                kb.op("act", lambda e, pm=pm, ev=ev: e.copy(out=ev[:], in_=pm[:]), reads=[r_pm], writes=[r_ev])
                kb.dma("pool", KCN[h, :, bs], ev[:], reads=[r_ev], writes=[r_KCN], chan_res=r_ev)
            for j in range(4):
                tok = b * 512 + j * 128
                pm, r_pm = ps_mm.next()
                for k in range(2):
                    kb.op("pe", lambda e, k=k, j=j, pm=pm, cn=cn: e.matmul(
                        pm[:].rearrange("p (h d) -> p h d", d=128), lhsT=cn[:, 4 + k, j * 128:(j + 1) * 128],
                        rhs=wukv[:, k, :].rearrange("p (h c) -> p h c", c=256)[:, :, 128:256], start=(k == 0), stop=(k == 1)),
                        reads=[r_wu, r_cn], writes=[r_pm])
                vs, r_vs = vst_pool.next()
                kb.op("act", lambda e, pm=pm, vs=vs: e.copy(out=vs[:, :, 0:128], in_=pm[:].rearrange("p (h d) -> p h d", d=128)),
                      reads=[r_pm], writes=[r_vs])
                kb.dma("pool", VC[tok:tok + 128, :, :], vs[:], reads=[r_vs], writes=[r_VC], chan_res=r_vs)
            rope_store(krin, r_krin, cs, r_cs, KCP[:, bs], r_KCP)
        kb.end_phase()

    def attn_block(P, kts, s_mm, tab, v_rhs, scale, finals):
        nk = len(kts)
        pss = {}
        AHEAD = 2
        for a in range(min(AHEAD, nk)):
            pss[a] = P.ps_s.next()
            s_mm(kts[a], pss[a][0], pss[a][1])
        for i, kt in enumerate(kts):
            ps, r_ps = pss.pop(i)
            if i + AHEAD < nk:
                pss[i + AHEAD] = P.ps_s.next()
                s_mm(kts[i + AHEAD], pss[i + AHEAD][0], pss[i + AHEAD][1])
            pt, r_pt = P.pt_pool.next()
            tb = tab(kt)
            if tb is None:
                kb.op("act", lambda e, ps=ps, pt=pt: e.activation(out=pt[:], in_=ps[:], func=AF.Exp, scale=scale),
                      reads=[r_ps], writes=[r_pt])
            else:
                ex, r_ex = P.ex_pool.next()
                kb.op("act", lambda e, ps=ps, ex=ex: e.activation(out=ex[:], in_=ps[:], func=AF.Exp, scale=scale),
                      reads=[r_ps], writes=[r_ex])
                kb.op("dve", lambda e, ex=ex, pt=pt, tb=tb: e.tensor_tensor(out=pt[:], in0=ex[:], in1=tb, op=ALU.mult),
                      reads=[r_ex, P.r_tab], writes=[r_pt])
            for g in range(4):
                po, r_po = P.ps_o[g]
                kb.op("pe", lambda e, g=g, po=po, pt=pt, kt=kt, i=i: e.matmul(
                    po[:, 0:129], lhsT=pt[:, g * 128:(g + 1) * 128], rhs=v_rhs(kt, g), start=(i == 0), stop=(i == nk - 1)),
                    reads=[r_pt, P.r_v], writes=[r_po])
        for (g, out_ap, r_out, extra) in finals:
            po, r_po = P.ps_o[g]
            dn, r_dn = P.den_pool.next()
            if extra is not None:
                kb.op("dve", lambda e, po=po, dn=dn, extra=extra: e.tensor_scalar(out=dn[:, 0:1], in0=po[:, 128:129], scalar1=extra,
                                                                                 scalar2=None, op0=ALU.add),
                      reads=[r_po, r_gn], writes=[r_dn])
                kb.op("dve", lambda e, dn=dn: e.reciprocal(out=dn[:, 1:2], in_=dn[:, 0:1]), reads=[r_dn], writes=[r_dn])
            else:
                kb.op("dve", lambda e, po=po, dn=dn: e.reciprocal(out=dn[:, 1:2], in_=po[:, 128:129]), reads=[r_po], writes=[r_dn])
            kb.op("dve", lambda e, po=po, dn=dn, out_ap=out_ap: e.tensor_scalar(out=out_ap, in0=po[:, 0:128], scalar1=dn[:, 1:2],
                                                                               scalar2=None, op0=ALU.mult),
                  reads=[r_po, r_dn], writes=[r_out])

    def attn_pools():
        P = Ctx()
        P.ps_s = Pool(kb, "pss", [128, 512], F32, 4, space="psum")
        P.ps_o = [(kb.psum("pso%d" % g, [128, 512], F32), kb.res("pso%d" % g)) for g in range(4)]
        P.pt_pool = Pool(kb, "pt", [128, 512], BF16, 4)
        P.ex_pool = Pool(kb, "ex", [128, 512], BF16, 3)
        P.den_pool = Pool(kb, "den", [128, 2], F32, 4)
        P.r_tab = kb.res("tab")
        P.r_v = kb.res("vres")
        return P

    def phase2a(l):
        kb.begin_phase()
        P = attn_pools()
        QA = kb.sbuf("QA", [128, 8, S], BF16)
        KA = kb.sbuf("KA", [128, 2, S], BF16)
        VAs = kb.sbuf("VAs", [128, NT, 2, 129], BF16)
        tab32 = kb.sbuf("tab32", [128, 3072], F32)
        tabA = kb.sbuf("tabAb", [128, 3, 8, 128], BF16)
        r_q = kb.res("QAres")
        kb.dma("sp", QA[:], HT[0:8, :, :].rearrange("c p t -> p c t"), reads=[r_HT], writes=[r_q], chan_res=r_q)
        kb.dma("sp", KA[:], HT[8:10, :, :].rearrange("c p t -> p c t"), reads=[r_HT], writes=[P.r_v], chan_res=P.r_v)
        kb.dma("sp", VAs[:], VA.rearrange("(n p) g d -> p n g d", p=128), reads=[r_VA], writes=[P.r_v], chan_res=P.r_v)
        kb.dma("sp", tab32[:], tabA_in, reads=[r_const], writes=[P.r_tab], chan_res=P.r_tab)
        kb.op("dve", lambda e: e.tensor_copy(out=tabA[:].rearrange("p a h q -> p (a h q)"), in_=tab32[:]), reads=[P.r_tab], writes=[P.r_tab])
        o_pool = Pool(kb, "ost", [128, 1024], F32, 2)
        for n in range(NT):
            ost, r_ost = o_pool.next()
            for kg in range(2):
                kts = [m for m in (n - 1, n, n + 1) if 0 <= m < NT]

                def s_mm(kt, ps, r_ps, kg=kg, n=n):
                    kb.op("pe", lambda e: e.matmul(ps[:].rearrange("p (h q) -> p h q", q=128), lhsT=KA[:, kg, kt * 128:(kt + 1) * 128],
                                                   rhs=QA[:, 4 * kg:4 * kg + 4, n * 128:(n + 1) * 128], start=True, stop=True),
                          reads=[P.r_v, r_q], writes=[r_ps])
                attn_block(P, kts, s_mm,
                           lambda kt, kg=kg, n=n: tabA[:, kt - n + 1, 4 * kg:4 * kg + 4, :].rearrange("p h q -> p (h q)"),
                           lambda kt, g, kg=kg: VAs[:, kt, kg, :], SC_A,
                           [(g, ost[:, (4 * kg + g) * 128:(4 * kg + g + 1) * 128], r_ost, esink[:, l, 4 * kg + g:4 * kg + g + 1]) for g in range(4)])
            kb.dma("pool", O[n * 128:(n + 1) * 128, 0:1024], ost[:], reads=[r_ost], writes=[r_O], chan_res=r_ost)
        kb.end_phase()

    def phase2b(l):
        kb.begin_phase()
        P = attn_pools()
        QB = kb.sbuf("QB", [128, 4, S], BF16)
        KBs = kb.sbuf("KBs", [128, 4, S], BF16)
        VBs = kb.sbuf("VBs", [128, NT, 4, 129], BF16)
        tabB = kb.sbuf("tabB", [128, NTAB, 4, 128], BF16)
        mk32 = kb.sbuf("mk32", [128, NTAB, 128], F32)
        bst_pool = Pool(kb, "bst", [128, 4, 128], F32, 2)
        r_q = kb.res("QBres")
        kb.dma("sp", QB[:], HT[12:16, :, :].rearrange("c p t -> p c t"), reads=[r_HT], writes=[r_q], chan_res=r_q)
        kb.dma("sp", KBs[:], HT[16:20, :, :].rearrange("c p t -> p c t"), reads=[r_HT], writes=[P.r_v], chan_res=P.r_v)
        kb.dma("sp", VBs[:], VB.rearrange("(n p) g d -> p n g d", p=128), reads=[r_VB], writes=[P.r_v], chan_res=P.r_v)
        kb.dma("sp", mk32[:], maskB_in, reads=[r_const], writes=[P.r_tab], chan_res=P.r_tab)
        for ti in range(NTAB):
            bst, r_bst = bst_pool.next()
            kb.dma("sp", bst[:], biasB_in[l, :, ti, :, :], reads=[r_const], writes=[r_bst], chan_res=r_bst)
            kb.op("act", lambda e, bst=bst: e.activation(out=bst[:], in_=bst[:], func=AF.Exp), reads=[r_bst], writes=[r_bst])
            for h in range(4):
                kb.op("dve", lambda e, bst=bst, ti=ti, h=h: e.tensor_tensor(out=tabB[:, ti, h, :], in0=bst[:, h, :], in1=mk32[:, ti, :], op=ALU.mult),
                      reads=[r_bst, P.r_tab], writes=[P.r_tab])
        o_pool = Pool(kb, "ost", [128, 512], F32, 2)
        for n in range(NT):
            ost, r_ost = o_pool.next()
            lst = schedB[n]
            tmap = dict(lst)

            def s_mm(kt, ps, r_ps, n=n):
                for h in range(4):
                    kb.op("pe", lambda e, h=h: e.matmul(ps[:, h * 128:(h + 1) * 128], lhsT=KBs[:, h, kt * 128:(kt + 1) * 128],
                                                        rhs=QB[:, h, n * 128:(n + 1) * 128], start=True, stop=True),
                          reads=[P.r_v, r_q], writes=[r_ps])
            attn_block(P, [m for m, _ in lst], s_mm,
                       lambda kt, tmap=tmap: tabB[:, tmap[kt], :, :].rearrange("p h q -> p (h q)"),
                       lambda kt, g: VBs[:, kt, g, :], SC_A,
                       [(g, ost[:, g * 128:(g + 1) * 128], r_ost, None) for g in range(4)])
            kb.dma("pool", O[n * 128:(n + 1) * 128, 1024:1536], ost[:], reads=[r_ost], writes=[r_O], chan_res=r_ost)
        kb.end_phase()

    def phase2c(l):
        kb.begin_phase()
        P = attn_pools()
        KN = kb.sbuf("KN", [128, 4, S], BF16)
        KP = kb.sbuf("KP", [64, S], BF16)
        VCs = kb.sbuf("VCs", [128, NT, 4, 129], BF16)
        kb.dma("sp", KN[:], KCN.rearrange("c p t -> p c t"), reads=[r_KCN], writes=[P.r_v], chan_res=P.r_v)
        kb.dma("sp", KP[:], KCP, reads=[r_KCP], writes=[P.r_v], chan_res=P.r_v)
        kb.dma("sp", VCs[:], VC.rearrange("(n p) g d -> p n g d", p=128), reads=[r_VC], writes=[P.r_v], chan_res=P.r_v)
        qn_pool = Pool(kb, "qn", [128, 4, 512], BF16, 2)
        qp_pool = Pool(kb, "qp", [64, 4, 512], BF16, 2)
        o_pool = Pool(kb, "ost", [128, 4, 512], F32, 2)
        for b in range(NB):
            bs = slice(b * 512, (b + 1) * 512)
            qn, r_qn = qn_pool.next()
            qp, r_qp = qp_pool.next()
            kb.dma("sp", qn[:], QCN[:, :, bs].rearrange("c p t -> p c t"), reads=[r_QCN], writes=[r_qn], chan_res=r_qn)
            kb.dma("sp", qp[:], QCP[:, :, bs].rearrange("c p t -> p c t"), reads=[r_QCP], writes=[r_qp], chan_res=r_qp)
            ost, r_ost = o_pool.next()
            for h in range(4):
                def s_mm(kt, ps, r_ps, h=h, qn=qn, qp=qp, r_qn=r_qn, r_qp=r_qp):
                    kb.op("pe", lambda e: e.matmul(ps[:], lhsT=KN[:, h, kt * 128:(kt + 1) * 128], rhs=qn[:, h, :], start=True, stop=False),
                          reads=[P.r_v, r_qn], writes=[r_ps])
                    kb.op("pe", lambda e: e.matmul(ps[:], lhsT=KP[:, kt * 128:(kt + 1) * 128], rhs=qp[:, h, :], start=False, stop=True),
                          reads=[P.r_v, r_qp], writes=[r_ps])
                attn_block(P, list(range(NT)), s_mm, lambda kt: None, lambda kt, g, h=h: VCs[:, kt, h, :], SC_C,
                           [(g, ost[:, g, h * 128:(h + 1) * 128], r_ost, None) for g in range(4)])
            kb.dma("pool", O[bs, 1536:2048].rearrange("(g p) d -> p g d", p=128), ost[:], reads=[r_ost], writes=[r_O], chan_res=r_ost)
        kb.end_phase()

    def phase3(l, x_src, r_xsrc, X1, r_X1):
        kb.begin_phase()
        wst_pool = Pool(kb, "wst", [128, 2048], F32, 2)
        wo = kb.sbuf("wo", [128, KC, D], BF16)
        r_wo = kb.res("wo")
        load_w(wst_pool, w_o[l], 22, D, l, wo, r_wo, r_win2)
        g2 = kb.sbuf("g2", [128, D], F32); r_g2 = kb.res("g2")
        kb.dma("sp", g2[:], ln2b_in[l], reads=[r_const], writes=[r_g2], chan_res=r_g2)
        ot_pool = Pool(kb, "ot", [128, D], F32, 2)
        xt_pool = Pool(kb, "xt", [128, D], F32, 2)
        x1_pool = Pool(kb, "x1", [128, D], F32, 2)
        xn_pool = Pool(kb, "xn", [128, D], BF16, 2)
        sq_pool = Pool(kb, "sq", [128, D], BF16, 1)
        st_pool = Pool(kb, "stat", [128, 4], F32, 4)
        oT_pool = Pool(kb, "oT", [128, KC, 128], BF16, 2)
        xn2T_pool = Pool(kb, "xn2T", [128, KC, 512], BF16, 1)
        ps_tp = Pool(kb, "tp", [128, KC, 128], BF16, 2, space="psum")
        ps_mm = Pool(kb, "mm", [128, 512], F32, 4, space="psum")
        rmsnorm_tile, transpose_tile = mk_norm_helpers(st_pool, sq_pool, ps_tp)
        def part_a(n):
            tok = n * 128
            c = Ctx()
            ot, r_ot = ot_pool.next()
            kb.dma("sp", ot[:], O[tok:tok + 128, :], reads=[r_O], writes=[r_ot], chan_res=r_ot)
            c.xt, c.r_xt = xt_pool.next()
            kb.dma("sp", c.xt[:], x_src[tok:tok + 128, :], reads=[r_xsrc], writes=[c.r_xt], chan_res=c.r_xt)
            on, r_on = xn_pool.next()
            rmsnorm_tile(ot, r_ot, [(0, 1024), (1024, 512), (1536, 512)], on, r_on)
            c.oT, c.r_oT = oT_pool.next()
            transpose_tile(on, r_on, c.oT, c.r_oT, 0)
            return c

        def part_b(n, c):
            tok = n * 128
            c.x1, c.r_x1 = x1_pool.next()
            for nb in range(4):
                pm, r_pm = ps_mm.next()
                for k in range(KC):
                    kb.op("pe", lambda e, k=k, nb=nb, pm=pm, oT=c.oT: e.matmul(
                        pm[:], lhsT=oT[:, k, :], rhs=wo[:, k, nb * 512:(nb + 1) * 512], start=(k == 0), stop=(k == KC - 1)),
                        reads=[r_wo, c.r_oT], writes=[r_pm])
                kb.op("dve", lambda e, nb=nb, pm=pm, x1=c.x1, xt=c.xt: e.tensor_tensor(
                    out=x1[:, nb * 512:(nb + 1) * 512], in0=pm[:], in1=xt[:, nb * 512:(nb + 1) * 512], op=ALU.add),
                    reads=[r_pm, c.r_xt], writes=[c.r_x1])
            kb.dma("pool", X1[tok:tok + 128, :], c.x1[:], reads=[c.r_x1], writes=[r_X1], chan_res=c.r_x1)

        cur = [None]

        def part_c(n, c):
            b, j = n // 4, n % 4
            if j == 0:
                cur[0] = xn2T_pool.next()
            xn2T, r_xn2T = cur[0]
            xn, r_xn = xn_pool.next()
            rmsnorm_tile(c.x1, c.r_x1, [(0, D)], xn, r_xn, gain=(g2, r_g2))
            transpose_tile(xn, r_xn, xn2T, r_xn2T, j)
            if j == 3:
                kb.dma("pool", XN2T[:, :, b * 512:(b + 1) * 512].rearrange("c p t -> p c t"), xn2T[:], reads=[r_xn2T], writes=[r_XN2T],
                       chan_res=r_xn2T)

        ca = part_a(0)
        for n in range(NT):
            part_b(n, ca)
            cn = part_a(n + 1) if n + 1 < NT else None
            part_c(n, ca)
            ca = cn
        kb.end_phase()

    def phase4a(l):
        kb.begin_phase()
        wst_pool = Pool(kb, "wst", [128, 1024], F32, 2)
        wq = kb.sbuf("wq", [128, KC, 1024], BF16)
        r_wq = kb.res("wq")
        load_w(wst_pool, w_pq[l], None, 1024, l, wq, r_wq, r_win2)
        xb_pool = Pool(kb, "xb", [128, KC, 512], BF16, 2)
        ps_mm = Pool(kb, "mm", [128, 512], F32, 4, space="psum")
        ev_pool = Pool(kb, "ev", [128, 512], BF16, 4)
        for b in range(NB):
            bs = slice(b * 512, (b + 1) * 512)
            xb, r_xb = xb_pool.next()
            kb.dma("sp", xb[:], XN2T[:, :, bs].rearrange("c p t -> p c t"), reads=[r_XN2T], writes=[r_xb], chan_res=r_xb)
            for h in range(8):
                pm, r_pm = ps_mm.next()
                for k in range(KC):
                    kb.op("pe", lambda e, k=k, h=h, pm=pm, xb=xb: e.matmul(
                        pm[:], lhsT=wq[:, k, h * 128:(h + 1) * 128], rhs=xb[:, k, :], start=(k == 0), stop=(k == KC - 1)),
                        reads=[r_wq, r_xb], writes=[r_pm])
                ev, r_ev = ev_pool.next()
                kb.op("act", lambda e, pm=pm, ev=ev: e.copy(out=ev[:], in_=pm[:]), reads=[r_pm], writes=[r_ev])
                kb.dma("pool", QT[h, :, bs], ev[:], reads=[r_ev], writes=[r_QT], chan_res=r_ev)
        kb.end_phase()

    DELTA = 1e-5

    def phase4b(l):
        kb.begin_phase()
        sk32 = kb.sbuf("sk32", [128, 256], F32)
        skb = kb.sbuf("skb", [128, 256], BF16)
        r_sk = kb.res("sk")
        kb.dma("sp", sk32[:], sk_in[l], reads=[r_const], writes=[r_sk], chan_res=r_sk)
        kb.op("dve", lambda e: e.tensor_copy(out=skb[:], in_=sk32[:]), reads=[r_sk], writes=[r_sk])
        q_pool = Pool(kb, "qt", [128, 8, 128], BF16, 2)
        s_pool = Pool(kb, "s", [128, 8, 2, 128], F32, 2)
        top_pool = Pool(kb, "top", [128, 8, 2, 16], F32, 2)
        tmp = kb.sbuf("tmpm", [128, 16, 128], F32)
        r_tmpc = [kb.res("tmpc%d" % i) for i in range(16)]
        cand = kb.sbuf("cand", [128, 8, 256], F32); r_cand = kb.res("cand")
        best = kb.sbuf("best", [128, 8, 16], F32); r_best = kb.res("best")
        dd = kb.sbuf("dd", [128, 8, 16], F32)
        zz = kb.sbuf("zz", [128, 4, 8], F32)
        cc = kb.sbuf("cc", [128, 8, 16], F32)
        cb = kb.sbuf("cb", [128, 8, 16], BF16)
        thr = kb.sbuf("thr", [128, 8, 16], F32)
        r_sm = kb.res("small")
        E2 = kb.sbuf("E2", [128, 8, 128], BF16); r_E2 = kb.res("E2")
        A = kb.sbuf("A", [128, 8, 16, 128], BF16); r_A = kb.res("A")
        Bms = [(kb.sbuf("Bm%d" % i, [128, 8, 16, 64], BF16), kb.res("Bm%d" % i)) for i in range(2)]
        AT = kb.sbuf("AT", [128, 128, 128], BF16); r_AT = kb.res("AT")
        BT = kb.sbuf("BT", [128, 64, 128], BF16); r_BT = kb.res("BT")
        cT = kb.sbuf("cT", [128, 128], BF16); r_cT = kb.res("cT")
        g_pool = Pool(kb, "gt", [128, 64, 128], BF16, 2)
        ps_s = Pool(kb, "pss", [128, 2, 256], F32, 1, space="psum")
        ps_tp = Pool(kb, "tp", [128, 16, 128], BF16, 2, space="psum")
        ps_g = Pool(kb, "pg", [128, 8, 64], F32, 3, space="psum")
        A2 = A[:].rearrange("p h a i -> p (h a) i")
        evtog = [0]
        def front1(n):
            ts = slice(n * 128, (n + 1) * 128)
            qt, r_qt = q_pool.next()
            kb.dma("sp", qt[:], QT[:, :, ts].rearrange("h p t -> p h t"), reads=[r_QT], writes=[r_qt], chan_res=r_qt)
            s, r_s = s_pool.next()
            for hp in range(4):
                ps, r_ps = ps_s.next()
                for hh in range(2):
                    kb.op("pe", lambda e, hp=hp, hh=hh, ps=ps, qt=qt: e.matmul(ps[:, hh, :], lhsT=qt[:, 2 * hp + hh, :], rhs=skb[:], start=True, stop=True),
                          reads=[r_qt, r_sk], writes=[r_ps])
                kb.op("act", lambda e, hp=hp, ps=ps, s=s: e.copy(out=s[:, 2 * hp:2 * hp + 2, :, :].rearrange("p h c n -> p h (c n)"), in_=ps[:]),
                      reads=[r_ps], writes=[r_s])
            return s, r_s

        nxt = front1(0)
        for n in range(NT):
            s, r_s = nxt
            top, _r_top_unused = top_pool.next()
            r_tc = [kb.res("topc") for _ in range(16)]
            r_top = kb.res("topall")
            for h in range(8):
                for c in range(2):
                    kb.op("dve", lambda e, h=h, c=c, top=top, s=s: e.max(out=top[:, h, c, 0:8], in_=s[:, h, c, :]), reads=[r_s], writes=[r_tc[2 * h + c]])
            for h in range(8):
                for c in range(2):
                    kb.op("dve", lambda e, h=h, c=c, top=top, s=s: e.match_replace(out=tmp[:, 2 * h + c, :], in_to_replace=top[:, h, c, 0:8],
                                                                                   in_values=s[:, h, c, :], imm_value=-1e30),
                          reads=[r_s, r_tc[2 * h + c]], writes=[r_tmpc[2 * h + c]])
            for h in range(8):
                for c in range(2):
                    kb.op("dve", lambda e, h=h, c=c, top=top: e.max(out=top[:, h, c, 8:16], in_=tmp[:, 2 * h + c, :]),
                          reads=[r_tmpc[2 * h + c]], writes=[r_tc[2 * h + c], r_top] if (h == 7 and c == 1) else [r_tc[2 * h + c]])
            kb.op("dve", lambda e, top=top: e.tensor_tensor(out=cand[:].rearrange("p h (a b) -> p h a b", a=16),
                                                            in0=top[:, :, 0, :].unsqueeze(3).to_broadcast([128, 8, 16, 16]),
                                                            in1=top[:, :, 1, :].unsqueeze(2).to_broadcast([128, 8, 16, 16]), op=ALU.add),
                  reads=r_tc, writes=[r_cand, r_top])
            r_bc = [kb.res("bestc") for _ in range(8)]
            tmp2 = tmp[:].rearrange("p (h c) n -> p h (c n)", c=2)
            for h in range(8):
                kb.op("dve", lambda e, h=h: e.max(out=best[:, h, 0:8], in_=cand[:, h, :]), reads=[r_cand], writes=[r_bc[h]])
            for h in range(8):
                kb.op("dve", lambda e, h=h: e.match_replace(out=tmp2[:, h, :], in_to_replace=best[:, h, 0:8], in_values=cand[:, h, :], imm_value=-1e30),
                      reads=[r_cand, r_bc[h]], writes=[r_tmpc[2 * h], r_tmpc[2 * h + 1]])
            for h in range(8):
                kb.op("dve", lambda e, h=h: e.max(out=best[:, h, 8:16], in_=tmp2[:, h, :]), reads=[r_tmpc[2 * h], r_tmpc[2 * h + 1]],
                      writes=[r_bc[h], r_best] if h == 7 else [r_bc[h]])
            kb.op("dve", lambda e: e.tensor_copy(out=dd[:, 0, 0:1], in_=best[:, 0, 0:1]), reads=r_bc, writes=[r_best, r_sm])
            kb.op("dve", lambda e: e.tensor_tensor(out=dd[:], in0=best[:], in1=best[:, :, 0:1].to_broadcast([128, 8, 16]), op=ALU.subtract),
                  reads=[r_best], writes=[r_sm])
            kb.op("act", lambda e: e.activation(out=dd[:], in_=dd[:], func=AF.Exp), reads=[r_sm], writes=[r_sm])
            kb.op("dve", lambda e: e.tensor_reduce(out=zz[:, 0, :], in_=dd[:], axis=AX.X, op=ALU.add), reads=[r_sm], writes=[r_sm])
            kb.op("act", lambda e: e.activation(out=zz[:, 1, :], in_=zz[:, 0, :], func=AF.Ln), reads=[r_sm], writes=[r_sm])
            kb.op("dve", lambda e: e.tensor_tensor(out=zz[:, 2, :], in0=zz[:, 1, :], in1=best[:, :, 0], op=ALU.add), reads=[r_sm, r_best], writes=[r_sm])
            kb.op("dve", lambda e, top=top: e.tensor_tensor(out=cc[:], in0=top[:, :, 0, :], in1=zz[:, 2, :].unsqueeze(2).to_broadcast([128, 8, 16]),
                                                            op=ALU.subtract), reads=[r_sm, r_top], writes=[r_sm])
            kb.op("act", lambda e: e.activation(out=cb[:], in_=cc[:], func=AF.Exp), reads=[r_sm], writes=[r_sm])
            kb.op("dve", lambda e: e.tensor_scalar(out=zz[:, 3, :], in0=best[:, :, 15], scalar1=-DELTA, scalar2=None, op0=ALU.add),
                  reads=[r_best], writes=[r_sm])
            kb.op("dve", lambda e, top=top: e.tensor_tensor(out=thr[:], in0=zz[:, 3, :].unsqueeze(2).to_broadcast([128, 8, 16]), in1=top[:, :, 0, :],
                                                            op=ALU.subtract), reads=[r_sm, r_top], writes=[r_sm])
            kb.op("act", lambda e, s=s: e.activation(out=E2[:], in_=s[:, :, 1, :], func=AF.Exp), reads=[r_s], writes=[r_E2])
            kb.op("dve", lambda e, s=s, top=top: e.tensor_tensor(out=A[:], in0=s[:, :, 0, :].unsqueeze(2).to_broadcast([128, 8, 16, 128]),
                                                                 in1=top[:, :, 0, :].unsqueeze(3).to_broadcast([128, 8, 16, 128]), op=ALU.is_equal),
                  reads=[r_s, r_top], writes=[r_A])
            pt, r_pt = ps_tp.next()
            kb.op("pe", lambda e, pt=pt: e.transpose(out=pt[:, 0, :], in_=cb[:].rearrange("p h a -> p (h a)"), identity=ident[:]),
                  reads=[r_sm, r_id], writes=[r_pt])
            kb.op("act", lambda e, pt=pt: e.copy(out=cT[:], in_=pt[:, 0, :]), reads=[r_pt], writes=[r_cT])
            def build_B(jh):
                Bm_, r_Bm_ = Bms[jh]
                js = slice(jh * 64, (jh + 1) * 64)
                kb.op("dve", lambda e, s=s, js=js, Bm_=Bm_: e.tensor_tensor(out=Bm_[:], in0=s[:, :, 1, js].unsqueeze(2).to_broadcast([128, 8, 16, 64]),
                                                                            in1=thr[:].unsqueeze(3).to_broadcast([128, 8, 16, 64]), op=ALU.is_ge),
                      reads=[r_s, r_sm], writes=[r_Bm_])
                kb.op("pool", lambda e, js=js, Bm_=Bm_: e.tensor_tensor(out=Bm_[:], in0=Bm_[:], in1=E2[:, :, js].unsqueeze(2).to_broadcast([128, 8, 16, 64]),
                                                                        op=ALU.mult), reads=[r_E2, r_Bm_], writes=[r_Bm_])

            a_pts = []
            for i0 in range(0, 128, 16):
                pt, r_pt = ps_tp.next()
                for ii in range(16):
                    kb.op("pe", lambda e, pt=pt, ii=ii, i0=i0: e.transpose(out=pt[:, ii, :], in_=A2[:, :, i0 + ii], identity=ident[:]),
                          reads=[r_A, r_id], writes=[r_pt])
                if i0 == 0:
                    build_B(0)
                kb.op("dve", lambda e, pt=pt, i0=i0: e.tensor_tensor(out=AT[:, i0:i0 + 16, :], in0=pt[:],
                                                                     in1=cT[:].unsqueeze(1).to_broadcast([128, 16, 128]), op=ALU.mult),
                      reads=[r_pt, r_cT], writes=[r_AT])
            build_B(1)
            if n + 1 < NT:
                nxt = front1(n + 1)
            for jh in range(2):
                Bm_, r_Bm_ = Bms[jh]
                B2_ = Bm_[:].rearrange("p h a j -> p (h a) j")
                js = slice(jh * 64, (jh + 1) * 64)
                for j0 in range(0, 64, 16):
                    pt, r_pt = ps_tp.next()
                    for jj in range(16):
                        kb.op("pe", lambda e, pt=pt, jj=jj, j0=j0, B2_=B2_: e.transpose(out=pt[:, jj, :], in_=B2_[:, :, j0 + jj], identity=ident[:]),
                              reads=[r_Bm_, r_id], writes=[r_pt])
                    kb.op("act", lambda e, pt=pt, j0=j0: e.copy(out=BT[:, j0:j0 + 16, :], in_=pt[:]), reads=[r_pt], writes=[r_BT])
                gt, r_gt = g_pool.next()
                for t0 in range(0, 128, 8):
                    pg, r_pg = ps_g.next()
                    for tt in range(8):
                        kb.op("pe", lambda e, pg=pg, tt=tt, t0=t0: e.matmul(pg[:, tt, :], lhsT=AT[:, :, t0 + tt], rhs=BT[:, :, t0 + tt], start=True, stop=True),
                              reads=[r_AT, r_BT], writes=[r_pg])
                    kb.op("act", lambda e, pg=pg, gt=gt, t0=t0: e.copy(out=gt[:, :, t0:t0 + 8], in_=pg[:].rearrange("p t j -> p j t")),
                          reads=[r_pg], writes=[r_gt])
                kb.dma("pool", GT[n, :, js, :], gt[:], reads=[r_gt], writes=[r_GT], chan_res=r_gt)
        kb.end_phase()

    def phase5(l, X1, r_X1, X2, r_X2):
        kb.begin_phase()
        TBT = min(TBT_MAX, NT)
        TB = TBT * 128
        nblk = NT // TBT
        NSC = 64
        xblk = kb.sbuf("xblk", [128, KC, TB], BF16); r_xblk = kb.res("xblk")
        y_sb = kb.sbuf("ysb", [128, TBT, D], F32)
        r_ysb = [[kb.res("ysb") for _ in range(4)] for _ in range(TBT)]
        ty_pool = Pool(kb, "ty", [128, 512], F32, 2)
        un_pool = Pool(kb, "un", [128, 2, D], BF16, 2)
        v_pool = Pool(kb, "vv", [128, 2, D], BF16, 3)
        uT_pool = Pool(kb, "uT", [128, KC, 128], BF16, 4)
        g_pool = Pool(kb, "gg", [128, TBT, 2, 128], BF16, 2)
        ge_pool = Pool(kb, "ge", [128, 512], BF16, 2)
        hs_pool = Pool(kb, "hs", [128, TB], BF16, 4)
        ps_tp = Pool(kb, "tp", [128, 8, 128], BF16, 3, space="psum")
        ps_a = Pool(kb, "pa", [128, 512], F32, 2, space="psum")
        ps_y = Pool(kb, "py", [128, 512], F32, 3, space="psum")
        U3 = peer_u[l].rearrange("(i j) d -> i j d", j=128)
        V3 = peer_v[l].rearrange("(i j) d -> i j d", j=128)
        for blk in range(nblk):
            t0 = blk * TB
            kb.dma("sp", xblk[:], XN2T[:, :, t0:t0 + TB].rearrange("c p t -> p c t"), reads=[r_XN2T], writes=[r_xblk], chan_res=r_xblk)
            ctx = {}
            for tile in range(TBT):
                tok = t0 + tile * 128
                kb.dma("sp", y_sb[:, tile, :], X1[tok:tok + 128, :], reads=[r_X1], writes=r_ysb[tile], chan_res=r_ysb[tile][1])

            def do_L(sc):
                j0 = 2 * sc
                c = Ctx()
                c.un, c.r_un = un_pool.next()
                kb.dma("pool", c.un[:], U3[:, j0:j0 + 2, :], reads=[r_const], writes=[c.r_un], chan_res=c.r_un)
                ctx[sc] = c

            def do_L2(sc):
                j0 = 2 * sc
                c = ctx[sc]
                c.gg, c.r_gg = g_pool.next()
                c.vv, c.r_vv = v_pool.next()
                kb.dma("sp", c.gg[:], GT[blk * TBT:(blk + 1) * TBT, :, j0:j0 + 2, :].rearrange("n i j t -> i n j t"), reads=[r_GT],
                       writes=[c.r_gg], chan_res=c.r_gg)
                kb.dma("pool", c.vv[:], V3[:, j0:j0 + 2, :], reads=[r_const], writes=[c.r_vv], chan_res=c.r_vv)

            def do_T(sc):
                c = ctx[sc]
                c.uT = []
                for jj in range(2):
                    uT, r_uT = uT_pool.next()
                    for hf in range(2):
                        pt, r_pt = ps_tp.next()
                        for k8 in range(8):
                            k = hf * 8 + k8
                            kb.op("pe", lambda e, k=k, k8=k8, jj=jj, pt=pt, un=c.un: e.transpose(out=pt[:, k8, :], in_=un[:, jj, k * 128:(k + 1) * 128],
                                                                                                identity=ident[:]),
                                  reads=[c.r_un, r_id], writes=[r_pt])
                        kb.op("act", lambda e, pt=pt, uT=uT, hf=hf: e.copy(out=uT[:, hf * 8:(hf + 1) * 8, :], in_=pt[:]), reads=[r_pt], writes=[r_uT])
                    c.uT.append((uT, r_uT))

            def do_A(sc):
                c = ctx[sc]
                c.hs = []
                for jj in range(2):
                    uT, r_uT = c.uT[jj]
                    h_t, r_ht = hs_pool.next()
                    for half in range(TB // 512):
                        pa, r_pa = ps_a.next()
                        for k in range(KC):
                            kb.op("pe", lambda e, k=k, pa=pa, uT=uT, half=half: e.matmul(pa[:], lhsT=uT[:, k, :], rhs=xblk[:, k, half * 512:(half + 1) * 512],
                                                                                          start=(k == 0), stop=(k == KC - 1)),
                                  reads=[r_uT, r_xblk], writes=[r_pa])
                        ge, r_ge = ge_pool.next()
                        kb.op("act", lambda e, pa=pa, ge=ge: e.activation(out=ge[:], in_=pa[:], func=AF.Gelu_apprx_tanh), reads=[r_pa], writes=[r_ge])
                        kb.op("dve", lambda e, ge=ge, h_t=h_t, gg=c.gg, jj=jj, half=half: e.tensor_tensor(
                            out=h_t[:, half * 512:(half + 1) * 512].rearrange("p (n t) -> p n t", t=128), in0=ge[:].rearrange("p (n t) -> p n t", t=128),
                            in1=gg[:, half * 4:(half + 1) * 4, jj, :], op=ALU.mult), reads=[r_ge, c.r_gg], writes=[r_ht])
                    c.hs.append((h_t, r_ht))

            def do_Y(sc):
                c = ctx[sc]
                for tile in range(TBT):
                    for nb in range(4):
                        py, r_py = ps_y.next()
                        for jj in range(2):
                            h_t, r_ht = c.hs[jj]
                            kb.op("pe", lambda e, jj=jj, py=py, h_t=h_t, vv=c.vv, tile=tile, nb=nb: e.matmul(
                                py[:], lhsT=h_t[:, tile * 128:(tile + 1) * 128], rhs=vv[:, jj, nb * 512:(nb + 1) * 512], start=(jj == 0), stop=(jj == 1)),
                                reads=[r_ht, c.r_vv], writes=[r_py])
                        ysl = y_sb[:, tile, nb * 512:(nb + 1) * 512]
                        if (tile * 4 + nb) % 3 == 2:
                            ty, r_ty = ty_pool.next()
                            kb.op("act", lambda e, py=py, ty=ty: e.copy(out=ty[:], in_=py[:]), reads=[r_py], writes=[r_ty])
                            kb.op("pool", lambda e, ty=ty, ysl=ysl: e.tensor_tensor(out=ysl, in0=ty[:], in1=ysl, op=ALU.add),
                                  reads=[r_ty, r_ysb[tile][nb]], writes=[r_ysb[tile][nb]])
                        else:
                            kb.op("dve", lambda e, py=py, ysl=ysl: e.tensor_tensor(out=ysl, in0=py[:], in1=ysl, op=ALU.add),
                                  reads=[r_py, r_ysb[tile][nb]], writes=[r_ysb[tile][nb]])
                del ctx[sc]

            do_L(0)
            do_L2(0)
            do_L(1)
            do_L2(1)
            do_T(0)
            for i in range(NSC):
                if i + 2 < NSC:
                    do_L(i + 2)
                if i + 1 < NSC:
                    do_T(i + 1)
                do_A(i)
                if i >= 1:
                    do_Y(i - 1)
                if i + 2 < NSC:
                    do_L2(i + 2)
            do_Y(NSC - 1)
            for tile in range(TBT):
                tok = t0 + tile * 128
                kb.dma("pool", X2[tok:tok + 128, :], y_sb[:, tile, :], reads=r_ysb[tile], writes=[r_X2], chan_res=r_ysb[tile][0])
        kb.end_phase()

    def phasef(x_src, r_xsrc):
        kb.begin_phase()
        fn = kb.sbuf("fn", [128, D], F32); r_fn = kb.res("fn")
        kb.dma("sp", fn[:], fnorm_in, reads=[r_const], writes=[r_fn], chan_res=r_fn)
        xt_pool = Pool(kb, "xt", [128, D], F32, 2)
        yo_pool = Pool(kb, "yo", [128, D], F32, 2)
        sq_pool = Pool(kb, "sq", [128, D], BF16, 1)
        st_pool = Pool(kb, "stat", [128, 4], F32, 4)
        for n in range(NT):
            xt, r_xt = xt_pool.next()
            kb.dma("sp", xt[:], x_src[n * 128:(n + 1) * 128, :], reads=[r_xsrc], writes=[r_xt], chan_res=r_xt)
            stt, r_st = st_pool.next()
            sq, r_sq = sq_pool.next()
            kb.op("act", lambda e, xt=xt, stt=stt, sq=sq: e.activation(out=sq[:], in_=xt[:], func=AF.Square, accum_out=stt[:, 0:1]),
                  reads=[r_xt], writes=[r_sq, r_st])
            kb.op("act", lambda e, stt=stt: e.activation(out=stt[:, 0:1], in_=stt[:, 0:1], func=AF.Ln, bias=epsb[:, 0:1], scale=1.0 / D),
                  reads=[r_st, r_gn], writes=[r_st])
            kb.op("act", lambda e, stt=stt: e.activation(out=stt[:, 0:1], in_=stt[:, 0:1], func=AF.Exp, scale=-0.5), reads=[r_st], writes=[r_st])
            yo, r_yo = yo_pool.next()
            kb.op("dve", lambda e, xt=xt, stt=stt, yo=yo: e.scalar_tensor_tensor(out=yo[:], in0=xt[:], scalar=stt[:, 0:1], in1=fn[:],
                                                                                 op0=ALU.mult, op1=ALU.mult),
                  reads=[r_xt, r_st, r_fn], writes=[r_yo])
            kb.dma("pool", y_out[n * 128:(n + 1) * 128, :], yo[:], reads=[r_yo], writes=[r_y], chan_res=r_yo)
        kb.end_phase()

    for l in range(L):
        x_src, r_xsrc = (x_in, r_x) if l == 0 else (XB, r_XB)
        phase1(l, x_src, r_xsrc)
        if upto >= 2:
            phase1c(l)
        if upto >= 3:
            phase2a(l)
            phase2b(l)
            phase2c(l)
        if upto >= 4:
            phase3(l, x_src, r_xsrc, XA, r_XA)
        if upto >= 5:
            phase4a(l)
            phase4b(l)
        if upto >= 6:
            phase5(l, XA, r_XA, XB, r_XB)
    if upto >= 7:
        phasef(XB, r_XB)
    outs = [r_HT, r_VA, r_VB, r_QCN, r_QCP, r_KCN, r_KCP, r_VC, r_O, r_XA, r_XB, r_XN2T, r_QT, r_GT, r_y]
    kb.finish_wait("pool", outs)
    kb.emit()
    return nc


def host_consts(S):
    c = {}
    c["ident"] = np.eye(128, dtype=np.float32)
    R = np.zeros((64, 64), np.float32)
    for m in range(32):
        R[m + 32, m] = -1.0
        R[m, m + 32] = 1.0
    c["rotR"] = R
    inv = 10000.0 ** (-(np.arange(32, dtype=np.float64)) / 32.0)
    ang = np.arange(S, dtype=np.float64)[None, :] * np.concatenate([inv, inv])[:, None]
    c["cossin"] = np.stack([np.cos(ang), np.sin(ang)]).astype(np.float32)
    ki = np.arange(128)[:, None]
    qi = np.arange(128)[None, :]
    slopes = np.array([2.0 ** (-8.0 * (h + 1) / 8) for h in range(8)], np.float64)
    tabA = np.zeros((128, 3, 8, 128), np.float64)
    for d, (dist, valid) in enumerate([(128 + qi - ki, qi <= ki), (np.abs(qi - ki), np.ones((128, 128), bool)),
                                       (128 + ki - qi, ki <= qi)]):
        for h in range(8):
            tabA[:, d, h, :] = np.where(valid, np.exp(-slopes[h] * dist), 0.0)
    c["tabA"] = tabA.astype(np.float32).reshape(128, 3 * 8 * 128)
    rows = S // 64
    kr = min(8, rows)
    NT = S // 128
    tabs = {}
    sched = []
    masks, dridx, dcidx = [], [], []
    for n in range(NT):
        qrow = 2 * n + np.arange(128) // 64
        qcol = np.arange(128) % 64
        rstart = np.clip(qrow - kr // 2, 0, rows - kr)
        cstart = np.clip(qcol - 8, 0, 64 - 16)
        lst = []
        for m in range(NT):
            krow = 2 * m + np.arange(128) // 64
            kcol = np.arange(128) % 64
            valid = ((krow[:, None] >= rstart[None, :]) & (krow[:, None] < rstart[None, :] + kr)
                     & (kcol[:, None] >= cstart[None, :]) & (kcol[:, None] < cstart[None, :] + 16))
            if not valid.any():
                continue
            dr = np.clip(krow[:, None] - qrow[None, :] + 7, 0, 14)
            dc = np.clip(kcol[:, None] - qcol[None, :] + 15, 0, 30)
            dr = np.where(valid, dr, 0)
            dc = np.where(valid, dc, 0)
            key = (valid.tobytes(), dr.tobytes(), dc.tobytes())
            if key not in tabs:
                tabs[key] = len(tabs)
                masks.append(valid.astype(np.float32))
                dridx.append(dr)
                dcidx.append(dc)
            lst.append((m, tabs[key]))
        sched.append(lst)
    c["maskB"] = np.ascontiguousarray(np.stack(masks, 1))
    c["_dr"] = np.stack(dridx, 1)
    c["_dc"] = np.stack(dcidx, 1)
    c["_schedB"] = sched
    c["_ntab"] = len(masks)
    return c


def gather_biasB(b_rel_bias_l, c):
    g = b_rel_bias_l[:, c["_dr"], c["_dc"]]
    return np.ascontiguousarray(np.transpose(g, (1, 2, 0, 3))).astype(np.float32)


def gains_pack(inp, L):
    g = np.zeros((L, 128, 64), np.float32)
    for l in range(L):
        g[l, :, 0:16] = inp["ln1"][l].reshape(16, 128).T
        g[l, :, 16:20] = inp["c_q_norm"][l].reshape(4, 128).T
        g[l, :, 20:22] = inp["c_kv_norm"][l].reshape(2, 128).T
        g[l, :, 22:38] = inp["out_norm"][l].reshape(16, 128).T
        g[l, :, 38:54] = inp["ln2"][l].reshape(16, 128).T
    return g


def sk_pack(inp, L):
    sk = np.zeros((L, 128, 256), np.float32)
    for l in range(L):
        sk[l, 0:64, 0:128] = inp["peer_sub_keys"][l, 0].T
        sk[l, 64:128, 128:256] = inp["peer_sub_keys"][l, 1].T
    return sk


def core_inputs(inp, xb, S, L, consts):
    m = dict(x=np.ascontiguousarray(xb), w_in=inp["w_in"][:L], gains=gains_pack(inp, L), ident=consts["ident"],
             cossin=consts["cossin"], rotR=consts["rotR"], tabA=consts["tabA"], maskB=consts["maskB"],
             biasB=np.stack([gather_biasB(inp["b_rel_bias"][l], consts) for l in range(L)]),
             sinkb=np.ascontiguousarray(np.broadcast_to(inp["a_sink"][:L][None], (128, L, 8))).astype(np.float32),
             c_w_uq=inp["c_w_uq"][:L], c_w_ukv=inp["c_w_ukv"][:L], w_o=inp["w_o"][:L],
             peer_w_q=inp["peer_w_q"][:L], peer_u=inp["peer_u"][:L], peer_v=inp["peer_v"][:L],
             sk=sk_pack(inp, L),
             ln2b=np.ascontiguousarray(np.broadcast_to(inp["ln2"][:L][:, None, :], (L, 128, 2048))).astype(np.float32),
             fnorm=np.ascontiguousarray(np.broadcast_to(inp["final_norm"][None], (128, 2048))).astype(np.float32))
    return m


N_CORES = 8
ACTIVE = [0, 1, 4, 5]
_CACHE = {}


def kernel(**inputs):
    inp = {k: np.asarray(v) for k, v in inputs.items()}
    B, S, _ = inp["x"].shape
    L = inp["w_in"].shape[0]
    key = (S, L)
    if key not in _CACHE:
        consts = host_consts(S)
        _CACHE[key] = (consts, build(S, L, consts, debug=False))
    consts, nc = _CACHE[key]
    active = ACTIVE[:B]
    in_maps = [None] * N_CORES
    for b in range(B):
        m = core_inputs(inp, inp["x"][b], S, L, consts)
        in_maps[active[b]] = {k: np.ascontiguousarray(v, dtype=np.float32) for k, v in m.items()}
    zero_map = {k: np.zeros_like(v) for k, v in in_maps[active[0]].items()}
    for c in range(N_CORES):
        if in_maps[c] is None:
            in_maps[c] = zero_map
    res = run_bass_kernel_spmd(nc, in_maps, core_ids=list(range(N_CORES)))
    out = np.stack([np.asarray(res.results[active[b]]["y"], dtype=np.float32) for b in range(B)], axis=0)
    return out
```
